# Optimizing a Trainium2 kernel written in Bass

```python
import functools
import jax, jax.numpy as jnp
from jax import lax
import numpy as np

D_MODEL = 1024
BATCH = 2
SEQ = 8192
DEPTH = 2
DEC_BATCH = 128
DEC_SEQ = 1
PAST_LEN = 2048
PAGE_SIZE = 128

HEAD_DIM = 64
QBLK = 128
NSA_HEADS = 8
NSA_KV_GROUPS = 2
NSA_REP = NSA_HEADS // NSA_KV_GROUPS
NSA_Q_W = NSA_HEADS * HEAD_DIM
NSA_KV_W = NSA_KV_GROUPS * HEAD_DIM
CMP_BLOCK = 32
SEL_BLOCK = 64
N_SEL = 16
NSA_WINDOW = 512
FORCE_SCORE = 1.0e4
D_RNN = D_MODEL // 2
RG_BLOCKS = 8
RG_BW = D_RNN // RG_BLOCKS
RG_C = 8.0
CONV_W = 4
CHUNK = 128
C_W = D_MODEL // 2
C_GROUPS = 8
C_GW = C_W // C_GROUPS
DIL_HEADS = 8
DIL_W = DIL_HEADS * HEAD_DIM
DIL_PATTERNS = ((128, 1), (512, 4), (2048, 16))
DIL_MAX = 2048
FFN_HIDDEN = -(-8 * D_MODEL // (3 * 256)) * 256
AB_SIZES = (NSA_Q_W, NSA_KV_W, NSA_KV_W, NSA_KV_W, NSA_KV_W, NSA_KV_W, NSA_KV_W, NSA_HEADS * 3, D_RNN, D_RNN)
CD_SIZES = (C_W, C_W, DIL_W, DIL_W, DIL_W)
IN_AB = NSA_Q_W + 6 * NSA_KV_W + NSA_HEADS * 3 + 2 * D_RNN
IN_CD = 2 * C_W + 3 * DIL_W
OUT_AB = NSA_Q_W + D_RNN
OUT_CD = C_W + DIL_W
RMS_EPS = 1e-6
NEG_INF = -1e30

kernel_name = "hybrid_nsa_rglru_gmlp_dilated_step"


def rms_norm(x, g):
    xf = x.astype(jnp.float32)
    y = xf * lax.rsqrt(jnp.mean(xf * xf, axis=-1, keepdims=True) + RMS_EPS)
    return (y * g.astype(jnp.float32)).astype(x.dtype)


def split_cols(z, sizes):
    offs = np.cumsum(sizes)[:-1].tolist()
    return jnp.split(z, offs, axis=-1)


def masked_softmax(s, mask):
    s = jnp.where(mask, s.astype(jnp.float32), NEG_INF)
    m = jnp.max(s, axis=-1, keepdims=True)
    e = jnp.where(mask, jnp.exp(s - m), 0.0)
    den = jnp.maximum(jnp.sum(e, axis=-1, keepdims=True), 1e-30)
    return e / den, (m + jnp.log(den))[..., 0]


def block_attend(q, k, v, mask):
    s = jnp.einsum('bnqgrd,bnkgd->bnqgrk', q, k) * (q.shape[-1] ** -0.5)
    p, _ = masked_softmax(s, mask[None, :, :, None, None, :])
    return jnp.einsum('bnqgrk,bnkgd->bnqgrd', p.astype(v.dtype), v)


def nsa_compress(rows, w1, w2, pos):
    B, L, G, hd = rows.shape
    blk = rows.reshape(B, L // CMP_BLOCK, CMP_BLOCK, G, hd) + pos[None, None, :, None, :]
    hid = jax.nn.gelu(jnp.einsum('bnlgd,lde->bnge', blk, w1))
    return jnp.einsum('bnge,ef->bngf', hid, w2)


def nsa_global_branches(q, q_pos, kc, vc, ks, vs):
    B, T, G, R, hd = q.shape
    NC = kc.shape[1]
    NB = ks.shape[1] // SEL_BLOCK
    scale = hd ** -0.5
    s = jnp.einsum('btgrd,bngd->btgrn', q, kc) * scale
    cmp_end = (jnp.arange(NC) + 1) * CMP_BLOCK - 1
    cmask = cmp_end[None, :] <= q_pos[:, None]
    p, _ = masked_softmax(s, cmask[None, :, None, None, :])
    o_cmp = jnp.einsum('btgrn,bngd->btgrd', p.astype(vc.dtype), vc)
    imp = p.sum(axis=3).reshape(B, T, G, NB, SEL_BLOCK // CMP_BLOCK).sum(-1)
    blk = jnp.arange(NB)
    forced = (blk[None, :] == 0) | (blk[None, :] == (q_pos // SEL_BLOCK)[:, None])
    valid = blk[None, :] * SEL_BLOCK <= q_pos[:, None]
    imp = jnp.where(forced[None, :, None, :], FORCE_SCORE, imp)
    imp = jnp.where(valid[None, :, None, :], imp, -1.0)
    n_sel = min(N_SEL, NB)
    _, idx = lax.top_k(imp, n_sel)
    idx_g = idx.transpose(0, 2, 1, 3)
    take = jax.vmap(jax.vmap(lambda blocks, i: blocks[i]))
    kb = ks.reshape(B, NB, SEL_BLOCK, G, hd).transpose(0, 3, 1, 2, 4)
    vb = vs.reshape(B, NB, SEL_BLOCK, G, hd).transpose(0, 3, 1, 2, 4)
    k_sel = take(kb, idx_g).reshape(B, G, T, n_sel * SEL_BLOCK, hd)
    v_sel = take(vb, idx_g).reshape(B, G, T, n_sel * SEL_BLOCK, hd)
    kpos = (idx_g[..., None] * SEL_BLOCK + jnp.arange(SEL_BLOCK)).reshape(B, G, T, n_sel * SEL_BLOCK)
    smask = (kpos <= q_pos[None, None, :, None]).transpose(0, 2, 1, 3)[:, :, :, None, :]
    s2 = jnp.einsum('btgrd,bgtkd->btgrk', q, k_sel) * scale
    p2, _ = masked_softmax(s2, smask)
    o_sel = jnp.einsum('btgrk,bgtkd->btgrd', p2.astype(v_sel.dtype), v_sel)
    return o_cmp, o_sel


def banded_window_attn(q, k, v):
    B, T, G, R, hd = q.shape
    nq, nb = T // QBLK, NSA_WINDOW // QBLK
    pad = ((0, 0), (NSA_WINDOW, 0), (0, 0), (0, 0))
    kp = jnp.pad(k, pad).reshape(B, nq + nb, QBLK, G, hd)
    vp = jnp.pad(v, pad).reshape(B, nq + nb, QBLK, G, hd)
    kband = jnp.concatenate([kp[:, j:j + nq] for j in range(nb + 1)], axis=2)
    vband = jnp.concatenate([vp[:, j:j + nq] for j in range(nb + 1)], axis=2)
    qpos = jnp.arange(nq)[:, None] * QBLK + jnp.arange(QBLK)[None, :]
    kpos = jnp.arange(nq)[:, None] * QBLK - NSA_WINDOW + jnp.arange((nb + 1) * QBLK)[None, :]
    dist = qpos[:, :, None] - kpos[:, None, :]
    mask = (kpos[:, None, :] >= 0) & (dist >= 0) & (dist <= NSA_WINDOW)
    o = block_attend(q.reshape(B, nq, QBLK, G, R, hd), kband, vband, mask)
    return o.reshape(B, T, G, R, hd)


def causal_conv(x, buf, w, b):
    T = x.shape[1]
    xp = jnp.concatenate([buf.astype(x.dtype), x], axis=1)
    y = b + sum(xp[:, i:i + T] * w[i] for i in range(CONV_W))
    return y, xp[:, T:]


def rg_lru(x, h0, wa, ba, wx, bx, lam):
    B, T, C = x.shape
    xb = x.reshape(B, T, RG_BLOCKS, RG_BW)
    r = jax.nn.sigmoid((jnp.einsum('btnc,ncd->btnd', xb, wa).reshape(B, T, C) + ba).astype(jnp.float32))
    i = jax.nn.sigmoid((jnp.einsum('btnc,ncd->btnd', xb, wx).reshape(B, T, C) + bx).astype(jnp.float32))
    log_a = -RG_C * r * jax.nn.softplus(-lam.astype(jnp.float32))
    a = jnp.exp(log_a)
    u = jnp.sqrt(-jnp.expm1(2.0 * log_a)) * (i * x.astype(jnp.float32))
    u = u.at[:, 0].add(a[:, 0] * h0.astype(jnp.float32))

    def combine(left, right):
        a_l, b_l = left
        a_r, b_r = right
        return a_l * a_r, a_r * b_l + b_r

    _, hs = lax.associative_scan(combine, (a, u), axis=1)
    return hs.astype(x.dtype), hs[:, -1].astype(h0.dtype)


def dilated_attn(q, q_pos, k, v, k_start):
    Lk = k.shape[1]
    scale = q.shape[-1] ** -0.5
    outs, lses = [], []
    for window, dil in DIL_PATTERNS:
        dist = jnp.arange(window // dil + 1, dtype=jnp.int32) * dil
        kpos = q_pos[:, None] - dist[None, :]
        row = kpos - k_start
        valid = (kpos >= 0) & (row >= 0)
        row = jnp.clip(row, 0, Lk - 1)
        kg, vg = k[:, row], v[:, row]
        s = jnp.einsum('bthd,btjhd->bthj', q, kg) * scale
        p, lse = masked_softmax(s, valid[None, :, None, :])
        outs.append(jnp.einsum('bthj,btjhd->bthd', p.astype(vg.dtype), vg))
        lses.append(lse)
    wts = jax.nn.softmax(jnp.stack(lses), axis=0)
    return jnp.sum(wts[..., None].astype(q.dtype) * jnp.stack(outs), axis=0)


def nsa_rglru_mixer(h, prm, past):
    B, T, _ = h.shape
    past_len = 0 if past is None else PAST_LEN
    q_pos = past_len + jnp.arange(T, dtype=jnp.int32)
    z = jnp.einsum('btd,de->bte', h, prm['w_in'])
    q, kc_t, vc_t, ks, vs, kw, vw, gl, xr, gr = split_cols(z, AB_SIZES)
    q = rms_norm(q.reshape(B, T, NSA_KV_GROUPS, NSA_REP, HEAD_DIM), prm['q_gain'])
    kv = lambda t: t.reshape(B, T, NSA_KV_GROUPS, HEAD_DIM)
    kc_t, vc_t, vs, vw = kv(kc_t), kv(vc_t), kv(vs), kv(vw)
    ks = rms_norm(kv(ks), prm['k_gain'][1])
    kw = rms_norm(kv(kw), prm['k_gain'][2])
    rows = jnp.stack([kc_t, vc_t, ks, vs], axis=2)
    win_rows = jnp.stack([kw, vw], axis=2)
    if past is None:
        full = rows
    else:
        past_rows = past['cache'][past['page_table']]
        full = jnp.concatenate([past_rows.reshape(B, -1, 4, NSA_KV_GROUPS, HEAD_DIM), rows], axis=1)
    L = full.shape[1]
    full = jnp.pad(full, ((0, 0), (0, -L % SEL_BLOCK), (0, 0), (0, 0), (0, 0)))
    kc = rms_norm(nsa_compress(full[:, :, 0], prm['cmp_w1'][0], prm['cmp_w2'][0], prm['cmp_pos'][0]), prm['k_gain'][0])
    vc = nsa_compress(full[:, :, 1], prm['cmp_w1'][1], prm['cmp_w2'][1], prm['cmp_pos'][1])
    ks_all, vs_all = full[:, :, 2], full[:, :, 3]
    if past is None:
        nq = T // QBLK
        qb = q.reshape(B, nq, QBLK, NSA_KV_GROUPS, NSA_REP, HEAD_DIM).swapaxes(0, 1)
        pb = q_pos.reshape(nq, QBLK)
        o_cmp, o_sel = lax.map(lambda a: nsa_global_branches(a[0], a[1], kc, vc, ks_all, vs_all), (qb, pb))
        o_cmp = o_cmp.swapaxes(0, 1).reshape(q.shape)
        o_sel = o_sel.swapaxes(0, 1).reshape(q.shape)
        o_win = banded_window_attn(q, kw, vw)
        new_win = win_rows[:, -min(NSA_WINDOW, T):]
    else:
        o_cmp, o_sel = nsa_global_branches(q, q_pos, kc, vc, ks_all, vs_all)
        wb = past['win'].shape[1]
        buf = jnp.concatenate([past['win'].astype(win_rows.dtype), win_rows], axis=1)
        kpos = PAST_LEN - wb + jnp.arange(wb + T)
        dist = q_pos[:, None] - kpos[None, :]
        mask = ((dist >= 0) & (dist <= NSA_WINDOW))[None]
        o_win = block_attend(q[:, None], buf[:, None, :, 0], buf[:, None, :, 1], mask)[:, 0]
        new_win = buf[:, -min(NSA_WINDOW, PAST_LEN + T):]
    g = jax.nn.sigmoid(gl.astype(jnp.float32)).reshape(B, T, NSA_KV_GROUPS, NSA_REP, 3).astype(h.dtype)
    o_nsa = (g[..., 0:1] * o_cmp + g[..., 1:2] * o_sel + g[..., 2:3] * o_win).reshape(B, T, NSA_Q_W)
    conv_buf = jnp.zeros((B, CONV_W - 1, D_RNN), h.dtype) if past is None else past['conv']
    h0 = jnp.zeros((B, D_RNN), h.dtype) if past is None else past['h']
    xc, new_conv = causal_conv(xr, conv_buf, prm['conv_w'], prm['conv_b'])
    hs, h_last = rg_lru(xc, h0, prm['wa'], prm['ba'], prm['wx'], prm['bx'], prm['lam'])
    o_rnn = hs * jax.nn.gelu(gr)
    out = jnp.einsum('bte,ed->btd', jnp.concatenate([o_nsa.astype(h.dtype), o_rnn], axis=-1), prm['w_out'])
    return out, (rows, new_win, h_last, new_conv)


def gmlp_dilated_mixer(h, prm, past):
    B, T, _ = h.shape
    past_len = 0 if past is None else PAST_LEN
    q_pos = past_len + jnp.arange(T, dtype=jnp.int32)
    z = jnp.einsum('btd,de->bte', h, prm['w_in'])
    u, v, q, k, vv = split_cols(z, CD_SIZES)
    u = jax.nn.gelu(u)
    v = rms_norm(jax.nn.gelu(v), prm['v_gain'])
    lc = min(T, CHUNK)
    ws = prm['ws'][:, :lc, :lc] * jnp.tril(jnp.ones((lc, lc), h.dtype))
    vch = v.reshape(B, T // lc, lc, C_GROUPS, C_GW)
    mixed = jnp.einsum('gts,bnsgc->bntgc', ws, vch) + prm['bs'][:, :lc].T[None, None, :, :, None]
    o_c = u * mixed.reshape(B, T, C_W)
    q = rms_norm(q.reshape(B, T, DIL_HEADS, HEAD_DIM), prm['q_gain'])
    k = rms_norm(k.reshape(B, T, DIL_HEADS, HEAD_DIM), prm['k_gain'])
    vv = vv.reshape(B, T, DIL_HEADS, HEAD_DIM)
    kv_rows = jnp.stack([k, vv], axis=2)
    if past is None:
        pad = ((0, 0), (DIL_MAX, 0), (0, 0), (0, 0))
        kp, vp = jnp.pad(k, pad), jnp.pad(vv, pad)
        nq = T // QBLK
        qb = q.reshape(B, nq, QBLK, DIL_HEADS, HEAD_DIM).swapaxes(0, 1)
        pb = q_pos.reshape(nq, QBLK)
        o_d = lax.map(lambda a: dilated_attn(a[0], a[1], kp, vp, -DIL_MAX), (qb, pb))
        o_d = o_d.swapaxes(0, 1).reshape(B, T, DIL_W)
        new_kv = kv_rows[:, -min(DIL_MAX, T):]
    else:
        wb = past['dil'].shape[1]
        buf = jnp.concatenate([past['dil'].astype(kv_rows.dtype), kv_rows], axis=1)
        o_d = dilated_attn(q, q_pos, buf[:, :, 0], buf[:, :, 1], PAST_LEN - wb).reshape(B, T, DIL_W)
        new_kv = buf[:, -min(DIL_MAX, PAST_LEN + T):]
    out = jnp.einsum('bte,ed->btd', jnp.concatenate([o_c, o_d.astype(h.dtype)], axis=-1), prm['w_out'])
    return out, (new_kv, v)


def residual_layer(x, c, ln_mix, ln_ffn, w_ada_l, b_ada_l, w_gate, w_up, w_down, mixer):
    mod = jnp.einsum('bd,de->be', jax.nn.silu(c), w_ada_l) + b_ada_l
    sh_m, sc_m, g_m, sh_f, sc_f, g_f = jnp.split(mod[:, None, :], 6, axis=-1)
    out, state = mixer(rms_norm(x, ln_mix) * (1 + sc_m) + sh_m)
    x = x + g_m * out.astype(x.dtype)
    hf = rms_norm(x, ln_ffn) * (1 + sc_f) + sh_f
    ffn = jnp.einsum('btf,fd->btd', jax.nn.silu(hf @ w_gate) * (hf @ w_up), w_down)
    return x + g_f * ffn, state


def setup_inputs(seed: int = 0) -> dict:
    key = jax.random.key(seed)
    keys = iter(jax.random.split(key, 64))

    def nrm(shape, scale):
        return jax.random.normal(next(keys), shape, jnp.float32) * scale

    def gain(shape):
        return 1.0 + nrm(shape, 0.1)

    n_even, n_odd = (DEPTH + 1) // 2, DEPTH // 2
    n_pages = PAST_LEN // PAGE_SIZE
    n_used = DEC_BATCH * n_pages
    n_pool = n_used + (n_used + 3) // 4
    page_table = jax.random.permutation(next(keys), n_pool)[:n_used].reshape(DEC_BATCH, n_pages).astype(jnp.int32)
    win_buf = min(NSA_WINDOW, PAST_LEN)
    dil_buf = min(DIL_MAX, PAST_LEN)
    a0 = jax.random.uniform(next(keys), (n_even, D_RNN), jnp.float32, 0.9, 0.999)
    return {
        "x_prompt": nrm((BATCH, SEQ, D_MODEL), 1.0),
        "x_sample": nrm((DEC_BATCH, DEC_SEQ, D_MODEL), 1.0),
        "cache_nsa_kv": nrm((n_even, n_pool, PAGE_SIZE, 4, NSA_KV_GROUPS, HEAD_DIM), 1.0),
        "state_nsa_win": nrm((n_even, DEC_BATCH, win_buf, 2, NSA_KV_GROUPS, HEAD_DIM), 1.0),
        "state_rglru_h": nrm((n_even, DEC_BATCH, D_RNN), 0.5),
        "state_rglru_conv": nrm((n_even, DEC_BATCH, CONV_W - 1, D_RNN), 1.0),
        "state_dil_kv": nrm((n_odd, DEC_BATCH, dil_buf, 2, DIL_HEADS, HEAD_DIM), 1.0),
        "page_table": page_table,
        "c_prompt": nrm((BATCH, D_MODEL), 1.0),
        "c_sample": nrm((DEC_BATCH, D_MODEL), 1.0),
        "norm_mix_g": gain((DEPTH, D_MODEL)),
        "norm_ffn_g": gain((DEPTH, D_MODEL)),
        "w_ada": nrm((DEPTH, D_MODEL, 6 * D_MODEL), 0.5 * D_MODEL ** -0.5),
        "b_ada": nrm((DEPTH, 6 * D_MODEL), 0.02),
        "w_ffn_gate": nrm((DEPTH, D_MODEL, FFN_HIDDEN), D_MODEL ** -0.5),
        "w_ffn_up": nrm((DEPTH, D_MODEL, FFN_HIDDEN), D_MODEL ** -0.5),
        "w_ffn_down": nrm((DEPTH, FFN_HIDDEN, D_MODEL), FFN_HIDDEN ** -0.5),
        "w_in_ab": nrm((n_even, D_MODEL, IN_AB), D_MODEL ** -0.5),
        "w_out_ab": nrm((n_even, OUT_AB, D_MODEL), OUT_AB ** -0.5),
        "nsa_q_gain": gain((n_even, HEAD_DIM)),
        "nsa_k_gain": gain((n_even, 3, HEAD_DIM)),
        "nsa_cmp_w1": nrm((n_even, 2, CMP_BLOCK, HEAD_DIM, HEAD_DIM), (CMP_BLOCK * HEAD_DIM) ** -0.5),
        "nsa_cmp_w2": nrm((n_even, 2, HEAD_DIM, HEAD_DIM), HEAD_DIM ** -0.5),
        "nsa_cmp_pos": nrm((n_even, 2, CMP_BLOCK, HEAD_DIM), 0.1),
        "rg_conv_w": nrm((n_even, CONV_W, D_RNN), CONV_W ** -0.5),
        "rg_conv_b": nrm((n_even, D_RNN), 0.02),
        "rg_wa": nrm((n_even, RG_BLOCKS, RG_BW, RG_BW), RG_BW ** -0.5),
        "rg_ba": nrm((n_even, D_RNN), 0.1),
        "rg_wx": nrm((n_even, RG_BLOCKS, RG_BW, RG_BW), RG_BW ** -0.5),
        "rg_bx": nrm((n_even, D_RNN), 0.1),
        "rg_lambda": jnp.log(a0) - jnp.log1p(-a0),
        "w_in_cd": nrm((n_odd, D_MODEL, IN_CD), D_MODEL ** -0.5),
        "w_out_cd": nrm((n_odd, OUT_CD, D_MODEL), OUT_CD ** -0.5),
        "gmlp_v_gain": gain((n_odd, C_W)),
        "gmlp_ws": nrm((n_odd, C_GROUPS, CHUNK, CHUNK), CHUNK ** -0.5),
        "gmlp_bs": 1.0 + nrm((n_odd, C_GROUPS, CHUNK), 0.1),
        "dil_q_gain": gain((n_odd, HEAD_DIM)),
        "dil_k_gain": gain((n_odd, HEAD_DIM)),
    }


def reference(x_prompt, x_sample, cache_nsa_kv, state_nsa_win, state_rglru_h, state_rglru_conv, state_dil_kv,
              page_table, c_prompt, c_sample, norm_mix_g, norm_ffn_g, w_ada, b_ada, w_ffn_gate, w_ffn_up,
              w_ffn_down, w_in_ab, w_out_ab, nsa_q_gain, nsa_k_gain, nsa_cmp_w1, nsa_cmp_w2, nsa_cmp_pos,
              rg_conv_w, rg_conv_b, rg_wa, rg_ba, rg_wx, rg_bx, rg_lambda, w_in_cd, w_out_cd, gmlp_v_gain,
              gmlp_ws, gmlp_bs, dil_q_gain, dil_k_gain):
    yp, ys = x_prompt, x_sample
    kv_p, kv_s, win_p, win_s, h_p, h_s, conv_p, conv_s, dil_p, dil_s, gv_s = ([] for _ in range(11))
    for layer in range(DEPTH):
        i = layer // 2
        common = (norm_mix_g[layer], norm_ffn_g[layer], w_ada[layer], b_ada[layer],
                  w_ffn_gate[layer], w_ffn_up[layer], w_ffn_down[layer])
        if layer % 2 == 0:
            prm = dict(w_in=w_in_ab[i], w_out=w_out_ab[i], q_gain=nsa_q_gain[i], k_gain=nsa_k_gain[i],
                       cmp_w1=nsa_cmp_w1[i], cmp_w2=nsa_cmp_w2[i], cmp_pos=nsa_cmp_pos[i],
                       conv_w=rg_conv_w[i], conv_b=rg_conv_b[i], wa=rg_wa[i], ba=rg_ba[i],
                       wx=rg_wx[i], bx=rg_bx[i], lam=rg_lambda[i])
            past = dict(cache=cache_nsa_kv[i], page_table=page_table, win=state_nsa_win[i],
                        h=state_rglru_h[i], conv=state_rglru_conv[i])
            yp, st = residual_layer(yp, c_prompt, *common, functools.partial(nsa_rglru_mixer, prm=prm, past=None))
            kv_p.append(st[0]); win_p.append(st[1]); h_p.append(st[2]); conv_p.append(st[3])
            ys, st = residual_layer(ys, c_sample, *common, functools.partial(nsa_rglru_mixer, prm=prm, past=past))
            kv_s.append(st[0]); win_s.append(st[1]); h_s.append(st[2]); conv_s.append(st[3])
        else:
            prm = dict(w_in=w_in_cd[i], w_out=w_out_cd[i], v_gain=gmlp_v_gain[i], ws=gmlp_ws[i], bs=gmlp_bs[i],
                       q_gain=dil_q_gain[i], k_gain=dil_k_gain[i])
            past = dict(dil=state_dil_kv[i])
            yp, st = residual_layer(yp, c_prompt, *common, functools.partial(gmlp_dilated_mixer, prm=prm, past=None))
            dil_p.append(st[0])
            ys, st = residual_layer(ys, c_sample, *common, functools.partial(gmlp_dilated_mixer, prm=prm, past=past))
            dil_s.append(st[0]); gv_s.append(st[1])
    return (yp, ys, jnp.stack(kv_p), jnp.stack(kv_s), jnp.stack(win_p), jnp.stack(win_s),
            jnp.stack(h_p), jnp.stack(h_s), jnp.stack(conv_p), jnp.stack(conv_s),
            jnp.stack(dil_p), jnp.stack(dil_s), jnp.stack(gv_s))
```

```python
from contextlib import ExitStack

import numpy as np
import concourse.bass as bass
import concourse.mybir as mybir
from concourse.bass_utils import run_bass_kernel_spmd

F32 = mybir.dt.float32
BF16 = mybir.dt.bfloat16
I32 = mybir.dt.int32
AF = mybir.ActivationFunctionType
ALU = mybir.AluOpType
AX = mybir.AxisListType

NEG = -30000.0
D = 1024
HD = 64
FFN = 2816
IN_AB = 2328
IN_CD = 2560
EPS = 1e-6


class Res:
    __slots__ = ("name", "w", "rs", "dsem", "dcnt")

    def __init__(self, name):
        self.name = name
        self.w = None
        self.rs = {}
        self.dsem = None
        self.dcnt = 0


class Sched:
    def __init__(self, nc, stack):
        self.nc = nc
        self.stack = stack
        self.eng = {"pe": nc.tensor, "act": nc.scalar, "dve": nc.vector, "pool": nc.gpsimd, "sp": nc.sync}
        self.sem = {}
        self.cnt = {}
        self.waited = {k: {} for k in self.eng}
        for k in self.eng:
            self.sem[k] = stack.enter_context(nc.semaphore("s_" + k))
            self.cnt[k] = 0
        self.semobj = {k: self.sem[k] for k in self.eng}
        self.nres = 0
        self.all_dma = []
        self.free_sems = []

    def res(self, name=None):
        self.nres += 1
        return Res(name or f"r{self.nres}")

    def _dsem(self, r):
        if r.dsem is None:
            key = f"d{len(self.all_dma)}_{r.name}"
            h = self.stack.enter_context(self.nc.semaphore())
            r.dsem = key
            self.semobj[key] = h
            self.all_dma.append(r)
        return r.dsem

    def _wait(self, e, dep):
        if dep is None:
            return
        key, val = dep
        if key == "pe" and e == "pe":
            return
        if self.waited[e].get(key, 0) >= val:
            return
        self.eng[e].wait_ge(self.semobj[key], val)
        self.waited[e][key] = val

    def _deps(self, e, reads, writes):
        for r in reads:
            self._wait(e, r.w)
        for r in writes:
            self._wait(e, r.w)
            for d in list(r.rs.items()):
                self._wait(e, d)

    def _mark(self, reads, writes, tok):
        for r in reads:
            if r.rs.get(tok[0], 0) < tok[1]:
                r.rs[tok[0]] = tok[1]
        for r in writes:
            r.w = tok
            r.rs = {}

    def op(self, e, fn, reads=(), writes=()):
        self._deps(e, reads, writes)
        ins = fn()
        self.cnt[e] += 1
        ins.then_inc(self.sem[e], 1)
        self._mark(reads, writes, (e, self.cnt[e]))
        return ins

    def dma(self, q, out, in_, reads, writes, own=None, **kw):
        if own is None:
            own = writes[0]
        self._deps(q, reads, writes)
        key = self._dsem(own)
        ins = self.eng[q].dma_start(out=out, in_=in_, **kw)
        own.dcnt += 16
        ins.then_inc(self.semobj[key], 16)
        self._mark(reads, writes, (key, own.dcnt))
        return ins

    def dma_fn(self, q, fn, reads, writes, own=None):
        if own is None:
            own = writes[0]
        self._deps(q, reads, writes)
        key = self._dsem(own)
        ins = fn()
        own.dcnt += 16
        ins.then_inc(self.semobj[key], 16)
        self._mark(reads, writes, (key, own.dcnt))
        return ins

    def barrier(self):
        for e in self.eng:
            for r in self.all_dma:
                self._wait(e, (r.dsem, r.dcnt))
            for k in self.eng:
                if k != e and self.cnt[k] > 0:
                    self._wait(e, (k, self.cnt[k]))

    def finish(self, e="sp"):
        for r in self.all_dma:
            self._wait(e, (r.dsem, r.dcnt))
        for k in self.eng:
            if k != e and self.cnt[k] > 0:
                self._wait(e, (k, self.cnt[k]))


class Buf:
    def __init__(self, t, r):
        self.t = t
        self.r = r

    def __getitem__(self, k):
        return self.t[k]


def make_consts(T):
    p = np.arange(128)
    c = {}
    c["c_ident"] = np.eye(128, dtype=np.float32)
    key, q = p[:, None], p[None, :]
    causal = np.where(key <= q, 0.0, NEG).astype(np.float32)
    anti = np.where(key >= q, 0.0, NEG).astype(np.float32)
    c["c_causal4"] = np.tile(causal, (1, 4))
    c["c_anti4"] = np.tile(anti, (1, 4))
    es = np.zeros((128, 32, 128), np.float32)
    for pi in range(32):
        for kk in range(128):
            es[(np.arange(128) % 64) == 2 * pi + kk // 64, pi, kk] = 1.0
    c["c_esmall"] = es.reshape(128, 32 * 128)
    k4 = np.arange(4)[:, None]
    qq = np.tile(p, 4)[None, :]
    c["c_cmpR"] = np.where(qq < 32 * k4 + 31, NEG, 0.0).astype(np.float32)
    z = np.zeros((4, 252), np.float32)
    for k in range(4):
        z[k, 124 + k] = 1.0
    c["c_cmpZ"] = z
    c["c_cmpmask"] = np.where(p[:, None] >= 32 * np.arange(4)[None, :] + 31, 0.0, NEG).astype(np.float32)
    f = np.zeros((128, 2), np.float32)
    f[:, 0] = np.where(p < 64, 1e4, -1.0)
    f[:, 1] = np.where(p >= 64, 1e4, -1.0)
    c["c_f12"] = f
    dm = np.zeros((128, 17, 128), np.float32)
    for dl in range(17):
        dist = 128 * dl + q - key
        m = ((dist >= 0) & (dist <= 128)).astype(np.float32)
        m += ((dist >= 0) & (dist <= 512) & (dist % 4 == 0))
        m += ((dist >= 0) & (dist <= 2048) & (dist % 16 == 0))
        dm[:, dl, :] = m
    c["c_dilmult"] = dm.reshape(128, 17 * 128)
    e16 = np.zeros((128, 16), np.float32)
    e16[np.arange(16), np.arange(16)] = 1.0
    c["c_eye16"] = e16
    t, s = p[:, None], p[None, :]
    c["c_tril"] = (s <= t).astype(np.float32)
    return c


CONST_SHAPES = lambda T: {k: v.shape for k, v in make_consts(T).items()}


class Prog:
    def __init__(self, T, NS=16, npool_rows=2560 * 128, dbg=False, do_l1=True, do_samples=True):
        self.T, self.NS = T, NS
        self.NT = T // 128
        self.MT = 2
        self.NM = self.NT // self.MT
        self.W = self.MT * 128 + 16
        self.dbg = dbg
        self.do_l1 = do_l1
        self.do_samples = do_samples
        self.npool_rows = npool_rows
        self.nc = bass.Bass("TRN2", target_bir_lowering=False)
        self.st = ExitStack()
        self.S = Sched(self.nc, self.st)
        self.ins = {}
        self.outs = {}
        self.wb = {}

    def din(self, name, shape, dt=F32):
        self.ins[name] = self.nc.dram_tensor(name, list(shape), dt, kind="ExternalInput").ap()
        return self.ins[name]

    def dout(self, name, shape, dt=F32):
        self.outs[name] = self.nc.dram_tensor(name, list(shape), dt, kind="ExternalOutput").ap()
        return self.outs[name]

    def sb(self, name, shape, dt=F32):
        t = self.st.enter_context(self.nc.sbuf_tensor(name, list(shape), dt))
        return Buf(t, self.S.res(name))

    def psum(self, name):
        t = self.st.enter_context(self.nc.psum_tensor(name, [128, 512], F32))
        return Buf(t, self.S.res(name))

    def mm(self, out, lhsT, rhs, start, stop, R, W, skip=False):
        nc = self.nc
        if skip:
            return self.S.op("pe", lambda: nc.tensor.matmul(out, lhsT=lhsT, rhs=rhs, start=start, stop=stop,
                                                            skip_group_check=True), R, W)
        return self.S.op("pe", lambda: nc.tensor.matmul(out, lhsT=lhsT, rhs=rhs, start=start, stop=stop), R, W)

    def tr(self, out, in_, ident, R, W):
        nc = self.nc
        return self.S.op("pe", lambda: nc.tensor.transpose(out, in_, ident), R, W)

    def act(self, out, in_, func, R, W, **kw):
        nc = self.nc
        return self.S.op("act", lambda: nc.scalar.activation(out=out, in_=in_, func=func, **kw), R, W)

    def ts(self, out, in0, s1, s2, op0, op1, R, W, eng="dve"):
        e = self.nc.vector if eng == "dve" else self.nc.gpsimd
        if op1 is None:
            return self.S.op(eng, lambda: e.tensor_scalar(out=out, in0=in0, scalar1=s1, scalar2=None, op0=op0), R, W)
        return self.S.op(eng, lambda: e.tensor_scalar(out=out, in0=in0, scalar1=s1, scalar2=s2, op0=op0, op1=op1), R, W)

    def tt(self, out, in0, in1, op, R, W, eng="dve"):
        e = self.nc.vector if eng == "dve" else self.nc.gpsimd
        return self.S.op(eng, lambda: e.tensor_tensor(out=out, in0=in0, in1=in1, op=op), R, W)

    def stt(self, out, in0, scalar, in1, op0, op1, R, W):
        nc = self.nc
        return self.S.op("dve", lambda: nc.vector.scalar_tensor_tensor(out=out, in0=in0, scalar=scalar, in1=in1,
                                                                       op0=op0, op1=op1), R, W)

    def cp(self, out, in_, R, W, eng="dve"):
        e = self.nc.vector if eng == "dve" else self.nc.gpsimd
        return self.S.op(eng, lambda: e.tensor_copy(out=out, in_=in_), R, W)

    def memset(self, ap, val, W, eng="pool"):
        e = self.nc.vector if eng == "dve" else self.nc.gpsimd
        return self.S.op(eng, lambda: e.memset(ap, val), [], W)

    def dma(self, q, out, in_, R, W, own=None, **kw):
        return self.S.dma(q, out, in_, R, W, own=own, **kw)

    def declare(self):
        T, NS = self.T, self.NS
        di = self.din
        di("xp", [T, D]); di("xs", [NS, D]); di("cmat", [33, D])
        di("norm_mix_g", [2, D]); di("norm_ffn_g", [2, D])
        di("w_ada", [2, D, 6 * D]); di("b_ada", [2, 6 * D])
        di("w_ffn_gate", [2, D, FFN]); di("w_ffn_up", [2, D, FFN]); di("w_ffn_down", [2, FFN, D])
        di("w_in_ab", [D, IN_AB]); di("w_out_ab", [D, D])
        di("nsa_q_gain", [1, 64]); di("nsa_k_gain", [3, 64])
        di("nsa_cmp_w1", [2, 32, 64, 64]); di("nsa_cmp_w2", [2, 64, 64]); di("nsa_cmp_pos", [2, 32, 64])
        di("rg_conv_w", [4, 512]); di("rg_conv_b", [1, 512]); di("rg_wa", [8, 64, 64]); di("rg_ba", [1, 512])
        di("rg_wx", [8, 64, 64]); di("rg_bx", [1, 512]); di("rg_lambda", [1, 512])
        di("w_in_cd", [D, IN_CD]); di("w_out_cd", [D, D])
        di("gmlp_v_gain", [1, 512]); di("gmlp_ws", [8, 128, 128]); di("gmlp_bs", [8, 128])
        di("dil_q_gain", [1, 64]); di("dil_k_gain", [1, 64])
        di("cache", [self.npool_rows, 512]); di("state_win", [NS, 512, 256]); di("state_h", [NS, 512])
        di("state_conv", [NS, 3, 512]); di("state_dil", [NS, 2048, 1024]); di("page_table", [NS, 16], I32)
        for k, shp in CONST_SHAPES(T).items():
            di(k, shp)
        do = self.dout
        do("y_p", [T, D]); do("y_s", [NS, D]); do("kv_p", [T, 512]); do("kv_s", [NS, 512])
        do("win_p", [min(512, T), 256]); do("win_s", [NS, 512, 256]); do("h_p", [512]); do("h_s", [NS, 512])
        do("conv_p", [3, 512]); do("conv_s", [NS, 3, 512]); do("dil_p", [min(2048, T), 1024])
        do("dil_s", [NS, 2048, 1024]); do("gv_s", [NS, 512])
        if self.dbg:
            do("dbg_x1", [T, D]); do("dbg_x1s", [NS, D]); do("dbg_xmid", [T, D]); do("dbg_onsa", [T, 512]); do("dbg_ornn", [T, 512]); do("dbg_kcT", [128, max(T // 32, 8)]); do("dbg_vc", [128, 130]); do("dbg_hid", [64, 16]); do("dbg_w1", [128, 4096]); do("dbg_raw", [128, 512])
        self.x1d = self.nc.dram_tensor("x1_scratch", [T, D], F32).ap()

    def alloc(self):
        T, W = self.T, self.W
        sb = self.sb
        self.wres = self.S.res("dram_in")
        self.ores = self.S.res("dram_out")
        self.pmm = [self.psum(f"pmm{i}") for i in range(2)]
        self.ptr = [self.psum(f"ptr{i}") for i in range(2)]
        self.patt = [self.psum(f"patt{i}") for i in range(2)]
        self.pacc = [self.psum(f"pacc{i}") for i in range(2)]
        self.pmm_i = self.ptr_i = self.patt_i = 0
        self.ident = sb("ident", [128, 128]); self.identb = sb("identb", [128, 128], BF16)
        self.ones = sb("ones", [128, 128])
        self.causal4 = sb("causal4", [128, 512], BF16); self.anti4 = sb("anti4", [128, 512], BF16)
        self.esmall = sb("esmall", [128, 32 * 128], BF16)
        self.cmpR = sb("cmpR", [4, 512], BF16); self.cmpZ = sb("cmpZ", [4, 252], BF16)
        self.cmpmask = sb("cmpmask", [128, 4]); self.f12 = sb("f12", [128, 2]); self.eye16 = sb("eye16", [128, 16])
        self.NSLOT = 3
        self.ring = [sb(f"slab{i}", [128, 8, 512], BF16) for i in range(self.NSLOT)]
        self.ring_i = 0
        self.xb = [sb(f"xb{i}", [128, D]) for i in range(self.MT)] + [sb("xbs", [128, D])]
        self.xn = sb("xn", [128, D])
        self.junk = sb("junk", [128, D], BF16)
        self.stat = sb("stat", [128, 8])
        self.hT = sb("hT", [128, 8, W], BF16)
        self.tmp16 = sb("tmp16", [128, 16])
        self.catT = sb("catT", [128, 8, W], BF16)
        self.hidT = sb("hidT", [128, 22, W], BF16)
        self.ftmp = sb("ftmp", [128, W])
        self.cT = sb("cT", [128, 8, 33], BF16)
        self.modT = sb("modT", [128, 48, 33])
        self.gs = [sb(f"gs{i}", [128, 8, 33]) for i in range(2)]
        self.gtm = sb("gtm", [128, 2, D])
        self.gbc = sb("gbc", [128, 2, D])
        self.badaT = sb("badaT", [128, 2, 48])
        self.gnT = sb("gnT", [128, 2, 2, 8])

    def next_pmm(self):
        b = self.pmm[self.pmm_i % 2]; self.pmm_i += 1; return b

    def next_ptr(self):
        b = self.ptr[self.ptr_i % 2]; self.ptr_i += 1; return b

    def next_patt(self):
        b = self.patt[self.patt_i % 2]; self.patt_i += 1; return b

    def load_slab(self, src, k0, nk, c0, ncols):
        slot = self.ring[self.ring_i % self.NSLOT]
        self.ring_i += 1
        if isinstance(src, tuple):
            ap, rr = self.wb[src[0]]
            if src[1] is not None:
                ap = ap[src[1]]
            q = "sp"
        else:
            ap, rr, q = src, self.wres, "pool"
        self.dma(q, slot.t[:, 0:nk, 0:ncols],
                 ap[k0 * 128:(k0 + nk) * 128, c0:c0 + ncols].rearrange("(k p) n -> p k n", p=128),
                 [rr], [slot.r])
        return slot

    def precast(self, names):
        for nm in names:
            src = self.ins[nm]
            shp = list(src.shape)
            dst = self.nc.dram_tensor("wb_" + nm, shp, BF16).ap()
            rr = self.S.res("wb_" + nm)
            self.wb[nm] = (dst, rr)
            s2 = src if len(shp) == 2 else src.rearrange("l r n -> (l r) n")
            d2 = dst if len(shp) == 2 else dst.rearrange("l r n -> (l r) n")
            rows = s2.shape[0]
            for r0 in range(0, rows, 128):
                r1 = min(rows, r0 + 128)
                self.dma("pool", d2[r0:r1, :], s2[r0:r1, :], [self.wres], [rr])

    def ld(self, buf, src, q="pool", **kw):
        return self.dma(q, buf.t[:] if not isinstance(buf, tuple) else buf[0], src, [self.wres],
                        [buf.r if not isinstance(buf, tuple) else buf[1]], **kw)

    def setup_consts(self):
        I = self.ins
        self.ld(self.ident, I["c_ident"][:, :])
        self.ld(self.identb, I["c_ident"][:, :])
        self.memset(self.ones.t[:], 1.0, [self.ones.r])
        self.ld(self.causal4, I["c_causal4"][:, :]); self.ld(self.anti4, I["c_anti4"][:, :])
        self.ld(self.esmall, I["c_esmall"][:, :])
        self.ld(self.cmpR, I["c_cmpR"][:, :]); self.ld(self.cmpZ, I["c_cmpZ"][:, :])
        self.ld(self.cmpmask, I["c_cmpmask"][:, :]); self.ld(self.f12, I["c_f12"][:, :]); self.ld(self.eye16, I["c_eye16"][:, :])

    def setup_mod_inputs(self):
        I = self.ins
        ctm = self.xn
        self.dma("pool", ctm.t[0:33, :], I["cmat"][:, :], [self.wres], [ctm.r])
        self.act(ctm.t[0:33, :], ctm.t[0:33, :], AF.Silu, [ctm.r], [ctm.r])
        for k in range(8):
            pt = self.next_ptr()
            self.tr(pt.t[:, 0:33], ctm.t[0:33, k * 128:(k + 1) * 128], self.ident.t[0:33, 0:33], [ctm.r, self.ident.r], [pt.r])
            self.cp(self.cT.t[:, k, :], pt.t[:, 0:33], [pt.r], [self.cT.r])
        for l in range(2):
            self.dma("pool", self.badaT.t[:, l, :], I["b_ada"][l].rearrange("(e p) -> p e", p=128), [self.wres], [self.badaT.r],
                     allow_slow_non_contiguous=True)
            self.dma("pool", self.gnT.t[:, 0, l, :], I["norm_mix_g"][l].rearrange("(k p) -> p k", p=128), [self.wres],
                     [self.gnT.r], allow_slow_non_contiguous=True)
            self.dma("pool", self.gnT.t[:, 1, l, :], I["norm_ffn_g"][l].rearrange("(k p) -> p k", p=128), [self.wres],
                     [self.gnT.r], allow_slow_non_contiguous=True)

    def compute_mod(self, l):
        I = self.ins
        for s in range(12):
            slab = self.load_slab(I["w_ada"][l], 0, 8, s * 512, 512)
            for j in range(4):
                e = 4 * s + j
                pm = self.next_pmm()
                for k in range(8):
                    self.mm(pm.t[:, 0:33], slab.t[:, k, j * 128:(j + 1) * 128], self.cT.t[:, k, :], k == 0, k == 7,
                            [slab.r, self.cT.r], [pm.r])
                self.ts(self.modT.t[:, e, :], pm.t[:, 0:33], self.badaT.t[:, l, e:e + 1], None, ALU.add, None,
                        [pm.r, self.badaT.r], [self.modT.r])
        for w, base in ((0, 8), (1, 32)):
            for k in range(8):
                gcol = self.gnT.t[:, w, l, k:k + 1]
                self.ts(self.gs[w].t[:, k, :], self.modT.t[:, base + k, :], gcol, gcol, ALU.mult, ALU.add,
                        [self.modT.r, self.gnT.r], [self.gs[w].r])
        for w, base in ((0, 16), (1, 40)):
            for half in range(2):
                pt = self.next_ptr()
                for j in range(4):
                    k = half * 4 + j
                    self.tr(pt.t[0:33, j * 128:(j + 1) * 128], self.modT.t[:, base + k, :], self.ident.t[:, :],
                            [self.modT.r, self.ident.r], [pt.r])
                self.cp(self.gtm.t[0:33, w, half * 512:(half + 1) * 512], pt.t[0:33, :], [pt.r], [self.gtm.r])
            for half in range(2):
                pm = self.next_pmm()
                self.mm(pm.t[:, :], self.ones.t[32:33, :], self.gtm.t[32:33, w, half * 512:(half + 1) * 512], True, True,
                        [self.ones.r, self.gtm.r], [pm.r])
                self.cp(self.gbc.t[:, w, half * 512:(half + 1) * 512], pm.t[:, :], [pm.r], [self.gbc.r])

    def tiles_of(self, m, with_samples):
        tl = [("p", i, 128, i * 128) for i in range(self.MT)]
        if with_samples:
            tl.append(("s", self.MT, self.NS, self.MT * 128))
        return tl

    def norm_tile(self, tile, w):
        kind, i, np_, c0 = tile
        x = self.xb[i]
        st = self.stat
        shbase = 0 if w == 0 else 24
        self.act(self.junk.t[0:np_, :], x.t[0:np_, :], AF.Square, [x.r], [self.junk.r, st.r], accum_out=st.t[0:np_, 0:1])
        self.ts(st.t[0:np_, 0:1], st.t[0:np_, 0:1], 1.0 / D, EPS, ALU.mult, ALU.add, [st.r], [st.r])
        self.act(st.t[0:np_, 0:1], st.t[0:np_, 0:1], AF.Sqrt, [st.r], [st.r])
        self.S.op("dve", lambda: self.nc.vector.reciprocal(out=st.t[0:np_, 0:1], in_=st.t[0:np_, 0:1]), [st.r], [st.r])
        self.ts(self.xn.t[0:np_, :], x.t[0:np_, :], st.t[0:np_, 0:1], None, ALU.mult, None, [x.r, st.r], [self.xn.r])
        for half in range(2):
            pt = self.next_ptr()
            for j in range(4):
                k = half * 4 + j
                self.tr(pt.t[:, j * 128:j * 128 + np_], self.xn.t[0:np_, k * 128:(k + 1) * 128], self.ident.t[0:np_, 0:np_],
                        [self.xn.r, self.ident.r], [pt.r])
            for j in range(4):
                k = half * 4 + j
                src = pt.t[:, j * 128:j * 128 + np_]
                dst = self.hT.t[:, k, c0:c0 + np_]
                if kind == "p":
                    self.ts(dst, src, self.gs[w].t[:, k, 32:33], self.modT.t[:, shbase + k, 32:33], ALU.mult, ALU.add,
                            [pt.r, self.gs[w].r, self.modT.r], [self.hT.r])
                else:
                    self.tt(self.tmp16.t[:, 0:np_], src, self.gs[w].t[:, k, 0:np_], ALU.mult, [pt.r, self.gs[w].r], [self.tmp16.r])
                    self.tt(dst, self.tmp16.t[:, 0:np_], self.modT.t[:, shbase + k, 0:np_], ALU.add,
                            [self.tmp16.r, self.modT.r], [self.hT.r])

    def proj_tm(self, tiles, src, c0, ncols, zcol, post=None):
        slab = self.load_slab(src, 0, 8, c0, ncols)
        for tile in tiles:
            kind, i, np_, h0 = tile
            pm = self.next_pmm()
            for k in range(8):
                self.mm(pm.t[0:np_, 0:ncols], self.hT.t[:, k, h0:h0 + np_], slab.t[:, k, 0:ncols], k == 0, k == 7,
                        [self.hT.r, slab.r], [pm.r])
            if post is None:
                self.act(self.zb[i].t[0:np_, zcol:zcol + ncols], pm.t[0:np_, 0:ncols], AF.Copy, [pm.r], [self.zb[i].r])
            else:
                post(tile, pm)

    def ffn(self, tiles, l, out_fn):
        I = self.ins
        lo = min(t[3] for t in tiles)
        Wt = max(t[3] + t[2] for t in tiles)
        hc = 0
        for s in range(6):
            ncols = 512 if s < 5 else FFN - 5 * 512
            sg = self.load_slab(("w_ffn_gate", l), 0, 8, s * 512, ncols)
            su = self.load_slab(("w_ffn_up", l), 0, 8, s * 512, ncols)
            for j in range(ncols // 128):
                pg = self.next_pmm()
                for k in range(8):
                    self.mm(pg.t[:, lo:Wt], sg.t[:, k, j * 128:(j + 1) * 128], self.hT.t[:, k, lo:Wt], k == 0, k == 7,
                            [sg.r, self.hT.r], [pg.r])
                pu = self.next_pmm()
                for k in range(8):
                    self.mm(pu.t[:, lo:Wt], su.t[:, k, j * 128:(j + 1) * 128], self.hT.t[:, k, lo:Wt], k == 0, k == 7,
                            [su.r, self.hT.r], [pu.r])
                self.act(self.ftmp.t[:, lo:Wt], pg.t[:, lo:Wt], AF.Silu, [pg.r], [self.ftmp.r])
                self.tt(self.hidT.t[:, hc, lo:Wt], self.ftmp.t[:, lo:Wt], pu.t[:, lo:Wt], ALU.mult, [self.ftmp.r, pu.r], [self.hidT.r])
                hc += 1
        accs = [self.pacc[0], self.pacc[1], self.patt[0]]
        for half in range(2):
            for ks, (k0, nk) in enumerate(((0, 8), (8, 8), (16, 6))):
                sd = self.load_slab(("w_ffn_down", l), k0, nk, half * 512, 512)
                for ti, tile in enumerate(tiles):
                    kind, i, np_, h0 = tile
                    for k in range(nk):
                        self.mm(accs[ti].t[0:np_, :], self.hidT.t[:, k0 + k, h0:h0 + np_], sd.t[:, k, :], (k0 + k) == 0,
                                (k0 + k) == 21, [self.hidT.r, sd.r], [accs[ti].r])
            for ti, tile in enumerate(tiles):
                out_fn(tile, half, accs[ti])

    def resid_add(self, tile, half, ps, w):
        kind, i, np_, h0 = tile
        x = self.xb[i]
        cs = slice(half * 512, (half + 1) * 512)
        g = self.gbc.t[:, w, cs] if kind == "p" else self.gtm.t[0:np_, w, cs]
        gr = self.gbc.r if kind == "p" else self.gtm.r
        self.tt(self.xn.t[0:np_, cs], ps.t[0:np_, :], g[0:np_] if kind == "p" else g, ALU.mult, [ps.r, gr], [self.xn.r])
        self.tt(x.t[0:np_, cs], x.t[0:np_, cs], self.xn.t[0:np_, cs], ALU.add, [x.r, self.xn.r], [x.r])


class ProgL0(Prog):
    def alloc_l0(self):
        T, NT, sb = self.T, self.NT, self.sb
        self.qg = sb("qg", [128, 64]); self.kg = sb("kg", [128, 3, 64])
        self.w1sb = sb("w1sb", [128, 2, 32, 64], BF16)
        self.posT = sb("posT", [128, 2, 32], BF16)
        self.cmpb = sb("cmpb", [128, 2])
        self.w2sb = sb("w2sb", [128, 2, 64], BF16)
        self.wabd = sb("wabd", [128, 4, 128], BF16); self.wxbd = sb("wxbd", [128, 4, 128], BF16)
        self.convw = sb("convw", [128, 4, 4]); self.rgc = sb("rgc", [128, 6, 4])
        self.xrs = sb("xrs", [128, 4, 16])
        self.ggr = sb("ggr", [128, 4, self.W])
        self.sq = sb("sq", [128, 512])
        self.qb = sb("qb", [128, 512], BF16)
        self.qp = [sb(f"qp{g}", [128, 4, 128], BF16) for g in range(2)]
        self.hid = sb("hid", [128, 2, 2, 8], BF16)
        self.kcrow = sb("kcrow", [128, 128]); self.vcrow = sb("vcrow", [128, 2, 65], BF16)
        self.gsig = sb("gsig", [128, 24])
        self.e32 = sb("e32", [128, 256]); self.impacc = sb("impacc", [128, 256])
        self.imp = sb("imp", [128, 128]); self.imp2 = sb("imp2", [128, 128]); self.negsel = sb("negsel", [128, 128])
        self.m8 = sb("m8", [128, 16]); self.den = sb("den", [128, 8])
        self.negT4 = [sb(f"negT4_{g}", [128, 4, 128], BF16) for g in range(2)]
        self.pT = [sb(f"pT{i}", [128, 512], BF16) for i in range(2)]
        self.pT_i = 0
        self.onsa = sb("onsa", [128, 512]); self.otmp = sb("otmp", [128, 256])
        self.zb = [sb(f"zb{i}", [128, 1304]) for i in range(self.MT)] + [sb("zbs", [128, 1304])]
        self.x1res = [self.S.res(f"x1_{t}") for t in range(self.NT)]
        self.rg = [sb(f"rg{i}", [128, 256]) for i in range(7)]
        self.xcb = sb("xcb", [128, 256], BF16)

    def next_pT(self):
        b = self.pT[self.pT_i % 2]; self.pT_i += 1; return b

    def alloc_l0_prompt(self):
        T, NT, sb = self.T, self.NT, self.sb
        self.ksT = sb("ksT", [128, T], BF16)
        self.vs_aug = sb("vs_aug", [128, NT, 2, 65], BF16)
        self.nbt_max = max(1, (T // 32 + 127) // 128)
        self.kcT = sb("kcT", [128, max(T // 32, 8)], BF16)
        self.vc_aug = sb("vc_aug", [128, self.nbt_max, 2, 65], BF16)
        self.kwT = sb("kwT", [128, 6 * 128], BF16)
        self.vw_aug = sb("vw_aug", [128, 6, 2, 65], BF16)
        self.xrbuf = sb("xrbuf", [128, 4, 3 + 256]); self.hstate = sb("hstate", [128, 4])
        self.rawTz = [sb(f"rawTz{g}", [128, 2, 256], BF16) for g in range(2)]
        self.memset(self.xrbuf.t[:], 0.0, [self.xrbuf.r]); self.memset(self.hstate.t[:], 0.0, [self.hstate.r])
        for b_ in (self.vs_aug, self.vc_aug, self.vw_aug):
            self.memset(b_.t[:], 1.0, [b_.r])
        self.memset(self.kwT.t[:], 0.0, [self.kwT.r])
        for g in range(2):
            self.memset(self.rawTz[g].t[:], 0.0, [self.rawTz[g].r])

    def setup_l0(self):
        I, nc = self.ins, self.nc
        slow = dict(allow_slow_non_contiguous=True)
        self.ld(self.qg, I["nsa_q_gain"][0:1, :].broadcast_to([128, 64]))
        self.ts(self.qg.t[:], self.qg.t[:], 0.125, None, ALU.mult, None, [self.qg.r], [self.qg.r])
        for j in range(3):
            self.dma("pool", self.kg.t[:, j, :], I["nsa_k_gain"][j:j + 1, :].broadcast_to([128, 64]), [self.wres], [self.kg.r])
        for c in range(2):
            for hlf in range(2):
                self.dma("pool", self.w1sb.t[hlf * 64:(hlf + 1) * 64, c, :, :], I["nsa_cmp_w1"][c].rearrange("l d e -> d l e"),
                         [self.wres], [self.w1sb.r])
            self.dma("pool", self.w2sb.t[0:64, c, :], I["nsa_cmp_w2"][c], [self.wres], [self.w2sb.r])
        for c in range(2):
            self.dma("pool", self.posT.t[0:64, c, :], I["nsa_cmp_pos"][c].rearrange("l d -> d l"), [self.wres], [self.posT.r], **slow)
        for c in range(2):
            pm = self.next_pmm()
            for l in range(32):
                self.mm(pm.t[0:64, 0:1], self.w1sb.t[0:64, c, l, :], self.posT.t[0:64, c, l:l + 1], l == 0, l == 31,
                        [self.w1sb.r, self.posT.r], [pm.r])
            self.cp(self.cmpb.t[0:64, c:c + 1], pm.t[0:64, 0:1], [pm.r], [self.cmpb.r])
        for wsb, nm in ((self.wabd, "rg_wa"), (self.wxbd, "rg_wx")):
            self.memset(wsb.t[:], 0.0, [wsb.r])
            for j in range(4):
                self.dma("pool", wsb.t[0:64, j, 0:64], I[nm][2 * j], [self.wres], [wsb.r])
                self.dma("pool", wsb.t[64:128, j, 64:128], I[nm][2 * j + 1], [self.wres], [wsb.r])
        for i_ in range(4):
            self.dma("pool", self.convw.t[:, :, i_], I["rg_conv_w"][i_].rearrange("(c p) -> p c", p=128), [self.wres], [self.convw.r], **slow)
        for j, nm in enumerate(("rg_conv_b", "rg_ba", "rg_bx", "rg_lambda")):
            self.dma("pool", self.rgc.t[:, j, :], I[nm].rearrange("o (c p) -> p (o c)", p=128), [self.wres], [self.rgc.r], **slow)
        r = self.rgc
        self.act(r.t[:, 4, :], r.t[:, 3, :], AF.Exp, [r.r], [r.r], scale=-1.0)
        self.act(r.t[:, 4, :], r.t[:, 4, :], AF.Ln, [r.r], [r.r], bias=1.0)
        self.ts(r.t[:, 4, :], r.t[:, 4, :], -8.0, None, ALU.mult, None, [r.r], [r.r])
        self.ts(r.t[:, 5, :], r.t[:, 4, :], 2.0, None, ALU.mult, None, [r.r], [r.r])
        self.memset(self.vcrow.t[:], 1.0, [self.vcrow.r])
        for g in range(2):
            self.memset(self.qp[g].t[:], 0.0, [self.qp[g].r])

    def headnorm(self, src_ap, np_, nh, gain_ap, out_ap, R, W, shape4=None):
        sq, st = self.sq, self.den
        sv = lambda ap: ap.rearrange("p (h d) -> p h d", d=64)
        self.tt(sq.t[0:np_, 0:nh * 64], src_ap, src_ap, ALU.mult, R, [sq.r])
        self.S.op("dve", lambda: self.nc.vector.tensor_reduce(out=st.t[0:np_, 0:nh], in_=sv(sq.t[0:np_, 0:nh * 64]), axis=AX.X,
                                                              op=ALU.add), [sq.r], [st.r])
        self.ts(st.t[0:np_, 0:nh], st.t[0:np_, 0:nh], 1.0 / 64, EPS, ALU.mult, ALU.add, [st.r], [st.r])
        self.act(st.t[0:np_, 0:nh], st.t[0:np_, 0:nh], AF.Sqrt, [st.r], [st.r])
        self.S.op("dve", lambda: self.nc.vector.reciprocal(out=st.t[0:np_, 0:nh], in_=st.t[0:np_, 0:nh]), [st.r], [st.r])
        self.tt(sv(sq.t[0:np_, 0:nh * 64]), sv(src_ap), st.t[0:np_, 0:nh, None].broadcast_to([np_, nh, 64]), ALU.mult,
                R + [st.r], [sq.r])
        if shape4 is None:
            self.tt(out_ap, sv(sq.t[0:np_, 0:nh * 64]), gain_ap[0:np_, None, :].broadcast_to([np_, nh, 64]), ALU.mult,
                    [sq.r] + R, W)
        else:
            a, b = shape4
            in0 = sq.t[0:np_, 0:nh * 64].rearrange("p (a b d) -> p a b d", a=a, b=b, d=64)
            in1 = gain_ap[0:np_, None, None, :].broadcast_to([np_, a, b, 64])
            self.tt(out_ap, in0, in1, ALU.mult, [sq.r] + R, W)

    def l0_post(self, tile, t):
        kind, i, np_, h0 = tile
        z = self.zb[i]
        O = self.outs
        sv = lambda ap: ap.rearrange("p (h d) -> p h d", d=64)
        self.headnorm(z.t[0:np_, 768:896], np_, 2, self.kg.t[:, 1, :], sv(z.t[0:np_, 768:896]), [z.r, self.kg.r], [z.r])
        self.headnorm(z.t[0:np_, 1024:1152], np_, 2, self.kg.t[:, 2, :], sv(z.t[0:np_, 1024:1152]), [z.r, self.kg.r], [z.r])
        if kind == "p":
            self.dma("pool", O["kv_p"][t * 128:(t + 1) * 128, :], z.t[0:128, 512:1024], [z.r], [self.ores], own=z.r)
            if t >= self.NT - min(4, self.NT):
                w0 = (t - (self.NT - min(4, self.NT))) * 128
                self.dma("pool", O["win_p"][w0:w0 + 128, :], z.t[0:128, 1024:1280], [z.r], [self.ores], own=z.r)
            pt = self.next_ptr()
            self.tr(pt.t[:, 0:128], z.t[0:128, 768:896], self.ident.t[:, :], [z.r, self.ident.r], [pt.r])
            self.tr(pt.t[:, 128:256], z.t[0:128, 1024:1152], self.ident.t[:, :], [z.r, self.ident.r], [pt.r])
            self.tr(pt.t[:, 256:384], z.t[0:128, 512:640], self.ident.t[:, :], [z.r, self.ident.r], [pt.r])
            self.tr(pt.t[:, 384:512], z.t[0:128, 640:768], self.ident.t[:, :], [z.r, self.ident.r], [pt.r])
            slot = t % 6
            self.cp(self.ksT.t[:, t * 128:(t + 1) * 128], pt.t[:, 0:128], [pt.r], [self.ksT.r])
            self.cp(self.kwT.t[:, slot * 128:(slot + 1) * 128], pt.t[:, 128:256], [pt.r], [self.kwT.r])
            for g in range(2):
                gs_ = slice(g * 64, (g + 1) * 64)
                self.cp(self.rawTz[g].t[gs_, :, i * 128:(i + 1) * 128], pt.t[gs_, 256:512].rearrange("p (c q) -> p c q", q=128), [pt.r],
                        [self.rawTz[g].r])
            self.cp(self.vs_aug.t[:, t, :, 0:64], sv(z.t[0:128, 896:1024]), [z.r], [self.vs_aug.r], eng="pool")
            self.cp(self.vw_aug.t[:, slot, :, 0:64], sv(z.t[0:128, 1152:1280]), [z.r], [self.vw_aug.r], eng="pool")
        else:
            self.dma("pool", O["kv_s"][:, :], z.t[0:np_, 512:1024], [z.r], [self.ores], own=z.r)

    def compress_macro(self, m):
        nb0 = 8 * m
        for c in range(2):
            pm = self.next_pmm()
            for g in range(2):
                for l in range(32):
                    self.mm(pm.t[0:64, g * 8:(g + 1) * 8], self.w1sb.t[:, c, l, :],
                            self.rawTz[g].t[:, c, l:256:32], l == 0, l == 31, [self.w1sb.r, self.rawTz[g].r], [pm.r])
            if self.dbg and c == 0 and m == self.NM - 1:
                self.cp(self.kcrow.t[0:64, 0:16], pm.t[0:64, 0:16], [pm.r], [self.kcrow.r])
                self.dma("pool", self.outs["dbg_hid"][:, :], self.kcrow.t[0:64, 0:16], [self.kcrow.r], [self.ores], own=self.kcrow.r)
                self.dma("pool", self.outs["dbg_w1"][:, :], self.w1sb.t[:, :, :, :].rearrange("p c l e -> p (c l e)"), [self.w1sb.r], [self.ores], own=self.w1sb.r)
            self.act(self.hid.t[0:64, c, :, :], pm.t[0:64, 0:16].rearrange("p (g n) -> p g n", n=8), AF.Gelu_apprx_tanh,
                     [pm.r, self.cmpb.r], [self.hid.r], bias=self.cmpb.t[0:64, c:c + 1])
            pm2 = self.next_pmm()
            for g in range(2):
                self.mm(pm2.t[0:8, g * 64:(g + 1) * 64], self.hid.t[0:64, c, g, :], self.w2sb.t[0:64, c, :], True, True,
                        [self.hid.r, self.w2sb.r], [pm2.r])
            if c == 0:
                self.cp(self.kcrow.t[0:8, :], pm2.t[0:8, 0:128], [pm2.r], [self.kcrow.r])
                sv = lambda ap: ap.rearrange("p (h d) -> p h d", d=64)
                self.headnorm(self.kcrow.t[0:8, :], 8, 2, self.kg.t[:, 0, :], sv(self.kcrow.t[0:8, :]), [self.kcrow.r, self.kg.r],
                              [self.kcrow.r])
                pt = self.next_ptr()
                self.tr(pt.t[:, 0:8], self.kcrow.t[0:8, :], self.ident.t[0:8, 0:8], [self.kcrow.r, self.ident.r], [pt.r])
                self.cp(self.kcT.t[:, nb0:nb0 + 8], pt.t[:, 0:8], [pt.r], [self.kcT.r])
            else:
                self.cp(self.vcrow.t[0:8, :, 0:64], pm2.t[0:8, 0:128].rearrange("p (g d) -> p g d", d=64), [pm2.r], [self.vcrow.r])
                p0, bt = nb0 % 128, nb0 // 128
                self.dma("pool", self.vc_aug.t[p0:p0 + 8, bt, :, :], self.vcrow.t[0:8, :, :], [self.vcrow.r], [self.vc_aug.r])

    def combine(self, g, br, first):
        o = self.pacc[g].t[:, 0:260].rearrange("p (r e) -> p r e", e=65)
        rd = self.den
        self.ts(rd.t[:, 0:4], o[:, :, 64], 1e-30, None, ALU.max, None, [self.pacc[g].r], [rd.r])
        self.S.op("dve", lambda: self.nc.vector.reciprocal(out=rd.t[:, 0:4], in_=rd.t[:, 0:4]), [rd.r], [rd.r])
        gate = self.gsig.t[:, g * 12:(g + 1) * 12].rearrange("p (r b) -> p r b", b=3)[:, :, br]
        self.tt(rd.t[:, 0:4], rd.t[:, 0:4], gate, ALU.mult, [rd.r, self.gsig.r], [rd.r])
        ov = self.onsa.t[:, g * 256:(g + 1) * 256].rearrange("p (r d) -> p r d", d=64)
        sc = rd.t[:, 0:4, None].broadcast_to([128, 4, 64])
        if first:
            self.tt(ov, o[:, :, 0:64], sc, ALU.mult, [self.pacc[g].r, rd.r], [self.onsa.r])
        else:
            tv = self.otmp.t[:, :].rearrange("p (r d) -> p r d", d=64)
            self.tt(tv, o[:, :, 0:64], sc, ALU.mult, [self.pacc[g].r, rd.r], [self.otmp.r])
            self.tt(ov, ov, tv, ALU.add, [self.onsa.r, self.otmp.r], [self.onsa.r])

    def pv4(self, g, pT, nk, v_ap, first, last, vres):
        for r in range(4):
            self.mm(self.pacc[g].t[:, r * 65:(r + 1) * 65], pT.t[0:nk, r * 128:(r + 1) * 128], v_ap, first and r == 0, last,
                    [pT.r, vres], [self.pacc[g].r], skip=True)

    def nsa_prompt_tile(self, t, i):
        nc = self.nc
        nblk, nsb = 4 * t + 4, 2 * t + 2
        topk = nsb > 16
        qf = lambda g: self.qp[g].t[:, :, :].rearrange("p r q -> p (r q)")
        for g in range(2 if topk else 0):
            for r in range(4):
                pm = self.next_pmm()
                self.mm(pm.t[:, 0:nblk], self.qp[g].t[:, r, :], self.kcT.t[:, 0:nblk], True, True,
                        [self.qp[g].r, self.kcT.r], [pm.r])
                self.tt(pm.t[:, nblk - 4:nblk], pm.t[:, nblk - 4:nblk], self.cmpmask.t[:, :], ALU.add, [pm.r, self.cmpmask.r], [pm.r])
                self.act(self.e32.t[:, 0:nblk], pm.t[:, 0:nblk], AF.Exp, [pm.r], [self.e32.r, self.den.r],
                         accum_out=self.den.t[:, 4:5])
                self.ts(self.den.t[:, 4:5], self.den.t[:, 4:5], 1e-30, None, ALU.max, None, [self.den.r], [self.den.r])
                self.S.op("dve", lambda: nc.vector.reciprocal(out=self.den.t[:, 4:5], in_=self.den.t[:, 4:5]), [self.den.r], [self.den.r])
                if r == 0:
                    self.ts(self.impacc.t[:, 0:nblk], self.e32.t[:, 0:nblk], self.den.t[:, 4:5], None, ALU.mult, None,
                            [self.e32.r, self.den.r], [self.impacc.r])
                else:
                    self.stt(self.impacc.t[:, 0:nblk], self.e32.t[:, 0:nblk], self.den.t[:, 4:5], self.impacc.t[:, 0:nblk],
                             ALU.mult, ALU.add, [self.e32.r, self.den.r, self.impacc.r], [self.impacc.r])
            if topk:
                imp = self.imp
                self.S.op("dve", lambda: nc.vector.tensor_reduce(
                    out=imp.t[:, 0:nsb], in_=self.impacc.t[:, 0:nblk].rearrange("p (n two) -> p n two", two=2), axis=AX.X,
                    op=ALU.add), [self.impacc.r], [imp.r])
                self.ts(imp.t[:, 2 * t:2 * t + 1], imp.t[:, 2 * t:2 * t + 1], self.f12.t[:, 0:1], None, ALU.max, None,
                        [imp.r, self.f12.r], [imp.r])
                self.cp(imp.t[:, 2 * t + 1:2 * t + 2], self.f12.t[:, 1:2], [self.f12.r, imp.r], [imp.r])
                self.memset(imp.t[:, 0:1], 1e4, [imp.r], eng="dve")
                self.S.op("dve", lambda: nc.vector.max(out=self.m8.t[:, 0:8], in_=imp.t[:, 0:nsb]), [imp.r], [self.m8.r])
                self.S.op("dve", lambda: nc.vector.match_replace(out=self.imp2.t[:, 0:nsb], in_to_replace=self.m8.t[:, 0:8],
                                                                 in_values=imp.t[:, 0:nsb], imm_value=-2.0),
                          [imp.r, self.m8.r], [self.imp2.r])
                self.S.op("dve", lambda: nc.vector.max(out=self.m8.t[:, 8:16], in_=self.imp2.t[:, 0:nsb]), [self.imp2.r], [self.m8.r])
                self.ts(self.negsel.t[:, 0:nsb], imp.t[:, 0:nsb], self.m8.t[:, 15:16], NEG, ALU.is_lt, ALU.mult,
                        [imp.r, self.m8.r], [self.negsel.r])
                pt = self.next_ptr()
                self.tr(pt.t[0:nsb, 0:128], self.negsel.t[:, 0:nsb], self.ident.t[:, :], [self.negsel.r, self.ident.r], [pt.r])
                self.cp(self.negT4[g].t[0:nsb, :, :], pt.t[0:nsb, None, 0:128].broadcast_to([nsb, 4, 128]), [pt.r], [self.negT4[g].r])
        nbt = (nblk + 127) // 128
        for g in range(2):
            for bt in range(nbt):
                nb_t = min(128, nblk - bt * 128)
                last = bt == nbt - 1
                pa = self.next_patt()
                self.mm(pa.t[0:nb_t, :], self.kcT.t[:, bt * 128:bt * 128 + nb_t], qf(g), True, not last,
                        [self.kcT.r, self.qp[g].r], [pa.r])
                if last:
                    n0 = nb_t - 4
                    self.mm(pa.t[0:nb_t, :], self.cmpZ.t[0:4, 124 - n0:124 - n0 + nb_t], self.cmpR.t[0:4, :], False, True,
                            [self.cmpZ.r, self.cmpR.r], [pa.r])
                pT = self.next_pT()
                self.act(pT.t[0:nb_t, :], pa.t[0:nb_t, :], AF.Exp, [pa.r], [pT.r])
                self.pv4(g, pT, nb_t, self.vc_aug.t[0:nb_t, bt, g, :], bt == 0, last, self.vc_aug.r)
            self.combine(g, 0, True)
        for g in range(2):
            for kt in range(t + 1):
                diag = kt == t
                pa = self.next_patt()
                self.mm(pa.t[:, :], self.ksT.t[:, kt * 128:(kt + 1) * 128], qf(g), True, not (topk or diag),
                        [self.ksT.r, self.qp[g].r], [pa.r])
                if topk:
                    base = 64 * ((2 * kt) // 64)
                    nrow = min(64, nsb - base)
                    pi_ = kt % 32
                    self.mm(pa.t[:, :], self.esmall.t[base:base + nrow, pi_ * 128:(pi_ + 1) * 128],
                            self.negT4[g].t[base:base + nrow, :, :].rearrange("p r q -> p (r q)"), False, not diag,
                            [self.esmall.r, self.negT4[g].r], [pa.r])
                if diag:
                    self.mm(pa.t[:, :], self.identb.t[:, :], self.causal4.t[:, :], False, True, [self.identb.r, self.causal4.r], [pa.r])
                pT = self.next_pT()
                self.act(pT.t[:, :], pa.t[:, :], AF.Exp, [pa.r], [pT.r])
                self.pv4(g, pT, 128, self.vs_aug.t[:, kt, g, :], kt == 0, diag, self.vs_aug.r)
            self.combine(g, 1, False)
        k0 = max(0, t - 4)
        for g in range(2):
            for kt in range(k0, t + 1):
                diag, far = kt == t, kt == t - 4
                slot = kt % 6
                pa = self.next_patt()
                self.mm(pa.t[:, :], self.kwT.t[:, slot * 128:(slot + 1) * 128], qf(g), True, not (far or diag),
                        [self.kwT.r, self.qp[g].r], [pa.r])
                if far:
                    self.mm(pa.t[:, :], self.identb.t[:, :], self.anti4.t[:, :], False, True, [self.identb.r, self.anti4.r], [pa.r])
                if diag:
                    self.mm(pa.t[:, :], self.identb.t[:, :], self.causal4.t[:, :], False, True, [self.identb.r, self.causal4.r], [pa.r])
                pT = self.next_pT()
                self.act(pT.t[:, :], pa.t[:, :], AF.Exp, [pa.r], [pT.r])
                self.pv4(g, pT, 128, self.vw_aug.t[:, slot, g, :], kt == k0, diag, self.vw_aug.r)
            self.combine(g, 2, False)
        if self.dbg:
            self.dma("pool", self.outs["dbg_onsa"][t * 128:(t + 1) * 128, :], self.onsa.t[:, :], [self.onsa.r], [self.ores], own=self.onsa.r)
        pt = self.next_ptr()
        for k in range(4):
            self.tr(pt.t[:, k * 128:(k + 1) * 128], self.onsa.t[:, k * 128:(k + 1) * 128], self.ident.t[:, :],
                    [self.onsa.r, self.ident.r], [pt.r])
        self.cp(self.catT.t[:, 0:4, i * 128:(i + 1) * 128], pt.t[:, :].rearrange("p (k q) -> p k q", q=128), [pt.r], [self.catT.r])

    def rglru_macro(self):
        nc = self.nc
        xc, r_, i_, a_, a2, u_, hs = self.rg
        c = self.rgc
        Wp = 256
        for j in range(4):
            xr = self.xrbuf
            self.ts(xc.t[:, :], xr.t[:, j, 0:Wp], self.convw.t[:, j, 0:1], c.t[:, 0, j:j + 1], ALU.mult, ALU.add,
                    [xr.r, self.convw.r, c.r], [xc.r])
            for k in range(1, 4):
                self.stt(xc.t[:, :], xr.t[:, j, k:k + Wp], self.convw.t[:, j, k:k + 1], xc.t[:, :], ALU.mult, ALU.add,
                         [xr.r, self.convw.r, xc.r], [xc.r])
            self.cp(self.xcb.t[:, :], xc.t[:, :], [xc.r], [self.xcb.r])
            pr = self.next_pmm()
            self.mm(pr.t[:, 0:Wp], self.wabd.t[:, j, :], self.xcb.t[:, :], True, True, [self.wabd.r, self.xcb.r], [pr.r])
            pi = self.next_pmm()
            self.mm(pi.t[:, 0:Wp], self.wxbd.t[:, j, :], self.xcb.t[:, :], True, True, [self.wxbd.r, self.xcb.r], [pi.r])
            self.act(r_.t[:, :], pr.t[:, 0:Wp], AF.Sigmoid, [pr.r, c.r], [r_.r], bias=c.t[:, 1, j:j + 1])
            self.act(i_.t[:, :], pi.t[:, 0:Wp], AF.Sigmoid, [pi.r, c.r], [i_.r], bias=c.t[:, 2, j:j + 1])
            self.act(a_.t[:, :], r_.t[:, :], AF.Exp, [r_.r, c.r], [a_.r], scale=c.t[:, 4, j:j + 1])
            self.act(a2.t[:, :], r_.t[:, :], AF.Exp, [r_.r, c.r], [a2.r], scale=c.t[:, 5, j:j + 1])
            self.ts(a2.t[:, :], a2.t[:, :], -1.0, 1.0, ALU.mult, ALU.add, [a2.r], [a2.r])
            self.act(a2.t[:, :], a2.t[:, :], AF.Sqrt, [a2.r], [a2.r])
            self.tt(u_.t[:, :], a2.t[:, :], i_.t[:, :], ALU.mult, [a2.r, i_.r], [u_.r])
            self.tt(u_.t[:, :], u_.t[:, :], xc.t[:, :], ALU.mult, [u_.r, xc.r], [u_.r])
            self.S.op("dve", lambda: nc.vector.tensor_tensor_scan(out=hs.t[:, :], data0=a_.t[:, :], data1=u_.t[:, :],
                                                                  initial=self.hstate.t[:, j:j + 1], op0=ALU.mult, op1=ALU.add),
                      [a_.r, u_.r, self.hstate.r], [hs.r])
            self.cp(self.hstate.t[:, j:j + 1], hs.t[:, Wp - 1:Wp], [hs.r], [self.hstate.r])
            self.tt(self.catT.t[:, 4 + j, 0:Wp], hs.t[:, :], self.ggr.t[:, j, 0:Wp], ALU.mult, [hs.r, self.ggr.r], [self.catT.r])
            self.cp(self.rg[1].t[:, 0:3], xr.t[:, j, Wp:Wp + 3], [xr.r, r_.r], [r_.r])
            self.cp(xr.t[:, j, 0:3], self.rg[1].t[:, 0:3], [r_.r, xr.r], [xr.r])

    def proj_fm(self, src, c0, lo, hi, post):
        slab = self.load_slab(src, 0, 8, c0, 512)
        for j in range(4):
            pm = self.next_pmm()
            for k in range(8):
                self.mm(pm.t[:, 0:hi - lo], slab.t[:, k, j * 128:(j + 1) * 128], self.hT.t[:, k, lo:hi], k == 0, k == 7,
                        [slab.r, self.hT.r], [pm.r])
            post(j, pm)

    def pass1_macro(self, m):
        I, O = self.ins, self.outs
        ws = False
        tiles = self.tiles_of(m, ws)
        Wt = 256
        for (kind, i, np_, h0) in tiles:
            t = m * self.MT + i
            self.dma("sp", self.xb[i].t[:, :], I["xp"][t * 128:(t + 1) * 128, :], [self.wres], [self.xb[i].r])
        for tile in tiles:
            self.norm_tile(tile, 0)
        self.proj_tm(tiles, ("w_in_ab", None), 0, 512, 0)
        self.proj_tm(tiles, ("w_in_ab", None), 512, 512, 512)
        self.proj_tm(tiles, ("w_in_ab", None), 1024, 280, 1024)

        def post_xr(j, pm):
            self.cp(self.xrbuf.t[:, j, 3:3 + 256], pm.t[:, 0:256], [pm.r], [self.xrbuf.r])
            if ws:
                self.cp(self.xrs.t[:, j, :], pm.t[:, 256:272], [pm.r], [self.xrs.r])

        def post_gr(j, pm):
            self.act(self.ggr.t[:, j, 0:Wt], pm.t[:, 0:Wt], AF.Gelu_apprx_tanh, [pm.r], [self.ggr.r])

        self.proj_fm(("w_in_ab", None), 1304, 0, Wt, post_xr)
        self.proj_fm(("w_in_ab", None), 1816, 0, Wt, post_gr)
        for tile in tiles:
            if tile[0] == "p":
                self.l0_post(tile, m * self.MT + tile[1])
                if tile[1] == self.MT - 1:
                    self.compress_macro(m)
        for tile in tiles:
            if tile[0] == "p":
                t = m * self.MT + tile[1]
                self.rebuild_qT(tile)
                self.nsa_prompt_tile(t, tile[1])
        self.rglru_macro()
        for half in range(2):
            slab = self.load_slab(("w_out_ab", None), 0, 8, half * 512, 512)
            for tile in tiles:
                kind, i, np_, h0 = tile
                pm = self.next_pmm()
                for k in range(8):
                    self.mm(pm.t[0:np_, :], self.catT.t[:, k, h0:h0 + np_], slab.t[:, k, :], k == 0, k == 7, [self.catT.r, slab.r], [pm.r])
                self.resid_add(tile, half, pm, 0)
        if self.dbg:
            for (kind, i, np_, h0) in tiles:
                if kind == "p":
                    t = m * self.MT + i
                    self.dma("pool", O["dbg_xmid"][t * 128:(t + 1) * 128, :], self.xb[i].t[:, :], [self.xb[i].r], [self.ores], own=self.xb[i].r)
        for tile in tiles:
            self.norm_tile(tile, 1)
        self.ffn(tiles, 0, lambda tile, half, ps: self.resid_add(tile, half, ps, 1))
        for (kind, i, np_, h0) in tiles:
            if kind == "p":
                t = m * self.MT + i
                self.dma("pool", self.x1d[t * 128:(t + 1) * 128, :], self.xb[i].t[:, :], [self.xb[i].r], [self.x1res[t]], own=self.xb[i].r)
                if self.dbg:
                    self.dma("pool", O["dbg_x1"][t * 128:(t + 1) * 128, :], self.xb[i].t[:, :], [self.xb[i].r], [self.ores], own=self.xb[i].r)
            elif self.dbg:
                self.dma("pool", O["dbg_x1s"][:, :], self.xb[i].t[0:np_, :], [self.xb[i].r], [self.ores], own=self.xb[i].r)

    def rebuild_qT(self, tile):
        kind, i, np_, h0 = tile
        z = self.zb[i]
        qout = self.qb.t[0:np_, :].rearrange("p (r g d) -> p g r d", r=4, g=2, d=64)
        self.headnorm(z.t[0:np_, 0:512], np_, 8, self.qg.t, qout, [z.r, self.qg.r], [self.qb.r], shape4=(2, 4))
        self.act(self.gsig.t[0:np_, :], z.t[0:np_, 1280:1304], AF.Sigmoid, [z.r], [self.gsig.r])
        pt = self.next_ptr()
        ptb = pt.t[:].bitcast(BF16)
        for r in range(4):
            self.tr(ptb[:, r * 128:r * 128 + np_], self.qb.t[0:np_, r * 128:(r + 1) * 128], self.identb.t[0:np_, 0:np_],
                    [self.qb.r, self.identb.r], [pt.r])
        for g in range(2):
            gs_ = slice(g * 64, (g + 1) * 64)
            self.cp(self.qp[g].t[gs_, :, 0:np_], ptb[gs_, 0:512].rearrange("p (r q) -> p r q", q=128)[:, :, 0:np_], [pt.r], [self.qp[g].r])


    def state_copies(self):
        I, O = self.ins, self.outs
        for s_ in range(self.NS):
            self.dma("act", O["win_s"][s_, 0:511, :], I["state_win"][s_, 1:512, :], [self.wres], [self.ores], own=self.ores)
            self.dma("act", O["dil_s"][s_, 0:2047, :], I["state_dil"][s_, 1:2048, :], [self.wres], [self.ores], own=self.ores)
        self.dma("act", O["conv_s"][:, 0:2, :], I["state_conv"][:, 1:3, :], [self.wres], [self.ores], own=self.ores)

    def alloc_l0_samp(self):
        sb = self.sb
        self.idx = sb("idx", [128, 256], I32); self.ptbc = sb("ptbc", [128, 256], I32); self.iop = sb("iop", [128, 2])
        self.pg = [sb(f"pg{i}", [128, 512]) for i in range(3)]
        self.ksTs = sb("ksTs", [128, 16 * 128], BF16)
        self.vsas = sb("vsas", [128, 16, 2, 65], BF16)
        self.rawTzs = [sb(f"rawTzs{g}", [128, 2, 512], BF16) for g in range(2)]
        self.kcTs = sb("kcTs", [128, 64], BF16); self.vcs = sb("vcs", [128, 2, 65], BF16)
        self.hids = sb("hids", [128, 2, 2, 64], BF16)
        self.kwTs = sb("kwTs", [128, 4 * 128], BF16); self.vwas = sb("vwas", [128, 4, 2, 65], BF16)
        self.wtile = [sb(f"wtile{i}", [128, 256]) for i in range(2)]
        self.selfT = sb("selfT", [128, 2, 16], BF16)
        self.vself = sb("vself", [128, 2, 2, 65], BF16)
        self.negT4s = [sb(f"negT4s{g}", [128, 4], BF16) for g in range(2)]
        self.orow = sb("orow", [128, 3, 512]); self.osamp = sb("osamp", [128, 3, 512])
        self.cstm = sb("cstm", [128, 512])
        self.csT = sb("csT", [128, 3, 4, 16]); self.h0T = sb("h0T", [128, 4, 16]); self.hsT = sb("hsT", [128, 4, 16])

    def l0_samples_pass(self):
        I, O, nc = self.ins, self.outs, self.nc
        NS = self.NS
        tile = ("s", self.MT, NS, self.MT * 128)
        lo, hi = tile[3], tile[3] + NS
        z = self.zb[2]
        sv = lambda ap: ap.rearrange("p (h d) -> p h d", d=64)
        for b_ in (self.vsas, self.vcs, self.vwas, self.vself):
            self.memset(b_.t[:], 1.0, [b_.r])
        for g in range(2):
            self.memset(self.rawTzs[g].t[:], 0.0, [self.rawTzs[g].r])
        self.dma("sp", self.ptbc.t[:, :], I["page_table"].rearrange("s j -> (s j)").rearrange("(o n) -> o n", o=1).broadcast_to([128, 256]),
                 [self.wres], [self.ptbc.r])
        self.S.op("pool", lambda: nc.gpsimd.iota(self.iop.t[:, 0:1], pattern=[[0, 1]], base=0, channel_multiplier=1,
                                                 allow_small_or_imprecise_dtypes=True), [], [self.iop.r])
        self.ts(self.idx.t[:, :], self.ptbc.t[:, :], 128.0, self.iop.t[:, 0:1], ALU.mult, ALU.add, [self.ptbc.r, self.iop.r], [self.idx.r])
        self.dma("sp", self.xb[2].t[0:NS, :], I["xs"][:, :], [self.wres], [self.xb[2].r])
        self.norm_tile(tile, 0)
        self.proj_tm([tile], ("w_in_ab", None), 0, 512, 0)
        self.proj_tm([tile], ("w_in_ab", None), 512, 512, 512)
        self.proj_tm([tile], ("w_in_ab", None), 1024, 280, 1024)
        self.proj_fm(("w_in_ab", None), 1304, lo, hi, lambda j, pm: self.cp(self.xrs.t[:, j, :], pm.t[:, 0:NS], [pm.r], [self.xrs.r]))
        self.proj_fm(("w_in_ab", None), 1816, lo, hi,
                     lambda j, pm: self.act(self.ggr.t[:, j, lo:hi], pm.t[:, 0:NS], AF.Gelu_apprx_tanh, [pm.r], [self.ggr.r]))
        self.l0_post(tile, None)
        self.dma("sp", O["win_s"][:, 511, :], z.t[0:NS, 1024:1280], [z.r], [self.ores], own=z.r)
        self.rebuild_qT(tile)
        pt = self.next_ptr()
        self.tr(pt.t[:, 0:NS], z.t[0:NS, 768:896], self.ident.t[0:NS, 0:NS], [z.r, self.ident.r], [pt.r])
        self.tr(pt.t[:, 16:16 + NS], z.t[0:NS, 1024:1152], self.ident.t[0:NS, 0:NS], [z.r, self.ident.r], [pt.r])
        self.cp(self.selfT.t[:, :, 0:NS], pt.t[:, 0:32].rearrange("p (w s) -> p w s", s=16)[:, :, 0:NS], [pt.r], [self.selfT.r])
        self.cp(self.vself.t[0:NS, 0, :, 0:64], sv(z.t[0:NS, 896:1024]), [z.r], [self.vself.r])
        self.cp(self.vself.t[0:NS, 1, :, 0:64], sv(z.t[0:NS, 1152:1280]), [z.r], [self.vself.r])
        self.rglru_samples(tile)
        for s_ in range(NS):
            self.nsa_sample(s_)
        for br in range(3):
            for g in range(2):
                gate = self.gsig.t[0:NS, g * 12:(g + 1) * 12].rearrange("p (r b) -> p r b", b=3)[:, :, br]
                src = self.osamp.t[0:NS, br, g * 256:(g + 1) * 256].rearrange("p (r d) -> p r d", d=64)
                ov = self.onsa.t[0:NS, g * 256:(g + 1) * 256].rearrange("p (r d) -> p r d", d=64)
                gb = gate[:, :, None].broadcast_to([NS, 4, 64])
                if br == 0:
                    self.tt(ov, src, gb, ALU.mult, [self.osamp.r, self.gsig.r], [self.onsa.r])
                else:
                    tv = self.otmp.t[0:NS, :].rearrange("p (r d) -> p r d", d=64)
                    self.tt(tv, src, gb, ALU.mult, [self.osamp.r, self.gsig.r], [self.otmp.r])
                    self.tt(ov, ov, tv, ALU.add, [self.onsa.r, self.otmp.r], [self.onsa.r])
        pt = self.next_ptr()
        for k in range(4):
            self.tr(pt.t[:, k * 16:k * 16 + NS], self.onsa.t[0:NS, k * 128:(k + 1) * 128], self.ident.t[0:NS, 0:NS],
                    [self.onsa.r, self.ident.r], [pt.r])
        self.cp(self.catT.t[:, 0:4, lo:hi], pt.t[:, 0:64].rearrange("p (k q) -> p k q", q=16)[:, :, 0:NS], [pt.r], [self.catT.r])
        for half in range(2):
            slab = self.load_slab(("w_out_ab", None), 0, 8, half * 512, 512)
            pm = self.next_pmm()
            for k in range(8):
                self.mm(pm.t[0:NS, :], self.catT.t[:, k, lo:hi], slab.t[:, k, :], k == 0, k == 7, [self.catT.r, slab.r], [pm.r])
            self.resid_add(tile, half, pm, 0)
        self.norm_tile(tile, 1)
        self.ffn([tile], 0, lambda tl, half, ps: self.resid_add(tl, half, ps, 1))
        if self.dbg:
            self.dma("sp", O["dbg_x1s"][:, :], self.xb[2].t[0:NS, :], [self.xb[2].r], [self.ores], own=self.xb[2].r)

    def rglru_samples(self, tile):
        I, O, nc = self.ins, self.outs, self.nc
        NS = self.NS
        lo, hi = tile[3], tile[3] + NS
        c = self.rgc
        xc, r_, i_, a_, a2, u_, hs = [b for b in self.rg]
        w = lambda b: b.t[:, 0:NS]
        for i3 in range(3):
            self.dma("sp", self.cstm.t[0:NS, :], I["state_conv"][:, i3, :], [self.wres], [self.cstm.r])
            pt = self.next_ptr()
            for j in range(4):
                self.tr(pt.t[:, j * 16:j * 16 + NS], self.cstm.t[0:NS, j * 128:(j + 1) * 128], self.ident.t[0:NS, 0:NS],
                        [self.cstm.r, self.ident.r], [pt.r])
            self.cp(self.csT.t[:, i3, :, 0:NS], pt.t[:, 0:64].rearrange("p (j q) -> p j q", q=16)[:, :, 0:NS], [pt.r], [self.csT.r])
        self.dma("sp", self.cstm.t[0:NS, :], I["state_h"][:, :], [self.wres], [self.cstm.r])
        pt = self.next_ptr()
        for j in range(4):
            self.tr(pt.t[:, j * 16:j * 16 + NS], self.cstm.t[0:NS, j * 128:(j + 1) * 128], self.ident.t[0:NS, 0:NS],
                    [self.cstm.r, self.ident.r], [pt.r])
        self.cp(self.h0T.t[:, :, 0:NS], pt.t[:, 0:64].rearrange("p (j q) -> p j q", q=16)[:, :, 0:NS], [pt.r], [self.h0T.r])
        for j in range(4):
            self.ts(w(xc), self.csT.t[:, 0, j, 0:NS], self.convw.t[:, j, 0:1], c.t[:, 0, j:j + 1], ALU.mult, ALU.add,
                    [self.csT.r, self.convw.r, c.r], [xc.r])
            for k in (1, 2):
                self.stt(w(xc), self.csT.t[:, k, j, 0:NS], self.convw.t[:, j, k:k + 1], w(xc), ALU.mult, ALU.add,
                         [self.csT.r, self.convw.r, xc.r], [xc.r])
            self.stt(w(xc), self.xrs.t[:, j, 0:NS], self.convw.t[:, j, 3:4], w(xc), ALU.mult, ALU.add,
                     [self.xrs.r, self.convw.r, xc.r], [xc.r])
            self.cp(self.xcb.t[:, 0:NS], w(xc), [xc.r], [self.xcb.r])
            pr = self.next_pmm()
            self.mm(pr.t[:, 0:NS], self.wabd.t[:, j, :], self.xcb.t[:, 0:NS], True, True, [self.wabd.r, self.xcb.r], [pr.r])
            pi = self.next_pmm()
            self.mm(pi.t[:, 0:NS], self.wxbd.t[:, j, :], self.xcb.t[:, 0:NS], True, True, [self.wxbd.r, self.xcb.r], [pi.r])
            self.act(w(r_), pr.t[:, 0:NS], AF.Sigmoid, [pr.r, c.r], [r_.r], bias=c.t[:, 1, j:j + 1])
            self.act(w(i_), pi.t[:, 0:NS], AF.Sigmoid, [pi.r, c.r], [i_.r], bias=c.t[:, 2, j:j + 1])
            self.act(w(a_), w(r_), AF.Exp, [r_.r, c.r], [a_.r], scale=c.t[:, 4, j:j + 1])
            self.act(w(a2), w(r_), AF.Exp, [r_.r, c.r], [a2.r], scale=c.t[:, 5, j:j + 1])
            self.ts(w(a2), w(a2), -1.0, 1.0, ALU.mult, ALU.add, [a2.r], [a2.r])
            self.act(w(a2), w(a2), AF.Sqrt, [a2.r], [a2.r])
            self.tt(w(u_), w(a2), w(i_), ALU.mult, [a2.r, i_.r], [u_.r])
            self.tt(w(u_), w(u_), w(xc), ALU.mult, [u_.r, xc.r], [u_.r])
            self.tt(w(hs), w(a_), self.h0T.t[:, j, 0:NS], ALU.mult, [a_.r, self.h0T.r], [hs.r])
            self.tt(self.hsT.t[:, j, 0:NS], w(hs), w(u_), ALU.add, [hs.r, u_.r], [self.hsT.r])
            self.tt(self.catT.t[:, 4 + j, lo:hi], self.hsT.t[:, j, 0:NS], self.ggr.t[:, j, lo:hi], ALU.mult,
                    [self.hsT.r, self.ggr.r], [self.catT.r])
        for src, dst in ((self.hsT, O["h_s"][:, :]), (self.xrs, O["conv_s"][:, 2, :])):
            pt = self.next_ptr()
            for j in range(4):
                self.tr(pt.t[0:NS, j * 128:(j + 1) * 128], src.t[:, j, 0:NS], self.ident.t[:, :], [src.r, self.ident.r], [pt.r])
            self.cp(self.cstm.t[0:NS, :], pt.t[0:NS, :], [pt.r], [self.cstm.r])
            self.dma("sp", dst, self.cstm.t[0:NS, :], [self.cstm.r], [self.ores], own=self.cstm.r)

    def fin_row(self, g, br):
        o = self.pacc[g].t[0:1, 0:260].rearrange("p (r e) -> p r e", e=65)
        rd = self.den
        self.ts(rd.t[0:1, 0:4], o[:, :, 64], 1e-30, None, ALU.max, None, [self.pacc[g].r], [rd.r])
        self.S.op("dve", lambda: self.nc.vector.reciprocal(out=rd.t[0:1, 0:4], in_=rd.t[0:1, 0:4]), [rd.r], [rd.r])
        ov = self.orow.t[0:1, br, g * 256:(g + 1) * 256].rearrange("p (r d) -> p r d", d=64)
        self.tt(ov, o[:, :, 0:64], rd.t[0:1, 0:4, None].broadcast_to([1, 4, 64]), ALU.mult, [self.pacc[g].r, rd.r], [self.orow.r])

    def pv4s(self, g, pT, nk, v_ap, first, last, vres):
        for r in range(4):
            self.mm(self.pacc[g].t[0:1, r * 65:(r + 1) * 65], pT.t[0:nk, r:r + 1], v_ap, first and r == 0, last,
                    [pT.r, vres], [self.pacc[g].r], skip=True)

    def nsa_sample(self, s_):
        I, nc = self.ins, self.nc
        sv = lambda ap: ap.rearrange("p (h d) -> p h d", d=64)
        qc = lambda g: self.qp[g].t[:, :, s_]
        for pgi in range(16):
            pg = self.pg[pgi % 3]
            col = s_ * 16 + pgi
            self.S.dma_fn("pool", lambda: nc.gpsimd.indirect_dma_start(
                out=pg.t[:, :], out_offset=None, in_=I["cache"][:, :],
                in_offset=bass.IndirectOffsetOnAxis(ap=self.idx.t[:, col:col + 1], axis=0)), [self.wres, self.idx.r], [pg.r])
            pt = self.next_ptr()
            self.tr(pt.t[:, 0:128], pg.t[:, 256:384], self.ident.t[:, :], [pg.r, self.ident.r], [pt.r])
            self.tr(pt.t[:, 128:256], pg.t[:, 0:128], self.ident.t[:, :], [pg.r, self.ident.r], [pt.r])
            self.tr(pt.t[:, 256:384], pg.t[:, 128:256], self.ident.t[:, :], [pg.r, self.ident.r], [pt.r])
            self.cp(self.ksTs.t[:, pgi * 128:(pgi + 1) * 128], pt.t[:, 0:128], [pt.r], [self.ksTs.r])
            q4 = pgi % 4
            for g in range(2):
                gs_ = slice(g * 64, (g + 1) * 64)
                self.cp(self.rawTzs[g].t[gs_, :, q4 * 128:(q4 + 1) * 128], pt.t[gs_, 128:384].rearrange("p (c q) -> p c q", q=128),
                        [pt.r], [self.rawTzs[g].r])
            self.act(self.vsas.t[:, pgi, :, 0:64], sv(pg.t[:, 384:512]), AF.Copy, [pg.r], [self.vsas.r])
            if q4 == 3:
                grp = pgi // 4
                for c in range(2):
                    pm = self.next_pmm()
                    for g in range(2):
                        for l in range(32):
                            self.mm(pm.t[0:64, g * 16:(g + 1) * 16], self.w1sb.t[:, c, l, :], self.rawTzs[g].t[:, c, l:512:32],
                                    l == 0, l == 31, [self.w1sb.r, self.rawTzs[g].r], [pm.r])
                    self.act(self.hids.t[0:64, c, :, grp * 16:(grp + 1) * 16], pm.t[0:64, 0:32].rearrange("p (g n) -> p g n", n=16),
                             AF.Gelu_apprx_tanh, [pm.r, self.cmpb.r], [self.hids.r], bias=self.cmpb.t[0:64, c:c + 1])
        for c in range(2):
            pm2 = self.next_pmm()
            for g in range(2):
                self.mm(pm2.t[0:64, g * 64:(g + 1) * 64], self.hids.t[0:64, c, g, :], self.w2sb.t[0:64, c, :], True, True,
                        [self.hids.r, self.w2sb.r], [pm2.r])
            if c == 0:
                self.cp(self.kcrow.t[0:64, :], pm2.t[0:64, 0:128], [pm2.r], [self.kcrow.r])
                self.headnorm(self.kcrow.t[0:64, :], 64, 2, self.kg.t[:, 0, :], sv(self.kcrow.t[0:64, :]), [self.kcrow.r, self.kg.r],
                              [self.kcrow.r])
                pt = self.next_ptr()
                self.tr(pt.t[:, 0:64], self.kcrow.t[0:64, :], self.ident.t[0:64, 0:64], [self.kcrow.r, self.ident.r], [pt.r])
                self.cp(self.kcTs.t[:, :], pt.t[:, 0:64], [pt.r], [self.kcTs.r])
            else:
                self.cp(self.vcs.t[0:64, :, 0:64], sv(pm2.t[0:64, 0:128]), [pm2.r], [self.vcs.r])
        for g in range(2):
            pm = self.next_pmm()
            for r in range(4):
                self.mm(pm.t[0:1, r * 64:(r + 1) * 64], self.qp[g].t[:, r, s_:s_ + 1], self.kcTs.t[:, :], True, True,
                        [self.qp[g].r, self.kcTs.r], [pm.r])
            for r in range(4):
                self.act(self.e32.t[0:1, r * 64:(r + 1) * 64], pm.t[0:1, r * 64:(r + 1) * 64], AF.Exp, [pm.r], [self.e32.r, self.den.r],
                         accum_out=self.den.t[0:1, 4 + r:5 + r])
            self.ts(self.den.t[0:1, 4:8], self.den.t[0:1, 4:8], 1e-30, None, ALU.max, None, [self.den.r], [self.den.r])
            self.S.op("dve", lambda: nc.vector.reciprocal(out=self.den.t[0:1, 4:8], in_=self.den.t[0:1, 4:8]), [self.den.r], [self.den.r])
            for r in range(4):
                if r == 0:
                    self.ts(self.impacc.t[0:1, 0:64], self.e32.t[0:1, 0:64], self.den.t[0:1, 4:5], None, ALU.mult, None,
                            [self.e32.r, self.den.r], [self.impacc.r])
                else:
                    self.stt(self.impacc.t[0:1, 0:64], self.e32.t[0:1, r * 64:(r + 1) * 64], self.den.t[0:1, 4 + r:5 + r],
                             self.impacc.t[0:1, 0:64], ALU.mult, ALU.add, [self.e32.r, self.den.r, self.impacc.r], [self.impacc.r])
            imp = self.imp
            self.S.op("dve", lambda: nc.vector.tensor_reduce(out=imp.t[0:1, 0:32],
                                                             in_=self.impacc.t[0:1, 0:64].rearrange("p (n two) -> p n two", two=2),
                                                             axis=AX.X, op=ALU.add), [self.impacc.r], [imp.r])
            self.memset(imp.t[0:1, 32:33], 1e4, [imp.r], eng="dve")
            self.memset(imp.t[0:1, 0:1], 1e4, [imp.r], eng="dve")
            self.S.op("dve", lambda: nc.vector.max(out=self.m8.t[0:1, 0:8], in_=imp.t[0:1, 0:33]), [imp.r], [self.m8.r])
            self.S.op("dve", lambda: nc.vector.match_replace(out=self.imp2.t[0:1, 0:33], in_to_replace=self.m8.t[0:1, 0:8],
                                                             in_values=imp.t[0:1, 0:33], imm_value=-2.0), [imp.r, self.m8.r], [self.imp2.r])
            self.S.op("dve", lambda: nc.vector.max(out=self.m8.t[0:1, 8:16], in_=self.imp2.t[0:1, 0:33]), [self.imp2.r], [self.m8.r])
            self.ts(self.negsel.t[0:1, 0:33], imp.t[0:1, 0:33], self.m8.t[0:1, 15:16], NEG, ALU.is_lt, ALU.mult,
                    [imp.r, self.m8.r], [self.negsel.r])
            pt = self.next_ptr()
            self.tr(pt.t[0:33, 0:1], self.negsel.t[0:1, 0:33], self.ident.t[0:1, 0:1], [self.negsel.r, self.ident.r], [pt.r])
            self.cp(self.negT4s[g].t[0:33, :], pt.t[0:33, 0:1].broadcast_to([33, 4]), [pt.r], [self.negT4s[g].r])
        for g in range(2):
            pa = self.next_patt()
            self.mm(pa.t[0:64, 0:4], self.kcTs.t[:, :], qc(g), True, True, [self.kcTs.r, self.qp[g].r], [pa.r])
            pT = self.next_pT()
            self.act(pT.t[0:64, 0:4], pa.t[0:64, 0:4], AF.Exp, [pa.r], [pT.r])
            self.pv4s(g, pT, 64, self.vcs.t[0:64, g, :], True, True, self.vcs.r)
            self.fin_row(g, 0)
        for g in range(2):
            for kt in range(16):
                pa = self.next_patt()
                self.mm(pa.t[:, 0:4], self.ksTs.t[:, kt * 128:(kt + 1) * 128], qc(g), True, False, [self.ksTs.r, self.qp[g].r], [pa.r])
                self.mm(pa.t[:, 0:4], self.esmall.t[0:33, kt * 128:(kt + 1) * 128], self.negT4s[g].t[0:33, :], False, True,
                        [self.esmall.r, self.negT4s[g].r], [pa.r])
                pT = self.next_pT()
                self.act(pT.t[:, 0:4], pa.t[:, 0:4], AF.Exp, [pa.r], [pT.r])
                self.pv4s(g, pT, 128, self.vsas.t[:, kt, g, :], kt == 0, False, self.vsas.r)
            pa = self.next_patt()
            self.mm(pa.t[0:16, 0:4], self.selfT.t[:, 0, :], qc(g), True, True, [self.selfT.r, self.qp[g].r], [pa.r])
            pT = self.next_pT()
            self.act(pT.t[0:16, 0:4], pa.t[0:16, 0:4], AF.Exp, [pa.r], [pT.r])
            self.ts(pT.t[0:16, 0:4], pT.t[0:16, 0:4], self.eye16.t[0:16, s_:s_ + 1], None, ALU.mult, None, [pT.r, self.eye16.r], [pT.r])
            self.pv4s(g, pT, 16, self.vself.t[0:16, 0, g, :], False, True, self.vself.r)
            self.fin_row(g, 1)
        for wt in range(4):
            w = self.wtile[wt % 2]
            self.dma("sp", w.t[:, :], I["state_win"][s_, wt * 128:(wt + 1) * 128, :], [self.wres], [w.r])
            pt = self.next_ptr()
            self.tr(pt.t[:, 0:128], w.t[:, 0:128], self.ident.t[:, :], [w.r, self.ident.r], [pt.r])
            self.cp(self.kwTs.t[:, wt * 128:(wt + 1) * 128], pt.t[:, 0:128], [pt.r], [self.kwTs.r])
            self.act(self.vwas.t[:, wt, :, 0:64], sv(w.t[:, 128:256]), AF.Copy, [w.r], [self.vwas.r])
        for g in range(2):
            for wt in range(4):
                pa = self.next_patt()
                self.mm(pa.t[:, 0:4], self.kwTs.t[:, wt * 128:(wt + 1) * 128], qc(g), True, True, [self.kwTs.r, self.qp[g].r], [pa.r])
                pT = self.next_pT()
                self.act(pT.t[:, 0:4], pa.t[:, 0:4], AF.Exp, [pa.r], [pT.r])
                self.pv4s(g, pT, 128, self.vwas.t[:, wt, g, :], wt == 0, False, self.vwas.r)
            pa = self.next_patt()
            self.mm(pa.t[0:16, 0:4], self.selfT.t[:, 1, :], qc(g), True, True, [self.selfT.r, self.qp[g].r], [pa.r])
            pT = self.next_pT()
            self.act(pT.t[0:16, 0:4], pa.t[0:16, 0:4], AF.Exp, [pa.r], [pT.r])
            self.ts(pT.t[0:16, 0:4], pT.t[0:16, 0:4], self.eye16.t[0:16, s_:s_ + 1], None, ALU.mult, None, [pT.r, self.eye16.r], [pT.r])
            self.pv4s(g, pT, 16, self.vself.t[0:16, 1, g, :], False, True, self.vself.r)
            self.fin_row(g, 2)
        self.dma("sp", self.osamp.t[s_:s_ + 1, :, :], self.orow.t[0:1, :, :], [self.orow.r], [self.osamp.r])

    def finish_l0_outputs(self):
        O = self.outs
        if self.dbg:
            self.dma("pool", O["dbg_kcT"][:, :], self.kcT.t[:, :], [self.kcT.r], [self.ores], own=self.kcT.r)
            self.dma("pool", O["dbg_vc"][:, :], self.vc_aug.t[:, 0, :, :].rearrange("p g e -> p (g e)"), [self.vc_aug.r], [self.ores], own=self.vc_aug.r)
        self.dma("pool", O["h_p"].rearrange("(c p) -> p c", p=128), self.hstate.t[:, :], [self.hstate.r], [self.ores],
                 own=self.hstate.r, allow_slow_non_contiguous=True)
        for i_ in range(3):
            self.dma("pool", O["conv_p"][i_].rearrange("(c p) -> p c", p=128), self.xrbuf.t[:, :, i_], [self.xrbuf.r], [self.ores],
                     own=self.xrbuf.r, allow_slow_non_contiguous=True)


class ProgL1(ProgL0):
    def alloc_l1(self):
        sb = self.sb
        self.NR = min(self.NT, 18)
        self.kT2 = sb("kT2", [128, 4, self.NR * 128], BF16)
        self.v2 = sb("v2", [128, self.NR, 8, 65], BF16)
        self.wsT = sb("wsT", [128, 8, 128], BF16); self.bsT = sb("bsT", [128, 8])
        self.vgain = sb("vgain", [128, 512]); self.dqg = sb("dqg", [128, 64]); self.dkg = sb("dkg", [128, 64])
        self.dilm = sb("dilm", [128, 17 * 128], BF16)
        self.tril = sb("tril", [128, 128])
        self.sq = sb("sq1", [128, 512]); self.den = sb("den1", [128, 8])
        self.qb = sb("qb1", [128, 512], BF16); self.qp2 = [sb(f"qp2_{g}", [128, 4, 128], BF16) for g in range(2)]
        self.vnb = sb("vnb", [128, 512], BF16)
        self.oc = sb("oc", [128, 512]); self.od = sb("od", [128, 512])
        self.pT = [sb(f"pT1_{i}", [128, 512], BF16) for i in range(2)]
        self.zb = [sb(f"zc{i}", [128, 2560]) for i in range(self.MT)] + [sb("zcs", [128, 2560])]
        if self.do_samples:
            self.wsb = sb("wsb", [128, 8]); self.bsb = sb("bsb", [128, 8])
            self.dtile = [sb(f"dtile{i}", [128, 1024]) for i in range(2)]
            self.kTs = sb("kTs", [128, 4, 128], BF16); self.vas = sb("vas", [128, 8, 65], BF16)
            self.kselfT = sb("kselfT", [128, 4, 16], BF16); self.vself2 = sb("vself2", [128, 8, 65], BF16)
            self.odrow = sb("odrow", [128, 512]); self.ods = sb("ods", [128, 512])

    def setup_l1(self):
        I = self.ins
        slow = dict(allow_slow_non_contiguous=True)
        self.ld(self.vgain, I["gmlp_v_gain"][0:1, :].broadcast_to([128, 512]))
        self.ld(self.dqg, I["dil_q_gain"][0:1, :].broadcast_to([128, 64]))
        self.ts(self.dqg.t[:], self.dqg.t[:], 0.125, None, ALU.mult, None, [self.dqg.r], [self.dqg.r])
        self.ld(self.dkg, I["dil_k_gain"][0:1, :].broadcast_to([128, 64]))
        self.ld(self.dilm, I["c_dilmult"][:, :])
        self.ld(self.tril, I["c_tril"][:, :])
        self.dma("pool", self.bsT.t[:, :], I["gmlp_bs"].rearrange("g t -> t g"), [self.wres], [self.bsT.r], **slow)
        for g in range(8):
            w = self.oc
            self.dma("pool", w.t[:, 0:128], I["gmlp_ws"][g], [self.wres], [w.r])
            self.tt(w.t[:, 0:128], w.t[:, 0:128], self.tril.t[:, :], ALU.mult, [w.r, self.tril.r], [w.r])
            pt = self.next_ptr()
            self.tr(pt.t[:, 0:128], w.t[:, 0:128], self.ident.t[:, :], [w.r, self.ident.r], [pt.r])
            self.cp(self.wsT.t[:, g, :], pt.t[:, 0:128], [pt.r], [self.wsT.r])
        self.memset(self.v2.t[:], 1.0, [self.v2.r])
        if self.do_samples:
            self.dma("pool", self.wsb.t[:, :], I["gmlp_ws"][:, 0:1, 0:1].rearrange("g a b -> (a b) g").broadcast_to([128, 8]),
                     [self.wres], [self.wsb.r], **slow)
            self.dma("pool", self.bsb.t[:, :], I["gmlp_bs"][:, 0:1].rearrange("g a -> a g").broadcast_to([128, 8]),
                     [self.wres], [self.bsb.r], **slow)
            self.memset(self.vas.t[:], 1.0, [self.vas.r]); self.memset(self.vself2.t[:], 1.0, [self.vself2.r])
        self.memset(self.kT2.t[:], 0.0, [self.kT2.r])
        for g in range(2):
            self.memset(self.qp2[g].t[:], 0.0, [self.qp2[g].r])

    def l1_post(self, tile, t):
        kind, i, np_, h0 = tile
        z = self.zb[i]
        O = self.outs
        st = self.den
        sv = lambda ap: ap.rearrange("p (h d) -> p h d", d=64)
        self.act(z.t[0:np_, 0:1024], z.t[0:np_, 0:1024], AF.Gelu_apprx_tanh, [z.r], [z.r])
        self.act(self.junk.t[0:np_, 0:512], z.t[0:np_, 512:1024], AF.Square, [z.r], [self.junk.r, st.r], accum_out=st.t[0:np_, 0:1])
        self.ts(st.t[0:np_, 0:1], st.t[0:np_, 0:1], 1.0 / 512, EPS, ALU.mult, ALU.add, [st.r], [st.r])
        self.act(st.t[0:np_, 0:1], st.t[0:np_, 0:1], AF.Sqrt, [st.r], [st.r])
        self.S.op("dve", lambda: self.nc.vector.reciprocal(out=st.t[0:np_, 0:1], in_=st.t[0:np_, 0:1]), [st.r], [st.r])
        self.stt(z.t[0:np_, 512:1024], z.t[0:np_, 512:1024], st.t[0:np_, 0:1], self.vgain.t[0:np_, :], ALU.mult, ALU.mult,
                 [z.r, st.r, self.vgain.r], [z.r])
        self.headnorm(z.t[0:np_, 1024:1536], np_, 8, self.dqg.t, sv(self.qb.t[0:np_, :]), [z.r, self.dqg.r], [self.qb.r])
        self.headnorm(z.t[0:np_, 1536:2048], np_, 8, self.dkg.t, sv(z.t[0:np_, 1536:2048]), [z.r, self.dkg.r], [z.r])
        pt = self.next_ptr()
        ptb = pt.t[:].bitcast(BF16)
        for j in range(4):
            self.tr(ptb[:, j * 128:j * 128 + np_], self.qb.t[0:np_, j * 128:(j + 1) * 128], self.identb.t[0:np_, 0:np_],
                    [self.qb.r, self.identb.r], [pt.r])
        for g in range(2):
            gs_ = slice(g * 64, (g + 1) * 64)
            self.cp(self.qp2[g].t[gs_, :, 0:np_], ptb[gs_, 0:512].rearrange("p (r q) -> p r q", q=128)[:, :, 0:np_], [pt.r], [self.qp2[g].r])
        if kind == "p":
            self.cp(self.vnb.t[:, :], z.t[0:128, 512:1024], [z.r], [self.vnb.r])
            base = self.T - min(2048, self.T)
            if t * 128 >= base:
                self.dma("pool", O["dil_p"][t * 128 - base:(t + 1) * 128 - base, :], z.t[0:128, 1536:2560], [z.r], [self.ores], own=z.r)
            slot = t % self.NR
            pt = self.next_ptr()
            for j in range(4):
                self.tr(pt.t[:, j * 128:(j + 1) * 128], z.t[0:128, 1536 + j * 128:1536 + (j + 1) * 128], self.ident.t[:, :],
                        [z.r, self.ident.r], [pt.r])
            self.cp(self.kT2.t[:, :, slot * 128:(slot + 1) * 128], pt.t[:, :].rearrange("p (j q) -> p j q", q=128), [pt.r], [self.kT2.r])
            self.cp(self.v2.t[:, slot, :, 0:64], sv(z.t[0:128, 2048:2560]), [z.r], [self.v2.r], eng="pool")
        else:
            self.dma("pool", O["gv_s"][:, :], z.t[0:np_, 512:1024], [z.r], [self.ores], own=z.r)

    def gmlp_prompt_tile(self, tile):
        kind, i, np_, h0 = tile
        z = self.zb[i]
        pm = self.next_pmm()
        for g in range(8):
            self.mm(pm.t[:, g * 64:(g + 1) * 64], self.wsT.t[:, g, :], self.vnb.t[:, g * 64:(g + 1) * 64], True, True,
                    [self.wsT.r, self.vnb.r], [pm.r])
        for g in range(8):
            cs = slice(g * 64, (g + 1) * 64)
            self.stt(self.oc.t[:, cs], pm.t[:, cs], self.bsT.t[:, g:g + 1], z.t[0:128, cs], ALU.add, ALU.mult,
                     [pm.r, self.bsT.r, z.r], [self.oc.r])
        pt = self.next_ptr()
        for k in range(4):
            self.tr(pt.t[:, k * 128:(k + 1) * 128], self.oc.t[:, k * 128:(k + 1) * 128], self.ident.t[:, :], [self.oc.r, self.ident.r], [pt.r])
        self.cp(self.catT.t[:, 0:4, h0:h0 + 128], pt.t[:, :].rearrange("p (k q) -> p k q", q=128), [pt.r], [self.catT.r])

    def dil_prompt_tile(self, tile, t):
        kind, i, np_, h0 = tile
        nc = self.nc
        dls = list(range(min(16, t), -1, -1))
        for hg in range(2):
            for idx, dl in enumerate(dls):
                kt = t - dl
                slot = kt % self.NR
                pa = self.next_patt()
                for jj in range(4):
                    h = 4 * hg + jj
                    par, j = h % 2, h // 2
                    self.mm(pa.t[:, jj * 128:(jj + 1) * 128], self.kT2.t[:, j, slot * 128:(slot + 1) * 128],
                            self.qp2[par].t[:, j, :], True, True, [self.kT2.r, self.qp2[par].r], [pa.r])
                pT = self.next_pT()
                self.act(pT.t[:, :], pa.t[:, :], AF.Exp, [pa.r], [pT.r])
                pv = pT.t[:, :].rearrange("p (r q) -> p r q", q=128)
                self.tt(pv, pv, self.dilm.t[:, None, dl * 128:(dl + 1) * 128].broadcast_to([128, 4, 128]), ALU.mult,
                        [pT.r, self.dilm.r], [pT.r], eng="pool")
                for jj in range(4):
                    h = 4 * hg + jj
                    self.mm(self.pacc[hg].t[:, jj * 65:(jj + 1) * 65], pT.t[:, jj * 128:(jj + 1) * 128], self.v2.t[:, slot, h, :],
                            idx == 0 and jj == 0, idx == len(dls) - 1, [pT.r, self.v2.r], [self.pacc[hg].r], skip=True)
            o = self.pacc[hg].t[:, 0:260].rearrange("p (r e) -> p r e", e=65)
            rd = self.den
            self.ts(rd.t[:, 0:4], o[:, :, 64], 1e-30, None, ALU.max, None, [self.pacc[hg].r], [rd.r])
            self.S.op("dve", lambda: nc.vector.reciprocal(out=rd.t[:, 0:4], in_=rd.t[:, 0:4]), [rd.r], [rd.r])
            ov = self.od.t[:, hg * 256:(hg + 1) * 256].rearrange("p (r d) -> p r d", d=64)
            self.tt(ov, o[:, :, 0:64], rd.t[:, 0:4, None].broadcast_to([128, 4, 64]), ALU.mult, [self.pacc[hg].r, rd.r], [self.od.r])
        pt = self.next_ptr()
        for k in range(4):
            self.tr(pt.t[:, k * 128:(k + 1) * 128], self.od.t[:, k * 128:(k + 1) * 128], self.ident.t[:, :], [self.od.r, self.ident.r], [pt.r])
        self.cp(self.catT.t[:, 4:8, h0:h0 + 128], pt.t[:, :].rearrange("p (k q) -> p k q", q=128), [pt.r], [self.catT.r])

    def l1_samples(self, tile):
        I, O, nc = self.ins, self.outs, self.nc
        kind, i, NS, lo = tile
        hi = lo + NS
        z = self.zb[i]
        sv = lambda ap: ap.rearrange("p (h d) -> p h d", d=64)
        for g in range(8):
            cs = slice(g * 64, (g + 1) * 64)
            self.ts(self.oc.t[0:NS, cs], z.t[0:NS, 512 + g * 64:512 + (g + 1) * 64], self.wsb.t[0:NS, g:g + 1], self.bsb.t[0:NS, g:g + 1],
                    ALU.mult, ALU.add, [z.r, self.wsb.r, self.bsb.r], [self.oc.r])
            self.tt(self.oc.t[0:NS, cs], self.oc.t[0:NS, cs], z.t[0:NS, cs], ALU.mult, [self.oc.r, z.r], [self.oc.r])
        pt = self.next_ptr()
        for k in range(4):
            self.tr(pt.t[:, k * 16:k * 16 + NS], self.oc.t[0:NS, k * 128:(k + 1) * 128], self.ident.t[0:NS, 0:NS], [self.oc.r, self.ident.r], [pt.r])
        self.cp(self.catT.t[:, 0:4, lo:hi], pt.t[:, 0:64].rearrange("p (k q) -> p k q", q=16)[:, :, 0:NS], [pt.r], [self.catT.r])
        self.dma("sp", O["dil_s"][:, 2047, :], z.t[0:NS, 1536:2560], [z.r], [self.ores], own=z.r)
        pt = self.next_ptr()
        for j in range(4):
            self.tr(pt.t[:, j * 16:j * 16 + NS], z.t[0:NS, 1536 + j * 128:1536 + (j + 1) * 128], self.ident.t[0:NS, 0:NS], [z.r, self.ident.r], [pt.r])
        self.cp(self.kselfT.t[:, :, 0:NS], pt.t[:, 0:64].rearrange("p (j q) -> p j q", q=16)[:, :, 0:NS], [pt.r], [self.kselfT.r])
        self.cp(self.vself2.t[0:NS, :, 0:64], sv(z.t[0:NS, 2048:2560]), [z.r], [self.vself2.r])
        for s_ in range(NS):
            for pi_, (d, start) in enumerate(((1, 1920), (4, 1536), (16, 0))):
                dt_ = self.dtile[pi_ % 2]
                self.dma("sp", dt_.t[:, :], I["state_dil"][s_, start:2048:d, :], [self.wres], [dt_.r])
                pt = self.next_ptr()
                for j in range(4):
                    self.tr(pt.t[:, j * 128:(j + 1) * 128], dt_.t[:, j * 128:(j + 1) * 128], self.ident.t[:, :], [dt_.r, self.ident.r], [pt.r])
                self.cp(self.kTs.t[:, :, :], pt.t[:, :].rearrange("p (j q) -> p j q", q=128), [pt.r], [self.kTs.r])
                self.act(self.vas.t[:, :, 0:64], sv(dt_.t[:, 512:1024]), AF.Copy, [dt_.r], [self.vas.r])
                pa = self.next_patt()
                for h in range(8):
                    par, j = h % 2, h // 2
                    self.mm(pa.t[:, h:h + 1], self.kTs.t[:, j, :], self.qp2[par].t[:, j, s_:s_ + 1], True, True,
                            [self.kTs.r, self.qp2[par].r], [pa.r])
                pT = self.next_pT()
                self.act(pT.t[:, 0:8], pa.t[:, 0:8], AF.Exp, [pa.r], [pT.r])
                for h in range(8):
                    self.mm(self.pacc[h // 4].t[0:1, (h % 4) * 65:(h % 4 + 1) * 65], pT.t[:, h:h + 1], self.vas.t[:, h, :],
                            pi_ == 0 and h % 4 == 0, False, [pT.r, self.vas.r], [self.pacc[h // 4].r], skip=True)
            pa = self.next_patt()
            for h in range(8):
                par, j = h % 2, h // 2
                self.mm(pa.t[0:16, h:h + 1], self.kselfT.t[:, j, :], self.qp2[par].t[:, j, s_:s_ + 1], True, True,
                        [self.kselfT.r, self.qp2[par].r], [pa.r])
            pT = self.next_pT()
            self.act(pT.t[0:16, 0:8], pa.t[0:16, 0:8], AF.Exp, [pa.r], [pT.r])
            self.ts(pT.t[0:16, 0:8], pT.t[0:16, 0:8], self.eye16.t[0:16, s_:s_ + 1], None, ALU.mult, None, [pT.r, self.eye16.r], [pT.r])
            self.ts(pT.t[0:16, 0:8], pT.t[0:16, 0:8], 3.0, None, ALU.mult, None, [pT.r], [pT.r])
            for h in range(8):
                self.mm(self.pacc[h // 4].t[0:1, (h % 4) * 65:(h % 4 + 1) * 65], pT.t[0:16, h:h + 1], self.vself2.t[0:16, h, :],
                        False, True, [pT.r, self.vself2.r], [self.pacc[h // 4].r], skip=True)
            for hg in range(2):
                o = self.pacc[hg].t[0:1, 0:260].rearrange("p (r e) -> p r e", e=65)
                rd = self.den
                self.ts(rd.t[0:1, 0:4], o[:, :, 64], 1e-30, None, ALU.max, None, [self.pacc[hg].r], [rd.r])
                self.S.op("dve", lambda: nc.vector.reciprocal(out=rd.t[0:1, 0:4], in_=rd.t[0:1, 0:4]), [rd.r], [rd.r])
                ov = self.odrow.t[0:1, hg * 256:(hg + 1) * 256].rearrange("p (r d) -> p r d", d=64)
                self.tt(ov, o[:, :, 0:64], rd.t[0:1, 0:4, None].broadcast_to([1, 4, 64]), ALU.mult, [self.pacc[hg].r, rd.r], [self.odrow.r])
            self.dma("sp", self.ods.t[s_:s_ + 1, :], self.odrow.t[0:1, :], [self.odrow.r], [self.ods.r])
        pt = self.next_ptr()
        for k in range(4):
            self.tr(pt.t[:, k * 16:k * 16 + NS], self.ods.t[0:NS, k * 128:(k + 1) * 128], self.ident.t[0:NS, 0:NS], [self.ods.r, self.ident.r], [pt.r])
        self.cp(self.catT.t[:, 4:8, lo:hi], pt.t[:, 0:64].rearrange("p (k q) -> p k q", q=16)[:, :, 0:NS], [pt.r], [self.catT.r])

    def pass2_macro(self, m):
        I, O = self.ins, self.outs
        ws = self.do_samples and m == 0
        tiles = self.tiles_of(m, ws)
        for (kind, i, np_, h0) in tiles:
            if kind == "p":
                t = m * self.MT + i
                self.dma("sp", self.xb[i].t[:, :], self.x1d[t * 128:(t + 1) * 128, :], [self.x1res[t]], [self.xb[i].r])
        for tile in tiles:
            self.norm_tile(tile, 0)
        for s in range(5):
            self.proj_tm(tiles, ("w_in_cd", None), s * 512, 512, s * 512)
        for tile in tiles:
            if tile[0] == "p":
                t = m * self.MT + tile[1]
                self.l1_post(tile, t)
                self.gmlp_prompt_tile(tile)
                self.dil_prompt_tile(tile, t)
            else:
                self.l1_post(tile, None)
                self.l1_samples(tile)
        for half in range(2):
            slab = self.load_slab(("w_out_cd", None), 0, 8, half * 512, 512)
            for tile in tiles:
                kind, i, np_, h0 = tile
                pm = self.next_pmm()
                for k in range(8):
                    self.mm(pm.t[0:np_, :], self.catT.t[:, k, h0:h0 + np_], slab.t[:, k, :], k == 0, k == 7, [self.catT.r, slab.r], [pm.r])
                self.resid_add(tile, half, pm, 0)
        for tile in tiles:
            self.norm_tile(tile, 1)
        self.ffn(tiles, 1, lambda tile, half, ps: self.resid_add(tile, half, ps, 1))
        for (kind, i, np_, h0) in tiles:
            if kind == "p":
                t = m * self.MT + i
                self.dma("pool", O["y_p"][t * 128:(t + 1) * 128, :], self.xb[i].t[:, :], [self.xb[i].r], [self.ores], own=self.xb[i].r)
            else:
                self.dma("pool", O["y_s"][:, :], self.xb[i].t[0:np_, :], [self.xb[i].r], [self.ores], own=self.xb[i].r)

    def build(self):
        self.declare()
        self.alloc()
        self.setup_consts()
        self.setup_mod_inputs()
        common = self.st
        with ExitStack() as st1:
            self.st = st1
            self.alloc_l0()
            self.precast(["w_in_ab", "w_out_ab", "w_ffn_gate", "w_ffn_up", "w_ffn_down"])
            self.compute_mod(0)
            if self.do_l1:
                self.precast(["w_in_cd", "w_out_cd"])
            self.setup_l0()
            if self.do_samples:
                self.state_copies()
                with ExitStack() as sts:
                    self.st = sts
                    self.alloc_l0_samp()
                    self.l0_samples_pass()
                    self.S.barrier()
                self.st = st1
            with ExitStack() as stp:
                self.st = stp
                self.alloc_l0_prompt()
                for m in range(self.NM):
                    self.pass1_macro(m)
                self.finish_l0_outputs()
                self.S.barrier()
            self.st = st1
        if self.do_l1:
            with ExitStack() as st2:
                self.st = st2
                self.alloc_l1()
                self.compute_mod(1)
                self.setup_l1()
                for m in range(self.NM):
                    self.pass2_macro(m)
                self.S.barrier()
        self.st = common
        self.S.finish("sp")
        self.st.close()
        return self.nc


def core_inputs(inp, c, T, NS, b, s0):
    f = lambda a: np.ascontiguousarray(a, dtype=np.float32)
    cm = np.zeros((33, D), np.float32)
    cm[0:NS] = inp["c_sample"][s0:s0 + NS]
    cm[32] = inp["c_prompt"][b]
    m = {
        "xp": f(inp["x_prompt"][b]), "xs": f(inp["x_sample"][s0:s0 + NS, 0]), "cmat": cm,
        "norm_mix_g": f(inp["norm_mix_g"]), "norm_ffn_g": f(inp["norm_ffn_g"]), "w_ada": f(inp["w_ada"]), "b_ada": f(inp["b_ada"]),
        "w_ffn_gate": f(inp["w_ffn_gate"]), "w_ffn_up": f(inp["w_ffn_up"]), "w_ffn_down": f(inp["w_ffn_down"]),
        "w_in_ab": f(inp["w_in_ab"][0]), "w_out_ab": f(inp["w_out_ab"][0]),
        "nsa_q_gain": f(inp["nsa_q_gain"]), "nsa_k_gain": f(inp["nsa_k_gain"][0]),
        "nsa_cmp_w1": f(inp["nsa_cmp_w1"][0]), "nsa_cmp_w2": f(inp["nsa_cmp_w2"][0]), "nsa_cmp_pos": f(inp["nsa_cmp_pos"][0]),
        "rg_conv_w": f(inp["rg_conv_w"][0]), "rg_conv_b": f(inp["rg_conv_b"]), "rg_wa": f(inp["rg_wa"][0]), "rg_ba": f(inp["rg_ba"]),
        "rg_wx": f(inp["rg_wx"][0]), "rg_bx": f(inp["rg_bx"]), "rg_lambda": f(inp["rg_lambda"]),
        "w_in_cd": f(inp["w_in_cd"][0]), "w_out_cd": f(inp["w_out_cd"][0]),
        "gmlp_v_gain": f(inp["gmlp_v_gain"]), "gmlp_ws": f(inp["gmlp_ws"][0]), "gmlp_bs": f(inp["gmlp_bs"][0]),
        "dil_q_gain": f(inp["dil_q_gain"]), "dil_k_gain": f(inp["dil_k_gain"]),
        "cache": f(inp["cache_nsa_kv"][0]).reshape(-1, 512),
        "state_win": f(inp["state_nsa_win"][0, s0:s0 + NS]).reshape(NS, 512, 256),
        "state_h": f(inp["state_rglru_h"][0, s0:s0 + NS]), "state_conv": f(inp["state_rglru_conv"][0, s0:s0 + NS]),
        "state_dil": f(inp["state_dil_kv"][0, s0:s0 + NS]).reshape(NS, 2048, 1024),
        "page_table": np.ascontiguousarray(inp["page_table"][s0:s0 + NS], dtype=np.int32),
    }
    m.update(make_consts(T))
    return m


T_FULL = 8192
NS_CORE = 16
N_CORES = 8


def kernel(**inputs):
    inp = {k: np.asarray(v) for k, v in inputs.items()}
    T, NS = T_FULL, NS_CORE
    prog = ProgL1(T, NS, npool_rows=inp["cache_nsa_kv"].shape[1] * 128, dbg=False)
    nc = prog.build()
    in_maps = [core_inputs(inp, c, T, NS, b=c % 2, s0=NS * c) for c in range(N_CORES)]
    res = run_bass_kernel_spmd(nc, in_maps, core_ids=list(range(N_CORES))).results
    cat = lambda name: np.concatenate([np.asarray(res[c][name]) for c in range(N_CORES)], axis=0)
    two = lambda name: np.stack([np.asarray(res[0][name]), np.asarray(res[1][name])], axis=0)
    f32 = lambda a: np.ascontiguousarray(a, dtype=np.float32)
    out = (
        f32(two("y_p").reshape(2, T, D)),
        f32(cat("y_s").reshape(128, 1, D)),
        f32(two("kv_p").reshape(1, 2, T, 4, 2, 64)),
        f32(cat("kv_s").reshape(1, 128, 1, 4, 2, 64)),
        f32(two("win_p").reshape(1, 2, 512, 2, 2, 64)),
        f32(cat("win_s").reshape(1, 128, 512, 2, 2, 64)),
        f32(two("h_p").reshape(1, 2, 512)),
        f32(cat("h_s").reshape(1, 128, 512)),
        f32(two("conv_p").reshape(1, 2, 3, 512)),
        f32(cat("conv_s").reshape(1, 128, 3, 512)),
        f32(two("dil_p").reshape(1, 2, 2048, 2, 8, 64)),
        f32(cat("dil_s").reshape(1, 128, 2048, 2, 8, 64)),
        f32(cat("gv_s").reshape(1, 128, 1, 512)),
    )
    return out
```

```python
from contextlib import ExitStack

import numpy as np
import concourse.bass as bass
import concourse.mybir as mybir
from concourse.bass_utils import run_bass_kernel_spmd

F32 = mybir.dt.float32
BF16 = mybir.dt.bfloat16
I32 = mybir.dt.int32
AF = mybir.ActivationFunctionType
ALU = mybir.AluOpType
AX = mybir.AxisListType

NEG = -30000.0
D = 1024
HD = 64
FFN = 2816
IN_AB = 2328
IN_CD = 2560
EPS = 1e-6


class Res:
    __slots__ = ("name", "w", "rs", "dsem", "dcnt")

    def __init__(self, name):
        self.name = name
        self.w = None
        self.rs = {}
        self.dsem = None
        self.dcnt = 0


class Sched:
    def __init__(self, nc, stack):
        self.nc = nc
        self.stack = stack
        self.eng = {"pe": nc.tensor, "act": nc.scalar, "dve": nc.vector, "pool": nc.gpsimd, "sp": nc.sync}
        self.sem = {}
        self.cnt = {}
        self.waited = {k: {} for k in self.eng}
        for k in self.eng:
            self.sem[k] = stack.enter_context(nc.semaphore("s_" + k))
            self.cnt[k] = 0
        self.semobj = {k: self.sem[k] for k in self.eng}
        self.nres = 0
        self.all_dma = []
        self.free_sems = []

    def res(self, name=None):
        self.nres += 1
        return Res(name or f"r{self.nres}")

    def _dsem(self, r):
        if r.dsem is None:
            key = f"d{len(self.all_dma)}_{r.name}"
            h = self.stack.enter_context(self.nc.semaphore())
            r.dsem = key
            self.semobj[key] = h
            self.all_dma.append(r)
        return r.dsem

    def _wait(self, e, dep):
        if dep is None:
            return
        key, val = dep
        if key == "pe" and e == "pe":
            return
        if self.waited[e].get(key, 0) >= val:
            return
        self.eng[e].wait_ge(self.semobj[key], val)
        self.waited[e][key] = val

    def _deps(self, e, reads, writes):
        for r in reads:
            self._wait(e, r.w)
        for r in writes:
            self._wait(e, r.w)
            for d in list(r.rs.items()):
                self._wait(e, d)

    def _mark(self, reads, writes, tok):
        for r in reads:
            if r.rs.get(tok[0], 0) < tok[1]:
                r.rs[tok[0]] = tok[1]
        for r in writes:
            r.w = tok
            r.rs = {}

    def op(self, e, fn, reads=(), writes=()):
        self._deps(e, reads, writes)
        ins = fn()
        self.cnt[e] += 1
        ins.then_inc(self.sem[e], 1)
        self._mark(reads, writes, (e, self.cnt[e]))
        return ins

    def dma(self, q, out, in_, reads, writes, own=None, **kw):
        if own is None:
            own = writes[0]
        self._deps(q, reads, writes)
        key = self._dsem(own)
        ins = self.eng[q].dma_start(out=out, in_=in_, **kw)
        own.dcnt += 16
        ins.then_inc(self.semobj[key], 16)
        self._mark(reads, writes, (key, own.dcnt))
        return ins

    def dma_fn(self, q, fn, reads, writes, own=None):
        if own is None:
            own = writes[0]
        self._deps(q, reads, writes)
        key = self._dsem(own)
        ins = fn()
        own.dcnt += 16
        ins.then_inc(self.semobj[key], 16)
        self._mark(reads, writes, (key, own.dcnt))
        return ins

    def barrier(self):
        for e in self.eng:
            for r in self.all_dma:
                self._wait(e, (r.dsem, r.dcnt))
            for k in self.eng:
                if k != e and self.cnt[k] > 0:
                    self._wait(e, (k, self.cnt[k]))

    def finish(self, e="sp"):
        for r in self.all_dma:
            self._wait(e, (r.dsem, r.dcnt))
        for k in self.eng:
            if k != e and self.cnt[k] > 0:
                self._wait(e, (k, self.cnt[k]))


class Buf:
    def __init__(self, t, r):
        self.t = t
        self.r = r

    def __getitem__(self, k):
        return self.t[k]


def make_consts(T):
    p = np.arange(128)
    c = {}
    c["c_ident"] = np.eye(128, dtype=np.float32)
    key, q = p[:, None], p[None, :]
    causal = np.where(key <= q, 0.0, NEG).astype(np.float32)
    anti = np.where(key >= q, 0.0, NEG).astype(np.float32)
    c["c_causal4"] = np.tile(causal, (1, 4))
    c["c_anti4"] = np.tile(anti, (1, 4))
    es = np.zeros((128, 32, 128), np.float32)
    for pi in range(32):
        for kk in range(128):
            es[(np.arange(128) % 64) == 2 * pi + kk // 64, pi, kk] = 1.0
    c["c_esmall"] = es.reshape(128, 32 * 128)
    k4 = np.arange(4)[:, None]
    qq = np.tile(p, 4)[None, :]
    c["c_cmpR"] = np.where(qq < 32 * k4 + 31, NEG, 0.0).astype(np.float32)
    z = np.zeros((4, 252), np.float32)
    for k in range(4):
        z[k, 124 + k] = 1.0
    c["c_cmpZ"] = z
    c["c_cmpmask"] = np.where(p[:, None] >= 32 * np.arange(4)[None, :] + 31, 0.0, NEG).astype(np.float32)
    f = np.zeros((128, 2), np.float32)
    f[:, 0] = np.where(p < 64, 1e4, -1.0)
    f[:, 1] = np.where(p >= 64, 1e4, -1.0)
    c["c_f12"] = f
    dm = np.zeros((128, 17, 128), np.float32)
    for dl in range(17):
        dist = 128 * dl + q - key
        m = ((dist >= 0) & (dist <= 128)).astype(np.float32)
        m += ((dist >= 0) & (dist <= 512) & (dist % 4 == 0))
        m += ((dist >= 0) & (dist <= 2048) & (dist % 16 == 0))
        dm[:, dl, :] = m
    c["c_dilmult"] = dm.reshape(128, 17 * 128)
    e16 = np.zeros((128, 16), np.float32)
    e16[np.arange(16), np.arange(16)] = 1.0
    c["c_eye16"] = e16
    t, s = p[:, None], p[None, :]
    c["c_tril"] = (s <= t).astype(np.float32)
    return c


CONST_SHAPES = lambda T: {k: v.shape for k, v in make_consts(T).items()}


class Prog:
    def __init__(self, T, NS=16, npool_rows=2560 * 128, dbg=False, do_l1=True, do_samples=True):
        self.T, self.NS = T, NS
        self.NT = T // 128
        self.MT = 2
        self.NM = self.NT // self.MT
        self.W = self.MT * 128 + 16
        self.dbg = dbg
        self.do_l1 = do_l1
        self.do_samples = do_samples
        self.npool_rows = npool_rows
        self.nc = bass.Bass("TRN2", target_bir_lowering=False)
        self.st = ExitStack()
        self.S = Sched(self.nc, self.st)
        self.ins = {}
        self.outs = {}
        self.wb = {}

    def din(self, name, shape, dt=F32):
        self.ins[name] = self.nc.dram_tensor(name, list(shape), dt, kind="ExternalInput").ap()
        return self.ins[name]

    def dout(self, name, shape, dt=F32):
        self.outs[name] = self.nc.dram_tensor(name, list(shape), dt, kind="ExternalOutput").ap()
        return self.outs[name]

    def sb(self, name, shape, dt=F32):
        t = self.st.enter_context(self.nc.sbuf_tensor(name, list(shape), dt))
        return Buf(t, self.S.res(name))

    def psum(self, name):
        t = self.st.enter_context(self.nc.psum_tensor(name, [128, 512], F32))
        return Buf(t, self.S.res(name))

    def mm(self, out, lhsT, rhs, start, stop, R, W, skip=False):
        nc = self.nc
        if skip:
            return self.S.op("pe", lambda: nc.tensor.matmul(out, lhsT=lhsT, rhs=rhs, start=start, stop=stop,
                                                            skip_group_check=True), R, W)
        return self.S.op("pe", lambda: nc.tensor.matmul(out, lhsT=lhsT, rhs=rhs, start=start, stop=stop), R, W)

    def tr(self, out, in_, ident, R, W):
        nc = self.nc
        return self.S.op("pe", lambda: nc.tensor.transpose(out, in_, ident), R, W)

    def act(self, out, in_, func, R, W, **kw):
        nc = self.nc
        return self.S.op("act", lambda: nc.scalar.activation(out=out, in_=in_, func=func, **kw), R, W)

    def ts(self, out, in0, s1, s2, op0, op1, R, W, eng="dve"):
        e = self.nc.vector if eng == "dve" else self.nc.gpsimd
        if op1 is None:
            return self.S.op(eng, lambda: e.tensor_scalar(out=out, in0=in0, scalar1=s1, scalar2=None, op0=op0), R, W)
        return self.S.op(eng, lambda: e.tensor_scalar(out=out, in0=in0, scalar1=s1, scalar2=s2, op0=op0, op1=op1), R, W)

    def tt(self, out, in0, in1, op, R, W, eng="dve"):
        e = self.nc.vector if eng == "dve" else self.nc.gpsimd
        return self.S.op(eng, lambda: e.tensor_tensor(out=out, in0=in0, in1=in1, op=op), R, W)

    def stt(self, out, in0, scalar, in1, op0, op1, R, W):
        nc = self.nc
        return self.S.op("dve", lambda: nc.vector.scalar_tensor_tensor(out=out, in0=in0, scalar=scalar, in1=in1,
                                                                       op0=op0, op1=op1), R, W)

    def cp(self, out, in_, R, W, eng="dve"):
        e = self.nc.vector if eng == "dve" else self.nc.gpsimd
        return self.S.op(eng, lambda: e.tensor_copy(out=out, in_=in_), R, W)

    def memset(self, ap, val, W, eng="pool"):
        e = self.nc.vector if eng == "dve" else self.nc.gpsimd
        return self.S.op(eng, lambda: e.memset(ap, val), [], W)

    def dma(self, q, out, in_, R, W, own=None, **kw):
        return self.S.dma(q, out, in_, R, W, own=own, **kw)

    def declare(self):
        T, NS = self.T, self.NS
        di = self.din
        di("xp", [T, D]); di("xs", [NS, D]); di("cmat", [33, D])
        di("norm_mix_g", [2, D]); di("norm_ffn_g", [2, D])
        di("w_ada", [2, D, 6 * D]); di("b_ada", [2, 6 * D])
        di("w_ffn_gate", [2, D, FFN]); di("w_ffn_up", [2, D, FFN]); di("w_ffn_down", [2, FFN, D])
        di("w_in_ab", [D, IN_AB]); di("w_out_ab", [D, D])
        di("nsa_q_gain", [1, 64]); di("nsa_k_gain", [3, 64])
        di("nsa_cmp_w1", [2, 32, 64, 64]); di("nsa_cmp_w2", [2, 64, 64]); di("nsa_cmp_pos", [2, 32, 64])
        di("rg_conv_w", [4, 512]); di("rg_conv_b", [1, 512]); di("rg_wa", [8, 64, 64]); di("rg_ba", [1, 512])
        di("rg_wx", [8, 64, 64]); di("rg_bx", [1, 512]); di("rg_lambda", [1, 512])
        di("w_in_cd", [D, IN_CD]); di("w_out_cd", [D, D])
        di("gmlp_v_gain", [1, 512]); di("gmlp_ws", [8, 128, 128]); di("gmlp_bs", [8, 128])
        di("dil_q_gain", [1, 64]); di("dil_k_gain", [1, 64])
        di("cache", [self.npool_rows, 512]); di("state_win", [NS, 512, 256]); di("state_h", [NS, 512])
        di("state_conv", [NS, 3, 512]); di("state_dil", [NS, 2048, 1024]); di("page_table", [NS, 16], I32)
        for k, shp in CONST_SHAPES(T).items():
            di(k, shp)
        do = self.dout
        do("y_p", [T, D]); do("y_s", [NS, D]); do("kv_p", [T, 512]); do("kv_s", [NS, 512])
        do("win_p", [min(512, T), 256]); do("win_s", [NS, 512, 256]); do("h_p", [512]); do("h_s", [NS, 512])
        do("conv_p", [3, 512]); do("conv_s", [NS, 3, 512]); do("dil_p", [min(2048, T), 1024])
        do("dil_s", [NS, 2048, 1024]); do("gv_s", [NS, 512])
        if self.dbg:
            do("dbg_x1", [T, D]); do("dbg_x1s", [NS, D]); do("dbg_xmid", [T, D]); do("dbg_onsa", [T, 512]); do("dbg_ornn", [T, 512]); do("dbg_kcT", [128, max(T // 32, 8)]); do("dbg_vc", [128, 130]); do("dbg_hid", [64, 16]); do("dbg_w1", [128, 4096]); do("dbg_raw", [128, 512])
        self.x1d = self.nc.dram_tensor("x1_scratch", [T, D], F32).ap()

    def alloc(self):
        T, W = self.T, self.W
        sb = self.sb
        self.wres = self.S.res("dram_in")
        self.ores = self.S.res("dram_out")
        self.pmm = [self.psum(f"pmm{i}") for i in range(2)]
        self.ptr = [self.psum(f"ptr{i}") for i in range(2)]
        self.patt = [self.psum(f"patt{i}") for i in range(2)]
        self.pacc = [self.psum(f"pacc{i}") for i in range(2)]
        self.pmm_i = self.ptr_i = self.patt_i = 0
        self.ident = sb("ident", [128, 128]); self.identb = sb("identb", [128, 128], BF16)
        self.ones = sb("ones", [128, 128])
        self.causal4 = sb("causal4", [128, 512], BF16); self.anti4 = sb("anti4", [128, 512], BF16)
        self.esmall = sb("esmall", [128, 32 * 128], BF16)
        self.cmpR = sb("cmpR", [4, 512], BF16); self.cmpZ = sb("cmpZ", [4, 252], BF16)
        self.cmpmask = sb("cmpmask", [128, 4]); self.f12 = sb("f12", [128, 2]); self.eye16 = sb("eye16", [128, 16])
        self.NSLOT = 3
        self.ring = [sb(f"slab{i}", [128, 8, 512], BF16) for i in range(self.NSLOT)]
        self.ring_i = 0
        self.xb = [sb(f"xb{i}", [128, D]) for i in range(self.MT)] + [sb("xbs", [128, D])]
        self.xn = sb("xn", [128, D])
        self.junk = sb("junk", [128, D], BF16)
        self.stat = sb("stat", [128, 8])
        self.hT = sb("hT", [128, 8, W], BF16)
        self.tmp16 = sb("tmp16", [128, 16])
        self.catT = sb("catT", [128, 8, W], BF16)
        self.hidT = sb("hidT", [128, 22, W], BF16)
        self.ftmp = sb("ftmp", [128, W])
        self.cT = sb("cT", [128, 8, 33], BF16)
        self.modT = sb("modT", [128, 48, 33])
        self.gs = [sb(f"gs{i}", [128, 8, 33]) for i in range(2)]
        self.gtm = sb("gtm", [128, 2, D])
        self.gbc = sb("gbc", [128, 2, D])
        self.badaT = sb("badaT", [128, 2, 48])
        self.gnT = sb("gnT", [128, 2, 2, 8])

    def next_pmm(self):
        b = self.pmm[self.pmm_i % 2]; self.pmm_i += 1; return b

    def next_ptr(self):
        b = self.ptr[self.ptr_i % 2]; self.ptr_i += 1; return b

    def next_patt(self):
        b = self.patt[self.patt_i % 2]; self.patt_i += 1; return b

    def load_slab(self, src, k0, nk, c0, ncols):
        slot = self.ring[self.ring_i % self.NSLOT]
        self.ring_i += 1
        if isinstance(src, tuple):
            ap, rr = self.wb[src[0]]
            if src[1] is not None:
                ap = ap[src[1]]
            q = "sp"
        else:
            ap, rr, q = src, self.wres, "pool"
        self.dma(q, slot.t[:, 0:nk, 0:ncols],
                 ap[k0 * 128:(k0 + nk) * 128, c0:c0 + ncols].rearrange("(k p) n -> p k n", p=128),
                 [rr], [slot.r])
        return slot

    def precast(self, names):
        for nm in names:
            src = self.ins[nm]
            shp = list(src.shape)
            dst = self.nc.dram_tensor("wb_" + nm, shp, BF16).ap()
            rr = self.S.res("wb_" + nm)
            self.wb[nm] = (dst, rr)
            s2 = src if len(shp) == 2 else src.rearrange("l r n -> (l r) n")
            d2 = dst if len(shp) == 2 else dst.rearrange("l r n -> (l r) n")
            rows = s2.shape[0]
            for r0 in range(0, rows, 128):
                r1 = min(rows, r0 + 128)
                self.dma("pool", d2[r0:r1, :], s2[r0:r1, :], [self.wres], [rr])

    def ld(self, buf, src, q="pool", **kw):
        return self.dma(q, buf.t[:] if not isinstance(buf, tuple) else buf[0], src, [self.wres],
                        [buf.r if not isinstance(buf, tuple) else buf[1]], **kw)

    def setup_consts(self):
        I = self.ins
        self.ld(self.ident, I["c_ident"][:, :])
        self.ld(self.identb, I["c_ident"][:, :])
        self.memset(self.ones.t[:], 1.0, [self.ones.r])
        self.ld(self.causal4, I["c_causal4"][:, :]); self.ld(self.anti4, I["c_anti4"][:, :])
        self.ld(self.esmall, I["c_esmall"][:, :])
        self.ld(self.cmpR, I["c_cmpR"][:, :]); self.ld(self.cmpZ, I["c_cmpZ"][:, :])
        self.ld(self.cmpmask, I["c_cmpmask"][:, :]); self.ld(self.f12, I["c_f12"][:, :]); self.ld(self.eye16, I["c_eye16"][:, :])

    def setup_mod_inputs(self):
        I = self.ins
        ctm = self.xn
        self.dma("pool", ctm.t[0:33, :], I["cmat"][:, :], [self.wres], [ctm.r])
        self.act(ctm.t[0:33, :], ctm.t[0:33, :], AF.Silu, [ctm.r], [ctm.r])
        for k in range(8):
            pt = self.next_ptr()
            self.tr(pt.t[:, 0:33], ctm.t[0:33, k * 128:(k + 1) * 128], self.ident.t[0:33, 0:33], [ctm.r, self.ident.r], [pt.r])
            self.cp(self.cT.t[:, k, :], pt.t[:, 0:33], [pt.r], [self.cT.r])
        for l in range(2):
            self.dma("pool", self.badaT.t[:, l, :], I["b_ada"][l].rearrange("(e p) -> p e", p=128), [self.wres], [self.badaT.r],
                     allow_slow_non_contiguous=True)
            self.dma("pool", self.gnT.t[:, 0, l, :], I["norm_mix_g"][l].rearrange("(k p) -> p k", p=128), [self.wres],
                     [self.gnT.r], allow_slow_non_contiguous=True)
            self.dma("pool", self.gnT.t[:, 1, l, :], I["norm_ffn_g"][l].rearrange("(k p) -> p k", p=128), [self.wres],
                     [self.gnT.r], allow_slow_non_contiguous=True)

    def compute_mod(self, l):
        I = self.ins
        for s in range(12):
            slab = self.load_slab(I["w_ada"][l], 0, 8, s * 512, 512)
            for j in range(4):
                e = 4 * s + j
                pm = self.next_pmm()
                for k in range(8):
                    self.mm(pm.t[:, 0:33], slab.t[:, k, j * 128:(j + 1) * 128], self.cT.t[:, k, :], k == 0, k == 7,
                            [slab.r, self.cT.r], [pm.r])
                self.ts(self.modT.t[:, e, :], pm.t[:, 0:33], self.badaT.t[:, l, e:e + 1], None, ALU.add, None,
                        [pm.r, self.badaT.r], [self.modT.r])
        for w, base in ((0, 8), (1, 32)):
            for k in range(8):
                gcol = self.gnT.t[:, w, l, k:k + 1]
                self.ts(self.gs[w].t[:, k, :], self.modT.t[:, base + k, :], gcol, gcol, ALU.mult, ALU.add,
                        [self.modT.r, self.gnT.r], [self.gs[w].r])
        for w, base in ((0, 16), (1, 40)):
            for half in range(2):
                pt = self.next_ptr()
                for j in range(4):
                    k = half * 4 + j
                    self.tr(pt.t[0:33, j * 128:(j + 1) * 128], self.modT.t[:, base + k, :], self.ident.t[:, :],
                            [self.modT.r, self.ident.r], [pt.r])
                self.cp(self.gtm.t[0:33, w, half * 512:(half + 1) * 512], pt.t[0:33, :], [pt.r], [self.gtm.r])
            for half in range(2):
                pm = self.next_pmm()
                self.mm(pm.t[:, :], self.ones.t[32:33, :], self.gtm.t[32:33, w, half * 512:(half + 1) * 512], True, True,
                        [self.ones.r, self.gtm.r], [pm.r])
                self.cp(self.gbc.t[:, w, half * 512:(half + 1) * 512], pm.t[:, :], [pm.r], [self.gbc.r])

    def tiles_of(self, m, with_samples):
        tl = [("p", i, 128, i * 128) for i in range(self.MT)]
        if with_samples:
            tl.append(("s", self.MT, self.NS, self.MT * 128))
        return tl

    def norm_tile(self, tile, w):
        kind, i, np_, c0 = tile
        x = self.xb[i]
        st = self.stat
        shbase = 0 if w == 0 else 24
        self.act(self.junk.t[0:np_, :], x.t[0:np_, :], AF.Square, [x.r], [self.junk.r, st.r], accum_out=st.t[0:np_, 0:1])
        self.ts(st.t[0:np_, 0:1], st.t[0:np_, 0:1], 1.0 / D, EPS, ALU.mult, ALU.add, [st.r], [st.r])
        self.act(st.t[0:np_, 0:1], st.t[0:np_, 0:1], AF.Sqrt, [st.r], [st.r])
        self.S.op("dve", lambda: self.nc.vector.reciprocal(out=st.t[0:np_, 0:1], in_=st.t[0:np_, 0:1]), [st.r], [st.r])
        self.ts(self.xn.t[0:np_, :], x.t[0:np_, :], st.t[0:np_, 0:1], None, ALU.mult, None, [x.r, st.r], [self.xn.r])
        for half in range(2):
            pt = self.next_ptr()
            for j in range(4):
                k = half * 4 + j
                self.tr(pt.t[:, j * 128:j * 128 + np_], self.xn.t[0:np_, k * 128:(k + 1) * 128], self.ident.t[0:np_, 0:np_],
                        [self.xn.r, self.ident.r], [pt.r])
            for j in range(4):
                k = half * 4 + j
                src = pt.t[:, j * 128:j * 128 + np_]
                dst = self.hT.t[:, k, c0:c0 + np_]
                if kind == "p":
                    self.ts(dst, src, self.gs[w].t[:, k, 32:33], self.modT.t[:, shbase + k, 32:33], ALU.mult, ALU.add,
                            [pt.r, self.gs[w].r, self.modT.r], [self.hT.r])
                else:
                    self.tt(self.tmp16.t[:, 0:np_], src, self.gs[w].t[:, k, 0:np_], ALU.mult, [pt.r, self.gs[w].r], [self.tmp16.r])
                    self.tt(dst, self.tmp16.t[:, 0:np_], self.modT.t[:, shbase + k, 0:np_], ALU.add,
                            [self.tmp16.r, self.modT.r], [self.hT.r])

    def proj_tm(self, tiles, src, c0, ncols, zcol, post=None):
        slab = self.load_slab(src, 0, 8, c0, ncols)
        for tile in tiles:
            kind, i, np_, h0 = tile
            pm = self.next_pmm()
            for k in range(8):
                self.mm(pm.t[0:np_, 0:ncols], self.hT.t[:, k, h0:h0 + np_], slab.t[:, k, 0:ncols], k == 0, k == 7,
                        [self.hT.r, slab.r], [pm.r])
            if post is None:
                self.act(self.zb[i].t[0:np_, zcol:zcol + ncols], pm.t[0:np_, 0:ncols], AF.Copy, [pm.r], [self.zb[i].r])
            else:
                post(tile, pm)

    def ffn(self, tiles, l, out_fn):
        I = self.ins
        lo = min(t[3] for t in tiles)
        Wt = max(t[3] + t[2] for t in tiles)
        hc = 0
        for s in range(6):
            ncols = 512 if s < 5 else FFN - 5 * 512
            sg = self.load_slab(("w_ffn_gate", l), 0, 8, s * 512, ncols)
            su = self.load_slab(("w_ffn_up", l), 0, 8, s * 512, ncols)
            for j in range(ncols // 128):
                fb = [self.pmm[0], self.pmm[1], self.ptr[0], self.ptr[1]]
                pg = fb[(2 * hc) % 4]
                for k in range(8):
                    self.mm(pg.t[:, lo:Wt], sg.t[:, k, j * 128:(j + 1) * 128], self.hT.t[:, k, lo:Wt], k == 0, k == 7,
                            [sg.r, self.hT.r], [pg.r])
                pu = fb[(2 * hc + 1) % 4]
                for k in range(8):
                    self.mm(pu.t[:, lo:Wt], su.t[:, k, j * 128:(j + 1) * 128], self.hT.t[:, k, lo:Wt], k == 0, k == 7,
                            [su.r, self.hT.r], [pu.r])
                self.act(self.ftmp.t[:, lo:Wt], pg.t[:, lo:Wt], AF.Silu, [pg.r], [self.ftmp.r])
                self.tt(self.hidT.t[:, hc, lo:Wt], self.ftmp.t[:, lo:Wt], pu.t[:, lo:Wt], ALU.mult, [self.ftmp.r, pu.r], [self.hidT.r])
                hc += 1
        accs = [self.pacc[0], self.pacc[1], self.patt[0]]
        for half in range(2):
            for ks, (k0, nk) in enumerate(((0, 8), (8, 8), (16, 6))):
                sd = self.load_slab(("w_ffn_down", l), k0, nk, half * 512, 512)
                for ti, tile in enumerate(tiles):
                    kind, i, np_, h0 = tile
                    for k in range(nk):
                        self.mm(accs[ti].t[0:np_, :], self.hidT.t[:, k0 + k, h0:h0 + np_], sd.t[:, k, :], (k0 + k) == 0,
                                (k0 + k) == 21, [self.hidT.r, sd.r], [accs[ti].r])
            for ti, tile in enumerate(tiles):
                out_fn(tile, half, accs[ti])

    def resid_add(self, tile, half, ps, w):
        kind, i, np_, h0 = tile
        x = self.xb[i]
        cs = slice(half * 512, (half + 1) * 512)
        g = self.gbc.t[:, w, cs] if kind == "p" else self.gtm.t[0:np_, w, cs]
        gr = self.gbc.r if kind == "p" else self.gtm.r
        self.tt(self.xn.t[0:np_, cs], ps.t[0:np_, :], g[0:np_] if kind == "p" else g, ALU.mult, [ps.r, gr], [self.xn.r])
        self.tt(x.t[0:np_, cs], x.t[0:np_, cs], self.xn.t[0:np_, cs], ALU.add, [x.r, self.xn.r], [x.r])


class ProgL0(Prog):
    def alloc_l0(self):
        T, NT, sb = self.T, self.NT, self.sb
        self.qg = sb("qg", [128, 64]); self.kg = sb("kg", [128, 3, 64])
        self.w1sb = sb("w1sb", [128, 2, 32, 64], BF16)
        self.posT = sb("posT", [128, 2, 32], BF16)
        self.cmpb = sb("cmpb", [128, 2])
        self.w2sb = sb("w2sb", [128, 2, 64], BF16)
        self.wabd = sb("wabd", [128, 4, 128], BF16); self.wxbd = sb("wxbd", [128, 4, 128], BF16)
        self.convw = sb("convw", [128, 4, 4]); self.rgc = sb("rgc", [128, 6, 4])
        self.xrs = sb("xrs", [128, 4, 16])
        self.ggr = sb("ggr", [128, 4, self.W])
        self.sq = sb("sq", [128, 512])
        self.qb = sb("qb", [128, 512], BF16)
        self.qp = [sb(f"qp{g}", [128, 4, 128], BF16) for g in range(2)]
        self.hid = sb("hid", [128, 2, 2, 8], BF16)
        self.kcrow = sb("kcrow", [128, 128]); self.vcrow = sb("vcrow", [128, 2, 65], BF16)
        self.gsig = sb("gsig", [128, 24])
        self.e32 = sb("e32", [128, 256]); self.impacc = sb("impacc", [128, 256])
        self.imp = sb("imp", [128, 128]); self.imp2 = sb("imp2", [128, 128]); self.negsel = sb("negsel", [128, 128])
        self.m8 = sb("m8", [128, 16]); self.den = sb("den", [128, 8])
        self.negT4 = [sb(f"negT4_{g}", [128, 4, 128], BF16) for g in range(2)]
        self.pT = [sb(f"pT{i}", [128, 512], BF16) for i in range(2)]
        self.pT_i = 0
        self.onsa = sb("onsa", [128, 512]); self.otmp = sb("otmp", [128, 256])
        self.zb = [sb(f"zb{i}", [128, 1304]) for i in range(self.MT)] + [sb("zbs", [128, 1304])]
        self.x1res = [self.S.res(f"x1_{t}") for t in range(self.NT)]
        self.rg = [sb(f"rg{i}", [128, 256]) for i in range(7)]
        self.xcb = sb("xcb", [128, 256], BF16)

    def next_pT(self):
        b = self.pT[self.pT_i % 2]; self.pT_i += 1; return b

    def alloc_l0_prompt(self):
        T, NT, sb = self.T, self.NT, self.sb
        self.ksT = sb("ksT", [128, T], BF16)
        self.vs_aug = sb("vs_aug", [128, NT, 2, 65], BF16)
        self.nbt_max = max(1, (T // 32 + 127) // 128)
        self.kcT = sb("kcT", [128, max(T // 32, 8)], BF16)
        self.vc_aug = sb("vc_aug", [128, self.nbt_max, 2, 65], BF16)
        self.kwT = sb("kwT", [128, 6 * 128], BF16)
        self.vw_aug = sb("vw_aug", [128, 6, 2, 65], BF16)
        self.xrbuf = sb("xrbuf", [128, 4, 3 + 256]); self.hstate = sb("hstate", [128, 4])
        self.rawTz = [sb(f"rawTz{g}", [128, 2, 256], BF16) for g in range(2)]
        self.memset(self.xrbuf.t[:], 0.0, [self.xrbuf.r]); self.memset(self.hstate.t[:], 0.0, [self.hstate.r])
        for b_ in (self.vs_aug, self.vc_aug, self.vw_aug):
            self.memset(b_.t[:], 1.0, [b_.r])
        self.memset(self.kwT.t[:], 0.0, [self.kwT.r])
        for g in range(2):
            self.memset(self.rawTz[g].t[:], 0.0, [self.rawTz[g].r])

    def setup_l0(self):
        I, nc = self.ins, self.nc
        slow = dict(allow_slow_non_contiguous=True)
        self.ld(self.qg, I["nsa_q_gain"][0:1, :].broadcast_to([128, 64]))
        self.ts(self.qg.t[:], self.qg.t[:], 0.125, None, ALU.mult, None, [self.qg.r], [self.qg.r])
        for j in range(3):
            self.dma("pool", self.kg.t[:, j, :], I["nsa_k_gain"][j:j + 1, :].broadcast_to([128, 64]), [self.wres], [self.kg.r])
        for c in range(2):
            for hlf in range(2):
                self.dma("pool", self.w1sb.t[hlf * 64:(hlf + 1) * 64, c, :, :], I["nsa_cmp_w1"][c].rearrange("l d e -> d l e"),
                         [self.wres], [self.w1sb.r])
            self.dma("pool", self.w2sb.t[0:64, c, :], I["nsa_cmp_w2"][c], [self.wres], [self.w2sb.r])
        for c in range(2):
            self.dma("pool", self.posT.t[0:64, c, :], I["nsa_cmp_pos"][c].rearrange("l d -> d l"), [self.wres], [self.posT.r], **slow)
        for c in range(2):
            pm = self.next_pmm()
            for l in range(32):
                self.mm(pm.t[0:64, 0:1], self.w1sb.t[0:64, c, l, :], self.posT.t[0:64, c, l:l + 1], l == 0, l == 31,
                        [self.w1sb.r, self.posT.r], [pm.r])
            self.cp(self.cmpb.t[0:64, c:c + 1], pm.t[0:64, 0:1], [pm.r], [self.cmpb.r])
        for wsb, nm in ((self.wabd, "rg_wa"), (self.wxbd, "rg_wx")):
            self.memset(wsb.t[:], 0.0, [wsb.r])
            for j in range(4):
                self.dma("pool", wsb.t[0:64, j, 0:64], I[nm][2 * j], [self.wres], [wsb.r])
                self.dma("pool", wsb.t[64:128, j, 64:128], I[nm][2 * j + 1], [self.wres], [wsb.r])
        for i_ in range(4):
            self.dma("pool", self.convw.t[:, :, i_], I["rg_conv_w"][i_].rearrange("(c p) -> p c", p=128), [self.wres], [self.convw.r], **slow)
        for j, nm in enumerate(("rg_conv_b", "rg_ba", "rg_bx", "rg_lambda")):
            self.dma("pool", self.rgc.t[:, j, :], I[nm].rearrange("o (c p) -> p (o c)", p=128), [self.wres], [self.rgc.r], **slow)
        r = self.rgc
        self.act(r.t[:, 4, :], r.t[:, 3, :], AF.Exp, [r.r], [r.r], scale=-1.0)
        self.act(r.t[:, 4, :], r.t[:, 4, :], AF.Ln, [r.r], [r.r], bias=1.0)
        self.ts(r.t[:, 4, :], r.t[:, 4, :], -8.0, None, ALU.mult, None, [r.r], [r.r])
        self.ts(r.t[:, 5, :], r.t[:, 4, :], 2.0, None, ALU.mult, None, [r.r], [r.r])
        self.memset(self.vcrow.t[:], 1.0, [self.vcrow.r])
        for g in range(2):
            self.memset(self.qp[g].t[:], 0.0, [self.qp[g].r])

    def headnorm(self, src_ap, np_, nh, gain_ap, out_ap, R, W, shape4=None):
        sq, st = self.sq, self.den
        sv = lambda ap: ap.rearrange("p (h d) -> p h d", d=64)
        self.tt(sq.t[0:np_, 0:nh * 64], src_ap, src_ap, ALU.mult, R, [sq.r])
        self.S.op("dve", lambda: self.nc.vector.tensor_reduce(out=st.t[0:np_, 0:nh], in_=sv(sq.t[0:np_, 0:nh * 64]), axis=AX.X,
                                                              op=ALU.add), [sq.r], [st.r])
        self.ts(st.t[0:np_, 0:nh], st.t[0:np_, 0:nh], 1.0 / 64, EPS, ALU.mult, ALU.add, [st.r], [st.r])
        self.act(st.t[0:np_, 0:nh], st.t[0:np_, 0:nh], AF.Sqrt, [st.r], [st.r])
        self.S.op("dve", lambda: self.nc.vector.reciprocal(out=st.t[0:np_, 0:nh], in_=st.t[0:np_, 0:nh]), [st.r], [st.r])
        self.tt(sv(sq.t[0:np_, 0:nh * 64]), sv(src_ap), st.t[0:np_, 0:nh, None].broadcast_to([np_, nh, 64]), ALU.mult,
                R + [st.r], [sq.r])
        if shape4 is None:
            self.tt(out_ap, sv(sq.t[0:np_, 0:nh * 64]), gain_ap[0:np_, None, :].broadcast_to([np_, nh, 64]), ALU.mult,
                    [sq.r] + R, W)
        else:
            a, b = shape4
            in0 = sq.t[0:np_, 0:nh * 64].rearrange("p (a b d) -> p a b d", a=a, b=b, d=64)
            in1 = gain_ap[0:np_, None, None, :].broadcast_to([np_, a, b, 64])
            self.tt(out_ap, in0, in1, ALU.mult, [sq.r] + R, W)

    def l0_post(self, tile, t):
        kind, i, np_, h0 = tile
        z = self.zb[i]
        O = self.outs
        sv = lambda ap: ap.rearrange("p (h d) -> p h d", d=64)
        self.headnorm(z.t[0:np_, 768:896], np_, 2, self.kg.t[:, 1, :], sv(z.t[0:np_, 768:896]), [z.r, self.kg.r], [z.r])
        self.headnorm(z.t[0:np_, 1024:1152], np_, 2, self.kg.t[:, 2, :], sv(z.t[0:np_, 1024:1152]), [z.r, self.kg.r], [z.r])
        if kind == "p":
            self.dma("pool", O["kv_p"][t * 128:(t + 1) * 128, :], z.t[0:128, 512:1024], [z.r], [self.ores], own=z.r)
            if t >= self.NT - min(4, self.NT):
                w0 = (t - (self.NT - min(4, self.NT))) * 128
                self.dma("pool", O["win_p"][w0:w0 + 128, :], z.t[0:128, 1024:1280], [z.r], [self.ores], own=z.r)
            pt = self.next_ptr()
            self.tr(pt.t[:, 0:128], z.t[0:128, 768:896], self.ident.t[:, :], [z.r, self.ident.r], [pt.r])
            self.tr(pt.t[:, 128:256], z.t[0:128, 1024:1152], self.ident.t[:, :], [z.r, self.ident.r], [pt.r])
            self.tr(pt.t[:, 256:384], z.t[0:128, 512:640], self.ident.t[:, :], [z.r, self.ident.r], [pt.r])
            self.tr(pt.t[:, 384:512], z.t[0:128, 640:768], self.ident.t[:, :], [z.r, self.ident.r], [pt.r])
            slot = t % 6
            self.cp(self.ksT.t[:, t * 128:(t + 1) * 128], pt.t[:, 0:128], [pt.r], [self.ksT.r])
            self.cp(self.kwT.t[:, slot * 128:(slot + 1) * 128], pt.t[:, 128:256], [pt.r], [self.kwT.r])
            for g in range(2):
                gs_ = slice(g * 64, (g + 1) * 64)
                self.cp(self.rawTz[g].t[gs_, :, i * 128:(i + 1) * 128], pt.t[gs_, 256:512].rearrange("p (c q) -> p c q", q=128), [pt.r],
                        [self.rawTz[g].r])
            self.cp(self.vs_aug.t[:, t, :, 0:64], sv(z.t[0:128, 896:1024]), [z.r], [self.vs_aug.r], eng="pool")
            self.cp(self.vw_aug.t[:, slot, :, 0:64], sv(z.t[0:128, 1152:1280]), [z.r], [self.vw_aug.r], eng="pool")
        else:
            self.dma("pool", O["kv_s"][:, :], z.t[0:np_, 512:1024], [z.r], [self.ores], own=z.r)

    def compress_macro(self, m):
        nb0 = 8 * m
        for c in range(2):
            pm = self.next_pmm()
            for g in range(2):
                for l in range(32):
                    self.mm(pm.t[0:64, g * 8:(g + 1) * 8], self.w1sb.t[:, c, l, :],
                            self.rawTz[g].t[:, c, l:256:32], l == 0, l == 31, [self.w1sb.r, self.rawTz[g].r], [pm.r])
            if self.dbg and c == 0 and m == self.NM - 1:
                self.cp(self.kcrow.t[0:64, 0:16], pm.t[0:64, 0:16], [pm.r], [self.kcrow.r])
                self.dma("pool", self.outs["dbg_hid"][:, :], self.kcrow.t[0:64, 0:16], [self.kcrow.r], [self.ores], own=self.kcrow.r)
                self.dma("pool", self.outs["dbg_w1"][:, :], self.w1sb.t[:, :, :, :].rearrange("p c l e -> p (c l e)"), [self.w1sb.r], [self.ores], own=self.w1sb.r)
            self.act(self.hid.t[0:64, c, :, :], pm.t[0:64, 0:16].rearrange("p (g n) -> p g n", n=8), AF.Gelu_apprx_tanh,
                     [pm.r, self.cmpb.r], [self.hid.r], bias=self.cmpb.t[0:64, c:c + 1])
            pm2 = self.next_pmm()
            for g in range(2):
                self.mm(pm2.t[0:8, g * 64:(g + 1) * 64], self.hid.t[0:64, c, g, :], self.w2sb.t[0:64, c, :], True, True,
                        [self.hid.r, self.w2sb.r], [pm2.r])
            if c == 0:
                self.cp(self.kcrow.t[0:8, :], pm2.t[0:8, 0:128], [pm2.r], [self.kcrow.r])
                sv = lambda ap: ap.rearrange("p (h d) -> p h d", d=64)
                self.headnorm(self.kcrow.t[0:8, :], 8, 2, self.kg.t[:, 0, :], sv(self.kcrow.t[0:8, :]), [self.kcrow.r, self.kg.r],
                              [self.kcrow.r])
                pt = self.next_ptr()
                self.tr(pt.t[:, 0:8], self.kcrow.t[0:8, :], self.ident.t[0:8, 0:8], [self.kcrow.r, self.ident.r], [pt.r])
                self.cp(self.kcT.t[:, nb0:nb0 + 8], pt.t[:, 0:8], [pt.r], [self.kcT.r])
            else:
                self.cp(self.vcrow.t[0:8, :, 0:64], pm2.t[0:8, 0:128].rearrange("p (g d) -> p g d", d=64), [pm2.r], [self.vcrow.r])
                p0, bt = nb0 % 128, nb0 // 128
                self.dma("pool", self.vc_aug.t[p0:p0 + 8, bt, :, :], self.vcrow.t[0:8, :, :], [self.vcrow.r], [self.vc_aug.r])

    def combine(self, g, br, first):
        o = self.pacc[g].t[:, 0:260].rearrange("p (r e) -> p r e", e=65)
        rd = self.den
        self.ts(rd.t[:, 0:4], o[:, :, 64], 1e-30, None, ALU.max, None, [self.pacc[g].r], [rd.r])
        self.S.op("dve", lambda: self.nc.vector.reciprocal(out=rd.t[:, 0:4], in_=rd.t[:, 0:4]), [rd.r], [rd.r])
        gate = self.gsig.t[:, g * 12:(g + 1) * 12].rearrange("p (r b) -> p r b", b=3)[:, :, br]
        self.tt(rd.t[:, 0:4], rd.t[:, 0:4], gate, ALU.mult, [rd.r, self.gsig.r], [rd.r])
        ov = self.onsa.t[:, g * 256:(g + 1) * 256].rearrange("p (r d) -> p r d", d=64)
        sc = rd.t[:, 0:4, None].broadcast_to([128, 4, 64])
        if first:
            self.tt(ov, o[:, :, 0:64], sc, ALU.mult, [self.pacc[g].r, rd.r], [self.onsa.r])
        else:
            tv = self.otmp.t[:, :].rearrange("p (r d) -> p r d", d=64)
            self.tt(tv, o[:, :, 0:64], sc, ALU.mult, [self.pacc[g].r, rd.r], [self.otmp.r])
            self.tt(ov, ov, tv, ALU.add, [self.onsa.r, self.otmp.r], [self.onsa.r])

    def pipelined(self, items, qk, pv):
        if not items:
            return
        cur = qk(items[0])
        for n, it in enumerate(items):
            nxt = qk(items[n + 1]) if n + 1 < len(items) else None
            pv(it, cur)
            cur = nxt

    def pv4(self, g, pT, nk, v_ap, first, last, vres):
        for r in range(4):
            self.mm(self.pacc[g].t[:, r * 65:(r + 1) * 65], pT.t[0:nk, r * 128:(r + 1) * 128], v_ap, first and r == 0, last,
                    [pT.r, vres], [self.pacc[g].r], skip=True)

    def nsa_prompt_tile(self, t, i):
        nc = self.nc
        nblk, nsb = 4 * t + 4, 2 * t + 2
        topk = nsb > 16
        qf = lambda g: self.qp[g].t[:, :, :].rearrange("p r q -> p (r q)")
        for g in range(2 if topk else 0):
            for r in range(4):
                pm = self.next_pmm()
                self.mm(pm.t[:, 0:nblk], self.qp[g].t[:, r, :], self.kcT.t[:, 0:nblk], True, True,
                        [self.qp[g].r, self.kcT.r], [pm.r])
                self.tt(pm.t[:, nblk - 4:nblk], pm.t[:, nblk - 4:nblk], self.cmpmask.t[:, :], ALU.add, [pm.r, self.cmpmask.r], [pm.r])
                self.act(self.e32.t[:, 0:nblk], pm.t[:, 0:nblk], AF.Exp, [pm.r], [self.e32.r, self.den.r],
                         accum_out=self.den.t[:, 4:5])
                self.ts(self.den.t[:, 4:5], self.den.t[:, 4:5], 1e-30, None, ALU.max, None, [self.den.r], [self.den.r])
                self.S.op("dve", lambda: nc.vector.reciprocal(out=self.den.t[:, 4:5], in_=self.den.t[:, 4:5]), [self.den.r], [self.den.r])
                if r == 0:
                    self.ts(self.impacc.t[:, 0:nblk], self.e32.t[:, 0:nblk], self.den.t[:, 4:5], None, ALU.mult, None,
                            [self.e32.r, self.den.r], [self.impacc.r])
                else:
                    self.stt(self.impacc.t[:, 0:nblk], self.e32.t[:, 0:nblk], self.den.t[:, 4:5], self.impacc.t[:, 0:nblk],
                             ALU.mult, ALU.add, [self.e32.r, self.den.r, self.impacc.r], [self.impacc.r])
            if topk:
                imp = self.imp
                self.S.op("dve", lambda: nc.vector.tensor_reduce(
                    out=imp.t[:, 0:nsb], in_=self.impacc.t[:, 0:nblk].rearrange("p (n two) -> p n two", two=2), axis=AX.X,
                    op=ALU.add), [self.impacc.r], [imp.r])
                self.ts(imp.t[:, 2 * t:2 * t + 1], imp.t[:, 2 * t:2 * t + 1], self.f12.t[:, 0:1], None, ALU.max, None,
                        [imp.r, self.f12.r], [imp.r])
                self.cp(imp.t[:, 2 * t + 1:2 * t + 2], self.f12.t[:, 1:2], [self.f12.r, imp.r], [imp.r])
                self.memset(imp.t[:, 0:1], 1e4, [imp.r], eng="dve")
                self.S.op("dve", lambda: nc.vector.max(out=self.m8.t[:, 0:8], in_=imp.t[:, 0:nsb]), [imp.r], [self.m8.r])
                self.S.op("dve", lambda: nc.vector.match_replace(out=self.imp2.t[:, 0:nsb], in_to_replace=self.m8.t[:, 0:8],
                                                                 in_values=imp.t[:, 0:nsb], imm_value=-2.0),
                          [imp.r, self.m8.r], [self.imp2.r])
                self.S.op("dve", lambda: nc.vector.max(out=self.m8.t[:, 8:16], in_=self.imp2.t[:, 0:nsb]), [self.imp2.r], [self.m8.r])
                self.ts(self.negsel.t[:, 0:nsb], imp.t[:, 0:nsb], self.m8.t[:, 15:16], NEG, ALU.is_lt, ALU.mult,
                        [imp.r, self.m8.r], [self.negsel.r])
                pt = self.next_ptr()
                self.tr(pt.t[0:nsb, 0:128], self.negsel.t[:, 0:nsb], self.ident.t[:, :], [self.negsel.r, self.ident.r], [pt.r])
                self.cp(self.negT4[g].t[0:nsb, :, :], pt.t[0:nsb, None, 0:128].broadcast_to([nsb, 4, 128]), [pt.r], [self.negT4[g].r])
        nbt = (nblk + 127) // 128
        for g in range(2):
            for bt in range(nbt):
                nb_t = min(128, nblk - bt * 128)
                last = bt == nbt - 1
                pa = self.next_patt()
                self.mm(pa.t[0:nb_t, :], self.kcT.t[:, bt * 128:bt * 128 + nb_t], qf(g), True, not last,
                        [self.kcT.r, self.qp[g].r], [pa.r])
                if last:
                    n0 = nb_t - 4
                    self.mm(pa.t[0:nb_t, :], self.cmpZ.t[0:4, 124 - n0:124 - n0 + nb_t], self.cmpR.t[0:4, :], False, True,
                            [self.cmpZ.r, self.cmpR.r], [pa.r])
                pT = self.next_pT()
                self.act(pT.t[0:nb_t, :], pa.t[0:nb_t, :], AF.Exp, [pa.r], [pT.r])
                self.pv4(g, pT, nb_t, self.vc_aug.t[0:nb_t, bt, g, :], bt == 0, last, self.vc_aug.r)
            self.combine(g, 0, True)
        def sel_qk(it):
            g, kt = it
            diag = kt == t
            pa = self.next_patt()
            self.mm(pa.t[:, :], self.ksT.t[:, kt * 128:(kt + 1) * 128], qf(g), True, not (topk or diag),
                    [self.ksT.r, self.qp[g].r], [pa.r])
            if topk:
                base = 64 * ((2 * kt) // 64)
                nrow = min(64, nsb - base)
                pi_ = kt % 32
                self.mm(pa.t[:, :], self.esmall.t[base:base + nrow, pi_ * 128:(pi_ + 1) * 128],
                        self.negT4[g].t[base:base + nrow, :, :].rearrange("p r q -> p (r q)"), False, not diag,
                        [self.esmall.r, self.negT4[g].r], [pa.r])
            if diag:
                self.mm(pa.t[:, :], self.identb.t[:, :], self.causal4.t[:, :], False, True, [self.identb.r, self.causal4.r], [pa.r])
            return pa

        def sel_pv(it, pa):
            g, kt = it
            pT = self.next_pT()
            self.act(pT.t[:, :], pa.t[:, :], AF.Exp, [pa.r], [pT.r])
            self.pv4(g, pT, 128, self.vs_aug.t[:, kt, g, :], kt == 0, kt == t, self.vs_aug.r)
            if kt == t:
                self.combine(g, 1, False)

        self.pipelined([(g, kt) for g in range(2) for kt in range(t + 1)], sel_qk, sel_pv)
        k0 = max(0, t - 4)

        def win_qk(it):
            g, kt = it
            diag, far = kt == t, kt == t - 4
            slot = kt % 6
            pa = self.next_patt()
            self.mm(pa.t[:, :], self.kwT.t[:, slot * 128:(slot + 1) * 128], qf(g), True, not (far or diag),
                    [self.kwT.r, self.qp[g].r], [pa.r])
            if far:
                self.mm(pa.t[:, :], self.identb.t[:, :], self.anti4.t[:, :], False, True, [self.identb.r, self.anti4.r], [pa.r])
            if diag:
                self.mm(pa.t[:, :], self.identb.t[:, :], self.causal4.t[:, :], False, True, [self.identb.r, self.causal4.r], [pa.r])
            return pa

        def win_pv(it, pa):
            g, kt = it
            slot = kt % 6
            pT = self.next_pT()
            self.act(pT.t[:, :], pa.t[:, :], AF.Exp, [pa.r], [pT.r])
            self.pv4(g, pT, 128, self.vw_aug.t[:, slot, g, :], kt == k0, kt == t, self.vw_aug.r)
            if kt == t:
                self.combine(g, 2, False)

        self.pipelined([(g, kt) for g in range(2) for kt in range(k0, t + 1)], win_qk, win_pv)
        if self.dbg:
            self.dma("pool", self.outs["dbg_onsa"][t * 128:(t + 1) * 128, :], self.onsa.t[:, :], [self.onsa.r], [self.ores], own=self.onsa.r)
        pt = self.next_ptr()
        for k in range(4):
            self.tr(pt.t[:, k * 128:(k + 1) * 128], self.onsa.t[:, k * 128:(k + 1) * 128], self.ident.t[:, :],
                    [self.onsa.r, self.ident.r], [pt.r])
        self.cp(self.catT.t[:, 0:4, i * 128:(i + 1) * 128], pt.t[:, :].rearrange("p (k q) -> p k q", q=128), [pt.r], [self.catT.r])

    def rglru_macro(self):
        nc = self.nc
        xc, r_, i_, a_, a2, u_, hs = self.rg
        c = self.rgc
        Wp = 256
        for j in range(4):
            xr = self.xrbuf
            self.ts(xc.t[:, :], xr.t[:, j, 0:Wp], self.convw.t[:, j, 0:1], c.t[:, 0, j:j + 1], ALU.mult, ALU.add,
                    [xr.r, self.convw.r, c.r], [xc.r])
            for k in range(1, 4):
                self.stt(xc.t[:, :], xr.t[:, j, k:k + Wp], self.convw.t[:, j, k:k + 1], xc.t[:, :], ALU.mult, ALU.add,
                         [xr.r, self.convw.r, xc.r], [xc.r])
            self.cp(self.xcb.t[:, :], xc.t[:, :], [xc.r], [self.xcb.r])
            pr = self.next_pmm()
            self.mm(pr.t[:, 0:Wp], self.wabd.t[:, j, :], self.xcb.t[:, :], True, True, [self.wabd.r, self.xcb.r], [pr.r])
            pi = self.next_pmm()
            self.mm(pi.t[:, 0:Wp], self.wxbd.t[:, j, :], self.xcb.t[:, :], True, True, [self.wxbd.r, self.xcb.r], [pi.r])
            self.act(r_.t[:, :], pr.t[:, 0:Wp], AF.Sigmoid, [pr.r, c.r], [r_.r], bias=c.t[:, 1, j:j + 1])
            self.act(i_.t[:, :], pi.t[:, 0:Wp], AF.Sigmoid, [pi.r, c.r], [i_.r], bias=c.t[:, 2, j:j + 1])
            self.act(a_.t[:, :], r_.t[:, :], AF.Exp, [r_.r, c.r], [a_.r], scale=c.t[:, 4, j:j + 1])
            self.act(a2.t[:, :], r_.t[:, :], AF.Exp, [r_.r, c.r], [a2.r], scale=c.t[:, 5, j:j + 1])
            self.ts(a2.t[:, :], a2.t[:, :], -1.0, 1.0, ALU.mult, ALU.add, [a2.r], [a2.r])
            self.act(a2.t[:, :], a2.t[:, :], AF.Sqrt, [a2.r], [a2.r])
            self.tt(u_.t[:, :], a2.t[:, :], i_.t[:, :], ALU.mult, [a2.r, i_.r], [u_.r])
            self.tt(u_.t[:, :], u_.t[:, :], xc.t[:, :], ALU.mult, [u_.r, xc.r], [u_.r])
            self.S.op("dve", lambda: nc.vector.tensor_tensor_scan(out=hs.t[:, :], data0=a_.t[:, :], data1=u_.t[:, :],
                                                                  initial=self.hstate.t[:, j:j + 1], op0=ALU.mult, op1=ALU.add),
                      [a_.r, u_.r, self.hstate.r], [hs.r])
            self.cp(self.hstate.t[:, j:j + 1], hs.t[:, Wp - 1:Wp], [hs.r], [self.hstate.r])
            self.tt(self.catT.t[:, 4 + j, 0:Wp], hs.t[:, :], self.ggr.t[:, j, 0:Wp], ALU.mult, [hs.r, self.ggr.r], [self.catT.r])
            self.cp(self.rg[1].t[:, 0:3], xr.t[:, j, Wp:Wp + 3], [xr.r, r_.r], [r_.r])
            self.cp(xr.t[:, j, 0:3], self.rg[1].t[:, 0:3], [r_.r, xr.r], [xr.r])

    def proj_fm(self, src, c0, lo, hi, post):
        slab = self.load_slab(src, 0, 8, c0, 512)
        for j in range(4):
            pm = self.next_pmm()
            for k in range(8):
                self.mm(pm.t[:, 0:hi - lo], slab.t[:, k, j * 128:(j + 1) * 128], self.hT.t[:, k, lo:hi], k == 0, k == 7,
                        [slab.r, self.hT.r], [pm.r])
            post(j, pm)

    def pass1_macro(self, m):
        I, O = self.ins, self.outs
        ws = False
        tiles = self.tiles_of(m, ws)
        Wt = 256
        for (kind, i, np_, h0) in tiles:
            t = m * self.MT + i
            self.dma("sp", self.xb[i].t[:, :], I["xp"][t * 128:(t + 1) * 128, :], [self.wres], [self.xb[i].r])
        for tile in tiles:
            self.norm_tile(tile, 0)
        self.proj_tm(tiles, ("w_in_ab", None), 0, 512, 0)
        self.proj_tm(tiles, ("w_in_ab", None), 512, 512, 512)
        self.proj_tm(tiles, ("w_in_ab", None), 1024, 280, 1024)

        def post_xr(j, pm):
            self.cp(self.xrbuf.t[:, j, 3:3 + 256], pm.t[:, 0:256], [pm.r], [self.xrbuf.r])
            if ws:
                self.cp(self.xrs.t[:, j, :], pm.t[:, 256:272], [pm.r], [self.xrs.r])

        def post_gr(j, pm):
            self.act(self.ggr.t[:, j, 0:Wt], pm.t[:, 0:Wt], AF.Gelu_apprx_tanh, [pm.r], [self.ggr.r])

        self.proj_fm(("w_in_ab", None), 1304, 0, Wt, post_xr)
        self.proj_fm(("w_in_ab", None), 1816, 0, Wt, post_gr)
        for tile in tiles:
            if tile[0] == "p":
                self.l0_post(tile, m * self.MT + tile[1])
                if tile[1] == self.MT - 1:
                    self.compress_macro(m)
        for tile in tiles:
            if tile[0] == "p":
                t = m * self.MT + tile[1]
                self.rebuild_qT(tile)
                self.nsa_prompt_tile(t, tile[1])
        self.rglru_macro()
        for half in range(2):
            slab = self.load_slab(("w_out_ab", None), 0, 8, half * 512, 512)
            for tile in tiles:
                kind, i, np_, h0 = tile
                pm = self.next_pmm()
                for k in range(8):
                    self.mm(pm.t[0:np_, :], self.catT.t[:, k, h0:h0 + np_], slab.t[:, k, :], k == 0, k == 7, [self.catT.r, slab.r], [pm.r])
                self.resid_add(tile, half, pm, 0)
        if self.dbg:
            for (kind, i, np_, h0) in tiles:
                if kind == "p":
                    t = m * self.MT + i
                    self.dma("pool", O["dbg_xmid"][t * 128:(t + 1) * 128, :], self.xb[i].t[:, :], [self.xb[i].r], [self.ores], own=self.xb[i].r)
        for tile in tiles:
            self.norm_tile(tile, 1)
        self.ffn(tiles, 0, lambda tile, half, ps: self.resid_add(tile, half, ps, 1))
        for (kind, i, np_, h0) in tiles:
            if kind == "p":
                t = m * self.MT + i
                self.dma("pool", self.x1d[t * 128:(t + 1) * 128, :], self.xb[i].t[:, :], [self.xb[i].r], [self.x1res[t]], own=self.xb[i].r)
                if self.dbg:
                    self.dma("pool", O["dbg_x1"][t * 128:(t + 1) * 128, :], self.xb[i].t[:, :], [self.xb[i].r], [self.ores], own=self.xb[i].r)
            elif self.dbg:
                self.dma("pool", O["dbg_x1s"][:, :], self.xb[i].t[0:np_, :], [self.xb[i].r], [self.ores], own=self.xb[i].r)

    def rebuild_qT(self, tile):
        kind, i, np_, h0 = tile
        z = self.zb[i]
        qout = self.qb.t[0:np_, :].rearrange("p (r g d) -> p g r d", r=4, g=2, d=64)
        self.headnorm(z.t[0:np_, 0:512], np_, 8, self.qg.t, qout, [z.r, self.qg.r], [self.qb.r], shape4=(2, 4))
        self.act(self.gsig.t[0:np_, :], z.t[0:np_, 1280:1304], AF.Sigmoid, [z.r], [self.gsig.r])
        pt = self.next_ptr()
        ptb = pt.t[:].bitcast(BF16)
        for r in range(4):
            self.tr(ptb[:, r * 128:r * 128 + np_], self.qb.t[0:np_, r * 128:(r + 1) * 128], self.identb.t[0:np_, 0:np_],
                    [self.qb.r, self.identb.r], [pt.r])
        for g in range(2):
            gs_ = slice(g * 64, (g + 1) * 64)
            self.cp(self.qp[g].t[gs_, :, 0:np_], ptb[gs_, 0:512].rearrange("p (r q) -> p r q", q=128)[:, :, 0:np_], [pt.r], [self.qp[g].r])


    def state_copies(self):
        I, O = self.ins, self.outs
        for s_ in range(self.NS):
            self.dma("act", O["win_s"][s_, 0:511, :], I["state_win"][s_, 1:512, :], [self.wres], [self.ores], own=self.ores)
            self.dma("act", O["dil_s"][s_, 0:2047, :], I["state_dil"][s_, 1:2048, :], [self.wres], [self.ores], own=self.ores)
        self.dma("act", O["conv_s"][:, 0:2, :], I["state_conv"][:, 1:3, :], [self.wres], [self.ores], own=self.ores)

    def alloc_l0_samp(self):
        sb = self.sb
        self.idx = sb("idx", [128, 256], I32); self.ptbc = sb("ptbc", [128, 256], I32); self.iop = sb("iop", [128, 2])
        self.pg = [sb(f"pg{i}", [128, 512]) for i in range(3)]
        self.ksTs = sb("ksTs", [128, 16 * 128], BF16)
        self.vsas = sb("vsas", [128, 16, 2, 65], BF16)
        self.rawTzs = [sb(f"rawTzs{g}", [128, 2, 512], BF16) for g in range(2)]
        self.kcTs = sb("kcTs", [128, 64], BF16); self.vcs = sb("vcs", [128, 2, 65], BF16)
        self.hids = sb("hids", [128, 2, 2, 64], BF16)
        self.kwTs = sb("kwTs", [128, 4 * 128], BF16); self.vwas = sb("vwas", [128, 4, 2, 65], BF16)
        self.wtile = [sb(f"wtile{i}", [128, 256]) for i in range(2)]
        self.selfT = sb("selfT", [128, 2, 16], BF16)
        self.vself = sb("vself", [128, 2, 2, 65], BF16)
        self.negT4s = [sb(f"negT4s{g}", [128, 4], BF16) for g in range(2)]
        self.orow = sb("orow", [128, 3, 512]); self.osamp = sb("osamp", [128, 3, 512])
        self.cstm = sb("cstm", [128, 512])
        self.csT = sb("csT", [128, 3, 4, 16]); self.h0T = sb("h0T", [128, 4, 16]); self.hsT = sb("hsT", [128, 4, 16])

    def l0_samples_pass(self):
        I, O, nc = self.ins, self.outs, self.nc
        NS = self.NS
        tile = ("s", self.MT, NS, self.MT * 128)
        lo, hi = tile[3], tile[3] + NS
        z = self.zb[2]
        sv = lambda ap: ap.rearrange("p (h d) -> p h d", d=64)
        for b_ in (self.vsas, self.vcs, self.vwas, self.vself):
            self.memset(b_.t[:], 1.0, [b_.r])
        for g in range(2):
            self.memset(self.rawTzs[g].t[:], 0.0, [self.rawTzs[g].r])
        self.dma("sp", self.ptbc.t[:, :], I["page_table"].rearrange("s j -> (s j)").rearrange("(o n) -> o n", o=1).broadcast_to([128, 256]),
                 [self.wres], [self.ptbc.r])
        self.S.op("pool", lambda: nc.gpsimd.iota(self.iop.t[:, 0:1], pattern=[[0, 1]], base=0, channel_multiplier=1,
                                                 allow_small_or_imprecise_dtypes=True), [], [self.iop.r])
        self.ts(self.idx.t[:, :], self.ptbc.t[:, :], 128.0, self.iop.t[:, 0:1], ALU.mult, ALU.add, [self.ptbc.r, self.iop.r], [self.idx.r])
        self.dma("sp", self.xb[2].t[0:NS, :], I["xs"][:, :], [self.wres], [self.xb[2].r])
        self.norm_tile(tile, 0)
        self.proj_tm([tile], ("w_in_ab", None), 0, 512, 0)
        self.proj_tm([tile], ("w_in_ab", None), 512, 512, 512)
        self.proj_tm([tile], ("w_in_ab", None), 1024, 280, 1024)
        self.proj_fm(("w_in_ab", None), 1304, lo, hi, lambda j, pm: self.cp(self.xrs.t[:, j, :], pm.t[:, 0:NS], [pm.r], [self.xrs.r]))
        self.proj_fm(("w_in_ab", None), 1816, lo, hi,
                     lambda j, pm: self.act(self.ggr.t[:, j, lo:hi], pm.t[:, 0:NS], AF.Gelu_apprx_tanh, [pm.r], [self.ggr.r]))
        self.l0_post(tile, None)
        self.dma("sp", O["win_s"][:, 511, :], z.t[0:NS, 1024:1280], [z.r], [self.ores], own=z.r)
        self.rebuild_qT(tile)
        pt = self.next_ptr()
        self.tr(pt.t[:, 0:NS], z.t[0:NS, 768:896], self.ident.t[0:NS, 0:NS], [z.r, self.ident.r], [pt.r])
        self.tr(pt.t[:, 16:16 + NS], z.t[0:NS, 1024:1152], self.ident.t[0:NS, 0:NS], [z.r, self.ident.r], [pt.r])
        self.cp(self.selfT.t[:, :, 0:NS], pt.t[:, 0:32].rearrange("p (w s) -> p w s", s=16)[:, :, 0:NS], [pt.r], [self.selfT.r])
        self.cp(self.vself.t[0:NS, 0, :, 0:64], sv(z.t[0:NS, 896:1024]), [z.r], [self.vself.r])
        self.cp(self.vself.t[0:NS, 1, :, 0:64], sv(z.t[0:NS, 1152:1280]), [z.r], [self.vself.r])
        self.rglru_samples(tile)
        for s_ in range(NS):
            self.nsa_sample(s_)
        for br in range(3):
            for g in range(2):
                gate = self.gsig.t[0:NS, g * 12:(g + 1) * 12].rearrange("p (r b) -> p r b", b=3)[:, :, br]
                src = self.osamp.t[0:NS, br, g * 256:(g + 1) * 256].rearrange("p (r d) -> p r d", d=64)
                ov = self.onsa.t[0:NS, g * 256:(g + 1) * 256].rearrange("p (r d) -> p r d", d=64)
                gb = gate[:, :, None].broadcast_to([NS, 4, 64])
                if br == 0:
                    self.tt(ov, src, gb, ALU.mult, [self.osamp.r, self.gsig.r], [self.onsa.r])
                else:
                    tv = self.otmp.t[0:NS, :].rearrange("p (r d) -> p r d", d=64)
                    self.tt(tv, src, gb, ALU.mult, [self.osamp.r, self.gsig.r], [self.otmp.r])
                    self.tt(ov, ov, tv, ALU.add, [self.onsa.r, self.otmp.r], [self.onsa.r])
        pt = self.next_ptr()
        for k in range(4):
            self.tr(pt.t[:, k * 16:k * 16 + NS], self.onsa.t[0:NS, k * 128:(k + 1) * 128], self.ident.t[0:NS, 0:NS],
                    [self.onsa.r, self.ident.r], [pt.r])
        self.cp(self.catT.t[:, 0:4, lo:hi], pt.t[:, 0:64].rearrange("p (k q) -> p k q", q=16)[:, :, 0:NS], [pt.r], [self.catT.r])
        for half in range(2):
            slab = self.load_slab(("w_out_ab", None), 0, 8, half * 512, 512)
            pm = self.next_pmm()
            for k in range(8):
                self.mm(pm.t[0:NS, :], self.catT.t[:, k, lo:hi], slab.t[:, k, :], k == 0, k == 7, [self.catT.r, slab.r], [pm.r])
            self.resid_add(tile, half, pm, 0)
        self.norm_tile(tile, 1)
        self.ffn([tile], 0, lambda tl, half, ps: self.resid_add(tl, half, ps, 1))
        if self.dbg:
            self.dma("sp", O["dbg_x1s"][:, :], self.xb[2].t[0:NS, :], [self.xb[2].r], [self.ores], own=self.xb[2].r)

    def rglru_samples(self, tile):
        I, O, nc = self.ins, self.outs, self.nc
        NS = self.NS
        lo, hi = tile[3], tile[3] + NS
        c = self.rgc
        xc, r_, i_, a_, a2, u_, hs = [b for b in self.rg]
        w = lambda b: b.t[:, 0:NS]
        for i3 in range(3):
            self.dma("sp", self.cstm.t[0:NS, :], I["state_conv"][:, i3, :], [self.wres], [self.cstm.r])
            pt = self.next_ptr()
            for j in range(4):
                self.tr(pt.t[:, j * 16:j * 16 + NS], self.cstm.t[0:NS, j * 128:(j + 1) * 128], self.ident.t[0:NS, 0:NS],
                        [self.cstm.r, self.ident.r], [pt.r])
            self.cp(self.csT.t[:, i3, :, 0:NS], pt.t[:, 0:64].rearrange("p (j q) -> p j q", q=16)[:, :, 0:NS], [pt.r], [self.csT.r])
        self.dma("sp", self.cstm.t[0:NS, :], I["state_h"][:, :], [self.wres], [self.cstm.r])
        pt = self.next_ptr()
        for j in range(4):
            self.tr(pt.t[:, j * 16:j * 16 + NS], self.cstm.t[0:NS, j * 128:(j + 1) * 128], self.ident.t[0:NS, 0:NS],
                    [self.cstm.r, self.ident.r], [pt.r])
        self.cp(self.h0T.t[:, :, 0:NS], pt.t[:, 0:64].rearrange("p (j q) -> p j q", q=16)[:, :, 0:NS], [pt.r], [self.h0T.r])
        for j in range(4):
            self.ts(w(xc), self.csT.t[:, 0, j, 0:NS], self.convw.t[:, j, 0:1], c.t[:, 0, j:j + 1], ALU.mult, ALU.add,
                    [self.csT.r, self.convw.r, c.r], [xc.r])
            for k in (1, 2):
                self.stt(w(xc), self.csT.t[:, k, j, 0:NS], self.convw.t[:, j, k:k + 1], w(xc), ALU.mult, ALU.add,
                         [self.csT.r, self.convw.r, xc.r], [xc.r])
            self.stt(w(xc), self.xrs.t[:, j, 0:NS], self.convw.t[:, j, 3:4], w(xc), ALU.mult, ALU.add,
                     [self.xrs.r, self.convw.r, xc.r], [xc.r])
            self.cp(self.xcb.t[:, 0:NS], w(xc), [xc.r], [self.xcb.r])
            pr = self.next_pmm()
            self.mm(pr.t[:, 0:NS], self.wabd.t[:, j, :], self.xcb.t[:, 0:NS], True, True, [self.wabd.r, self.xcb.r], [pr.r])
            pi = self.next_pmm()
            self.mm(pi.t[:, 0:NS], self.wxbd.t[:, j, :], self.xcb.t[:, 0:NS], True, True, [self.wxbd.r, self.xcb.r], [pi.r])
            self.act(w(r_), pr.t[:, 0:NS], AF.Sigmoid, [pr.r, c.r], [r_.r], bias=c.t[:, 1, j:j + 1])
            self.act(w(i_), pi.t[:, 0:NS], AF.Sigmoid, [pi.r, c.r], [i_.r], bias=c.t[:, 2, j:j + 1])
            self.act(w(a_), w(r_), AF.Exp, [r_.r, c.r], [a_.r], scale=c.t[:, 4, j:j + 1])
            self.act(w(a2), w(r_), AF.Exp, [r_.r, c.r], [a2.r], scale=c.t[:, 5, j:j + 1])
            self.ts(w(a2), w(a2), -1.0, 1.0, ALU.mult, ALU.add, [a2.r], [a2.r])
            self.act(w(a2), w(a2), AF.Sqrt, [a2.r], [a2.r])
            self.tt(w(u_), w(a2), w(i_), ALU.mult, [a2.r, i_.r], [u_.r])
            self.tt(w(u_), w(u_), w(xc), ALU.mult, [u_.r, xc.r], [u_.r])
            self.tt(w(hs), w(a_), self.h0T.t[:, j, 0:NS], ALU.mult, [a_.r, self.h0T.r], [hs.r])
            self.tt(self.hsT.t[:, j, 0:NS], w(hs), w(u_), ALU.add, [hs.r, u_.r], [self.hsT.r])
            self.tt(self.catT.t[:, 4 + j, lo:hi], self.hsT.t[:, j, 0:NS], self.ggr.t[:, j, lo:hi], ALU.mult,
                    [self.hsT.r, self.ggr.r], [self.catT.r])
        for src, dst in ((self.hsT, O["h_s"][:, :]), (self.xrs, O["conv_s"][:, 2, :])):
            pt = self.next_ptr()
            for j in range(4):
                self.tr(pt.t[0:NS, j * 128:(j + 1) * 128], src.t[:, j, 0:NS], self.ident.t[:, :], [src.r, self.ident.r], [pt.r])
            self.cp(self.cstm.t[0:NS, :], pt.t[0:NS, :], [pt.r], [self.cstm.r])
            self.dma("sp", dst, self.cstm.t[0:NS, :], [self.cstm.r], [self.ores], own=self.cstm.r)

    def fin_row(self, g, br):
        o = self.pacc[g].t[0:1, 0:260].rearrange("p (r e) -> p r e", e=65)
        rd = self.den
        self.ts(rd.t[0:1, 0:4], o[:, :, 64], 1e-30, None, ALU.max, None, [self.pacc[g].r], [rd.r])
        self.S.op("dve", lambda: self.nc.vector.reciprocal(out=rd.t[0:1, 0:4], in_=rd.t[0:1, 0:4]), [rd.r], [rd.r])
        ov = self.orow.t[0:1, br, g * 256:(g + 1) * 256].rearrange("p (r d) -> p r d", d=64)
        self.tt(ov, o[:, :, 0:64], rd.t[0:1, 0:4, None].broadcast_to([1, 4, 64]), ALU.mult, [self.pacc[g].r, rd.r], [self.orow.r])

    def pv4s(self, g, pT, nk, v_ap, first, last, vres):
        for r in range(4):
            self.mm(self.pacc[g].t[0:1, r * 65:(r + 1) * 65], pT.t[0:nk, r:r + 1], v_ap, first and r == 0, last,
                    [pT.r, vres], [self.pacc[g].r], skip=True)

    def nsa_sample(self, s_):
        I, nc = self.ins, self.nc
        sv = lambda ap: ap.rearrange("p (h d) -> p h d", d=64)
        qc = lambda g: self.qp[g].t[:, :, s_]
        for pgi in range(16):
            pg = self.pg[pgi % 3]
            col = s_ * 16 + pgi
            self.S.dma_fn("pool", lambda: nc.gpsimd.indirect_dma_start(
                out=pg.t[:, :], out_offset=None, in_=I["cache"][:, :],
                in_offset=bass.IndirectOffsetOnAxis(ap=self.idx.t[:, col:col + 1], axis=0)), [self.wres, self.idx.r], [pg.r])
            pt = self.next_ptr()
            self.tr(pt.t[:, 0:128], pg.t[:, 256:384], self.ident.t[:, :], [pg.r, self.ident.r], [pt.r])
            self.tr(pt.t[:, 128:256], pg.t[:, 0:128], self.ident.t[:, :], [pg.r, self.ident.r], [pt.r])
            self.tr(pt.t[:, 256:384], pg.t[:, 128:256], self.ident.t[:, :], [pg.r, self.ident.r], [pt.r])
            self.cp(self.ksTs.t[:, pgi * 128:(pgi + 1) * 128], pt.t[:, 0:128], [pt.r], [self.ksTs.r])
            q4 = pgi % 4
            for g in range(2):
                gs_ = slice(g * 64, (g + 1) * 64)
                self.cp(self.rawTzs[g].t[gs_, :, q4 * 128:(q4 + 1) * 128], pt.t[gs_, 128:384].rearrange("p (c q) -> p c q", q=128),
                        [pt.r], [self.rawTzs[g].r])
            self.act(self.vsas.t[:, pgi, :, 0:64], sv(pg.t[:, 384:512]), AF.Copy, [pg.r], [self.vsas.r])
            if q4 == 3:
                grp = pgi // 4
                for c in range(2):
                    pm = self.next_pmm()
                    for g in range(2):
                        for l in range(32):
                            self.mm(pm.t[0:64, g * 16:(g + 1) * 16], self.w1sb.t[:, c, l, :], self.rawTzs[g].t[:, c, l:512:32],
                                    l == 0, l == 31, [self.w1sb.r, self.rawTzs[g].r], [pm.r])
                    self.act(self.hids.t[0:64, c, :, grp * 16:(grp + 1) * 16], pm.t[0:64, 0:32].rearrange("p (g n) -> p g n", n=16),
                             AF.Gelu_apprx_tanh, [pm.r, self.cmpb.r], [self.hids.r], bias=self.cmpb.t[0:64, c:c + 1])
        for c in range(2):
            pm2 = self.next_pmm()
            for g in range(2):
                self.mm(pm2.t[0:64, g * 64:(g + 1) * 64], self.hids.t[0:64, c, g, :], self.w2sb.t[0:64, c, :], True, True,
                        [self.hids.r, self.w2sb.r], [pm2.r])
            if c == 0:
                self.cp(self.kcrow.t[0:64, :], pm2.t[0:64, 0:128], [pm2.r], [self.kcrow.r])
                self.headnorm(self.kcrow.t[0:64, :], 64, 2, self.kg.t[:, 0, :], sv(self.kcrow.t[0:64, :]), [self.kcrow.r, self.kg.r],
                              [self.kcrow.r])
                pt = self.next_ptr()
                self.tr(pt.t[:, 0:64], self.kcrow.t[0:64, :], self.ident.t[0:64, 0:64], [self.kcrow.r, self.ident.r], [pt.r])
                self.cp(self.kcTs.t[:, :], pt.t[:, 0:64], [pt.r], [self.kcTs.r])
            else:
                self.cp(self.vcs.t[0:64, :, 0:64], sv(pm2.t[0:64, 0:128]), [pm2.r], [self.vcs.r])
        for g in range(2):
            pm = self.next_pmm()
            for r in range(4):
                self.mm(pm.t[0:1, r * 64:(r + 1) * 64], self.qp[g].t[:, r, s_:s_ + 1], self.kcTs.t[:, :], True, True,
                        [self.qp[g].r, self.kcTs.r], [pm.r])
            for r in range(4):
                self.act(self.e32.t[0:1, r * 64:(r + 1) * 64], pm.t[0:1, r * 64:(r + 1) * 64], AF.Exp, [pm.r], [self.e32.r, self.den.r],
                         accum_out=self.den.t[0:1, 4 + r:5 + r])
            self.ts(self.den.t[0:1, 4:8], self.den.t[0:1, 4:8], 1e-30, None, ALU.max, None, [self.den.r], [self.den.r])
            self.S.op("dve", lambda: nc.vector.reciprocal(out=self.den.t[0:1, 4:8], in_=self.den.t[0:1, 4:8]), [self.den.r], [self.den.r])
            for r in range(4):
                if r == 0:
                    self.ts(self.impacc.t[0:1, 0:64], self.e32.t[0:1, 0:64], self.den.t[0:1, 4:5], None, ALU.mult, None,
                            [self.e32.r, self.den.r], [self.impacc.r])
                else:
                    self.stt(self.impacc.t[0:1, 0:64], self.e32.t[0:1, r * 64:(r + 1) * 64], self.den.t[0:1, 4 + r:5 + r],
                             self.impacc.t[0:1, 0:64], ALU.mult, ALU.add, [self.e32.r, self.den.r, self.impacc.r], [self.impacc.r])
            imp = self.imp
            self.S.op("dve", lambda: nc.vector.tensor_reduce(out=imp.t[0:1, 0:32],
                                                             in_=self.impacc.t[0:1, 0:64].rearrange("p (n two) -> p n two", two=2),
                                                             axis=AX.X, op=ALU.add), [self.impacc.r], [imp.r])
            self.memset(imp.t[0:1, 32:33], 1e4, [imp.r], eng="dve")
            self.memset(imp.t[0:1, 0:1], 1e4, [imp.r], eng="dve")
            self.S.op("dve", lambda: nc.vector.max(out=self.m8.t[0:1, 0:8], in_=imp.t[0:1, 0:33]), [imp.r], [self.m8.r])
            self.S.op("dve", lambda: nc.vector.match_replace(out=self.imp2.t[0:1, 0:33], in_to_replace=self.m8.t[0:1, 0:8],
                                                             in_values=imp.t[0:1, 0:33], imm_value=-2.0), [imp.r, self.m8.r], [self.imp2.r])
            self.S.op("dve", lambda: nc.vector.max(out=self.m8.t[0:1, 8:16], in_=self.imp2.t[0:1, 0:33]), [self.imp2.r], [self.m8.r])
            self.ts(self.negsel.t[0:1, 0:33], imp.t[0:1, 0:33], self.m8.t[0:1, 15:16], NEG, ALU.is_lt, ALU.mult,
                    [imp.r, self.m8.r], [self.negsel.r])
            pt = self.next_ptr()
            self.tr(pt.t[0:33, 0:1], self.negsel.t[0:1, 0:33], self.ident.t[0:1, 0:1], [self.negsel.r, self.ident.r], [pt.r])
            self.cp(self.negT4s[g].t[0:33, :], pt.t[0:33, 0:1].broadcast_to([33, 4]), [pt.r], [self.negT4s[g].r])
        for g in range(2):
            pa = self.next_patt()
            self.mm(pa.t[0:64, 0:4], self.kcTs.t[:, :], qc(g), True, True, [self.kcTs.r, self.qp[g].r], [pa.r])
            pT = self.next_pT()
            self.act(pT.t[0:64, 0:4], pa.t[0:64, 0:4], AF.Exp, [pa.r], [pT.r])
            self.pv4s(g, pT, 64, self.vcs.t[0:64, g, :], True, True, self.vcs.r)
            self.fin_row(g, 0)
        for g in range(2):
            for kt in range(16):
                pa = self.next_patt()
                self.mm(pa.t[:, 0:4], self.ksTs.t[:, kt * 128:(kt + 1) * 128], qc(g), True, False, [self.ksTs.r, self.qp[g].r], [pa.r])
                self.mm(pa.t[:, 0:4], self.esmall.t[0:33, kt * 128:(kt + 1) * 128], self.negT4s[g].t[0:33, :], False, True,
                        [self.esmall.r, self.negT4s[g].r], [pa.r])
                pT = self.next_pT()
                self.act(pT.t[:, 0:4], pa.t[:, 0:4], AF.Exp, [pa.r], [pT.r])
                self.pv4s(g, pT, 128, self.vsas.t[:, kt, g, :], kt == 0, False, self.vsas.r)
            pa = self.next_patt()
            self.mm(pa.t[0:16, 0:4], self.selfT.t[:, 0, :], qc(g), True, True, [self.selfT.r, self.qp[g].r], [pa.r])
            pT = self.next_pT()
            self.act(pT.t[0:16, 0:4], pa.t[0:16, 0:4], AF.Exp, [pa.r], [pT.r])
            self.ts(pT.t[0:16, 0:4], pT.t[0:16, 0:4], self.eye16.t[0:16, s_:s_ + 1], None, ALU.mult, None, [pT.r, self.eye16.r], [pT.r])
            self.pv4s(g, pT, 16, self.vself.t[0:16, 0, g, :], False, True, self.vself.r)
            self.fin_row(g, 1)
        for wt in range(4):
            w = self.wtile[wt % 2]
            self.dma("sp", w.t[:, :], I["state_win"][s_, wt * 128:(wt + 1) * 128, :], [self.wres], [w.r])
            pt = self.next_ptr()
            self.tr(pt.t[:, 0:128], w.t[:, 0:128], self.ident.t[:, :], [w.r, self.ident.r], [pt.r])
            self.cp(self.kwTs.t[:, wt * 128:(wt + 1) * 128], pt.t[:, 0:128], [pt.r], [self.kwTs.r])
            self.act(self.vwas.t[:, wt, :, 0:64], sv(w.t[:, 128:256]), AF.Copy, [w.r], [self.vwas.r])
        for g in range(2):
            for wt in range(4):
                pa = self.next_patt()
                self.mm(pa.t[:, 0:4], self.kwTs.t[:, wt * 128:(wt + 1) * 128], qc(g), True, True, [self.kwTs.r, self.qp[g].r], [pa.r])
                pT = self.next_pT()
                self.act(pT.t[:, 0:4], pa.t[:, 0:4], AF.Exp, [pa.r], [pT.r])
                self.pv4s(g, pT, 128, self.vwas.t[:, wt, g, :], wt == 0, False, self.vwas.r)
            pa = self.next_patt()
            self.mm(pa.t[0:16, 0:4], self.selfT.t[:, 1, :], qc(g), True, True, [self.selfT.r, self.qp[g].r], [pa.r])
            pT = self.next_pT()
            self.act(pT.t[0:16, 0:4], pa.t[0:16, 0:4], AF.Exp, [pa.r], [pT.r])
            self.ts(pT.t[0:16, 0:4], pT.t[0:16, 0:4], self.eye16.t[0:16, s_:s_ + 1], None, ALU.mult, None, [pT.r, self.eye16.r], [pT.r])
            self.pv4s(g, pT, 16, self.vself.t[0:16, 1, g, :], False, True, self.vself.r)
            self.fin_row(g, 2)
        self.dma("sp", self.osamp.t[s_:s_ + 1, :, :], self.orow.t[0:1, :, :], [self.orow.r], [self.osamp.r])

    def finish_l0_outputs(self):
        O = self.outs
        if self.dbg:
            self.dma("pool", O["dbg_kcT"][:, :], self.kcT.t[:, :], [self.kcT.r], [self.ores], own=self.kcT.r)
            self.dma("pool", O["dbg_vc"][:, :], self.vc_aug.t[:, 0, :, :].rearrange("p g e -> p (g e)"), [self.vc_aug.r], [self.ores], own=self.vc_aug.r)
        self.dma("pool", O["h_p"].rearrange("(c p) -> p c", p=128), self.hstate.t[:, :], [self.hstate.r], [self.ores],
                 own=self.hstate.r, allow_slow_non_contiguous=True)
        for i_ in range(3):
            self.dma("pool", O["conv_p"][i_].rearrange("(c p) -> p c", p=128), self.xrbuf.t[:, :, i_], [self.xrbuf.r], [self.ores],
                     own=self.xrbuf.r, allow_slow_non_contiguous=True)


class ProgL1(ProgL0):
    def alloc_l1(self):
        sb = self.sb
        self.NR = min(self.NT, 18)
        self.kT2 = sb("kT2", [128, 4, self.NR * 128], BF16)
        self.v2 = sb("v2", [128, self.NR, 8, 65], BF16)
        self.wsT = sb("wsT", [128, 8, 128], BF16); self.bsT = sb("bsT", [128, 8])
        self.vgain = sb("vgain", [128, 512]); self.dqg = sb("dqg", [128, 64]); self.dkg = sb("dkg", [128, 64])
        self.dilm = sb("dilm", [128, 17 * 128], BF16)
        self.tril = sb("tril", [128, 128])
        self.sq = sb("sq1", [128, 512]); self.den = sb("den1", [128, 8])
        self.qb = sb("qb1", [128, 512], BF16); self.qp2 = [sb(f"qp2_{g}", [128, 4, 128], BF16) for g in range(2)]
        self.vnb = sb("vnb", [128, 512], BF16)
        self.oc = sb("oc", [128, 512]); self.od = sb("od", [128, 512])
        self.pT = [sb(f"pT1_{i}", [128, 512], BF16) for i in range(2)]
        self.zb = [sb(f"zc{i}", [128, 2560]) for i in range(self.MT)] + [sb("zcs", [128, 2560])]
        if self.do_samples:
            self.wsb = sb("wsb", [128, 8]); self.bsb = sb("bsb", [128, 8])
            self.dtile = [sb(f"dtile{i}", [128, 1024]) for i in range(2)]
            self.kTs = sb("kTs", [128, 4, 128], BF16); self.vas = sb("vas", [128, 8, 65], BF16)
            self.kselfT = sb("kselfT", [128, 4, 16], BF16); self.vself2 = sb("vself2", [128, 8, 65], BF16)
            self.odrow = sb("odrow", [128, 512]); self.ods = sb("ods", [128, 512])

    def setup_l1(self):
        I = self.ins
        slow = dict(allow_slow_non_contiguous=True)
        self.ld(self.vgain, I["gmlp_v_gain"][0:1, :].broadcast_to([128, 512]))
        self.ld(self.dqg, I["dil_q_gain"][0:1, :].broadcast_to([128, 64]))
        self.ts(self.dqg.t[:], self.dqg.t[:], 0.125, None, ALU.mult, None, [self.dqg.r], [self.dqg.r])
        self.ld(self.dkg, I["dil_k_gain"][0:1, :].broadcast_to([128, 64]))
        self.ld(self.dilm, I["c_dilmult"][:, :])
        self.ld(self.tril, I["c_tril"][:, :])
        self.dma("pool", self.bsT.t[:, :], I["gmlp_bs"].rearrange("g t -> t g"), [self.wres], [self.bsT.r], **slow)
        for g in range(8):
            w = self.oc
            self.dma("pool", w.t[:, 0:128], I["gmlp_ws"][g], [self.wres], [w.r])
            self.tt(w.t[:, 0:128], w.t[:, 0:128], self.tril.t[:, :], ALU.mult, [w.r, self.tril.r], [w.r])
            pt = self.next_ptr()
            self.tr(pt.t[:, 0:128], w.t[:, 0:128], self.ident.t[:, :], [w.r, self.ident.r], [pt.r])
            self.cp(self.wsT.t[:, g, :], pt.t[:, 0:128], [pt.r], [self.wsT.r])
        self.memset(self.v2.t[:], 1.0, [self.v2.r])
        if self.do_samples:
            self.dma("pool", self.wsb.t[:, :], I["gmlp_ws"][:, 0:1, 0:1].rearrange("g a b -> (a b) g").broadcast_to([128, 8]),
                     [self.wres], [self.wsb.r], **slow)
            self.dma("pool", self.bsb.t[:, :], I["gmlp_bs"][:, 0:1].rearrange("g a -> a g").broadcast_to([128, 8]),
                     [self.wres], [self.bsb.r], **slow)
            self.memset(self.vas.t[:], 1.0, [self.vas.r]); self.memset(self.vself2.t[:], 1.0, [self.vself2.r])
        self.memset(self.kT2.t[:], 0.0, [self.kT2.r])
        for g in range(2):
            self.memset(self.qp2[g].t[:], 0.0, [self.qp2[g].r])

    def l1_post(self, tile, t):
        kind, i, np_, h0 = tile
        z = self.zb[i]
        O = self.outs
        st = self.den
        sv = lambda ap: ap.rearrange("p (h d) -> p h d", d=64)
        self.act(z.t[0:np_, 0:1024], z.t[0:np_, 0:1024], AF.Gelu_apprx_tanh, [z.r], [z.r])
        self.act(self.junk.t[0:np_, 0:512], z.t[0:np_, 512:1024], AF.Square, [z.r], [self.junk.r, st.r], accum_out=st.t[0:np_, 0:1])
        self.ts(st.t[0:np_, 0:1], st.t[0:np_, 0:1], 1.0 / 512, EPS, ALU.mult, ALU.add, [st.r], [st.r])
        self.act(st.t[0:np_, 0:1], st.t[0:np_, 0:1], AF.Sqrt, [st.r], [st.r])
        self.S.op("dve", lambda: self.nc.vector.reciprocal(out=st.t[0:np_, 0:1], in_=st.t[0:np_, 0:1]), [st.r], [st.r])
        self.stt(z.t[0:np_, 512:1024], z.t[0:np_, 512:1024], st.t[0:np_, 0:1], self.vgain.t[0:np_, :], ALU.mult, ALU.mult,
                 [z.r, st.r, self.vgain.r], [z.r])
        self.headnorm(z.t[0:np_, 1024:1536], np_, 8, self.dqg.t, sv(self.qb.t[0:np_, :]), [z.r, self.dqg.r], [self.qb.r])
        self.headnorm(z.t[0:np_, 1536:2048], np_, 8, self.dkg.t, sv(z.t[0:np_, 1536:2048]), [z.r, self.dkg.r], [z.r])
        pt = self.next_ptr()
        ptb = pt.t[:].bitcast(BF16)
        for j in range(4):
            self.tr(ptb[:, j * 128:j * 128 + np_], self.qb.t[0:np_, j * 128:(j + 1) * 128], self.identb.t[0:np_, 0:np_],
                    [self.qb.r, self.identb.r], [pt.r])
        for g in range(2):
            gs_ = slice(g * 64, (g + 1) * 64)
            self.cp(self.qp2[g].t[gs_, :, 0:np_], ptb[gs_, 0:512].rearrange("p (r q) -> p r q", q=128)[:, :, 0:np_], [pt.r], [self.qp2[g].r])
        if kind == "p":
            self.cp(self.vnb.t[:, :], z.t[0:128, 512:1024], [z.r], [self.vnb.r])
            base = self.T - min(2048, self.T)
            if t * 128 >= base:
                self.dma("pool", O["dil_p"][t * 128 - base:(t + 1) * 128 - base, :], z.t[0:128, 1536:2560], [z.r], [self.ores], own=z.r)
            slot = t % self.NR
            pt = self.next_ptr()
            for j in range(4):
                self.tr(pt.t[:, j * 128:(j + 1) * 128], z.t[0:128, 1536 + j * 128:1536 + (j + 1) * 128], self.ident.t[:, :],
                        [z.r, self.ident.r], [pt.r])
            self.cp(self.kT2.t[:, :, slot * 128:(slot + 1) * 128], pt.t[:, :].rearrange("p (j q) -> p j q", q=128), [pt.r], [self.kT2.r])
            self.cp(self.v2.t[:, slot, :, 0:64], sv(z.t[0:128, 2048:2560]), [z.r], [self.v2.r], eng="pool")
        else:
            self.dma("pool", O["gv_s"][:, :], z.t[0:np_, 512:1024], [z.r], [self.ores], own=z.r)

    def gmlp_prompt_tile(self, tile):
        kind, i, np_, h0 = tile
        z = self.zb[i]
        pm = self.next_pmm()
        for g in range(8):
            self.mm(pm.t[:, g * 64:(g + 1) * 64], self.wsT.t[:, g, :], self.vnb.t[:, g * 64:(g + 1) * 64], True, True,
                    [self.wsT.r, self.vnb.r], [pm.r])
        for g in range(8):
            cs = slice(g * 64, (g + 1) * 64)
            self.stt(self.oc.t[:, cs], pm.t[:, cs], self.bsT.t[:, g:g + 1], z.t[0:128, cs], ALU.add, ALU.mult,
                     [pm.r, self.bsT.r, z.r], [self.oc.r])
        pt = self.next_ptr()
        for k in range(4):
            self.tr(pt.t[:, k * 128:(k + 1) * 128], self.oc.t[:, k * 128:(k + 1) * 128], self.ident.t[:, :], [self.oc.r, self.ident.r], [pt.r])
        self.cp(self.catT.t[:, 0:4, h0:h0 + 128], pt.t[:, :].rearrange("p (k q) -> p k q", q=128), [pt.r], [self.catT.r])

    def dil_prompt_tile(self, tile, t):
        kind, i, np_, h0 = tile
        nc = self.nc
        dls = list(range(min(16, t), -1, -1))

        def d_qk(it):
            hg, idx, dl = it
            kt = t - dl
            slot = kt % self.NR
            pa = self.next_patt()
            for jj in range(4):
                h = 4 * hg + jj
                par, j = h % 2, h // 2
                self.mm(pa.t[:, jj * 128:(jj + 1) * 128], self.kT2.t[:, j, slot * 128:(slot + 1) * 128],
                        self.qp2[par].t[:, j, :], True, True, [self.kT2.r, self.qp2[par].r], [pa.r])
            return pa

        def d_pv(it, pa):
            hg, idx, dl = it
            kt = t - dl
            slot = kt % self.NR
            pT = self.next_pT()
            self.act(pT.t[:, :], pa.t[:, :], AF.Exp, [pa.r], [pT.r])
            pv = pT.t[:, :].rearrange("p (r q) -> p r q", q=128)
            self.tt(pv, pv, self.dilm.t[:, None, dl * 128:(dl + 1) * 128].broadcast_to([128, 4, 128]), ALU.mult,
                    [pT.r, self.dilm.r], [pT.r])
            for jj in range(4):
                h = 4 * hg + jj
                self.mm(self.pacc[hg].t[:, jj * 65:(jj + 1) * 65], pT.t[:, jj * 128:(jj + 1) * 128], self.v2.t[:, slot, h, :],
                        idx == 0 and jj == 0, idx == len(dls) - 1, [pT.r, self.v2.r], [self.pacc[hg].r], skip=True)
            if idx == len(dls) - 1:
                o = self.pacc[hg].t[:, 0:260].rearrange("p (r e) -> p r e", e=65)
                rd = self.den
                self.ts(rd.t[:, 0:4], o[:, :, 64], 1e-30, None, ALU.max, None, [self.pacc[hg].r], [rd.r])
                self.S.op("dve", lambda: nc.vector.reciprocal(out=rd.t[:, 0:4], in_=rd.t[:, 0:4]), [rd.r], [rd.r])
                ov = self.od.t[:, hg * 256:(hg + 1) * 256].rearrange("p (r d) -> p r d", d=64)
                self.tt(ov, o[:, :, 0:64], rd.t[:, 0:4, None].broadcast_to([128, 4, 64]), ALU.mult, [self.pacc[hg].r, rd.r], [self.od.r])

        self.pipelined([(hg, idx, dl) for hg in range(2) for idx, dl in enumerate(dls)], d_qk, d_pv)
        pt = self.next_ptr()
        for k in range(4):
            self.tr(pt.t[:, k * 128:(k + 1) * 128], self.od.t[:, k * 128:(k + 1) * 128], self.ident.t[:, :], [self.od.r, self.ident.r], [pt.r])
        self.cp(self.catT.t[:, 4:8, h0:h0 + 128], pt.t[:, :].rearrange("p (k q) -> p k q", q=128), [pt.r], [self.catT.r])

    def l1_samples(self, tile):
        I, O, nc = self.ins, self.outs, self.nc
        kind, i, NS, lo = tile
        hi = lo + NS
        z = self.zb[i]
        sv = lambda ap: ap.rearrange("p (h d) -> p h d", d=64)
        for g in range(8):
            cs = slice(g * 64, (g + 1) * 64)
            self.ts(self.oc.t[0:NS, cs], z.t[0:NS, 512 + g * 64:512 + (g + 1) * 64], self.wsb.t[0:NS, g:g + 1], self.bsb.t[0:NS, g:g + 1],
                    ALU.mult, ALU.add, [z.r, self.wsb.r, self.bsb.r], [self.oc.r])
            self.tt(self.oc.t[0:NS, cs], self.oc.t[0:NS, cs], z.t[0:NS, cs], ALU.mult, [self.oc.r, z.r], [self.oc.r])
        pt = self.next_ptr()
        for k in range(4):
            self.tr(pt.t[:, k * 16:k * 16 + NS], self.oc.t[0:NS, k * 128:(k + 1) * 128], self.ident.t[0:NS, 0:NS], [self.oc.r, self.ident.r], [pt.r])
        self.cp(self.catT.t[:, 0:4, lo:hi], pt.t[:, 0:64].rearrange("p (k q) -> p k q", q=16)[:, :, 0:NS], [pt.r], [self.catT.r])
        self.dma("sp", O["dil_s"][:, 2047, :], z.t[0:NS, 1536:2560], [z.r], [self.ores], own=z.r)
        pt = self.next_ptr()
        for j in range(4):
            self.tr(pt.t[:, j * 16:j * 16 + NS], z.t[0:NS, 1536 + j * 128:1536 + (j + 1) * 128], self.ident.t[0:NS, 0:NS], [z.r, self.ident.r], [pt.r])
        self.cp(self.kselfT.t[:, :, 0:NS], pt.t[:, 0:64].rearrange("p (j q) -> p j q", q=16)[:, :, 0:NS], [pt.r], [self.kselfT.r])
        self.cp(self.vself2.t[0:NS, :, 0:64], sv(z.t[0:NS, 2048:2560]), [z.r], [self.vself2.r])
        for s_ in range(NS):
            for pi_, (d, start) in enumerate(((1, 1920), (4, 1536), (16, 0))):
                dt_ = self.dtile[pi_ % 2]
                self.dma("sp", dt_.t[:, :], I["state_dil"][s_, start:2048:d, :], [self.wres], [dt_.r])
                pt = self.next_ptr()
                for j in range(4):
                    self.tr(pt.t[:, j * 128:(j + 1) * 128], dt_.t[:, j * 128:(j + 1) * 128], self.ident.t[:, :], [dt_.r, self.ident.r], [pt.r])
                self.cp(self.kTs.t[:, :, :], pt.t[:, :].rearrange("p (j q) -> p j q", q=128), [pt.r], [self.kTs.r])
                self.act(self.vas.t[:, :, 0:64], sv(dt_.t[:, 512:1024]), AF.Copy, [dt_.r], [self.vas.r])
                pa = self.next_patt()
                for h in range(8):
                    par, j = h % 2, h // 2
                    self.mm(pa.t[:, h:h + 1], self.kTs.t[:, j, :], self.qp2[par].t[:, j, s_:s_ + 1], True, True,
                            [self.kTs.r, self.qp2[par].r], [pa.r])
                pT = self.next_pT()
                self.act(pT.t[:, 0:8], pa.t[:, 0:8], AF.Exp, [pa.r], [pT.r])
                for h in range(8):
                    self.mm(self.pacc[h // 4].t[0:1, (h % 4) * 65:(h % 4 + 1) * 65], pT.t[:, h:h + 1], self.vas.t[:, h, :],
                            pi_ == 0 and h % 4 == 0, False, [pT.r, self.vas.r], [self.pacc[h // 4].r], skip=True)
            pa = self.next_patt()
            for h in range(8):
                par, j = h % 2, h // 2
                self.mm(pa.t[0:16, h:h + 1], self.kselfT.t[:, j, :], self.qp2[par].t[:, j, s_:s_ + 1], True, True,
                        [self.kselfT.r, self.qp2[par].r], [pa.r])
            pT = self.next_pT()
            self.act(pT.t[0:16, 0:8], pa.t[0:16, 0:8], AF.Exp, [pa.r], [pT.r])
            self.ts(pT.t[0:16, 0:8], pT.t[0:16, 0:8], self.eye16.t[0:16, s_:s_ + 1], None, ALU.mult, None, [pT.r, self.eye16.r], [pT.r])
            self.ts(pT.t[0:16, 0:8], pT.t[0:16, 0:8], 3.0, None, ALU.mult, None, [pT.r], [pT.r])
            for h in range(8):
                self.mm(self.pacc[h // 4].t[0:1, (h % 4) * 65:(h % 4 + 1) * 65], pT.t[0:16, h:h + 1], self.vself2.t[0:16, h, :],
                        False, True, [pT.r, self.vself2.r], [self.pacc[h // 4].r], skip=True)
            for hg in range(2):
                o = self.pacc[hg].t[0:1, 0:260].rearrange("p (r e) -> p r e", e=65)
                rd = self.den
                self.ts(rd.t[0:1, 0:4], o[:, :, 64], 1e-30, None, ALU.max, None, [self.pacc[hg].r], [rd.r])
                self.S.op("dve", lambda: nc.vector.reciprocal(out=rd.t[0:1, 0:4], in_=rd.t[0:1, 0:4]), [rd.r], [rd.r])
                ov = self.odrow.t[0:1, hg * 256:(hg + 1) * 256].rearrange("p (r d) -> p r d", d=64)
                self.tt(ov, o[:, :, 0:64], rd.t[0:1, 0:4, None].broadcast_to([1, 4, 64]), ALU.mult, [self.pacc[hg].r, rd.r], [self.odrow.r])
            self.dma("sp", self.ods.t[s_:s_ + 1, :], self.odrow.t[0:1, :], [self.odrow.r], [self.ods.r])
        pt = self.next_ptr()
        for k in range(4):
            self.tr(pt.t[:, k * 16:k * 16 + NS], self.ods.t[0:NS, k * 128:(k + 1) * 128], self.ident.t[0:NS, 0:NS], [self.ods.r, self.ident.r], [pt.r])
        self.cp(self.catT.t[:, 4:8, lo:hi], pt.t[:, 0:64].rearrange("p (k q) -> p k q", q=16)[:, :, 0:NS], [pt.r], [self.catT.r])

    def pass2_macro(self, m):
        I, O = self.ins, self.outs
        ws = self.do_samples and m == 0
        tiles = self.tiles_of(m, ws)
        for (kind, i, np_, h0) in tiles:
            if kind == "p":
                t = m * self.MT + i
                self.dma("sp", self.xb[i].t[:, :], self.x1d[t * 128:(t + 1) * 128, :], [self.x1res[t]], [self.xb[i].r])
        for tile in tiles:
            self.norm_tile(tile, 0)
        for s in range(5):
            self.proj_tm(tiles, ("w_in_cd", None), s * 512, 512, s * 512)
        for tile in tiles:
            if tile[0] == "p":
                t = m * self.MT + tile[1]
                self.l1_post(tile, t)
                self.gmlp_prompt_tile(tile)
                self.dil_prompt_tile(tile, t)
            else:
                self.l1_post(tile, None)
                self.l1_samples(tile)
        for half in range(2):
            slab = self.load_slab(("w_out_cd", None), 0, 8, half * 512, 512)
            for tile in tiles:
                kind, i, np_, h0 = tile
                pm = self.next_pmm()
                for k in range(8):
                    self.mm(pm.t[0:np_, :], self.catT.t[:, k, h0:h0 + np_], slab.t[:, k, :], k == 0, k == 7, [self.catT.r, slab.r], [pm.r])
                self.resid_add(tile, half, pm, 0)
        for tile in tiles:
            self.norm_tile(tile, 1)
        self.ffn(tiles, 1, lambda tile, half, ps: self.resid_add(tile, half, ps, 1))
        for (kind, i, np_, h0) in tiles:
            if kind == "p":
                t = m * self.MT + i
                self.dma("pool", O["y_p"][t * 128:(t + 1) * 128, :], self.xb[i].t[:, :], [self.xb[i].r], [self.ores], own=self.xb[i].r)
            else:
                self.dma("pool", O["y_s"][:, :], self.xb[i].t[0:np_, :], [self.xb[i].r], [self.ores], own=self.xb[i].r)

    def build(self):
        self.declare()
        self.alloc()
        self.setup_consts()
        self.setup_mod_inputs()
        common = self.st
        with ExitStack() as st1:
            self.st = st1
            self.alloc_l0()
            self.precast(["w_in_ab", "w_out_ab", "w_ffn_gate", "w_ffn_up", "w_ffn_down"])
            self.compute_mod(0)
            if self.do_l1:
                self.precast(["w_in_cd", "w_out_cd"])
            self.setup_l0()
            if self.do_samples:
                self.state_copies()
                with ExitStack() as sts:
                    self.st = sts
                    self.alloc_l0_samp()
                    self.l0_samples_pass()
                    self.S.barrier()
                self.st = st1
            with ExitStack() as stp:
                self.st = stp
                self.alloc_l0_prompt()
                for m in range(self.NM):
                    self.pass1_macro(m)
                self.finish_l0_outputs()
                self.S.barrier()
            self.st = st1
        if self.do_l1:
            with ExitStack() as st2:
                self.st = st2
                self.alloc_l1()
                self.compute_mod(1)
                self.setup_l1()
                for m in range(self.NM):
                    self.pass2_macro(m)
                self.S.barrier()
        self.st = common
        self.S.finish("sp")
        self.st.close()
        return self.nc


def core_inputs(inp, c, T, NS, b, s0):
    f = lambda a: np.ascontiguousarray(a, dtype=np.float32)
    cm = np.zeros((33, D), np.float32)
    cm[0:NS] = inp["c_sample"][s0:s0 + NS]
    cm[32] = inp["c_prompt"][b]
    m = {
        "xp": f(inp["x_prompt"][b]), "xs": f(inp["x_sample"][s0:s0 + NS, 0]), "cmat": cm,
        "norm_mix_g": f(inp["norm_mix_g"]), "norm_ffn_g": f(inp["norm_ffn_g"]), "w_ada": f(inp["w_ada"]), "b_ada": f(inp["b_ada"]),
        "w_ffn_gate": f(inp["w_ffn_gate"]), "w_ffn_up": f(inp["w_ffn_up"]), "w_ffn_down": f(inp["w_ffn_down"]),
        "w_in_ab": f(inp["w_in_ab"][0]), "w_out_ab": f(inp["w_out_ab"][0]),
        "nsa_q_gain": f(inp["nsa_q_gain"]), "nsa_k_gain": f(inp["nsa_k_gain"][0]),
        "nsa_cmp_w1": f(inp["nsa_cmp_w1"][0]), "nsa_cmp_w2": f(inp["nsa_cmp_w2"][0]), "nsa_cmp_pos": f(inp["nsa_cmp_pos"][0]),
        "rg_conv_w": f(inp["rg_conv_w"][0]), "rg_conv_b": f(inp["rg_conv_b"]), "rg_wa": f(inp["rg_wa"][0]), "rg_ba": f(inp["rg_ba"]),
        "rg_wx": f(inp["rg_wx"][0]), "rg_bx": f(inp["rg_bx"]), "rg_lambda": f(inp["rg_lambda"]),
        "w_in_cd": f(inp["w_in_cd"][0]), "w_out_cd": f(inp["w_out_cd"][0]),
        "gmlp_v_gain": f(inp["gmlp_v_gain"]), "gmlp_ws": f(inp["gmlp_ws"][0]), "gmlp_bs": f(inp["gmlp_bs"][0]),
        "dil_q_gain": f(inp["dil_q_gain"]), "dil_k_gain": f(inp["dil_k_gain"]),
        "cache": f(inp["cache_nsa_kv"][0]).reshape(-1, 512),
        "state_win": f(inp["state_nsa_win"][0, s0:s0 + NS]).reshape(NS, 512, 256),
        "state_h": f(inp["state_rglru_h"][0, s0:s0 + NS]), "state_conv": f(inp["state_rglru_conv"][0, s0:s0 + NS]),
        "state_dil": f(inp["state_dil_kv"][0, s0:s0 + NS]).reshape(NS, 2048, 1024),
        "page_table": np.ascontiguousarray(inp["page_table"][s0:s0 + NS], dtype=np.int32),
    }
    m.update(make_consts(T))
    return m


T_FULL = 8192
NS_CORE = 16
N_CORES = 8


def kernel(**inputs):
    inp = {k: np.asarray(v) for k, v in inputs.items()}
    T, NS = T_FULL, NS_CORE
    prog = ProgL1(T, NS, npool_rows=inp["cache_nsa_kv"].shape[1] * 128, dbg=False)
    nc = prog.build()
    in_maps = [core_inputs(inp, c, T, NS, b=c % 2, s0=NS * c) for c in range(N_CORES)]
    res = run_bass_kernel_spmd(nc, in_maps, core_ids=list(range(N_CORES))).results
    cat = lambda name: np.concatenate([np.asarray(res[c][name]) for c in range(N_CORES)], axis=0)
    two = lambda name: np.stack([np.asarray(res[0][name]), np.asarray(res[1][name])], axis=0)
    f32 = lambda a: np.ascontiguousarray(a, dtype=np.float32)
    out = (
        f32(two("y_p").reshape(2, T, D)),
        f32(cat("y_s").reshape(128, 1, D)),
        f32(two("kv_p").reshape(1, 2, T, 4, 2, 64)),
        f32(cat("kv_s").reshape(1, 128, 1, 4, 2, 64)),
        f32(two("win_p").reshape(1, 2, 512, 2, 2, 64)),
        f32(cat("win_s").reshape(1, 128, 512, 2, 2, 64)),
        f32(two("h_p").reshape(1, 2, 512)),
        f32(cat("h_s").reshape(1, 128, 512)),
        f32(two("conv_p").reshape(1, 2, 3, 512)),
        f32(cat("conv_s").reshape(1, 128, 3, 512)),
        f32(two("dil_p").reshape(1, 2, 2048, 2, 8, 64)),
        f32(cat("dil_s").reshape(1, 128, 2048, 2, 8, 64)),
        f32(cat("gv_s").reshape(1, 128, 1, 512)),
    )
    return out
```

```python
from contextlib import ExitStack

import numpy as np
import concourse.bass as bass
import concourse.mybir as mybir
from concourse.bass_utils import run_bass_kernel_spmd

F32 = mybir.dt.float32
BF16 = mybir.dt.bfloat16
I32 = mybir.dt.int32
AF = mybir.ActivationFunctionType
ALU = mybir.AluOpType
AX = mybir.AxisListType

NEG = -30000.0
D = 1024
HD = 64
FFN = 2816
IN_AB = 2328
IN_CD = 2560
EPS = 1e-6


class Res:
    __slots__ = ("name", "w", "rs", "dsem", "dcnt")

    def __init__(self, name):
        self.name = name
        self.w = None
        self.rs = {}
        self.dsem = None
        self.dcnt = 0


class Sched:
    def __init__(self, nc, stack):
        self.nc = nc
        self.stack = stack
        self.eng = {"pe": nc.tensor, "act": nc.scalar, "dve": nc.vector, "pool": nc.gpsimd, "sp": nc.sync}
        self.sem = {}
        self.cnt = {}
        self.waited = {k: {} for k in self.eng}
        for k in self.eng:
            self.sem[k] = stack.enter_context(nc.semaphore("s_" + k))
            self.cnt[k] = 0
        self.semobj = {k: self.sem[k] for k in self.eng}
        self.nres = 0
        self.all_dma = []
        self.free_sems = []

    def res(self, name=None):
        self.nres += 1
        return Res(name or f"r{self.nres}")

    def _dsem(self, r):
        if r.dsem is None:
            key = f"d{len(self.all_dma)}_{r.name}"
            h = self.stack.enter_context(self.nc.semaphore())
            r.dsem = key
            self.semobj[key] = h
            self.all_dma.append(r)
        return r.dsem

    def _wait(self, e, dep):
        if dep is None:
            return
        key, val = dep
        if key == "pe" and e == "pe":
            return
        if self.waited[e].get(key, 0) >= val:
            return
        self.eng[e].wait_ge(self.semobj[key], val)
        self.waited[e][key] = val

    def _deps(self, e, reads, writes):
        for r in reads:
            self._wait(e, r.w)
        for r in writes:
            self._wait(e, r.w)
            for d in list(r.rs.items()):
                self._wait(e, d)

    def _mark(self, reads, writes, tok):
        for r in reads:
            if r.rs.get(tok[0], 0) < tok[1]:
                r.rs[tok[0]] = tok[1]
        for r in writes:
            r.w = tok
            r.rs = {}

    def op(self, e, fn, reads=(), writes=()):
        self._deps(e, reads, writes)
        ins = fn()
        self.cnt[e] += 1
        ins.then_inc(self.sem[e], 1)
        self._mark(reads, writes, (e, self.cnt[e]))
        return ins

    def dma(self, q, out, in_, reads, writes, own=None, **kw):
        if own is None:
            own = writes[0]
        self._deps(q, reads, writes)
        key = self._dsem(own)
        ins = self.eng[q].dma_start(out=out, in_=in_, **kw)
        own.dcnt += 16
        ins.then_inc(self.semobj[key], 16)
        self._mark(reads, writes, (key, own.dcnt))
        return ins

    def dma_fn(self, q, fn, reads, writes, own=None):
        if own is None:
            own = writes[0]
        self._deps(q, reads, writes)
        key = self._dsem(own)
        ins = fn()
        own.dcnt += 16
        ins.then_inc(self.semobj[key], 16)
        self._mark(reads, writes, (key, own.dcnt))
        return ins

    def barrier(self):
        for e in self.eng:
            for r in self.all_dma:
                self._wait(e, (r.dsem, r.dcnt))
            for k in self.eng:
                if k != e and self.cnt[k] > 0:
                    self._wait(e, (k, self.cnt[k]))

    def finish(self, e="sp"):
        for r in self.all_dma:
            self._wait(e, (r.dsem, r.dcnt))
        for k in self.eng:
            if k != e and self.cnt[k] > 0:
                self._wait(e, (k, self.cnt[k]))


class Buf:
    def __init__(self, t, r):
        self.t = t
        self.r = r

    def __getitem__(self, k):
        return self.t[k]


def make_consts(T):
    p = np.arange(128)
    c = {}
    c["c_ident"] = np.eye(128, dtype=np.float32)
    key, q = p[:, None], p[None, :]
    causal = np.where(key <= q, 0.0, NEG).astype(np.float32)
    anti = np.where(key >= q, 0.0, NEG).astype(np.float32)
    c["c_causal4"] = np.tile(causal, (1, 4))
    c["c_anti4"] = np.tile(anti, (1, 4))
    es = np.zeros((128, 32, 128), np.float32)
    for pi in range(32):
        for kk in range(128):
            es[(np.arange(128) % 64) == 2 * pi + kk // 64, pi, kk] = 1.0
    c["c_esmall"] = es.reshape(128, 32 * 128)
    k4 = np.arange(4)[:, None]
    qq = np.tile(p, 4)[None, :]
    c["c_cmpR"] = np.where(qq < 32 * k4 + 31, NEG, 0.0).astype(np.float32)
    z = np.zeros((4, 252), np.float32)
    for k in range(4):
        z[k, 124 + k] = 1.0
    c["c_cmpZ"] = z
    c["c_cmpmask"] = np.where(p[:, None] >= 32 * np.arange(4)[None, :] + 31, 0.0, NEG).astype(np.float32)
    f = np.zeros((128, 2), np.float32)
    f[:, 0] = np.where(p < 64, 1e4, -1.0)
    f[:, 1] = np.where(p >= 64, 1e4, -1.0)
    c["c_f12"] = f
    dm = np.zeros((128, 17, 128), np.float32)
    for dl in range(17):
        dist = 128 * dl + q - key
        m = ((dist >= 0) & (dist <= 128)).astype(np.float32)
        m += ((dist >= 0) & (dist <= 512) & (dist % 4 == 0))
        m += ((dist >= 0) & (dist <= 2048) & (dist % 16 == 0))
        dm[:, dl, :] = m
    c["c_dilmult"] = dm.reshape(128, 17 * 128)
    e16 = np.zeros((128, 16), np.float32)
    e16[np.arange(16), np.arange(16)] = 1.0
    c["c_eye16"] = e16
    t, s = p[:, None], p[None, :]
    c["c_tril"] = (s <= t).astype(np.float32)
    return c


CONST_SHAPES = lambda T: {k: v.shape for k, v in make_consts(T).items()}


class Prog:
    def __init__(self, T, NS=16, npool_rows=2560 * 128, dbg=False, do_l1=True, do_samples=True):
        self.T, self.NS = T, NS
        self.NT = T // 128
        self.MT = 2
        self.NM = self.NT // self.MT
        self.W = self.MT * 128 + 16
        self.dbg = dbg
        self.do_l1 = do_l1
        self.do_samples = do_samples
        self.npool_rows = npool_rows
        self.nc = bass.Bass("TRN2", target_bir_lowering=False)
        self.st = ExitStack()
        self.S = Sched(self.nc, self.st)
        self.ins = {}
        self.outs = {}
        self.wb = {}
        self.side_jobs = []

    def din(self, name, shape, dt=F32):
        self.ins[name] = self.nc.dram_tensor(name, list(shape), dt, kind="ExternalInput").ap()
        return self.ins[name]

    def dout(self, name, shape, dt=F32):
        self.outs[name] = self.nc.dram_tensor(name, list(shape), dt, kind="ExternalOutput").ap()
        return self.outs[name]

    def sb(self, name, shape, dt=F32):
        t = self.st.enter_context(self.nc.sbuf_tensor(name, list(shape), dt))
        return Buf(t, self.S.res(name))

    def psum(self, name):
        t = self.st.enter_context(self.nc.psum_tensor(name, [128, 512], F32))
        return Buf(t, self.S.res(name))

    def mm(self, out, lhsT, rhs, start, stop, R, W, skip=False):
        nc = self.nc
        if skip:
            return self.S.op("pe", lambda: nc.tensor.matmul(out, lhsT=lhsT, rhs=rhs, start=start, stop=stop,
                                                            skip_group_check=True), R, W)
        return self.S.op("pe", lambda: nc.tensor.matmul(out, lhsT=lhsT, rhs=rhs, start=start, stop=stop), R, W)

    def tr(self, out, in_, ident, R, W):
        nc = self.nc
        return self.S.op("pe", lambda: nc.tensor.transpose(out, in_, ident), R, W)

    def act(self, out, in_, func, R, W, **kw):
        nc = self.nc
        return self.S.op("act", lambda: nc.scalar.activation(out=out, in_=in_, func=func, **kw), R, W)

    def ts(self, out, in0, s1, s2, op0, op1, R, W, eng="dve"):
        e = self.nc.vector if eng == "dve" else self.nc.gpsimd
        if op1 is None:
            return self.S.op(eng, lambda: e.tensor_scalar(out=out, in0=in0, scalar1=s1, scalar2=None, op0=op0), R, W)
        return self.S.op(eng, lambda: e.tensor_scalar(out=out, in0=in0, scalar1=s1, scalar2=s2, op0=op0, op1=op1), R, W)

    def tt(self, out, in0, in1, op, R, W, eng="dve"):
        e = self.nc.vector if eng == "dve" else self.nc.gpsimd
        return self.S.op(eng, lambda: e.tensor_tensor(out=out, in0=in0, in1=in1, op=op), R, W)

    def stt(self, out, in0, scalar, in1, op0, op1, R, W):
        nc = self.nc
        return self.S.op("dve", lambda: nc.vector.scalar_tensor_tensor(out=out, in0=in0, scalar=scalar, in1=in1,
                                                                       op0=op0, op1=op1), R, W)

    def cp(self, out, in_, R, W, eng="dve"):
        e = self.nc.vector if eng == "dve" else self.nc.gpsimd
        return self.S.op(eng, lambda: e.tensor_copy(out=out, in_=in_), R, W)

    def memset(self, ap, val, W, eng="pool"):
        e = self.nc.vector if eng == "dve" else self.nc.gpsimd
        return self.S.op(eng, lambda: e.memset(ap, val), [], W)

    def dma(self, q, out, in_, R, W, own=None, **kw):
        return self.S.dma(q, out, in_, R, W, own=own, **kw)

    def declare(self):
        T, NS = self.T, self.NS
        di = self.din
        di("xp", [T, D]); di("xs", [NS, D]); di("cmat", [33, D])
        di("norm_mix_g", [2, D]); di("norm_ffn_g", [2, D])
        di("w_ada", [2, D, 6 * D]); di("b_ada", [2, 6 * D])
        di("w_ffn_gate", [2, D, FFN]); di("w_ffn_up", [2, D, FFN]); di("w_ffn_down", [2, FFN, D])
        di("w_in_ab", [D, IN_AB]); di("w_out_ab", [D, D])
        di("nsa_q_gain", [1, 64]); di("nsa_k_gain", [3, 64])
        di("nsa_cmp_w1", [2, 32, 64, 64]); di("nsa_cmp_w2", [2, 64, 64]); di("nsa_cmp_pos", [2, 32, 64])
        di("rg_conv_w", [4, 512]); di("rg_conv_b", [1, 512]); di("rg_wa", [8, 64, 64]); di("rg_ba", [1, 512])
        di("rg_wx", [8, 64, 64]); di("rg_bx", [1, 512]); di("rg_lambda", [1, 512])
        di("w_in_cd", [D, IN_CD]); di("w_out_cd", [D, D])
        di("gmlp_v_gain", [1, 512]); di("gmlp_ws", [8, 128, 128]); di("gmlp_bs", [8, 128])
        di("dil_q_gain", [1, 64]); di("dil_k_gain", [1, 64])
        di("cache", [self.npool_rows, 512]); di("state_win", [NS, 512, 256]); di("state_h", [NS, 512])
        di("state_conv", [NS, 3, 512]); di("state_dil", [NS, 2048, 1024]); di("page_table", [NS, 16], I32)
        for k, shp in CONST_SHAPES(T).items():
            di(k, shp)
        do = self.dout
        do("y_p", [T, D]); do("y_s", [NS, D]); do("kv_p", [T, 512]); do("kv_s", [NS, 512])
        do("win_p", [min(512, T), 256]); do("win_s", [NS, 512, 256]); do("h_p", [512]); do("h_s", [NS, 512])
        do("conv_p", [3, 512]); do("conv_s", [NS, 3, 512]); do("dil_p", [min(2048, T), 1024])
        do("dil_s", [NS, 2048, 1024]); do("gv_s", [NS, 512])
        if self.dbg:
            do("dbg_x1", [T, D]); do("dbg_x1s", [NS, D]); do("dbg_xmid", [T, D]); do("dbg_onsa", [T, 512]); do("dbg_ornn", [T, 512]); do("dbg_kcT", [128, max(T // 32, 8)]); do("dbg_vc", [128, 130]); do("dbg_hid", [64, 16]); do("dbg_w1", [128, 4096]); do("dbg_raw", [128, 512])
        self.x1d = self.nc.dram_tensor("x1_scratch", [T, D], F32).ap()

    def alloc(self):
        T, W = self.T, self.W
        sb = self.sb
        self.wres = self.S.res("dram_in")
        self.ores = self.S.res("dram_out")
        self.pmm = [self.psum(f"pmm{i}") for i in range(2)]
        self.ptr = [self.psum(f"ptr{i}") for i in range(2)]
        self.patt = [self.psum(f"patt{i}") for i in range(2)]
        self.pacc = [self.psum(f"pacc{i}") for i in range(2)]
        self.pmm_i = self.ptr_i = self.patt_i = 0
        self.ident = sb("ident", [128, 128]); self.identb = sb("identb", [128, 128], BF16)
        self.ones = sb("ones", [128, 128])
        self.causal4 = sb("causal4", [128, 512], BF16); self.anti4 = sb("anti4", [128, 512], BF16)
        self.esmall = sb("esmall", [128, 32 * 128], BF16)
        self.cmpR = sb("cmpR", [4, 512], BF16); self.cmpZ = sb("cmpZ", [4, 252], BF16)
        self.cmpmask = sb("cmpmask", [128, 4]); self.f12 = sb("f12", [128, 2]); self.eye16 = sb("eye16", [128, 16])
        self.NSLOT = 3
        self.ring = [sb(f"slab{i}", [128, 8, 512], BF16) for i in range(self.NSLOT)]
        self.ring_i = 0
        self.xb = [sb(f"xb{i}", [128, D]) for i in range(self.MT)] + [sb("xbs", [128, D])]
        self.xn = sb("xn", [128, D])
        self.junk = sb("junk", [128, D], BF16)
        self.stat = sb("stat", [128, 8])
        self.hT = sb("hT", [128, 8, W], BF16)
        self.tmp16 = sb("tmp16", [128, 16])
        self.catT = sb("catT", [128, 8, W], BF16)
        self.catT2r = self.S.res("catT_hi")
        self.hidT = sb("hidT", [128, 22, W], BF16)
        self.ftmp = sb("ftmp", [128, W])
        self.cT = sb("cT", [128, 8, 33], BF16)
        self.modT = sb("modT", [128, 48, 33])
        self.gs = [sb(f"gs{i}", [128, 8, 33]) for i in range(2)]
        self.gtm = sb("gtm", [128, 2, D])
        self.gbc = sb("gbc", [128, 2, D])
        self.badaT = sb("badaT", [128, 2, 48])
        self.gnT = sb("gnT", [128, 2, 2, 8])

    def next_pmm(self):
        b = self.pmm[self.pmm_i % 2]; self.pmm_i += 1; return b

    def next_ptr(self):
        b = self.ptr[self.ptr_i % 2]; self.ptr_i += 1; return b

    def next_patt(self):
        b = self.patt[self.patt_i % 2]; self.patt_i += 1; return b

    def load_slab(self, src, k0, nk, c0, ncols):
        slot = self.ring[self.ring_i % self.NSLOT]
        self.ring_i += 1
        if isinstance(src, tuple):
            ap, rr = self.wb[src[0]]
            if src[1] is not None:
                ap = ap[src[1]]
            q = "sp"
        else:
            ap, rr, q = src, self.wres, "pool"
        self.dma(q, slot.t[:, 0:nk, 0:ncols],
                 ap[k0 * 128:(k0 + nk) * 128, c0:c0 + ncols].rearrange("(k p) n -> p k n", p=128),
                 [rr], [slot.r])
        return slot

    def precast(self, names):
        for nm in names:
            src = self.ins[nm]
            shp = list(src.shape)
            dst = self.nc.dram_tensor("wb_" + nm, shp, BF16).ap()
            rr = self.S.res("wb_" + nm)
            self.wb[nm] = (dst, rr)
            s2 = src if len(shp) == 2 else src.rearrange("l r n -> (l r) n")
            d2 = dst if len(shp) == 2 else dst.rearrange("l r n -> (l r) n")
            rows = s2.shape[0]
            for r0 in range(0, rows, 128):
                r1 = min(rows, r0 + 128)
                self.dma("pool", d2[r0:r1, :], s2[r0:r1, :], [self.wres], [rr])

    def ld(self, buf, src, q="pool", **kw):
        return self.dma(q, buf.t[:] if not isinstance(buf, tuple) else buf[0], src, [self.wres],
                        [buf.r if not isinstance(buf, tuple) else buf[1]], **kw)

    def setup_consts(self):
        I = self.ins
        self.ld(self.ident, I["c_ident"][:, :])
        self.ld(self.identb, I["c_ident"][:, :])
        self.memset(self.ones.t[:], 1.0, [self.ones.r])
        self.ld(self.causal4, I["c_causal4"][:, :]); self.ld(self.anti4, I["c_anti4"][:, :])
        self.ld(self.esmall, I["c_esmall"][:, :])
        self.ld(self.cmpR, I["c_cmpR"][:, :]); self.ld(self.cmpZ, I["c_cmpZ"][:, :])
        self.ld(self.cmpmask, I["c_cmpmask"][:, :]); self.ld(self.f12, I["c_f12"][:, :]); self.ld(self.eye16, I["c_eye16"][:, :])

    def setup_mod_inputs(self):
        I = self.ins
        ctm = self.xn
        self.dma("pool", ctm.t[0:33, :], I["cmat"][:, :], [self.wres], [ctm.r])
        self.act(ctm.t[0:33, :], ctm.t[0:33, :], AF.Silu, [ctm.r], [ctm.r])
        for k in range(8):
            pt = self.next_ptr()
            self.tr(pt.t[:, 0:33], ctm.t[0:33, k * 128:(k + 1) * 128], self.ident.t[0:33, 0:33], [ctm.r, self.ident.r], [pt.r])
            self.cp(self.cT.t[:, k, :], pt.t[:, 0:33], [pt.r], [self.cT.r])
        for l in range(2):
            self.dma("pool", self.badaT.t[:, l, :], I["b_ada"][l].rearrange("(e p) -> p e", p=128), [self.wres], [self.badaT.r],
                     allow_slow_non_contiguous=True)
            self.dma("pool", self.gnT.t[:, 0, l, :], I["norm_mix_g"][l].rearrange("(k p) -> p k", p=128), [self.wres],
                     [self.gnT.r], allow_slow_non_contiguous=True)
            self.dma("pool", self.gnT.t[:, 1, l, :], I["norm_ffn_g"][l].rearrange("(k p) -> p k", p=128), [self.wres],
                     [self.gnT.r], allow_slow_non_contiguous=True)

    def compute_mod(self, l):
        I = self.ins
        for s in range(12):
            slab = self.load_slab(I["w_ada"][l], 0, 8, s * 512, 512)
            for j in range(4):
                e = 4 * s + j
                pm = self.next_pmm()
                for k in range(8):
                    self.mm(pm.t[:, 0:33], slab.t[:, k, j * 128:(j + 1) * 128], self.cT.t[:, k, :], k == 0, k == 7,
                            [slab.r, self.cT.r], [pm.r])
                self.ts(self.modT.t[:, e, :], pm.t[:, 0:33], self.badaT.t[:, l, e:e + 1], None, ALU.add, None,
                        [pm.r, self.badaT.r], [self.modT.r])
        for w, base in ((0, 8), (1, 32)):
            for k in range(8):
                gcol = self.gnT.t[:, w, l, k:k + 1]
                self.ts(self.gs[w].t[:, k, :], self.modT.t[:, base + k, :], gcol, gcol, ALU.mult, ALU.add,
                        [self.modT.r, self.gnT.r], [self.gs[w].r])
        for w, base in ((0, 16), (1, 40)):
            for half in range(2):
                pt = self.next_ptr()
                for j in range(4):
                    k = half * 4 + j
                    self.tr(pt.t[0:33, j * 128:(j + 1) * 128], self.modT.t[:, base + k, :], self.ident.t[:, :],
                            [self.modT.r, self.ident.r], [pt.r])
                self.cp(self.gtm.t[0:33, w, half * 512:(half + 1) * 512], pt.t[0:33, :], [pt.r], [self.gtm.r])
            for half in range(2):
                pm = self.next_pmm()
                self.mm(pm.t[:, :], self.ones.t[32:33, :], self.gtm.t[32:33, w, half * 512:(half + 1) * 512], True, True,
                        [self.ones.r, self.gtm.r], [pm.r])
                self.cp(self.gbc.t[:, w, half * 512:(half + 1) * 512], pm.t[:, :], [pm.r], [self.gbc.r])

    def tiles_of(self, m, with_samples):
        tl = [("p", i, 128, i * 128) for i in range(self.MT)]
        if with_samples:
            tl.append(("s", self.MT, self.NS, self.MT * 128))
        return tl

    def norm_tile(self, tile, w):
        kind, i, np_, c0 = tile
        x = self.xb[i]
        st = self.stat
        shbase = 0 if w == 0 else 24
        self.act(self.junk.t[0:np_, :], x.t[0:np_, :], AF.Square, [x.r], [self.junk.r, st.r], accum_out=st.t[0:np_, 0:1])
        self.ts(st.t[0:np_, 0:1], st.t[0:np_, 0:1], 1.0 / D, EPS, ALU.mult, ALU.add, [st.r], [st.r])
        self.act(st.t[0:np_, 0:1], st.t[0:np_, 0:1], AF.Sqrt, [st.r], [st.r])
        self.S.op("dve", lambda: self.nc.vector.reciprocal(out=st.t[0:np_, 0:1], in_=st.t[0:np_, 0:1]), [st.r], [st.r])
        self.ts(self.xn.t[0:np_, :], x.t[0:np_, :], st.t[0:np_, 0:1], None, ALU.mult, None, [x.r, st.r], [self.xn.r])
        for half in range(2):
            pt = self.next_ptr()
            for j in range(4):
                k = half * 4 + j
                self.tr(pt.t[:, j * 128:j * 128 + np_], self.xn.t[0:np_, k * 128:(k + 1) * 128], self.ident.t[0:np_, 0:np_],
                        [self.xn.r, self.ident.r], [pt.r])
            for j in range(4):
                k = half * 4 + j
                src = pt.t[:, j * 128:j * 128 + np_]
                dst = self.hT.t[:, k, c0:c0 + np_]
                if kind == "p":
                    self.ts(dst, src, self.gs[w].t[:, k, 32:33], self.modT.t[:, shbase + k, 32:33], ALU.mult, ALU.add,
                            [pt.r, self.gs[w].r, self.modT.r], [self.hT.r])
                else:
                    self.tt(self.tmp16.t[:, 0:np_], src, self.gs[w].t[:, k, 0:np_], ALU.mult, [pt.r, self.gs[w].r], [self.tmp16.r])
                    self.tt(dst, self.tmp16.t[:, 0:np_], self.modT.t[:, shbase + k, 0:np_], ALU.add,
                            [self.tmp16.r, self.modT.r], [self.hT.r])

    def proj_tm(self, tiles, src, c0, ncols, zcol, post=None):
        slab = self.load_slab(src, 0, 8, c0, ncols)
        for tile in tiles:
            kind, i, np_, h0 = tile
            pm = self.next_pmm()
            for k in range(8):
                self.mm(pm.t[0:np_, 0:ncols], self.hT.t[:, k, h0:h0 + np_], slab.t[:, k, 0:ncols], k == 0, k == 7,
                        [self.hT.r, slab.r], [pm.r])
            if post is None:
                self.act(self.zb[i].t[0:np_, zcol:zcol + ncols], pm.t[0:np_, 0:ncols], AF.Copy, [pm.r], [self.zb[i].r])
            else:
                post(tile, pm)

    def ffn(self, tiles, l, out_fn):
        I = self.ins
        lo = min(t[3] for t in tiles)
        Wt = max(t[3] + t[2] for t in tiles)
        hc = 0
        for s in range(6):
            ncols = 512 if s < 5 else FFN - 5 * 512
            sg = self.load_slab(("w_ffn_gate", l), 0, 8, s * 512, ncols)
            su = self.load_slab(("w_ffn_up", l), 0, 8, s * 512, ncols)
            for j in range(ncols // 128):
                fb = [self.pmm[0], self.pmm[1], self.ptr[0], self.ptr[1]]
                pg = fb[(2 * hc) % 4]
                for k in range(8):
                    self.mm(pg.t[:, lo:Wt], sg.t[:, k, j * 128:(j + 1) * 128], self.hT.t[:, k, lo:Wt], k == 0, k == 7,
                            [sg.r, self.hT.r], [pg.r])
                pu = fb[(2 * hc + 1) % 4]
                for k in range(8):
                    self.mm(pu.t[:, lo:Wt], su.t[:, k, j * 128:(j + 1) * 128], self.hT.t[:, k, lo:Wt], k == 0, k == 7,
                            [su.r, self.hT.r], [pu.r])
                self.act(self.ftmp.t[:, lo:Wt], pg.t[:, lo:Wt], AF.Silu, [pg.r], [self.ftmp.r])
                self.tt(self.hidT.t[:, hc, lo:Wt], self.ftmp.t[:, lo:Wt], pu.t[:, lo:Wt], ALU.mult, [self.ftmp.r, pu.r], [self.hidT.r])
                hc += 1
        accs = [self.pacc[0], self.pacc[1], self.patt[0]]
        for half in range(2):
            for ks, (k0, nk) in enumerate(((0, 8), (8, 8), (16, 6))):
                sd = self.load_slab(("w_ffn_down", l), k0, nk, half * 512, 512)
                for ti, tile in enumerate(tiles):
                    kind, i, np_, h0 = tile
                    for k in range(nk):
                        self.mm(accs[ti].t[0:np_, :], self.hidT.t[:, k0 + k, h0:h0 + np_], sd.t[:, k, :], (k0 + k) == 0,
                                (k0 + k) == 21, [self.hidT.r, sd.r], [accs[ti].r])
            for ti, tile in enumerate(tiles):
                out_fn(tile, half, accs[ti])

    def resid_add(self, tile, half, ps, w):
        kind, i, np_, h0 = tile
        x = self.xb[i]
        cs = slice(half * 512, (half + 1) * 512)
        g = self.gbc.t[:, w, cs] if kind == "p" else self.gtm.t[0:np_, w, cs]
        gr = self.gbc.r if kind == "p" else self.gtm.r
        self.tt(self.xn.t[0:np_, cs], ps.t[0:np_, :], g[0:np_] if kind == "p" else g, ALU.mult, [ps.r, gr], [self.xn.r])
        self.tt(x.t[0:np_, cs], x.t[0:np_, cs], self.xn.t[0:np_, cs], ALU.add, [x.r, self.xn.r], [x.r])


class ProgL0(Prog):
    def alloc_l0(self):
        T, NT, sb = self.T, self.NT, self.sb
        self.qg = sb("qg", [128, 64]); self.kg = sb("kg", [128, 3, 64])
        self.w1sb = sb("w1sb", [128, 2, 32, 64], BF16)
        self.posT = sb("posT", [128, 2, 32], BF16)
        self.cmpb = sb("cmpb", [128, 2])
        self.w2sb = sb("w2sb", [128, 2, 64], BF16)
        self.wabd = sb("wabd", [128, 4, 128], BF16); self.wxbd = sb("wxbd", [128, 4, 128], BF16)
        self.convw = sb("convw", [128, 4, 4]); self.rgc = sb("rgc", [128, 6, 4])
        self.xrs = sb("xrs", [128, 4, 16])
        self.ggr = sb("ggr", [128, 4, self.W])
        self.sq = sb("sq", [128, 512])
        self.qb = sb("qb", [128, 512], BF16)
        self.qp = [sb(f"qp{g}", [128, 4, 128], BF16) for g in range(2)]
        self.hid = sb("hid", [128, 2, 2, 8], BF16)
        self.kcrow = sb("kcrow", [128, 128]); self.vcrow = sb("vcrow", [128, 2, 65], BF16)
        self.gsig = sb("gsig", [128, 24])
        self.e32 = sb("e32", [128, 256]); self.impacc = sb("impacc", [128, 256])
        self.imp = sb("imp", [128, 128]); self.imp2 = sb("imp2", [128, 128]); self.negsel = sb("negsel", [128, 128])
        self.m8 = sb("m8", [128, 16]); self.den = sb("den", [128, 8])
        self.negT4 = [sb(f"negT4_{g}", [128, 4, 128], BF16) for g in range(2)]
        self.pT = [sb(f"pT{i}", [128, 512], BF16) for i in range(2)]
        self.pT_i = 0
        self.onsa = sb("onsa", [128, 512]); self.otmp = sb("otmp", [128, 256])
        self.zb = [sb(f"zb{i}", [128, 1304]) for i in range(self.MT)] + [sb("zbs", [128, 1304])]
        self.x1res = [self.S.res(f"x1_{t}") for t in range(self.NT)]
        self.rg = [sb(f"rg{i}", [128, 256]) for i in range(7)]
        self.xcb = sb("xcb", [128, 256], BF16)

    def next_pT(self):
        b = self.pT[self.pT_i % 2]; self.pT_i += 1; return b

    def alloc_l0_prompt(self):
        T, NT, sb = self.T, self.NT, self.sb
        self.ksT = sb("ksT", [128, T], BF16)
        self.vs_aug = sb("vs_aug", [128, NT, 2, 65], BF16)
        self.nbt_max = max(1, (T // 32 + 127) // 128)
        self.kcT = sb("kcT", [128, max(T // 32, 8)], BF16)
        self.vc_aug = sb("vc_aug", [128, self.nbt_max, 2, 65], BF16)
        self.kwT = sb("kwT", [128, 6 * 128], BF16)
        self.vw_aug = sb("vw_aug", [128, 6, 2, 65], BF16)
        self.xrbuf = sb("xrbuf", [128, 4, 3 + 256]); self.hstate = sb("hstate", [128, 4])
        self.rawTz = [sb(f"rawTz{g}", [128, 2, 256], BF16) for g in range(2)]
        self.memset(self.xrbuf.t[:], 0.0, [self.xrbuf.r]); self.memset(self.hstate.t[:], 0.0, [self.hstate.r])
        for b_ in (self.vs_aug, self.vc_aug, self.vw_aug):
            self.memset(b_.t[:], 1.0, [b_.r])
        self.memset(self.kwT.t[:], 0.0, [self.kwT.r])
        for g in range(2):
            self.memset(self.rawTz[g].t[:], 0.0, [self.rawTz[g].r])

    def setup_l0(self):
        I, nc = self.ins, self.nc
        slow = dict(allow_slow_non_contiguous=True)
        self.ld(self.qg, I["nsa_q_gain"][0:1, :].broadcast_to([128, 64]))
        self.ts(self.qg.t[:], self.qg.t[:], 0.125, None, ALU.mult, None, [self.qg.r], [self.qg.r])
        for j in range(3):
            self.dma("pool", self.kg.t[:, j, :], I["nsa_k_gain"][j:j + 1, :].broadcast_to([128, 64]), [self.wres], [self.kg.r])
        for c in range(2):
            for hlf in range(2):
                self.dma("pool", self.w1sb.t[hlf * 64:(hlf + 1) * 64, c, :, :], I["nsa_cmp_w1"][c].rearrange("l d e -> d l e"),
                         [self.wres], [self.w1sb.r])
            self.dma("pool", self.w2sb.t[0:64, c, :], I["nsa_cmp_w2"][c], [self.wres], [self.w2sb.r])
        for c in range(2):
            self.dma("pool", self.posT.t[0:64, c, :], I["nsa_cmp_pos"][c].rearrange("l d -> d l"), [self.wres], [self.posT.r], **slow)
        for c in range(2):
            pm = self.next_pmm()
            for l in range(32):
                self.mm(pm.t[0:64, 0:1], self.w1sb.t[0:64, c, l, :], self.posT.t[0:64, c, l:l + 1], l == 0, l == 31,
                        [self.w1sb.r, self.posT.r], [pm.r])
            self.cp(self.cmpb.t[0:64, c:c + 1], pm.t[0:64, 0:1], [pm.r], [self.cmpb.r])
        for wsb, nm in ((self.wabd, "rg_wa"), (self.wxbd, "rg_wx")):
            self.memset(wsb.t[:], 0.0, [wsb.r])
            for j in range(4):
                self.dma("pool", wsb.t[0:64, j, 0:64], I[nm][2 * j], [self.wres], [wsb.r])
                self.dma("pool", wsb.t[64:128, j, 64:128], I[nm][2 * j + 1], [self.wres], [wsb.r])
        for i_ in range(4):
            self.dma("pool", self.convw.t[:, :, i_], I["rg_conv_w"][i_].rearrange("(c p) -> p c", p=128), [self.wres], [self.convw.r], **slow)
        for j, nm in enumerate(("rg_conv_b", "rg_ba", "rg_bx", "rg_lambda")):
            self.dma("pool", self.rgc.t[:, j, :], I[nm].rearrange("o (c p) -> p (o c)", p=128), [self.wres], [self.rgc.r], **slow)
        r = self.rgc
        self.act(r.t[:, 4, :], r.t[:, 3, :], AF.Exp, [r.r], [r.r], scale=-1.0)
        self.act(r.t[:, 4, :], r.t[:, 4, :], AF.Ln, [r.r], [r.r], bias=1.0)
        self.ts(r.t[:, 4, :], r.t[:, 4, :], -8.0, None, ALU.mult, None, [r.r], [r.r])
        self.ts(r.t[:, 5, :], r.t[:, 4, :], 2.0, None, ALU.mult, None, [r.r], [r.r])
        self.memset(self.vcrow.t[:], 1.0, [self.vcrow.r])
        for g in range(2):
            self.memset(self.qp[g].t[:], 0.0, [self.qp[g].r])

    def headnorm(self, src_ap, np_, nh, gain_ap, out_ap, R, W, shape4=None):
        sq, st = self.sq, self.den
        sv = lambda ap: ap.rearrange("p (h d) -> p h d", d=64)
        self.tt(sq.t[0:np_, 0:nh * 64], src_ap, src_ap, ALU.mult, R, [sq.r])
        self.S.op("dve", lambda: self.nc.vector.tensor_reduce(out=st.t[0:np_, 0:nh], in_=sv(sq.t[0:np_, 0:nh * 64]), axis=AX.X,
                                                              op=ALU.add), [sq.r], [st.r])
        self.ts(st.t[0:np_, 0:nh], st.t[0:np_, 0:nh], 1.0 / 64, EPS, ALU.mult, ALU.add, [st.r], [st.r])
        self.act(st.t[0:np_, 0:nh], st.t[0:np_, 0:nh], AF.Sqrt, [st.r], [st.r])
        self.S.op("dve", lambda: self.nc.vector.reciprocal(out=st.t[0:np_, 0:nh], in_=st.t[0:np_, 0:nh]), [st.r], [st.r])
        self.tt(sv(sq.t[0:np_, 0:nh * 64]), sv(src_ap), st.t[0:np_, 0:nh, None].broadcast_to([np_, nh, 64]), ALU.mult,
                R + [st.r], [sq.r])
        if shape4 is None:
            self.tt(out_ap, sv(sq.t[0:np_, 0:nh * 64]), gain_ap[0:np_, None, :].broadcast_to([np_, nh, 64]), ALU.mult,
                    [sq.r] + R, W)
        else:
            a, b = shape4
            in0 = sq.t[0:np_, 0:nh * 64].rearrange("p (a b d) -> p a b d", a=a, b=b, d=64)
            in1 = gain_ap[0:np_, None, None, :].broadcast_to([np_, a, b, 64])
            self.tt(out_ap, in0, in1, ALU.mult, [sq.r] + R, W)

    def l0_post(self, tile, t):
        kind, i, np_, h0 = tile
        z = self.zb[i]
        O = self.outs
        sv = lambda ap: ap.rearrange("p (h d) -> p h d", d=64)
        self.headnorm(z.t[0:np_, 768:896], np_, 2, self.kg.t[:, 1, :], sv(z.t[0:np_, 768:896]), [z.r, self.kg.r], [z.r])
        self.headnorm(z.t[0:np_, 1024:1152], np_, 2, self.kg.t[:, 2, :], sv(z.t[0:np_, 1024:1152]), [z.r, self.kg.r], [z.r])
        if kind == "p":
            self.dma("pool", O["kv_p"][t * 128:(t + 1) * 128, :], z.t[0:128, 512:1024], [z.r], [self.ores], own=z.r)
            if t >= self.NT - min(4, self.NT):
                w0 = (t - (self.NT - min(4, self.NT))) * 128
                self.dma("pool", O["win_p"][w0:w0 + 128, :], z.t[0:128, 1024:1280], [z.r], [self.ores], own=z.r)
            pt = self.next_ptr()
            self.tr(pt.t[:, 0:128], z.t[0:128, 768:896], self.ident.t[:, :], [z.r, self.ident.r], [pt.r])
            self.tr(pt.t[:, 128:256], z.t[0:128, 1024:1152], self.ident.t[:, :], [z.r, self.ident.r], [pt.r])
            self.tr(pt.t[:, 256:384], z.t[0:128, 512:640], self.ident.t[:, :], [z.r, self.ident.r], [pt.r])
            self.tr(pt.t[:, 384:512], z.t[0:128, 640:768], self.ident.t[:, :], [z.r, self.ident.r], [pt.r])
            slot = t % 6
            self.cp(self.ksT.t[:, t * 128:(t + 1) * 128], pt.t[:, 0:128], [pt.r], [self.ksT.r])
            self.cp(self.kwT.t[:, slot * 128:(slot + 1) * 128], pt.t[:, 128:256], [pt.r], [self.kwT.r])
            for g in range(2):
                gs_ = slice(g * 64, (g + 1) * 64)
                self.cp(self.rawTz[g].t[gs_, :, i * 128:(i + 1) * 128], pt.t[gs_, 256:512].rearrange("p (c q) -> p c q", q=128), [pt.r],
                        [self.rawTz[g].r])
            self.cp(self.vs_aug.t[:, t, :, 0:64], sv(z.t[0:128, 896:1024]), [z.r], [self.vs_aug.r], eng="pool")
            self.cp(self.vw_aug.t[:, slot, :, 0:64], sv(z.t[0:128, 1152:1280]), [z.r], [self.vw_aug.r], eng="pool")
        else:
            self.dma("pool", O["kv_s"][:, :], z.t[0:np_, 512:1024], [z.r], [self.ores], own=z.r)

    def compress_macro(self, m):
        nb0 = 8 * m
        for c in range(2):
            pm = self.next_pmm()
            for g in range(2):
                for l in range(32):
                    self.mm(pm.t[0:64, g * 8:(g + 1) * 8], self.w1sb.t[:, c, l, :],
                            self.rawTz[g].t[:, c, l:256:32], l == 0, l == 31, [self.w1sb.r, self.rawTz[g].r], [pm.r])
            if self.dbg and c == 0 and m == self.NM - 1:
                self.cp(self.kcrow.t[0:64, 0:16], pm.t[0:64, 0:16], [pm.r], [self.kcrow.r])
                self.dma("pool", self.outs["dbg_hid"][:, :], self.kcrow.t[0:64, 0:16], [self.kcrow.r], [self.ores], own=self.kcrow.r)
                self.dma("pool", self.outs["dbg_w1"][:, :], self.w1sb.t[:, :, :, :].rearrange("p c l e -> p (c l e)"), [self.w1sb.r], [self.ores], own=self.w1sb.r)
            self.act(self.hid.t[0:64, c, :, :], pm.t[0:64, 0:16].rearrange("p (g n) -> p g n", n=8), AF.Gelu_apprx_tanh,
                     [pm.r, self.cmpb.r], [self.hid.r], bias=self.cmpb.t[0:64, c:c + 1])
            pm2 = self.next_pmm()
            for g in range(2):
                self.mm(pm2.t[0:8, g * 64:(g + 1) * 64], self.hid.t[0:64, c, g, :], self.w2sb.t[0:64, c, :], True, True,
                        [self.hid.r, self.w2sb.r], [pm2.r])
            if c == 0:
                self.cp(self.kcrow.t[0:8, :], pm2.t[0:8, 0:128], [pm2.r], [self.kcrow.r])
                sv = lambda ap: ap.rearrange("p (h d) -> p h d", d=64)
                self.headnorm(self.kcrow.t[0:8, :], 8, 2, self.kg.t[:, 0, :], sv(self.kcrow.t[0:8, :]), [self.kcrow.r, self.kg.r],
                              [self.kcrow.r])
                pt = self.next_ptr()
                self.tr(pt.t[:, 0:8], self.kcrow.t[0:8, :], self.ident.t[0:8, 0:8], [self.kcrow.r, self.ident.r], [pt.r])
                self.cp(self.kcT.t[:, nb0:nb0 + 8], pt.t[:, 0:8], [pt.r], [self.kcT.r])
            else:
                self.cp(self.vcrow.t[0:8, :, 0:64], pm2.t[0:8, 0:128].rearrange("p (g d) -> p g d", d=64), [pm2.r], [self.vcrow.r])
                p0, bt = nb0 % 128, nb0 // 128
                self.dma("pool", self.vc_aug.t[p0:p0 + 8, bt, :, :], self.vcrow.t[0:8, :, :], [self.vcrow.r], [self.vc_aug.r])

    def combine(self, g, br, first):
        o = self.pacc[g].t[:, 0:260].rearrange("p (r e) -> p r e", e=65)
        rd = self.den
        self.ts(rd.t[:, 0:4], o[:, :, 64], 1e-30, None, ALU.max, None, [self.pacc[g].r], [rd.r])
        self.S.op("dve", lambda: self.nc.vector.reciprocal(out=rd.t[:, 0:4], in_=rd.t[:, 0:4]), [rd.r], [rd.r])
        gate = self.gsig.t[:, g * 12:(g + 1) * 12].rearrange("p (r b) -> p r b", b=3)[:, :, br]
        self.tt(rd.t[:, 0:4], rd.t[:, 0:4], gate, ALU.mult, [rd.r, self.gsig.r], [rd.r])
        ov = self.onsa.t[:, g * 256:(g + 1) * 256].rearrange("p (r d) -> p r d", d=64)
        sc = rd.t[:, 0:4, None].broadcast_to([128, 4, 64])
        if first:
            self.tt(ov, o[:, :, 0:64], sc, ALU.mult, [self.pacc[g].r, rd.r], [self.onsa.r])
        else:
            tv = self.otmp.t[:, :].rearrange("p (r d) -> p r d", d=64)
            self.tt(tv, o[:, :, 0:64], sc, ALU.mult, [self.pacc[g].r, rd.r], [self.otmp.r])
            self.tt(ov, ov, tv, ALU.add, [self.onsa.r, self.otmp.r], [self.onsa.r])

    def pipelined(self, items, qk, pv):
        if not items:
            return
        cur = qk(items[0])
        for n, it in enumerate(items):
            nxt = qk(items[n + 1]) if n + 1 < len(items) else None
            pv(it, cur)
            cur = nxt
            if self.side_jobs and n % 2 == 1:
                self.side_jobs.pop(0)()

    def pv4(self, g, pT, nk, v_ap, first, last, vres):
        for r in range(4):
            self.mm(self.pacc[g].t[:, r * 65:(r + 1) * 65], pT.t[0:nk, r * 128:(r + 1) * 128], v_ap, first and r == 0, last,
                    [pT.r, vres], [self.pacc[g].r], skip=True)

    def nsa_prompt_tile(self, t, i):
        nc = self.nc
        nblk, nsb = 4 * t + 4, 2 * t + 2
        topk = nsb > 16
        qf = lambda g: self.qp[g].t[:, :, :].rearrange("p r q -> p (r q)")
        for g in range(2 if topk else 0):
            for r in range(4):
                pm = self.next_pmm()
                self.mm(pm.t[:, 0:nblk], self.qp[g].t[:, r, :], self.kcT.t[:, 0:nblk], True, True,
                        [self.qp[g].r, self.kcT.r], [pm.r])
                self.tt(pm.t[:, nblk - 4:nblk], pm.t[:, nblk - 4:nblk], self.cmpmask.t[:, :], ALU.add, [pm.r, self.cmpmask.r], [pm.r])
                self.act(self.e32.t[:, 0:nblk], pm.t[:, 0:nblk], AF.Exp, [pm.r], [self.e32.r, self.den.r],
                         accum_out=self.den.t[:, 4:5])
                self.ts(self.den.t[:, 4:5], self.den.t[:, 4:5], 1e-30, None, ALU.max, None, [self.den.r], [self.den.r])
                self.S.op("dve", lambda: nc.vector.reciprocal(out=self.den.t[:, 4:5], in_=self.den.t[:, 4:5]), [self.den.r], [self.den.r])
                if r == 0:
                    self.ts(self.impacc.t[:, 0:nblk], self.e32.t[:, 0:nblk], self.den.t[:, 4:5], None, ALU.mult, None,
                            [self.e32.r, self.den.r], [self.impacc.r])
                else:
                    self.stt(self.impacc.t[:, 0:nblk], self.e32.t[:, 0:nblk], self.den.t[:, 4:5], self.impacc.t[:, 0:nblk],
                             ALU.mult, ALU.add, [self.e32.r, self.den.r, self.impacc.r], [self.impacc.r])
            if topk:
                imp = self.imp
                self.S.op("dve", lambda: nc.vector.tensor_reduce(
                    out=imp.t[:, 0:nsb], in_=self.impacc.t[:, 0:nblk].rearrange("p (n two) -> p n two", two=2), axis=AX.X,
                    op=ALU.add), [self.impacc.r], [imp.r])
                self.ts(imp.t[:, 2 * t:2 * t + 1], imp.t[:, 2 * t:2 * t + 1], self.f12.t[:, 0:1], None, ALU.max, None,
                        [imp.r, self.f12.r], [imp.r])
                self.cp(imp.t[:, 2 * t + 1:2 * t + 2], self.f12.t[:, 1:2], [self.f12.r, imp.r], [imp.r])
                self.memset(imp.t[:, 0:1], 1e4, [imp.r], eng="dve")
                self.S.op("dve", lambda: nc.vector.max(out=self.m8.t[:, 0:8], in_=imp.t[:, 0:nsb]), [imp.r], [self.m8.r])
                self.S.op("dve", lambda: nc.vector.match_replace(out=self.imp2.t[:, 0:nsb], in_to_replace=self.m8.t[:, 0:8],
                                                                 in_values=imp.t[:, 0:nsb], imm_value=-2.0),
                          [imp.r, self.m8.r], [self.imp2.r])
                self.S.op("dve", lambda: nc.vector.max(out=self.m8.t[:, 8:16], in_=self.imp2.t[:, 0:nsb]), [self.imp2.r], [self.m8.r])
                self.ts(self.negsel.t[:, 0:nsb], imp.t[:, 0:nsb], self.m8.t[:, 15:16], NEG, ALU.is_lt, ALU.mult,
                        [imp.r, self.m8.r], [self.negsel.r])
                pt = self.next_ptr()
                self.tr(pt.t[0:nsb, 0:128], self.negsel.t[:, 0:nsb], self.ident.t[:, :], [self.negsel.r, self.ident.r], [pt.r])
                self.cp(self.negT4[g].t[0:nsb, :, :], pt.t[0:nsb, None, 0:128].broadcast_to([nsb, 4, 128]), [pt.r], [self.negT4[g].r])
        nbt = (nblk + 127) // 128
        for g in range(2):
            for bt in range(nbt):
                nb_t = min(128, nblk - bt * 128)
                last = bt == nbt - 1
                pa = self.next_patt()
                self.mm(pa.t[0:nb_t, :], self.kcT.t[:, bt * 128:bt * 128 + nb_t], qf(g), True, not last,
                        [self.kcT.r, self.qp[g].r], [pa.r])
                if last:
                    n0 = nb_t - 4
                    self.mm(pa.t[0:nb_t, :], self.cmpZ.t[0:4, 124 - n0:124 - n0 + nb_t], self.cmpR.t[0:4, :], False, True,
                            [self.cmpZ.r, self.cmpR.r], [pa.r])
                pT = self.next_pT()
                self.act(pT.t[0:nb_t, :], pa.t[0:nb_t, :], AF.Exp, [pa.r], [pT.r])
                self.pv4(g, pT, nb_t, self.vc_aug.t[0:nb_t, bt, g, :], bt == 0, last, self.vc_aug.r)
            self.combine(g, 0, True)
        def sel_qk(it):
            g, kt = it
            diag = kt == t
            pa = self.next_patt()
            self.mm(pa.t[:, :], self.ksT.t[:, kt * 128:(kt + 1) * 128], qf(g), True, not (topk or diag),
                    [self.ksT.r, self.qp[g].r], [pa.r])
            if topk:
                base = 64 * ((2 * kt) // 64)
                nrow = min(64, nsb - base)
                pi_ = kt % 32
                self.mm(pa.t[:, :], self.esmall.t[base:base + nrow, pi_ * 128:(pi_ + 1) * 128],
                        self.negT4[g].t[base:base + nrow, :, :].rearrange("p r q -> p (r q)"), False, not diag,
                        [self.esmall.r, self.negT4[g].r], [pa.r])
            if diag:
                self.mm(pa.t[:, :], self.identb.t[:, :], self.causal4.t[:, :], False, True, [self.identb.r, self.causal4.r], [pa.r])
            return pa

        def sel_pv(it, pa):
            g, kt = it
            pT = self.next_pT()
            self.act(pT.t[:, :], pa.t[:, :], AF.Exp, [pa.r], [pT.r])
            self.pv4(g, pT, 128, self.vs_aug.t[:, kt, g, :], kt == 0, kt == t, self.vs_aug.r)
            if kt == t:
                self.combine(g, 1, False)

        self.pipelined([(g, kt) for g in range(2) for kt in range(t + 1)], sel_qk, sel_pv)
        k0 = max(0, t - 4)

        def win_qk(it):
            g, kt = it
            diag, far = kt == t, kt == t - 4
            slot = kt % 6
            pa = self.next_patt()
            self.mm(pa.t[:, :], self.kwT.t[:, slot * 128:(slot + 1) * 128], qf(g), True, not (far or diag),
                    [self.kwT.r, self.qp[g].r], [pa.r])
            if far:
                self.mm(pa.t[:, :], self.identb.t[:, :], self.anti4.t[:, :], False, True, [self.identb.r, self.anti4.r], [pa.r])
            if diag:
                self.mm(pa.t[:, :], self.identb.t[:, :], self.causal4.t[:, :], False, True, [self.identb.r, self.causal4.r], [pa.r])
            return pa

        def win_pv(it, pa):
            g, kt = it
            slot = kt % 6
            pT = self.next_pT()
            self.act(pT.t[:, :], pa.t[:, :], AF.Exp, [pa.r], [pT.r])
            self.pv4(g, pT, 128, self.vw_aug.t[:, slot, g, :], kt == k0, kt == t, self.vw_aug.r)
            if kt == t:
                self.combine(g, 2, False)

        self.pipelined([(g, kt) for g in range(2) for kt in range(k0, t + 1)], win_qk, win_pv)
        if self.dbg:
            self.dma("pool", self.outs["dbg_onsa"][t * 128:(t + 1) * 128, :], self.onsa.t[:, :], [self.onsa.r], [self.ores], own=self.onsa.r)
        pt = self.next_ptr()
        for k in range(4):
            self.tr(pt.t[:, k * 128:(k + 1) * 128], self.onsa.t[:, k * 128:(k + 1) * 128], self.ident.t[:, :],
                    [self.onsa.r, self.ident.r], [pt.r])
        self.cp(self.catT.t[:, 0:4, i * 128:(i + 1) * 128], pt.t[:, :].rearrange("p (k q) -> p k q", q=128), [pt.r], [self.catT.r])

    def rglru_macro(self):
        nc = self.nc
        xc, r_, i_, a_, a2, u_, hs = self.rg
        c = self.rgc
        Wp = 256
        for j in range(4):
            self.side_jobs.append(lambda j=j: self.rglru_chunk(j))

    def rglru_chunk(self, j):
        nc = self.nc
        xc, r_, i_, a_, a2, u_, hs = self.rg
        c = self.rgc
        Wp = 256
        if True:
            xr = self.xrbuf
            self.ts(xc.t[:, :], xr.t[:, j, 0:Wp], self.convw.t[:, j, 0:1], c.t[:, 0, j:j + 1], ALU.mult, ALU.add,
                    [xr.r, self.convw.r, c.r], [xc.r])
            for k in range(1, 4):
                self.stt(xc.t[:, :], xr.t[:, j, k:k + Wp], self.convw.t[:, j, k:k + 1], xc.t[:, :], ALU.mult, ALU.add,
                         [xr.r, self.convw.r, xc.r], [xc.r])
            self.cp(self.xcb.t[:, :], xc.t[:, :], [xc.r], [self.xcb.r])
            pr = self.next_pmm()
            self.mm(pr.t[:, 0:Wp], self.wabd.t[:, j, :], self.xcb.t[:, :], True, True, [self.wabd.r, self.xcb.r], [pr.r])
            pi = self.next_pmm()
            self.mm(pi.t[:, 0:Wp], self.wxbd.t[:, j, :], self.xcb.t[:, :], True, True, [self.wxbd.r, self.xcb.r], [pi.r])
            self.act(r_.t[:, :], pr.t[:, 0:Wp], AF.Sigmoid, [pr.r, c.r], [r_.r], bias=c.t[:, 1, j:j + 1])
            self.act(i_.t[:, :], pi.t[:, 0:Wp], AF.Sigmoid, [pi.r, c.r], [i_.r], bias=c.t[:, 2, j:j + 1])
            self.act(a_.t[:, :], r_.t[:, :], AF.Exp, [r_.r, c.r], [a_.r], scale=c.t[:, 4, j:j + 1])
            self.act(a2.t[:, :], r_.t[:, :], AF.Exp, [r_.r, c.r], [a2.r], scale=c.t[:, 5, j:j + 1])
            self.ts(a2.t[:, :], a2.t[:, :], -1.0, 1.0, ALU.mult, ALU.add, [a2.r], [a2.r])
            self.act(a2.t[:, :], a2.t[:, :], AF.Sqrt, [a2.r], [a2.r])
            self.tt(u_.t[:, :], a2.t[:, :], i_.t[:, :], ALU.mult, [a2.r, i_.r], [u_.r])
            self.tt(u_.t[:, :], u_.t[:, :], xc.t[:, :], ALU.mult, [u_.r, xc.r], [u_.r])
            self.S.op("dve", lambda: nc.vector.tensor_tensor_scan(out=hs.t[:, :], data0=a_.t[:, :], data1=u_.t[:, :],
                                                                  initial=self.hstate.t[:, j:j + 1], op0=ALU.mult, op1=ALU.add),
                      [a_.r, u_.r, self.hstate.r], [hs.r])
            self.cp(self.hstate.t[:, j:j + 1], hs.t[:, Wp - 1:Wp], [hs.r], [self.hstate.r])
            self.tt(self.catT.t[:, 4 + j, 0:Wp], hs.t[:, :], self.ggr.t[:, j, 0:Wp], ALU.mult, [hs.r, self.ggr.r], [self.catT2r])
            self.cp(self.rg[1].t[:, 0:3], xr.t[:, j, Wp:Wp + 3], [xr.r, r_.r], [r_.r])
            self.cp(xr.t[:, j, 0:3], self.rg[1].t[:, 0:3], [r_.r, xr.r], [xr.r])

    def proj_fm(self, src, c0, lo, hi, post):
        slab = self.load_slab(src, 0, 8, c0, 512)
        for j in range(4):
            pm = self.next_pmm()
            for k in range(8):
                self.mm(pm.t[:, 0:hi - lo], slab.t[:, k, j * 128:(j + 1) * 128], self.hT.t[:, k, lo:hi], k == 0, k == 7,
                        [slab.r, self.hT.r], [pm.r])
            post(j, pm)

    def pass1_macro(self, m):
        I, O = self.ins, self.outs
        ws = False
        tiles = self.tiles_of(m, ws)
        Wt = 256
        for (kind, i, np_, h0) in tiles:
            t = m * self.MT + i
            self.dma("sp", self.xb[i].t[:, :], I["xp"][t * 128:(t + 1) * 128, :], [self.wres], [self.xb[i].r])
        for tile in tiles:
            self.norm_tile(tile, 0)
        self.proj_tm(tiles, ("w_in_ab", None), 0, 512, 0)
        self.proj_tm(tiles, ("w_in_ab", None), 512, 512, 512)
        self.proj_tm(tiles, ("w_in_ab", None), 1024, 280, 1024)

        def post_xr(j, pm):
            self.cp(self.xrbuf.t[:, j, 3:3 + 256], pm.t[:, 0:256], [pm.r], [self.xrbuf.r])
            if ws:
                self.cp(self.xrs.t[:, j, :], pm.t[:, 256:272], [pm.r], [self.xrs.r])

        def post_gr(j, pm):
            self.act(self.ggr.t[:, j, 0:Wt], pm.t[:, 0:Wt], AF.Gelu_apprx_tanh, [pm.r], [self.ggr.r])

        self.proj_fm(("w_in_ab", None), 1304, 0, Wt, post_xr)
        self.proj_fm(("w_in_ab", None), 1816, 0, Wt, post_gr)
        for tile in tiles:
            if tile[0] == "p":
                self.l0_post(tile, m * self.MT + tile[1])
                if tile[1] == self.MT - 1:
                    self.compress_macro(m)
        self.rglru_macro()
        for tile in tiles:
            if tile[0] == "p":
                t = m * self.MT + tile[1]
                self.rebuild_qT(tile)
                self.nsa_prompt_tile(t, tile[1])
        while self.side_jobs:
            self.side_jobs.pop(0)()
        for half in range(2):
            slab = self.load_slab(("w_out_ab", None), 0, 8, half * 512, 512)
            for tile in tiles:
                kind, i, np_, h0 = tile
                pm = self.next_pmm()
                for k in range(8):
                    self.mm(pm.t[0:np_, :], self.catT.t[:, k, h0:h0 + np_], slab.t[:, k, :], k == 0, k == 7, [self.catT.r, slab.r], [pm.r])
                self.resid_add(tile, half, pm, 0)
        if self.dbg:
            for (kind, i, np_, h0) in tiles:
                if kind == "p":
                    t = m * self.MT + i
                    self.dma("pool", O["dbg_xmid"][t * 128:(t + 1) * 128, :], self.xb[i].t[:, :], [self.xb[i].r], [self.ores], own=self.xb[i].r)
        for tile in tiles:
            self.norm_tile(tile, 1)
        self.ffn(tiles, 0, lambda tile, half, ps: self.resid_add(tile, half, ps, 1))
        for (kind, i, np_, h0) in tiles:
            if kind == "p":
                t = m * self.MT + i
                self.dma("pool", self.x1d[t * 128:(t + 1) * 128, :], self.xb[i].t[:, :], [self.xb[i].r], [self.x1res[t]], own=self.xb[i].r)
                if self.dbg:
                    self.dma("pool", O["dbg_x1"][t * 128:(t + 1) * 128, :], self.xb[i].t[:, :], [self.xb[i].r], [self.ores], own=self.xb[i].r)
            elif self.dbg:
                self.dma("pool", O["dbg_x1s"][:, :], self.xb[i].t[0:np_, :], [self.xb[i].r], [self.ores], own=self.xb[i].r)

    def rebuild_qT(self, tile):
        kind, i, np_, h0 = tile
        z = self.zb[i]
        qout = self.qb.t[0:np_, :].rearrange("p (r g d) -> p g r d", r=4, g=2, d=64)
        self.headnorm(z.t[0:np_, 0:512], np_, 8, self.qg.t, qout, [z.r, self.qg.r], [self.qb.r], shape4=(2, 4))
        self.act(self.gsig.t[0:np_, :], z.t[0:np_, 1280:1304], AF.Sigmoid, [z.r], [self.gsig.r])
        pt = self.next_ptr()
        ptb = pt.t[:].bitcast(BF16)
        for r in range(4):
            self.tr(ptb[:, r * 128:r * 128 + np_], self.qb.t[0:np_, r * 128:(r + 1) * 128], self.identb.t[0:np_, 0:np_],
                    [self.qb.r, self.identb.r], [pt.r])
        for g in range(2):
            gs_ = slice(g * 64, (g + 1) * 64)
            self.cp(self.qp[g].t[gs_, :, 0:np_], ptb[gs_, 0:512].rearrange("p (r q) -> p r q", q=128)[:, :, 0:np_], [pt.r], [self.qp[g].r])


    def state_copies(self):
        I, O = self.ins, self.outs
        for s_ in range(self.NS):
            self.dma("act", O["win_s"][s_, 0:511, :], I["state_win"][s_, 1:512, :], [self.wres], [self.ores], own=self.ores)
            self.dma("act", O["dil_s"][s_, 0:2047, :], I["state_dil"][s_, 1:2048, :], [self.wres], [self.ores], own=self.ores)
        self.dma("act", O["conv_s"][:, 0:2, :], I["state_conv"][:, 1:3, :], [self.wres], [self.ores], own=self.ores)

    def alloc_l0_samp(self):
        sb = self.sb
        self.idx = sb("idx", [128, 256], I32); self.ptbc = sb("ptbc", [128, 256], I32); self.iop = sb("iop", [128, 2])
        self.pg = [sb(f"pg{i}", [128, 512]) for i in range(3)]
        self.ksTs = sb("ksTs", [128, 16 * 128], BF16)
        self.vsas = sb("vsas", [128, 16, 2, 65], BF16)
        self.rawTzs = [sb(f"rawTzs{g}", [128, 2, 512], BF16) for g in range(2)]
        self.kcTs = sb("kcTs", [128, 64], BF16); self.vcs = sb("vcs", [128, 2, 65], BF16)
        self.hids = sb("hids", [128, 2, 2, 64], BF16)
        self.kwTs = sb("kwTs", [128, 4 * 128], BF16); self.vwas = sb("vwas", [128, 4, 2, 65], BF16)
        self.wtile = [sb(f"wtile{i}", [128, 256]) for i in range(2)]
        self.selfT = sb("selfT", [128, 2, 16], BF16)
        self.vself = sb("vself", [128, 2, 2, 65], BF16)
        self.negT4s = [sb(f"negT4s{g}", [128, 4], BF16) for g in range(2)]
        self.orow = sb("orow", [128, 3, 512]); self.osamp = sb("osamp", [128, 3, 512])
        self.cstm = sb("cstm", [128, 512])
        self.csT = sb("csT", [128, 3, 4, 16]); self.h0T = sb("h0T", [128, 4, 16]); self.hsT = sb("hsT", [128, 4, 16])

    def l0_samples_pass(self):
        I, O, nc = self.ins, self.outs, self.nc
        NS = self.NS
        tile = ("s", self.MT, NS, self.MT * 128)
        lo, hi = tile[3], tile[3] + NS
        z = self.zb[2]
        sv = lambda ap: ap.rearrange("p (h d) -> p h d", d=64)
        for b_ in (self.vsas, self.vcs, self.vwas, self.vself):
            self.memset(b_.t[:], 1.0, [b_.r])
        for g in range(2):
            self.memset(self.rawTzs[g].t[:], 0.0, [self.rawTzs[g].r])
        self.dma("sp", self.ptbc.t[:, :], I["page_table"].rearrange("s j -> (s j)").rearrange("(o n) -> o n", o=1).broadcast_to([128, 256]),
                 [self.wres], [self.ptbc.r])
        self.S.op("pool", lambda: nc.gpsimd.iota(self.iop.t[:, 0:1], pattern=[[0, 1]], base=0, channel_multiplier=1,
                                                 allow_small_or_imprecise_dtypes=True), [], [self.iop.r])
        self.ts(self.idx.t[:, :], self.ptbc.t[:, :], 128.0, self.iop.t[:, 0:1], ALU.mult, ALU.add, [self.ptbc.r, self.iop.r], [self.idx.r])
        self.dma("sp", self.xb[2].t[0:NS, :], I["xs"][:, :], [self.wres], [self.xb[2].r])
        self.norm_tile(tile, 0)
        self.proj_tm([tile], ("w_in_ab", None), 0, 512, 0)
        self.proj_tm([tile], ("w_in_ab", None), 512, 512, 512)
        self.proj_tm([tile], ("w_in_ab", None), 1024, 280, 1024)
        self.proj_fm(("w_in_ab", None), 1304, lo, hi, lambda j, pm: self.cp(self.xrs.t[:, j, :], pm.t[:, 0:NS], [pm.r], [self.xrs.r]))
        self.proj_fm(("w_in_ab", None), 1816, lo, hi,
                     lambda j, pm: self.act(self.ggr.t[:, j, lo:hi], pm.t[:, 0:NS], AF.Gelu_apprx_tanh, [pm.r], [self.ggr.r]))
        self.l0_post(tile, None)
        self.dma("sp", O["win_s"][:, 511, :], z.t[0:NS, 1024:1280], [z.r], [self.ores], own=z.r)
        self.rebuild_qT(tile)
        pt = self.next_ptr()
        self.tr(pt.t[:, 0:NS], z.t[0:NS, 768:896], self.ident.t[0:NS, 0:NS], [z.r, self.ident.r], [pt.r])
        self.tr(pt.t[:, 16:16 + NS], z.t[0:NS, 1024:1152], self.ident.t[0:NS, 0:NS], [z.r, self.ident.r], [pt.r])
        self.cp(self.selfT.t[:, :, 0:NS], pt.t[:, 0:32].rearrange("p (w s) -> p w s", s=16)[:, :, 0:NS], [pt.r], [self.selfT.r])
        self.cp(self.vself.t[0:NS, 0, :, 0:64], sv(z.t[0:NS, 896:1024]), [z.r], [self.vself.r])
        self.cp(self.vself.t[0:NS, 1, :, 0:64], sv(z.t[0:NS, 1152:1280]), [z.r], [self.vself.r])
        self.rglru_samples(tile)
        for s_ in range(NS):
            self.nsa_sample(s_)
        for br in range(3):
            for g in range(2):
                gate = self.gsig.t[0:NS, g * 12:(g + 1) * 12].rearrange("p (r b) -> p r b", b=3)[:, :, br]
                src = self.osamp.t[0:NS, br, g * 256:(g + 1) * 256].rearrange("p (r d) -> p r d", d=64)
                ov = self.onsa.t[0:NS, g * 256:(g + 1) * 256].rearrange("p (r d) -> p r d", d=64)
                gb = gate[:, :, None].broadcast_to([NS, 4, 64])
                if br == 0:
                    self.tt(ov, src, gb, ALU.mult, [self.osamp.r, self.gsig.r], [self.onsa.r])
                else:
                    tv = self.otmp.t[0:NS, :].rearrange("p (r d) -> p r d", d=64)
                    self.tt(tv, src, gb, ALU.mult, [self.osamp.r, self.gsig.r], [self.otmp.r])
                    self.tt(ov, ov, tv, ALU.add, [self.onsa.r, self.otmp.r], [self.onsa.r])
        pt = self.next_ptr()
        for k in range(4):
            self.tr(pt.t[:, k * 16:k * 16 + NS], self.onsa.t[0:NS, k * 128:(k + 1) * 128], self.ident.t[0:NS, 0:NS],
                    [self.onsa.r, self.ident.r], [pt.r])
        self.cp(self.catT.t[:, 0:4, lo:hi], pt.t[:, 0:64].rearrange("p (k q) -> p k q", q=16)[:, :, 0:NS], [pt.r], [self.catT.r])
        for half in range(2):
            slab = self.load_slab(("w_out_ab", None), 0, 8, half * 512, 512)
            pm = self.next_pmm()
            for k in range(8):
                self.mm(pm.t[0:NS, :], self.catT.t[:, k, lo:hi], slab.t[:, k, :], k == 0, k == 7, [self.catT.r, slab.r], [pm.r])
            self.resid_add(tile, half, pm, 0)
        self.norm_tile(tile, 1)
        self.ffn([tile], 0, lambda tl, half, ps: self.resid_add(tl, half, ps, 1))
        if self.dbg:
            self.dma("sp", O["dbg_x1s"][:, :], self.xb[2].t[0:NS, :], [self.xb[2].r], [self.ores], own=self.xb[2].r)

    def rglru_samples(self, tile):
        I, O, nc = self.ins, self.outs, self.nc
        NS = self.NS
        lo, hi = tile[3], tile[3] + NS
        c = self.rgc
        xc, r_, i_, a_, a2, u_, hs = [b for b in self.rg]
        w = lambda b: b.t[:, 0:NS]
        for i3 in range(3):
            self.dma("sp", self.cstm.t[0:NS, :], I["state_conv"][:, i3, :], [self.wres], [self.cstm.r])
            pt = self.next_ptr()
            for j in range(4):
                self.tr(pt.t[:, j * 16:j * 16 + NS], self.cstm.t[0:NS, j * 128:(j + 1) * 128], self.ident.t[0:NS, 0:NS],
                        [self.cstm.r, self.ident.r], [pt.r])
            self.cp(self.csT.t[:, i3, :, 0:NS], pt.t[:, 0:64].rearrange("p (j q) -> p j q", q=16)[:, :, 0:NS], [pt.r], [self.csT.r])
        self.dma("sp", self.cstm.t[0:NS, :], I["state_h"][:, :], [self.wres], [self.cstm.r])
        pt = self.next_ptr()
        for j in range(4):
            self.tr(pt.t[:, j * 16:j * 16 + NS], self.cstm.t[0:NS, j * 128:(j + 1) * 128], self.ident.t[0:NS, 0:NS],
                    [self.cstm.r, self.ident.r], [pt.r])
        self.cp(self.h0T.t[:, :, 0:NS], pt.t[:, 0:64].rearrange("p (j q) -> p j q", q=16)[:, :, 0:NS], [pt.r], [self.h0T.r])
        for j in range(4):
            self.ts(w(xc), self.csT.t[:, 0, j, 0:NS], self.convw.t[:, j, 0:1], c.t[:, 0, j:j + 1], ALU.mult, ALU.add,
                    [self.csT.r, self.convw.r, c.r], [xc.r])
            for k in (1, 2):
                self.stt(w(xc), self.csT.t[:, k, j, 0:NS], self.convw.t[:, j, k:k + 1], w(xc), ALU.mult, ALU.add,
                         [self.csT.r, self.convw.r, xc.r], [xc.r])
            self.stt(w(xc), self.xrs.t[:, j, 0:NS], self.convw.t[:, j, 3:4], w(xc), ALU.mult, ALU.add,
                     [self.xrs.r, self.convw.r, xc.r], [xc.r])
            self.cp(self.xcb.t[:, 0:NS], w(xc), [xc.r], [self.xcb.r])
            pr = self.next_pmm()
            self.mm(pr.t[:, 0:NS], self.wabd.t[:, j, :], self.xcb.t[:, 0:NS], True, True, [self.wabd.r, self.xcb.r], [pr.r])
            pi = self.next_pmm()
            self.mm(pi.t[:, 0:NS], self.wxbd.t[:, j, :], self.xcb.t[:, 0:NS], True, True, [self.wxbd.r, self.xcb.r], [pi.r])
            self.act(w(r_), pr.t[:, 0:NS], AF.Sigmoid, [pr.r, c.r], [r_.r], bias=c.t[:, 1, j:j + 1])
            self.act(w(i_), pi.t[:, 0:NS], AF.Sigmoid, [pi.r, c.r], [i_.r], bias=c.t[:, 2, j:j + 1])
            self.act(w(a_), w(r_), AF.Exp, [r_.r, c.r], [a_.r], scale=c.t[:, 4, j:j + 1])
            self.act(w(a2), w(r_), AF.Exp, [r_.r, c.r], [a2.r], scale=c.t[:, 5, j:j + 1])
            self.ts(w(a2), w(a2), -1.0, 1.0, ALU.mult, ALU.add, [a2.r], [a2.r])
            self.act(w(a2), w(a2), AF.Sqrt, [a2.r], [a2.r])
            self.tt(w(u_), w(a2), w(i_), ALU.mult, [a2.r, i_.r], [u_.r])
            self.tt(w(u_), w(u_), w(xc), ALU.mult, [u_.r, xc.r], [u_.r])
            self.tt(w(hs), w(a_), self.h0T.t[:, j, 0:NS], ALU.mult, [a_.r, self.h0T.r], [hs.r])
            self.tt(self.hsT.t[:, j, 0:NS], w(hs), w(u_), ALU.add, [hs.r, u_.r], [self.hsT.r])
            self.tt(self.catT.t[:, 4 + j, lo:hi], self.hsT.t[:, j, 0:NS], self.ggr.t[:, j, lo:hi], ALU.mult,
                    [self.hsT.r, self.ggr.r], [self.catT.r])
        for src, dst in ((self.hsT, O["h_s"][:, :]), (self.xrs, O["conv_s"][:, 2, :])):
            pt = self.next_ptr()
            for j in range(4):
                self.tr(pt.t[0:NS, j * 128:(j + 1) * 128], src.t[:, j, 0:NS], self.ident.t[:, :], [src.r, self.ident.r], [pt.r])
            self.cp(self.cstm.t[0:NS, :], pt.t[0:NS, :], [pt.r], [self.cstm.r])
            self.dma("sp", dst, self.cstm.t[0:NS, :], [self.cstm.r], [self.ores], own=self.cstm.r)

    def fin_row(self, g, br):
        o = self.pacc[g].t[0:1, 0:260].rearrange("p (r e) -> p r e", e=65)
        rd = self.den
        self.ts(rd.t[0:1, 0:4], o[:, :, 64], 1e-30, None, ALU.max, None, [self.pacc[g].r], [rd.r])
        self.S.op("dve", lambda: self.nc.vector.reciprocal(out=rd.t[0:1, 0:4], in_=rd.t[0:1, 0:4]), [rd.r], [rd.r])
        ov = self.orow.t[0:1, br, g * 256:(g + 1) * 256].rearrange("p (r d) -> p r d", d=64)
        self.tt(ov, o[:, :, 0:64], rd.t[0:1, 0:4, None].broadcast_to([1, 4, 64]), ALU.mult, [self.pacc[g].r, rd.r], [self.orow.r])

    def pv4s(self, g, pT, nk, v_ap, first, last, vres):
        for r in range(4):
            self.mm(self.pacc[g].t[0:1, r * 65:(r + 1) * 65], pT.t[0:nk, r:r + 1], v_ap, first and r == 0, last,
                    [pT.r, vres], [self.pacc[g].r], skip=True)

    def nsa_sample(self, s_):
        I, nc = self.ins, self.nc
        sv = lambda ap: ap.rearrange("p (h d) -> p h d", d=64)
        qc = lambda g: self.qp[g].t[:, :, s_]
        for pgi in range(16):
            pg = self.pg[pgi % 3]
            col = s_ * 16 + pgi
            self.S.dma_fn("pool", lambda: nc.gpsimd.indirect_dma_start(
                out=pg.t[:, :], out_offset=None, in_=I["cache"][:, :],
                in_offset=bass.IndirectOffsetOnAxis(ap=self.idx.t[:, col:col + 1], axis=0)), [self.wres, self.idx.r], [pg.r])
            pt = self.next_ptr()
            self.tr(pt.t[:, 0:128], pg.t[:, 256:384], self.ident.t[:, :], [pg.r, self.ident.r], [pt.r])
            self.tr(pt.t[:, 128:256], pg.t[:, 0:128], self.ident.t[:, :], [pg.r, self.ident.r], [pt.r])
            self.tr(pt.t[:, 256:384], pg.t[:, 128:256], self.ident.t[:, :], [pg.r, self.ident.r], [pt.r])
            self.cp(self.ksTs.t[:, pgi * 128:(pgi + 1) * 128], pt.t[:, 0:128], [pt.r], [self.ksTs.r])
            q4 = pgi % 4
            for g in range(2):
                gs_ = slice(g * 64, (g + 1) * 64)
                self.cp(self.rawTzs[g].t[gs_, :, q4 * 128:(q4 + 1) * 128], pt.t[gs_, 128:384].rearrange("p (c q) -> p c q", q=128),
                        [pt.r], [self.rawTzs[g].r])
            self.act(self.vsas.t[:, pgi, :, 0:64], sv(pg.t[:, 384:512]), AF.Copy, [pg.r], [self.vsas.r])
            if q4 == 3:
                grp = pgi // 4
                for c in range(2):
                    pm = self.next_pmm()
                    for g in range(2):
                        for l in range(32):
                            self.mm(pm.t[0:64, g * 16:(g + 1) * 16], self.w1sb.t[:, c, l, :], self.rawTzs[g].t[:, c, l:512:32],
                                    l == 0, l == 31, [self.w1sb.r, self.rawTzs[g].r], [pm.r])
                    self.act(self.hids.t[0:64, c, :, grp * 16:(grp + 1) * 16], pm.t[0:64, 0:32].rearrange("p (g n) -> p g n", n=16),
                             AF.Gelu_apprx_tanh, [pm.r, self.cmpb.r], [self.hids.r], bias=self.cmpb.t[0:64, c:c + 1])
        for c in range(2):
            pm2 = self.next_pmm()
            for g in range(2):
                self.mm(pm2.t[0:64, g * 64:(g + 1) * 64], self.hids.t[0:64, c, g, :], self.w2sb.t[0:64, c, :], True, True,
                        [self.hids.r, self.w2sb.r], [pm2.r])
            if c == 0:
                self.cp(self.kcrow.t[0:64, :], pm2.t[0:64, 0:128], [pm2.r], [self.kcrow.r])
                self.headnorm(self.kcrow.t[0:64, :], 64, 2, self.kg.t[:, 0, :], sv(self.kcrow.t[0:64, :]), [self.kcrow.r, self.kg.r],
                              [self.kcrow.r])
                pt = self.next_ptr()
                self.tr(pt.t[:, 0:64], self.kcrow.t[0:64, :], self.ident.t[0:64, 0:64], [self.kcrow.r, self.ident.r], [pt.r])
                self.cp(self.kcTs.t[:, :], pt.t[:, 0:64], [pt.r], [self.kcTs.r])
            else:
                self.cp(self.vcs.t[0:64, :, 0:64], sv(pm2.t[0:64, 0:128]), [pm2.r], [self.vcs.r])
        for g in range(2):
            pm = self.next_pmm()
            for r in range(4):
                self.mm(pm.t[0:1, r * 64:(r + 1) * 64], self.qp[g].t[:, r, s_:s_ + 1], self.kcTs.t[:, :], True, True,
                        [self.qp[g].r, self.kcTs.r], [pm.r])
            for r in range(4):
                self.act(self.e32.t[0:1, r * 64:(r + 1) * 64], pm.t[0:1, r * 64:(r + 1) * 64], AF.Exp, [pm.r], [self.e32.r, self.den.r],
                         accum_out=self.den.t[0:1, 4 + r:5 + r])
            self.ts(self.den.t[0:1, 4:8], self.den.t[0:1, 4:8], 1e-30, None, ALU.max, None, [self.den.r], [self.den.r])
            self.S.op("dve", lambda: nc.vector.reciprocal(out=self.den.t[0:1, 4:8], in_=self.den.t[0:1, 4:8]), [self.den.r], [self.den.r])
            for r in range(4):
                if r == 0:
                    self.ts(self.impacc.t[0:1, 0:64], self.e32.t[0:1, 0:64], self.den.t[0:1, 4:5], None, ALU.mult, None,
                            [self.e32.r, self.den.r], [self.impacc.r])
                else:
                    self.stt(self.impacc.t[0:1, 0:64], self.e32.t[0:1, r * 64:(r + 1) * 64], self.den.t[0:1, 4 + r:5 + r],
                             self.impacc.t[0:1, 0:64], ALU.mult, ALU.add, [self.e32.r, self.den.r, self.impacc.r], [self.impacc.r])
            imp = self.imp
            self.S.op("dve", lambda: nc.vector.tensor_reduce(out=imp.t[0:1, 0:32],
                                                             in_=self.impacc.t[0:1, 0:64].rearrange("p (n two) -> p n two", two=2),
                                                             axis=AX.X, op=ALU.add), [self.impacc.r], [imp.r])
            self.memset(imp.t[0:1, 32:33], 1e4, [imp.r], eng="dve")
            self.memset(imp.t[0:1, 0:1], 1e4, [imp.r], eng="dve")
            self.S.op("dve", lambda: nc.vector.max(out=self.m8.t[0:1, 0:8], in_=imp.t[0:1, 0:33]), [imp.r], [self.m8.r])
            self.S.op("dve", lambda: nc.vector.match_replace(out=self.imp2.t[0:1, 0:33], in_to_replace=self.m8.t[0:1, 0:8],
                                                             in_values=imp.t[0:1, 0:33], imm_value=-2.0), [imp.r, self.m8.r], [self.imp2.r])
            self.S.op("dve", lambda: nc.vector.max(out=self.m8.t[0:1, 8:16], in_=self.imp2.t[0:1, 0:33]), [self.imp2.r], [self.m8.r])
            self.ts(self.negsel.t[0:1, 0:33], imp.t[0:1, 0:33], self.m8.t[0:1, 15:16], NEG, ALU.is_lt, ALU.mult,
                    [imp.r, self.m8.r], [self.negsel.r])
            pt = self.next_ptr()
            self.tr(pt.t[0:33, 0:1], self.negsel.t[0:1, 0:33], self.ident.t[0:1, 0:1], [self.negsel.r, self.ident.r], [pt.r])
            self.cp(self.negT4s[g].t[0:33, :], pt.t[0:33, 0:1].broadcast_to([33, 4]), [pt.r], [self.negT4s[g].r])
        for g in range(2):
            pa = self.next_patt()
            self.mm(pa.t[0:64, 0:4], self.kcTs.t[:, :], qc(g), True, True, [self.kcTs.r, self.qp[g].r], [pa.r])
            pT = self.next_pT()
            self.act(pT.t[0:64, 0:4], pa.t[0:64, 0:4], AF.Exp, [pa.r], [pT.r])
            self.pv4s(g, pT, 64, self.vcs.t[0:64, g, :], True, True, self.vcs.r)
            self.fin_row(g, 0)
        def s_qk(it):
            g, kt = it
            pa = self.next_patt()
            if kt < 16:
                self.mm(pa.t[:, 0:4], self.ksTs.t[:, kt * 128:(kt + 1) * 128], qc(g), True, False, [self.ksTs.r, self.qp[g].r], [pa.r])
                self.mm(pa.t[:, 0:4], self.esmall.t[0:33, kt * 128:(kt + 1) * 128], self.negT4s[g].t[0:33, :], False, True,
                        [self.esmall.r, self.negT4s[g].r], [pa.r])
            else:
                self.mm(pa.t[0:16, 0:4], self.selfT.t[:, 0, :], qc(g), True, True, [self.selfT.r, self.qp[g].r], [pa.r])
            return pa

        def s_pv(it, pa):
            g, kt = it
            pT = self.next_pT()
            if kt < 16:
                self.act(pT.t[:, 0:4], pa.t[:, 0:4], AF.Exp, [pa.r], [pT.r])
                self.pv4s(g, pT, 128, self.vsas.t[:, kt, g, :], kt == 0, False, self.vsas.r)
            else:
                self.act(pT.t[0:16, 0:4], pa.t[0:16, 0:4], AF.Exp, [pa.r], [pT.r])
                self.ts(pT.t[0:16, 0:4], pT.t[0:16, 0:4], self.eye16.t[0:16, s_:s_ + 1], None, ALU.mult, None, [pT.r, self.eye16.r], [pT.r])
                self.pv4s(g, pT, 16, self.vself.t[0:16, 0, g, :], False, True, self.vself.r)
                self.fin_row(g, 1)

        for wt in range(4):
            w = self.wtile[wt % 2]
            self.dma("sp", w.t[:, :], I["state_win"][s_, wt * 128:(wt + 1) * 128, :], [self.wres], [w.r])
            pt = self.next_ptr()
            self.tr(pt.t[:, 0:128], w.t[:, 0:128], self.ident.t[:, :], [w.r, self.ident.r], [pt.r])
            self.cp(self.kwTs.t[:, wt * 128:(wt + 1) * 128], pt.t[:, 0:128], [pt.r], [self.kwTs.r])
            self.act(self.vwas.t[:, wt, :, 0:64], sv(w.t[:, 128:256]), AF.Copy, [w.r], [self.vwas.r])

        def w_qk(it):
            g, wt = it
            pa = self.next_patt()
            if wt < 4:
                self.mm(pa.t[:, 0:4], self.kwTs.t[:, wt * 128:(wt + 1) * 128], qc(g), True, True, [self.kwTs.r, self.qp[g].r], [pa.r])
            else:
                self.mm(pa.t[0:16, 0:4], self.selfT.t[:, 1, :], qc(g), True, True, [self.selfT.r, self.qp[g].r], [pa.r])
            return pa

        def w_pv(it, pa):
            g, wt = it
            pT = self.next_pT()
            if wt < 4:
                self.act(pT.t[:, 0:4], pa.t[:, 0:4], AF.Exp, [pa.r], [pT.r])
                self.pv4s(g, pT, 128, self.vwas.t[:, wt, g, :], wt == 0, False, self.vwas.r)
            else:
                self.act(pT.t[0:16, 0:4], pa.t[0:16, 0:4], AF.Exp, [pa.r], [pT.r])
                self.ts(pT.t[0:16, 0:4], pT.t[0:16, 0:4], self.eye16.t[0:16, s_:s_ + 1], None, ALU.mult, None, [pT.r, self.eye16.r], [pT.r])
                self.pv4s(g, pT, 16, self.vself.t[0:16, 1, g, :], False, True, self.vself.r)
                self.fin_row(g, 2)

        self.pipelined([(g, kt) for g in range(2) for kt in range(17)], s_qk, s_pv)
        self.pipelined([(g, wt) for g in range(2) for wt in range(5)], w_qk, w_pv)
        self.dma("sp", self.osamp.t[s_:s_ + 1, :, :], self.orow.t[0:1, :, :], [self.orow.r], [self.osamp.r])

    def finish_l0_outputs(self):
        O = self.outs
        if self.dbg:
            self.dma("pool", O["dbg_kcT"][:, :], self.kcT.t[:, :], [self.kcT.r], [self.ores], own=self.kcT.r)
            self.dma("pool", O["dbg_vc"][:, :], self.vc_aug.t[:, 0, :, :].rearrange("p g e -> p (g e)"), [self.vc_aug.r], [self.ores], own=self.vc_aug.r)
        self.dma("pool", O["h_p"].rearrange("(c p) -> p c", p=128), self.hstate.t[:, :], [self.hstate.r], [self.ores],
                 own=self.hstate.r, allow_slow_non_contiguous=True)
        for i_ in range(3):
            self.dma("pool", O["conv_p"][i_].rearrange("(c p) -> p c", p=128), self.xrbuf.t[:, :, i_], [self.xrbuf.r], [self.ores],
                     own=self.xrbuf.r, allow_slow_non_contiguous=True)


class ProgL1(ProgL0):
    def alloc_l1(self):
        sb = self.sb
        self.NR = min(self.NT, 18)
        self.kT2 = sb("kT2", [128, 4, self.NR * 128], BF16)
        self.v2 = sb("v2", [128, self.NR, 8, 65], BF16)
        self.wsT = sb("wsT", [128, 8, 128], BF16); self.bsT = sb("bsT", [128, 8])
        self.vgain = sb("vgain", [128, 512]); self.dqg = sb("dqg", [128, 64]); self.dkg = sb("dkg", [128, 64])
        self.dilm = sb("dilm", [128, 17 * 128], BF16)
        self.tril = sb("tril", [128, 128])
        self.sq = sb("sq1", [128, 512]); self.den = sb("den1", [128, 8])
        self.qb = sb("qb1", [128, 512], BF16); self.qp2 = [sb(f"qp2_{g}", [128, 4, 128], BF16) for g in range(2)]
        self.vnb = sb("vnb", [128, 512], BF16)
        self.oc = sb("oc", [128, 512]); self.od = sb("od", [128, 512])
        self.pT = [sb(f"pT1_{i}", [128, 512], BF16) for i in range(2)]
        self.zb = [sb(f"zc{i}", [128, 2560]) for i in range(self.MT)] + [sb("zcs", [128, 2560])]
        if self.do_samples:
            self.wsb = sb("wsb", [128, 8]); self.bsb = sb("bsb", [128, 8])
            self.dtile = [sb(f"dtile{i}", [128, 1024]) for i in range(2)]
            self.kTs = sb("kTs", [128, 4, 128], BF16); self.vas = sb("vas", [128, 8, 65], BF16)
            self.kselfT = sb("kselfT", [128, 4, 16], BF16); self.vself2 = sb("vself2", [128, 8, 65], BF16)
            self.odrow = sb("odrow", [128, 512]); self.ods = sb("ods", [128, 512])

    def setup_l1(self):
        I = self.ins
        slow = dict(allow_slow_non_contiguous=True)
        self.ld(self.vgain, I["gmlp_v_gain"][0:1, :].broadcast_to([128, 512]))
        self.ld(self.dqg, I["dil_q_gain"][0:1, :].broadcast_to([128, 64]))
        self.ts(self.dqg.t[:], self.dqg.t[:], 0.125, None, ALU.mult, None, [self.dqg.r], [self.dqg.r])
        self.ld(self.dkg, I["dil_k_gain"][0:1, :].broadcast_to([128, 64]))
        self.ld(self.dilm, I["c_dilmult"][:, :])
        self.ld(self.tril, I["c_tril"][:, :])
        self.dma("pool", self.bsT.t[:, :], I["gmlp_bs"].rearrange("g t -> t g"), [self.wres], [self.bsT.r], **slow)
        for g in range(8):
            w = self.oc
            self.dma("pool", w.t[:, 0:128], I["gmlp_ws"][g], [self.wres], [w.r])
            self.tt(w.t[:, 0:128], w.t[:, 0:128], self.tril.t[:, :], ALU.mult, [w.r, self.tril.r], [w.r])
            pt = self.next_ptr()
            self.tr(pt.t[:, 0:128], w.t[:, 0:128], self.ident.t[:, :], [w.r, self.ident.r], [pt.r])
            self.cp(self.wsT.t[:, g, :], pt.t[:, 0:128], [pt.r], [self.wsT.r])
        self.memset(self.v2.t[:], 1.0, [self.v2.r])
        if self.do_samples:
            self.dma("pool", self.wsb.t[:, :], I["gmlp_ws"][:, 0:1, 0:1].rearrange("g a b -> (a b) g").broadcast_to([128, 8]),
                     [self.wres], [self.wsb.r], **slow)
            self.dma("pool", self.bsb.t[:, :], I["gmlp_bs"][:, 0:1].rearrange("g a -> a g").broadcast_to([128, 8]),
                     [self.wres], [self.bsb.r], **slow)
            self.memset(self.vas.t[:], 1.0, [self.vas.r]); self.memset(self.vself2.t[:], 1.0, [self.vself2.r])
        self.memset(self.kT2.t[:], 0.0, [self.kT2.r])
        for g in range(2):
            self.memset(self.qp2[g].t[:], 0.0, [self.qp2[g].r])

    def l1_post(self, tile, t):
        kind, i, np_, h0 = tile
        z = self.zb[i]
        O = self.outs
        st = self.den
        sv = lambda ap: ap.rearrange("p (h d) -> p h d", d=64)
        self.act(z.t[0:np_, 0:1024], z.t[0:np_, 0:1024], AF.Gelu_apprx_tanh, [z.r], [z.r])
        self.act(self.junk.t[0:np_, 0:512], z.t[0:np_, 512:1024], AF.Square, [z.r], [self.junk.r, st.r], accum_out=st.t[0:np_, 0:1])
        self.ts(st.t[0:np_, 0:1], st.t[0:np_, 0:1], 1.0 / 512, EPS, ALU.mult, ALU.add, [st.r], [st.r])
        self.act(st.t[0:np_, 0:1], st.t[0:np_, 0:1], AF.Sqrt, [st.r], [st.r])
        self.S.op("dve", lambda: self.nc.vector.reciprocal(out=st.t[0:np_, 0:1], in_=st.t[0:np_, 0:1]), [st.r], [st.r])
        self.stt(z.t[0:np_, 512:1024], z.t[0:np_, 512:1024], st.t[0:np_, 0:1], self.vgain.t[0:np_, :], ALU.mult, ALU.mult,
                 [z.r, st.r, self.vgain.r], [z.r])
        self.headnorm(z.t[0:np_, 1024:1536], np_, 8, self.dqg.t, sv(self.qb.t[0:np_, :]), [z.r, self.dqg.r], [self.qb.r])
        self.headnorm(z.t[0:np_, 1536:2048], np_, 8, self.dkg.t, sv(z.t[0:np_, 1536:2048]), [z.r, self.dkg.r], [z.r])
        pt = self.next_ptr()
        ptb = pt.t[:].bitcast(BF16)
        for j in range(4):
            self.tr(ptb[:, j * 128:j * 128 + np_], self.qb.t[0:np_, j * 128:(j + 1) * 128], self.identb.t[0:np_, 0:np_],
                    [self.qb.r, self.identb.r], [pt.r])
        for g in range(2):
            gs_ = slice(g * 64, (g + 1) * 64)
            self.cp(self.qp2[g].t[gs_, :, 0:np_], ptb[gs_, 0:512].rearrange("p (r q) -> p r q", q=128)[:, :, 0:np_], [pt.r], [self.qp2[g].r])
        if kind == "p":
            self.cp(self.vnb.t[:, :], z.t[0:128, 512:1024], [z.r], [self.vnb.r])
            base = self.T - min(2048, self.T)
            if t * 128 >= base:
                self.dma("pool", O["dil_p"][t * 128 - base:(t + 1) * 128 - base, :], z.t[0:128, 1536:2560], [z.r], [self.ores], own=z.r)
            slot = t % self.NR
            pt = self.next_ptr()
            for j in range(4):
                self.tr(pt.t[:, j * 128:(j + 1) * 128], z.t[0:128, 1536 + j * 128:1536 + (j + 1) * 128], self.ident.t[:, :],
                        [z.r, self.ident.r], [pt.r])
            self.cp(self.kT2.t[:, :, slot * 128:(slot + 1) * 128], pt.t[:, :].rearrange("p (j q) -> p j q", q=128), [pt.r], [self.kT2.r])
            self.cp(self.v2.t[:, slot, :, 0:64], sv(z.t[0:128, 2048:2560]), [z.r], [self.v2.r], eng="pool")
        else:
            self.dma("pool", O["gv_s"][:, :], z.t[0:np_, 512:1024], [z.r], [self.ores], own=z.r)

    def gmlp_prompt_tile(self, tile):
        kind, i, np_, h0 = tile
        z = self.zb[i]
        pm = self.next_pmm()
        for g in range(8):
            self.mm(pm.t[:, g * 64:(g + 1) * 64], self.wsT.t[:, g, :], self.vnb.t[:, g * 64:(g + 1) * 64], True, True,
                    [self.wsT.r, self.vnb.r], [pm.r])
        for g in range(8):
            cs = slice(g * 64, (g + 1) * 64)
            self.stt(self.oc.t[:, cs], pm.t[:, cs], self.bsT.t[:, g:g + 1], z.t[0:128, cs], ALU.add, ALU.mult,
                     [pm.r, self.bsT.r, z.r], [self.oc.r])
        pt = self.next_ptr()
        for k in range(4):
            self.tr(pt.t[:, k * 128:(k + 1) * 128], self.oc.t[:, k * 128:(k + 1) * 128], self.ident.t[:, :], [self.oc.r, self.ident.r], [pt.r])
        self.cp(self.catT.t[:, 0:4, h0:h0 + 128], pt.t[:, :].rearrange("p (k q) -> p k q", q=128), [pt.r], [self.catT.r])

    def dil_prompt_tile(self, tile, t):
        kind, i, np_, h0 = tile
        nc = self.nc
        dls = list(range(min(16, t), -1, -1))

        def d_qk(it):
            hg, idx, dl = it
            kt = t - dl
            slot = kt % self.NR
            pa = self.next_patt()
            for jj in range(4):
                h = 4 * hg + jj
                par, j = h % 2, h // 2
                self.mm(pa.t[:, jj * 128:(jj + 1) * 128], self.kT2.t[:, j, slot * 128:(slot + 1) * 128],
                        self.qp2[par].t[:, j, :], True, True, [self.kT2.r, self.qp2[par].r], [pa.r])
            return pa

        def d_pv(it, pa):
            hg, idx, dl = it
            kt = t - dl
            slot = kt % self.NR
            pT = self.next_pT()
            self.act(pT.t[:, :], pa.t[:, :], AF.Exp, [pa.r], [pT.r])
            pv = pT.t[:, :].rearrange("p (r q) -> p r q", q=128)
            self.tt(pv, pv, self.dilm.t[:, None, dl * 128:(dl + 1) * 128].broadcast_to([128, 4, 128]), ALU.mult,
                    [pT.r, self.dilm.r], [pT.r])
            for jj in range(4):
                h = 4 * hg + jj
                self.mm(self.pacc[hg].t[:, jj * 65:(jj + 1) * 65], pT.t[:, jj * 128:(jj + 1) * 128], self.v2.t[:, slot, h, :],
                        idx == 0 and jj == 0, idx == len(dls) - 1, [pT.r, self.v2.r], [self.pacc[hg].r], skip=True)
            if idx == len(dls) - 1:
                o = self.pacc[hg].t[:, 0:260].rearrange("p (r e) -> p r e", e=65)
                rd = self.den
                self.ts(rd.t[:, 0:4], o[:, :, 64], 1e-30, None, ALU.max, None, [self.pacc[hg].r], [rd.r])
                self.S.op("dve", lambda: nc.vector.reciprocal(out=rd.t[:, 0:4], in_=rd.t[:, 0:4]), [rd.r], [rd.r])
                ov = self.od.t[:, hg * 256:(hg + 1) * 256].rearrange("p (r d) -> p r d", d=64)
                self.tt(ov, o[:, :, 0:64], rd.t[:, 0:4, None].broadcast_to([128, 4, 64]), ALU.mult, [self.pacc[hg].r, rd.r], [self.od.r])

        self.pipelined([(hg, idx, dl) for hg in range(2) for idx, dl in enumerate(dls)], d_qk, d_pv)
        pt = self.next_ptr()
        for k in range(4):
            self.tr(pt.t[:, k * 128:(k + 1) * 128], self.od.t[:, k * 128:(k + 1) * 128], self.ident.t[:, :], [self.od.r, self.ident.r], [pt.r])
        self.cp(self.catT.t[:, 4:8, h0:h0 + 128], pt.t[:, :].rearrange("p (k q) -> p k q", q=128), [pt.r], [self.catT.r])

    def l1_samples(self, tile):
        I, O, nc = self.ins, self.outs, self.nc
        kind, i, NS, lo = tile
        hi = lo + NS
        z = self.zb[i]
        sv = lambda ap: ap.rearrange("p (h d) -> p h d", d=64)
        for g in range(8):
            cs = slice(g * 64, (g + 1) * 64)
            self.ts(self.oc.t[0:NS, cs], z.t[0:NS, 512 + g * 64:512 + (g + 1) * 64], self.wsb.t[0:NS, g:g + 1], self.bsb.t[0:NS, g:g + 1],
                    ALU.mult, ALU.add, [z.r, self.wsb.r, self.bsb.r], [self.oc.r])
            self.tt(self.oc.t[0:NS, cs], self.oc.t[0:NS, cs], z.t[0:NS, cs], ALU.mult, [self.oc.r, z.r], [self.oc.r])
        pt = self.next_ptr()
        for k in range(4):
            self.tr(pt.t[:, k * 16:k * 16 + NS], self.oc.t[0:NS, k * 128:(k + 1) * 128], self.ident.t[0:NS, 0:NS], [self.oc.r, self.ident.r], [pt.r])
        self.cp(self.catT.t[:, 0:4, lo:hi], pt.t[:, 0:64].rearrange("p (k q) -> p k q", q=16)[:, :, 0:NS], [pt.r], [self.catT.r])
        self.dma("sp", O["dil_s"][:, 2047, :], z.t[0:NS, 1536:2560], [z.r], [self.ores], own=z.r)
        pt = self.next_ptr()
        for j in range(4):
            self.tr(pt.t[:, j * 16:j * 16 + NS], z.t[0:NS, 1536 + j * 128:1536 + (j + 1) * 128], self.ident.t[0:NS, 0:NS], [z.r, self.ident.r], [pt.r])
        self.cp(self.kselfT.t[:, :, 0:NS], pt.t[:, 0:64].rearrange("p (j q) -> p j q", q=16)[:, :, 0:NS], [pt.r], [self.kselfT.r])
        self.cp(self.vself2.t[0:NS, :, 0:64], sv(z.t[0:NS, 2048:2560]), [z.r], [self.vself2.r])
        for s_ in range(NS):
            for pi_, (d, start) in enumerate(((1, 1920), (4, 1536), (16, 0))):
                dt_ = self.dtile[pi_ % 2]
                self.dma("sp", dt_.t[:, :], I["state_dil"][s_, start:2048:d, :], [self.wres], [dt_.r])
                pt = self.next_ptr()
                for j in range(4):
                    self.tr(pt.t[:, j * 128:(j + 1) * 128], dt_.t[:, j * 128:(j + 1) * 128], self.ident.t[:, :], [dt_.r, self.ident.r], [pt.r])
                self.cp(self.kTs.t[:, :, :], pt.t[:, :].rearrange("p (j q) -> p j q", q=128), [pt.r], [self.kTs.r])
                self.act(self.vas.t[:, :, 0:64], sv(dt_.t[:, 512:1024]), AF.Copy, [dt_.r], [self.vas.r])
                pa = self.next_patt()
                for h in range(8):
                    par, j = h % 2, h // 2
                    self.mm(pa.t[:, h:h + 1], self.kTs.t[:, j, :], self.qp2[par].t[:, j, s_:s_ + 1], True, True,
                            [self.kTs.r, self.qp2[par].r], [pa.r])
                pT = self.next_pT()
                self.act(pT.t[:, 0:8], pa.t[:, 0:8], AF.Exp, [pa.r], [pT.r])
                for h in range(8):
                    self.mm(self.pacc[h // 4].t[0:1, (h % 4) * 65:(h % 4 + 1) * 65], pT.t[:, h:h + 1], self.vas.t[:, h, :],
                            pi_ == 0 and h % 4 == 0, False, [pT.r, self.vas.r], [self.pacc[h // 4].r], skip=True)
            pa = self.next_patt()
            for h in range(8):
                par, j = h % 2, h // 2
                self.mm(pa.t[0:16, h:h + 1], self.kselfT.t[:, j, :], self.qp2[par].t[:, j, s_:s_ + 1], True, True,
                        [self.kselfT.r, self.qp2[par].r], [pa.r])
            pT = self.next_pT()
            self.act(pT.t[0:16, 0:8], pa.t[0:16, 0:8], AF.Exp, [pa.r], [pT.r])
            self.ts(pT.t[0:16, 0:8], pT.t[0:16, 0:8], self.eye16.t[0:16, s_:s_ + 1], None, ALU.mult, None, [pT.r, self.eye16.r], [pT.r])
            self.ts(pT.t[0:16, 0:8], pT.t[0:16, 0:8], 3.0, None, ALU.mult, None, [pT.r], [pT.r])
            for h in range(8):
                self.mm(self.pacc[h // 4].t[0:1, (h % 4) * 65:(h % 4 + 1) * 65], pT.t[0:16, h:h + 1], self.vself2.t[0:16, h, :],
                        False, True, [pT.r, self.vself2.r], [self.pacc[h // 4].r], skip=True)
            for hg in range(2):
                o = self.pacc[hg].t[0:1, 0:260].rearrange("p (r e) -> p r e", e=65)
                rd = self.den
                self.ts(rd.t[0:1, 0:4], o[:, :, 64], 1e-30, None, ALU.max, None, [self.pacc[hg].r], [rd.r])
                self.S.op("dve", lambda: nc.vector.reciprocal(out=rd.t[0:1, 0:4], in_=rd.t[0:1, 0:4]), [rd.r], [rd.r])
                ov = self.odrow.t[0:1, hg * 256:(hg + 1) * 256].rearrange("p (r d) -> p r d", d=64)
                self.tt(ov, o[:, :, 0:64], rd.t[0:1, 0:4, None].broadcast_to([1, 4, 64]), ALU.mult, [self.pacc[hg].r, rd.r], [self.odrow.r])
            self.dma("sp", self.ods.t[s_:s_ + 1, :], self.odrow.t[0:1, :], [self.odrow.r], [self.ods.r])
        pt = self.next_ptr()
        for k in range(4):
            self.tr(pt.t[:, k * 16:k * 16 + NS], self.ods.t[0:NS, k * 128:(k + 1) * 128], self.ident.t[0:NS, 0:NS], [self.ods.r, self.ident.r], [pt.r])
        self.cp(self.catT.t[:, 4:8, lo:hi], pt.t[:, 0:64].rearrange("p (k q) -> p k q", q=16)[:, :, 0:NS], [pt.r], [self.catT.r])

    def pass2_macro(self, m):
        I, O = self.ins, self.outs
        ws = self.do_samples and m == 0
        tiles = self.tiles_of(m, ws)
        for (kind, i, np_, h0) in tiles:
            if kind == "p":
                t = m * self.MT + i
                self.dma("sp", self.xb[i].t[:, :], self.x1d[t * 128:(t + 1) * 128, :], [self.x1res[t]], [self.xb[i].r])
        for tile in tiles:
            self.norm_tile(tile, 0)
        for s in range(5):
            self.proj_tm(tiles, ("w_in_cd", None), s * 512, 512, s * 512)
        for tile in tiles:
            if tile[0] == "p":
                t = m * self.MT + tile[1]
                self.l1_post(tile, t)
                self.gmlp_prompt_tile(tile)
                self.dil_prompt_tile(tile, t)
            else:
                self.l1_post(tile, None)
                self.l1_samples(tile)
        for half in range(2):
            slab = self.load_slab(("w_out_cd", None), 0, 8, half * 512, 512)
            for tile in tiles:
                kind, i, np_, h0 = tile
                pm = self.next_pmm()
                for k in range(8):
                    self.mm(pm.t[0:np_, :], self.catT.t[:, k, h0:h0 + np_], slab.t[:, k, :], k == 0, k == 7, [self.catT.r, slab.r], [pm.r])
                self.resid_add(tile, half, pm, 0)
        for tile in tiles:
            self.norm_tile(tile, 1)
        self.ffn(tiles, 1, lambda tile, half, ps: self.resid_add(tile, half, ps, 1))
        for (kind, i, np_, h0) in tiles:
            if kind == "p":
                t = m * self.MT + i
                self.dma("pool", O["y_p"][t * 128:(t + 1) * 128, :], self.xb[i].t[:, :], [self.xb[i].r], [self.ores], own=self.xb[i].r)
            else:
                self.dma("pool", O["y_s"][:, :], self.xb[i].t[0:np_, :], [self.xb[i].r], [self.ores], own=self.xb[i].r)

    def build(self):
        self.declare()
        self.alloc()
        self.setup_consts()
        self.setup_mod_inputs()
        common = self.st
        with ExitStack() as st1:
            self.st = st1
            self.alloc_l0()
            self.precast(["w_in_ab", "w_out_ab", "w_ffn_gate", "w_ffn_up", "w_ffn_down"])
            self.compute_mod(0)
            if self.do_l1:
                self.precast(["w_in_cd", "w_out_cd"])
            self.setup_l0()
            if self.do_samples:
                self.state_copies()
                with ExitStack() as sts:
                    self.st = sts
                    self.alloc_l0_samp()
                    self.l0_samples_pass()
                    self.S.barrier()
                self.st = st1
            with ExitStack() as stp:
                self.st = stp
                self.alloc_l0_prompt()
                for m in range(self.NM):
                    self.pass1_macro(m)
                self.finish_l0_outputs()
                self.S.barrier()
            self.st = st1
        if self.do_l1:
            with ExitStack() as st2:
                self.st = st2
                self.alloc_l1()
                self.compute_mod(1)
                self.setup_l1()
                for m in range(self.NM):
                    self.pass2_macro(m)
                self.S.barrier()
        self.st = common
        self.S.finish("sp")
        self.st.close()
        return self.nc


def core_inputs(inp, c, T, NS, b, s0):
    f = lambda a: np.ascontiguousarray(a, dtype=np.float32)
    cm = np.zeros((33, D), np.float32)
    cm[0:NS] = inp["c_sample"][s0:s0 + NS]
    cm[32] = inp["c_prompt"][b]
    m = {
        "xp": f(inp["x_prompt"][b]), "xs": f(inp["x_sample"][s0:s0 + NS, 0]), "cmat": cm,
        "norm_mix_g": f(inp["norm_mix_g"]), "norm_ffn_g": f(inp["norm_ffn_g"]), "w_ada": f(inp["w_ada"]), "b_ada": f(inp["b_ada"]),
        "w_ffn_gate": f(inp["w_ffn_gate"]), "w_ffn_up": f(inp["w_ffn_up"]), "w_ffn_down": f(inp["w_ffn_down"]),
        "w_in_ab": f(inp["w_in_ab"][0]), "w_out_ab": f(inp["w_out_ab"][0]),
        "nsa_q_gain": f(inp["nsa_q_gain"]), "nsa_k_gain": f(inp["nsa_k_gain"][0]),
        "nsa_cmp_w1": f(inp["nsa_cmp_w1"][0]), "nsa_cmp_w2": f(inp["nsa_cmp_w2"][0]), "nsa_cmp_pos": f(inp["nsa_cmp_pos"][0]),
        "rg_conv_w": f(inp["rg_conv_w"][0]), "rg_conv_b": f(inp["rg_conv_b"]), "rg_wa": f(inp["rg_wa"][0]), "rg_ba": f(inp["rg_ba"]),
        "rg_wx": f(inp["rg_wx"][0]), "rg_bx": f(inp["rg_bx"]), "rg_lambda": f(inp["rg_lambda"]),
        "w_in_cd": f(inp["w_in_cd"][0]), "w_out_cd": f(inp["w_out_cd"][0]),
        "gmlp_v_gain": f(inp["gmlp_v_gain"]), "gmlp_ws": f(inp["gmlp_ws"][0]), "gmlp_bs": f(inp["gmlp_bs"][0]),
        "dil_q_gain": f(inp["dil_q_gain"]), "dil_k_gain": f(inp["dil_k_gain"]),
        "cache": f(inp["cache_nsa_kv"][0]).reshape(-1, 512),
        "state_win": f(inp["state_nsa_win"][0, s0:s0 + NS]).reshape(NS, 512, 256),
        "state_h": f(inp["state_rglru_h"][0, s0:s0 + NS]), "state_conv": f(inp["state_rglru_conv"][0, s0:s0 + NS]),
        "state_dil": f(inp["state_dil_kv"][0, s0:s0 + NS]).reshape(NS, 2048, 1024),
        "page_table": np.ascontiguousarray(inp["page_table"][s0:s0 + NS], dtype=np.int32),
    }
    m.update(make_consts(T))
    return m


T_FULL = 8192
NS_CORE = 16
N_CORES = 8


def kernel(**inputs):
    inp = {k: np.asarray(v) for k, v in inputs.items()}
    T, NS = T_FULL, NS_CORE
    prog = ProgL1(T, NS, npool_rows=inp["cache_nsa_kv"].shape[1] * 128, dbg=False)
    nc = prog.build()
    in_maps = [core_inputs(inp, c, T, NS, b=c % 2, s0=NS * c) for c in range(N_CORES)]
    res = run_bass_kernel_spmd(nc, in_maps, core_ids=list(range(N_CORES))).results
    cat = lambda name: np.concatenate([np.asarray(res[c][name]) for c in range(N_CORES)], axis=0)
    two = lambda name: np.stack([np.asarray(res[0][name]), np.asarray(res[1][name])], axis=0)
    f32 = lambda a: np.ascontiguousarray(a, dtype=np.float32)
    out = (
        f32(two("y_p").reshape(2, T, D)),
        f32(cat("y_s").reshape(128, 1, D)),
        f32(two("kv_p").reshape(1, 2, T, 4, 2, 64)),
        f32(cat("kv_s").reshape(1, 128, 1, 4, 2, 64)),
        f32(two("win_p").reshape(1, 2, 512, 2, 2, 64)),
        f32(cat("win_s").reshape(1, 128, 512, 2, 2, 64)),
        f32(two("h_p").reshape(1, 2, 512)),
        f32(cat("h_s").reshape(1, 128, 512)),
        f32(two("conv_p").reshape(1, 2, 3, 512)),
        f32(cat("conv_s").reshape(1, 128, 3, 512)),
        f32(two("dil_p").reshape(1, 2, 2048, 2, 8, 64)),
        f32(cat("dil_s").reshape(1, 128, 2048, 2, 8, 64)),
        f32(cat("gv_s").reshape(1, 128, 1, 512)),
    )
    return out
```

```python
from contextlib import ExitStack

import numpy as np
import concourse.bass as bass
import concourse.mybir as mybir
from concourse.bass_utils import run_bass_kernel_spmd

F32 = mybir.dt.float32
BF16 = mybir.dt.bfloat16
I32 = mybir.dt.int32
AF = mybir.ActivationFunctionType
ALU = mybir.AluOpType
AX = mybir.AxisListType

NEG = -30000.0
D = 1024
HD = 64
FFN = 2816
IN_AB = 2328
IN_CD = 2560
EPS = 1e-6


class Res:
    __slots__ = ("name", "w", "rs", "dsem", "dcnt")

    def __init__(self, name):
        self.name = name
        self.w = None
        self.rs = {}
        self.dsem = None
        self.dcnt = 0


class Sched:
    def __init__(self, nc, stack):
        self.nc = nc
        self.stack = stack
        self.eng = {"pe": nc.tensor, "act": nc.scalar, "dve": nc.vector, "pool": nc.gpsimd, "sp": nc.sync}
        self.sem = {}
        self.cnt = {}
        self.waited = {k: {} for k in self.eng}
        for k in self.eng:
            self.sem[k] = stack.enter_context(nc.semaphore("s_" + k))
            self.cnt[k] = 0
        self.semobj = {k: self.sem[k] for k in self.eng}
        self.nres = 0
        self.all_dma = []
        self.free_sems = []

    def res(self, name=None):
        self.nres += 1
        return Res(name or f"r{self.nres}")

    def _dsem(self, r):
        if r.dsem is None:
            key = f"d{len(self.all_dma)}_{r.name}"
            h = self.stack.enter_context(self.nc.semaphore())
            r.dsem = key
            self.semobj[key] = h
            self.all_dma.append(r)
        return r.dsem

    def _wait(self, e, dep):
        if dep is None:
            return
        key, val = dep
        if key == "pe" and e == "pe":
            return
        if self.waited[e].get(key, 0) >= val:
            return
        self.eng[e].wait_ge(self.semobj[key], val)
        self.waited[e][key] = val

    def _deps(self, e, reads, writes):
        for r in reads:
            self._wait(e, r.w)
        for r in writes:
            self._wait(e, r.w)
            for d in list(r.rs.items()):
                self._wait(e, d)

    def _mark(self, reads, writes, tok):
        for r in reads:
            if r.rs.get(tok[0], 0) < tok[1]:
                r.rs[tok[0]] = tok[1]
        for r in writes:
            r.w = tok
            r.rs = {}

    def op(self, e, fn, reads=(), writes=()):
        self._deps(e, reads, writes)
        ins = fn()
        self.cnt[e] += 1
        ins.then_inc(self.sem[e], 1)
        self._mark(reads, writes, (e, self.cnt[e]))
        return ins

    def dma(self, q, out, in_, reads, writes, own=None, **kw):
        if own is None:
            own = writes[0]
        self._deps(q, reads, writes)
        key = self._dsem(own)
        ins = self.eng[q].dma_start(out=out, in_=in_, **kw)
        own.dcnt += 16
        ins.then_inc(self.semobj[key], 16)
        self._mark(reads, writes, (key, own.dcnt))
        return ins

    def dma_fn(self, q, fn, reads, writes, own=None):
        if own is None:
            own = writes[0]
        self._deps(q, reads, writes)
        key = self._dsem(own)
        ins = fn()
        own.dcnt += 16
        ins.then_inc(self.semobj[key], 16)
        self._mark(reads, writes, (key, own.dcnt))
        return ins

    def barrier(self):
        for e in self.eng:
            for r in self.all_dma:
                self._wait(e, (r.dsem, r.dcnt))
            for k in self.eng:
                if k != e and self.cnt[k] > 0:
                    self._wait(e, (k, self.cnt[k]))

    def finish(self, e="sp"):
        for r in self.all_dma:
            self._wait(e, (r.dsem, r.dcnt))
        for k in self.eng:
            if k != e and self.cnt[k] > 0:
                self._wait(e, (k, self.cnt[k]))


class Buf:
    def __init__(self, t, r):
        self.t = t
        self.r = r

    def __getitem__(self, k):
        return self.t[k]


def make_consts(T):
    p = np.arange(128)
    c = {}
    c["c_ident"] = np.eye(128, dtype=np.float32)
    key, q = p[:, None], p[None, :]
    causal = np.where(key <= q, 0.0, NEG).astype(np.float32)
    anti = np.where(key >= q, 0.0, NEG).astype(np.float32)
    c["c_causal4"] = np.tile(causal, (1, 4))
    c["c_anti4"] = np.tile(anti, (1, 4))
    es = np.zeros((128, 32, 128), np.float32)
    for pi in range(32):
        for kk in range(128):
            es[(np.arange(128) % 64) == 2 * pi + kk // 64, pi, kk] = 1.0
    c["c_esmall"] = es.reshape(128, 32 * 128)
    k4 = np.arange(4)[:, None]
    qq = np.tile(p, 4)[None, :]
    c["c_cmpR"] = np.where(qq < 32 * k4 + 31, NEG, 0.0).astype(np.float32)
    z = np.zeros((4, 252), np.float32)
    for k in range(4):
        z[k, 124 + k] = 1.0
    c["c_cmpZ"] = z
    c["c_cmpmask"] = np.where(p[:, None] >= 32 * np.arange(4)[None, :] + 31, 0.0, NEG).astype(np.float32)
    f = np.zeros((128, 2), np.float32)
    f[:, 0] = np.where(p < 64, 1e4, -1.0)
    f[:, 1] = np.where(p >= 64, 1e4, -1.0)
    c["c_f12"] = f
    dm = np.zeros((128, 17, 128), np.float32)
    for dl in range(17):
        dist = 128 * dl + q - key
        m = ((dist >= 0) & (dist <= 128)).astype(np.float32)
        m += ((dist >= 0) & (dist <= 512) & (dist % 4 == 0))
        m += ((dist >= 0) & (dist <= 2048) & (dist % 16 == 0))
        dm[:, dl, :] = m
    c["c_dilmult"] = dm.reshape(128, 17 * 128)
    e16 = np.zeros((128, 16), np.float32)
    e16[np.arange(16), np.arange(16)] = 1.0
    c["c_eye16"] = e16
    t, s = p[:, None], p[None, :]
    c["c_tril"] = (s <= t).astype(np.float32)
    return c


CONST_SHAPES = lambda T: {k: v.shape for k, v in make_consts(T).items()}


class Prog:
    def __init__(self, T, NS=16, npool_rows=2560 * 128, dbg=False, do_l1=True, do_samples=True):
        self.T, self.NS = T, NS
        self.NT = T // 128
        self.MT = 2
        self.NM = self.NT // self.MT
        self.W = self.MT * 128 + 16
        self.dbg = dbg
        self.do_l1 = do_l1
        self.do_samples = do_samples
        self.npool_rows = npool_rows
        self.nc = bass.Bass("TRN2", target_bir_lowering=False)
        self.st = ExitStack()
        self.S = Sched(self.nc, self.st)
        self.ins = {}
        self.outs = {}
        self.wb = {}
        self.side_jobs = []

    def din(self, name, shape, dt=F32):
        self.ins[name] = self.nc.dram_tensor(name, list(shape), dt, kind="ExternalInput").ap()
        return self.ins[name]

    def dout(self, name, shape, dt=F32):
        self.outs[name] = self.nc.dram_tensor(name, list(shape), dt, kind="ExternalOutput").ap()
        return self.outs[name]

    def sb(self, name, shape, dt=F32):
        t = self.st.enter_context(self.nc.sbuf_tensor(name, list(shape), dt))
        return Buf(t, self.S.res(name))

    def psum(self, name):
        t = self.st.enter_context(self.nc.psum_tensor(name, [128, 512], F32))
        return Buf(t, self.S.res(name))

    def mm(self, out, lhsT, rhs, start, stop, R, W, skip=False):
        nc = self.nc
        if skip:
            return self.S.op("pe", lambda: nc.tensor.matmul(out, lhsT=lhsT, rhs=rhs, start=start, stop=stop,
                                                            skip_group_check=True), R, W)
        return self.S.op("pe", lambda: nc.tensor.matmul(out, lhsT=lhsT, rhs=rhs, start=start, stop=stop), R, W)

    def tr(self, out, in_, ident, R, W):
        nc = self.nc
        return self.S.op("pe", lambda: nc.tensor.transpose(out, in_, ident), R, W)

    def act(self, out, in_, func, R, W, **kw):
        nc = self.nc
        return self.S.op("act", lambda: nc.scalar.activation(out=out, in_=in_, func=func, **kw), R, W)

    def ts(self, out, in0, s1, s2, op0, op1, R, W, eng="dve"):
        e = self.nc.vector if eng == "dve" else self.nc.gpsimd
        if op1 is None:
            return self.S.op(eng, lambda: e.tensor_scalar(out=out, in0=in0, scalar1=s1, scalar2=None, op0=op0), R, W)
        return self.S.op(eng, lambda: e.tensor_scalar(out=out, in0=in0, scalar1=s1, scalar2=s2, op0=op0, op1=op1), R, W)

    def tt(self, out, in0, in1, op, R, W, eng="dve"):
        e = self.nc.vector if eng == "dve" else self.nc.gpsimd
        return self.S.op(eng, lambda: e.tensor_tensor(out=out, in0=in0, in1=in1, op=op), R, W)

    def stt(self, out, in0, scalar, in1, op0, op1, R, W):
        nc = self.nc
        return self.S.op("dve", lambda: nc.vector.scalar_tensor_tensor(out=out, in0=in0, scalar=scalar, in1=in1,
                                                                       op0=op0, op1=op1), R, W)

    def powp(self, out, in_, expo, R, W):
        nc = self.nc
        return self.S.op("pool", lambda: nc.gpsimd.tensor_tensor(out=out, in0=in_, in1=expo, op=ALU.pow), R, W)

    def cp(self, out, in_, R, W, eng="dve"):
        e = self.nc.vector if eng == "dve" else self.nc.gpsimd
        return self.S.op(eng, lambda: e.tensor_copy(out=out, in_=in_), R, W)

    def memset(self, ap, val, W, eng="pool"):
        e = self.nc.vector if eng == "dve" else self.nc.gpsimd
        return self.S.op(eng, lambda: e.memset(ap, val), [], W)

    def dma(self, q, out, in_, R, W, own=None, **kw):
        return self.S.dma(q, out, in_, R, W, own=own, **kw)

    def declare(self):
        T, NS = self.T, self.NS
        di = self.din
        di("xp", [T, D]); di("xs", [NS, D]); di("cmat", [33, D])
        di("norm_mix_g", [2, D]); di("norm_ffn_g", [2, D])
        di("w_ada", [2, D, 6 * D]); di("b_ada", [2, 6 * D])
        di("w_ffn_gate", [2, D, FFN]); di("w_ffn_up", [2, D, FFN]); di("w_ffn_down", [2, FFN, D])
        di("w_in_ab", [D, IN_AB]); di("w_out_ab", [D, D])
        di("nsa_q_gain", [1, 64]); di("nsa_k_gain", [3, 64])
        di("nsa_cmp_w1", [2, 32, 64, 64]); di("nsa_cmp_w2", [2, 64, 64]); di("nsa_cmp_pos", [2, 32, 64])
        di("rg_conv_w", [4, 512]); di("rg_conv_b", [1, 512]); di("rg_wa", [8, 64, 64]); di("rg_ba", [1, 512])
        di("rg_wx", [8, 64, 64]); di("rg_bx", [1, 512]); di("rg_lambda", [1, 512])
        di("w_in_cd", [D, IN_CD]); di("w_out_cd", [D, D])
        di("gmlp_v_gain", [1, 512]); di("gmlp_ws", [8, 128, 128]); di("gmlp_bs", [8, 128])
        di("dil_q_gain", [1, 64]); di("dil_k_gain", [1, 64])
        di("cache", [self.npool_rows, 512]); di("state_win", [NS, 512, 256]); di("state_h", [NS, 512])
        di("state_conv", [NS, 3, 512]); di("state_dil", [NS, 2048, 1024]); di("page_table", [NS, 16], I32)
        for k, shp in CONST_SHAPES(T).items():
            di(k, shp)
        do = self.dout
        do("y_p", [T, D]); do("y_s", [NS, D]); do("kv_p", [T, 512]); do("kv_s", [NS, 512])
        do("win_p", [min(512, T), 256]); do("win_s", [NS, 512, 256]); do("h_p", [512]); do("h_s", [NS, 512])
        do("conv_p", [3, 512]); do("conv_s", [NS, 3, 512]); do("dil_p", [min(2048, T), 1024])
        do("dil_s", [NS, 2048, 1024]); do("gv_s", [NS, 512])
        if self.dbg:
            do("dbg_x1", [T, D]); do("dbg_x1s", [NS, D]); do("dbg_xmid", [T, D]); do("dbg_onsa", [T, 512]); do("dbg_ornn", [T, 512]); do("dbg_kcT", [128, max(T // 32, 8)]); do("dbg_vc", [128, 130]); do("dbg_hid", [64, 16]); do("dbg_w1", [128, 4096]); do("dbg_raw", [128, 512])
        self.x1d = self.nc.dram_tensor("x1_scratch", [T, D], F32).ap()

    def alloc(self):
        T, W = self.T, self.W
        sb = self.sb
        self.wres = self.S.res("dram_in")
        self.ores = self.S.res("dram_out")
        self.pmm = [self.psum(f"pmm{i}") for i in range(2)]
        self.ptr = [self.psum(f"ptr{i}") for i in range(2)]
        self.patt = [self.psum(f"patt{i}") for i in range(2)]
        self.pacc = [self.psum(f"pacc{i}") for i in range(2)]
        self.pmm_i = self.ptr_i = self.patt_i = 0
        self.ident = sb("ident", [128, 128]); self.identb = sb("identb", [128, 128], BF16)
        self.ones = sb("ones", [128, 128])
        self.causal4 = sb("causal4", [128, 512], BF16); self.anti4 = sb("anti4", [128, 512], BF16)
        self.esmall = sb("esmall", [128, 32 * 128], BF16)
        self.cmpR = sb("cmpR", [4, 512], BF16); self.cmpZ = sb("cmpZ", [4, 252], BF16)
        self.cmpmask = sb("cmpmask", [128, 4]); self.f12 = sb("f12", [128, 2]); self.eye16 = sb("eye16", [128, 16])
        self.expn = sb("expn", [128, 8])
        self.NSLOT = 3
        self.ring = [sb(f"slab{i}", [128, 8, 512], BF16) for i in range(self.NSLOT)]
        self.ring_i = 0
        self.xb = [sb(f"xb{i}", [128, D]) for i in range(self.MT)] + [sb("xbs", [128, D])]
        self.xn = sb("xn", [128, D])
        self.junk = sb("junk", [128, D], BF16)
        self.stat = sb("stat", [128, 8])
        self.hT = sb("hT", [128, 8, W], BF16)
        self.tmp16 = sb("tmp16", [128, 16])
        self.catT = sb("catT", [128, 8, W], BF16)
        self.catT2r = self.S.res("catT_hi")
        self.hidT = sb("hidT", [128, 22, W], BF16)
        self.ftmp = sb("ftmp", [128, W])
        self.cT = sb("cT", [128, 8, 33], BF16)
        self.modT = sb("modT", [128, 48, 33])
        self.gs = [sb(f"gs{i}", [128, 8, 33]) for i in range(2)]
        self.gtm = sb("gtm", [128, 2, D])
        self.gbc = sb("gbc", [128, 2, D])
        self.badaT = sb("badaT", [128, 2, 48])
        self.gnT = sb("gnT", [128, 2, 2, 8])

    def next_pmm(self):
        b = self.pmm[self.pmm_i % 2]; self.pmm_i += 1; return b

    def next_ptr(self):
        b = self.ptr[self.ptr_i % 2]; self.ptr_i += 1; return b

    def next_patt(self):
        b = self.patt[self.patt_i % 2]; self.patt_i += 1; return b

    def load_slab(self, src, k0, nk, c0, ncols):
        slot = self.ring[self.ring_i % self.NSLOT]
        self.ring_i += 1
        if isinstance(src, tuple):
            ap, rr = self.wb[src[0]]
            if src[1] is not None:
                ap = ap[src[1]]
            q = "sp"
        else:
            ap, rr, q = src, self.wres, "pool"
        self.dma(q, slot.t[:, 0:nk, 0:ncols],
                 ap[k0 * 128:(k0 + nk) * 128, c0:c0 + ncols].rearrange("(k p) n -> p k n", p=128),
                 [rr], [slot.r])
        return slot

    def precast(self, names):
        for nm in names:
            src = self.ins[nm]
            shp = list(src.shape)
            dst = self.nc.dram_tensor("wb_" + nm, shp, BF16).ap()
            rr = self.S.res("wb_" + nm)
            self.wb[nm] = (dst, rr)
            s2 = src if len(shp) == 2 else src.rearrange("l r n -> (l r) n")
            d2 = dst if len(shp) == 2 else dst.rearrange("l r n -> (l r) n")
            rows = s2.shape[0]
            for r0 in range(0, rows, 128):
                r1 = min(rows, r0 + 128)
                self.dma("pool", d2[r0:r1, :], s2[r0:r1, :], [self.wres], [rr])

    def ld(self, buf, src, q="pool", **kw):
        return self.dma(q, buf.t[:] if not isinstance(buf, tuple) else buf[0], src, [self.wres],
                        [buf.r if not isinstance(buf, tuple) else buf[1]], **kw)

    def setup_consts(self):
        I = self.ins
        self.ld(self.ident, I["c_ident"][:, :])
        self.ld(self.identb, I["c_ident"][:, :])
        self.memset(self.ones.t[:], 1.0, [self.ones.r])
        self.memset(self.expn.t[:], -0.5, [self.expn.r])
        self.ld(self.causal4, I["c_causal4"][:, :]); self.ld(self.anti4, I["c_anti4"][:, :])
        self.ld(self.esmall, I["c_esmall"][:, :])
        self.ld(self.cmpR, I["c_cmpR"][:, :]); self.ld(self.cmpZ, I["c_cmpZ"][:, :])
        self.ld(self.cmpmask, I["c_cmpmask"][:, :]); self.ld(self.f12, I["c_f12"][:, :]); self.ld(self.eye16, I["c_eye16"][:, :])

    def setup_mod_inputs(self):
        I = self.ins
        ctm = self.xn
        self.dma("pool", ctm.t[0:33, :], I["cmat"][:, :], [self.wres], [ctm.r])
        self.act(ctm.t[0:33, :], ctm.t[0:33, :], AF.Silu, [ctm.r], [ctm.r])
        for k in range(8):
            pt = self.next_ptr()
            self.tr(pt.t[:, 0:33], ctm.t[0:33, k * 128:(k + 1) * 128], self.ident.t[0:33, 0:33], [ctm.r, self.ident.r], [pt.r])
            self.cp(self.cT.t[:, k, :], pt.t[:, 0:33], [pt.r], [self.cT.r])
        for l in range(2):
            self.dma("pool", self.badaT.t[:, l, :], I["b_ada"][l].rearrange("(e p) -> p e", p=128), [self.wres], [self.badaT.r],
                     allow_slow_non_contiguous=True)
            self.dma("pool", self.gnT.t[:, 0, l, :], I["norm_mix_g"][l].rearrange("(k p) -> p k", p=128), [self.wres],
                     [self.gnT.r], allow_slow_non_contiguous=True)
            self.dma("pool", self.gnT.t[:, 1, l, :], I["norm_ffn_g"][l].rearrange("(k p) -> p k", p=128), [self.wres],
                     [self.gnT.r], allow_slow_non_contiguous=True)

    def compute_mod(self, l):
        I = self.ins
        for s in range(12):
            slab = self.load_slab(I["w_ada"][l], 0, 8, s * 512, 512)
            for j in range(4):
                e = 4 * s + j
                pm = self.next_pmm()
                for k in range(8):
                    self.mm(pm.t[:, 0:33], slab.t[:, k, j * 128:(j + 1) * 128], self.cT.t[:, k, :], k == 0, k == 7,
                            [slab.r, self.cT.r], [pm.r])
                self.ts(self.modT.t[:, e, :], pm.t[:, 0:33], self.badaT.t[:, l, e:e + 1], None, ALU.add, None,
                        [pm.r, self.badaT.r], [self.modT.r])
        for w, base in ((0, 8), (1, 32)):
            for k in range(8):
                gcol = self.gnT.t[:, w, l, k:k + 1]
                self.ts(self.gs[w].t[:, k, :], self.modT.t[:, base + k, :], gcol, gcol, ALU.mult, ALU.add,
                        [self.modT.r, self.gnT.r], [self.gs[w].r])
        for w, base in ((0, 16), (1, 40)):
            for half in range(2):
                pt = self.next_ptr()
                for j in range(4):
                    k = half * 4 + j
                    self.tr(pt.t[0:33, j * 128:(j + 1) * 128], self.modT.t[:, base + k, :], self.ident.t[:, :],
                            [self.modT.r, self.ident.r], [pt.r])
                self.cp(self.gtm.t[0:33, w, half * 512:(half + 1) * 512], pt.t[0:33, :], [pt.r], [self.gtm.r])
            for half in range(2):
                pm = self.next_pmm()
                self.mm(pm.t[:, :], self.ones.t[32:33, :], self.gtm.t[32:33, w, half * 512:(half + 1) * 512], True, True,
                        [self.ones.r, self.gtm.r], [pm.r])
                self.cp(self.gbc.t[:, w, half * 512:(half + 1) * 512], pm.t[:, :], [pm.r], [self.gbc.r])

    def tiles_of(self, m, with_samples):
        tl = [("p", i, 128, i * 128) for i in range(self.MT)]
        if with_samples:
            tl.append(("s", self.MT, self.NS, self.MT * 128))
        return tl

    def norm_tile(self, tile, w):
        kind, i, np_, c0 = tile
        x = self.xb[i]
        st = self.stat
        shbase = 0 if w == 0 else 24
        self.act(self.junk.t[0:np_, :], x.t[0:np_, :], AF.Square, [x.r], [self.junk.r, st.r], accum_out=st.t[0:np_, 0:1])
        self.ts(st.t[0:np_, 0:1], st.t[0:np_, 0:1], 1.0 / D, EPS, ALU.mult, ALU.add, [st.r], [st.r])
        self.powp(st.t[0:np_, 0:1], st.t[0:np_, 0:1], self.expn.t[0:np_, 0:1], [st.r, self.expn.r], [st.r])
        self.ts(self.xn.t[0:np_, :], x.t[0:np_, :], st.t[0:np_, 0:1], None, ALU.mult, None, [x.r, st.r], [self.xn.r])
        for half in range(2):
            pt = self.next_ptr()
            for j in range(4):
                k = half * 4 + j
                self.tr(pt.t[:, j * 128:j * 128 + np_], self.xn.t[0:np_, k * 128:(k + 1) * 128], self.ident.t[0:np_, 0:np_],
                        [self.xn.r, self.ident.r], [pt.r])
            for j in range(4):
                k = half * 4 + j
                src = pt.t[:, j * 128:j * 128 + np_]
                dst = self.hT.t[:, k, c0:c0 + np_]
                if kind == "p":
                    self.ts(dst, src, self.gs[w].t[:, k, 32:33], self.modT.t[:, shbase + k, 32:33], ALU.mult, ALU.add,
                            [pt.r, self.gs[w].r, self.modT.r], [self.hT.r])
                else:
                    self.tt(self.tmp16.t[:, 0:np_], src, self.gs[w].t[:, k, 0:np_], ALU.mult, [pt.r, self.gs[w].r], [self.tmp16.r])
                    self.tt(dst, self.tmp16.t[:, 0:np_], self.modT.t[:, shbase + k, 0:np_], ALU.add,
                            [self.tmp16.r, self.modT.r], [self.hT.r])

    def proj_tm(self, tiles, src, c0, ncols, zcol, post=None):
        slab = self.load_slab(src, 0, 8, c0, ncols)
        for tile in tiles:
            kind, i, np_, h0 = tile
            pm = self.next_pmm()
            for k in range(8):
                self.mm(pm.t[0:np_, 0:ncols], self.hT.t[:, k, h0:h0 + np_], slab.t[:, k, 0:ncols], k == 0, k == 7,
                        [self.hT.r, slab.r], [pm.r])
            if post is None:
                self.act(self.zb[i].t[0:np_, zcol:zcol + ncols], pm.t[0:np_, 0:ncols], AF.Copy, [pm.r], [self.zb[i].r])
            else:
                post(tile, pm)

    def ffn(self, tiles, l, out_fn):
        I = self.ins
        lo = min(t[3] for t in tiles)
        Wt = max(t[3] + t[2] for t in tiles)
        hc = 0
        for s in range(6):
            ncols = 512 if s < 5 else FFN - 5 * 512
            sg = self.load_slab(("w_ffn_gate", l), 0, 8, s * 512, ncols)
            su = self.load_slab(("w_ffn_up", l), 0, 8, s * 512, ncols)
            for j in range(ncols // 128):
                fb = [self.pmm[0], self.pmm[1], self.ptr[0], self.ptr[1]]
                pg = fb[(2 * hc) % 4]
                for k in range(8):
                    self.mm(pg.t[:, lo:Wt], sg.t[:, k, j * 128:(j + 1) * 128], self.hT.t[:, k, lo:Wt], k == 0, k == 7,
                            [sg.r, self.hT.r], [pg.r])
                pu = fb[(2 * hc + 1) % 4]
                for k in range(8):
                    self.mm(pu.t[:, lo:Wt], su.t[:, k, j * 128:(j + 1) * 128], self.hT.t[:, k, lo:Wt], k == 0, k == 7,
                            [su.r, self.hT.r], [pu.r])
                self.act(self.ftmp.t[:, lo:Wt], pg.t[:, lo:Wt], AF.Silu, [pg.r], [self.ftmp.r])
                self.tt(self.hidT.t[:, hc, lo:Wt], self.ftmp.t[:, lo:Wt], pu.t[:, lo:Wt], ALU.mult, [self.ftmp.r, pu.r], [self.hidT.r])
                hc += 1
        accs = [self.pacc[0], self.pacc[1], self.patt[0]]
        for half in range(2):
            for ks, (k0, nk) in enumerate(((0, 8), (8, 8), (16, 6))):
                sd = self.load_slab(("w_ffn_down", l), k0, nk, half * 512, 512)
                for ti, tile in enumerate(tiles):
                    kind, i, np_, h0 = tile
                    for k in range(nk):
                        self.mm(accs[ti].t[0:np_, :], self.hidT.t[:, k0 + k, h0:h0 + np_], sd.t[:, k, :], (k0 + k) == 0,
                                (k0 + k) == 21, [self.hidT.r, sd.r], [accs[ti].r])
            for ti, tile in enumerate(tiles):
                out_fn(tile, half, accs[ti])

    def resid_add(self, tile, half, ps, w):
        kind, i, np_, h0 = tile
        x = self.xb[i]
        cs = slice(half * 512, (half + 1) * 512)
        g = self.gbc.t[:, w, cs] if kind == "p" else self.gtm.t[0:np_, w, cs]
        gr = self.gbc.r if kind == "p" else self.gtm.r
        self.tt(self.xn.t[0:np_, cs], ps.t[0:np_, :], g[0:np_] if kind == "p" else g, ALU.mult, [ps.r, gr], [self.xn.r])
        self.tt(x.t[0:np_, cs], x.t[0:np_, cs], self.xn.t[0:np_, cs], ALU.add, [x.r, self.xn.r], [x.r])


class ProgL0(Prog):
    def alloc_l0(self):
        T, NT, sb = self.T, self.NT, self.sb
        self.qg = sb("qg", [128, 64]); self.kg = sb("kg", [128, 3, 64])
        self.w1sb = sb("w1sb", [128, 2, 32, 64], BF16)
        self.posT = sb("posT", [128, 2, 32], BF16)
        self.cmpb = sb("cmpb", [128, 2])
        self.w2sb = sb("w2sb", [128, 2, 64], BF16)
        self.wabd = sb("wabd", [128, 4, 128], BF16); self.wxbd = sb("wxbd", [128, 4, 128], BF16)
        self.convw = sb("convw", [128, 4, 4]); self.rgc = sb("rgc", [128, 6, 4])
        self.xrs = sb("xrs", [128, 4, 16])
        self.ggr = sb("ggr", [128, 4, self.W])
        self.sq = sb("sq", [128, 512])
        self.qb = sb("qb", [128, 512], BF16)
        self.qp = [sb(f"qp{g}", [128, 4, 128], BF16) for g in range(2)]
        self.hid = sb("hid", [128, 2, 2, 8], BF16)
        self.kcrow = sb("kcrow", [128, 128]); self.vcrow = sb("vcrow", [128, 2, 65], BF16)
        self.gsig = sb("gsig", [128, 24])
        self.e32 = sb("e32", [128, 256]); self.impacc = sb("impacc", [128, 256])
        self.imp = sb("imp", [128, 128]); self.imp2 = sb("imp2", [128, 128]); self.negsel = sb("negsel", [128, 128])
        self.m8 = sb("m8", [128, 16]); self.den = sb("den", [128, 8])
        self.negT4 = [sb(f"negT4_{g}", [128, 4, 128], BF16) for g in range(2)]
        self.pT = [sb(f"pT{i}", [128, 512], BF16) for i in range(2)]
        self.pT_i = 0
        self.onsa = sb("onsa", [128, 512]); self.otmp = sb("otmp", [128, 256])
        self.zb = [sb(f"zb{i}", [128, 1304]) for i in range(self.MT)] + [sb("zbs", [128, 1304])]
        self.x1res = [self.S.res(f"x1_{t}") for t in range(self.NT)]
        self.rg = [sb(f"rg{i}", [128, 256]) for i in range(7)]
        self.xcb = sb("xcb", [128, 256], BF16)

    def next_pT(self):
        b = self.pT[self.pT_i % 2]; self.pT_i += 1; return b

    def alloc_l0_prompt(self):
        T, NT, sb = self.T, self.NT, self.sb
        self.ksT = sb("ksT", [128, T], BF16)
        self.vs_aug = sb("vs_aug", [128, NT, 2, 65], BF16)
        self.nbt_max = max(1, (T // 32 + 127) // 128)
        self.kcT = sb("kcT", [128, max(T // 32, 8)], BF16)
        self.vc_aug = sb("vc_aug", [128, self.nbt_max, 2, 65], BF16)
        self.kwT = sb("kwT", [128, 6 * 128], BF16)
        self.vw_aug = sb("vw_aug", [128, 6, 2, 65], BF16)
        self.xrbuf = sb("xrbuf", [128, 4, 3 + 256]); self.hstate = sb("hstate", [128, 4])
        self.rawTz = [sb(f"rawTz{g}", [128, 2, 256], BF16) for g in range(2)]
        self.memset(self.xrbuf.t[:], 0.0, [self.xrbuf.r]); self.memset(self.hstate.t[:], 0.0, [self.hstate.r])
        for b_ in (self.vs_aug, self.vc_aug, self.vw_aug):
            self.memset(b_.t[:], 1.0, [b_.r])
        self.memset(self.kwT.t[:], 0.0, [self.kwT.r])
        for g in range(2):
            self.memset(self.rawTz[g].t[:], 0.0, [self.rawTz[g].r])

    def setup_l0(self):
        I, nc = self.ins, self.nc
        slow = dict(allow_slow_non_contiguous=True)
        self.ld(self.qg, I["nsa_q_gain"][0:1, :].broadcast_to([128, 64]))
        self.ts(self.qg.t[:], self.qg.t[:], 0.125, None, ALU.mult, None, [self.qg.r], [self.qg.r])
        for j in range(3):
            self.dma("pool", self.kg.t[:, j, :], I["nsa_k_gain"][j:j + 1, :].broadcast_to([128, 64]), [self.wres], [self.kg.r])
        for c in range(2):
            for hlf in range(2):
                self.dma("pool", self.w1sb.t[hlf * 64:(hlf + 1) * 64, c, :, :], I["nsa_cmp_w1"][c].rearrange("l d e -> d l e"),
                         [self.wres], [self.w1sb.r])
            self.dma("pool", self.w2sb.t[0:64, c, :], I["nsa_cmp_w2"][c], [self.wres], [self.w2sb.r])
        for c in range(2):
            self.dma("pool", self.posT.t[0:64, c, :], I["nsa_cmp_pos"][c].rearrange("l d -> d l"), [self.wres], [self.posT.r], **slow)
        for c in range(2):
            pm = self.next_pmm()
            for l in range(32):
                self.mm(pm.t[0:64, 0:1], self.w1sb.t[0:64, c, l, :], self.posT.t[0:64, c, l:l + 1], l == 0, l == 31,
                        [self.w1sb.r, self.posT.r], [pm.r])
            self.cp(self.cmpb.t[0:64, c:c + 1], pm.t[0:64, 0:1], [pm.r], [self.cmpb.r])
        for wsb, nm in ((self.wabd, "rg_wa"), (self.wxbd, "rg_wx")):
            self.memset(wsb.t[:], 0.0, [wsb.r])
            for j in range(4):
                self.dma("pool", wsb.t[0:64, j, 0:64], I[nm][2 * j], [self.wres], [wsb.r])
                self.dma("pool", wsb.t[64:128, j, 64:128], I[nm][2 * j + 1], [self.wres], [wsb.r])
        for i_ in range(4):
            self.dma("pool", self.convw.t[:, :, i_], I["rg_conv_w"][i_].rearrange("(c p) -> p c", p=128), [self.wres], [self.convw.r], **slow)
        for j, nm in enumerate(("rg_conv_b", "rg_ba", "rg_bx", "rg_lambda")):
            self.dma("pool", self.rgc.t[:, j, :], I[nm].rearrange("o (c p) -> p (o c)", p=128), [self.wres], [self.rgc.r], **slow)
        r = self.rgc
        self.act(r.t[:, 4, :], r.t[:, 3, :], AF.Exp, [r.r], [r.r], scale=-1.0)
        self.act(r.t[:, 4, :], r.t[:, 4, :], AF.Ln, [r.r], [r.r], bias=1.0)
        self.ts(r.t[:, 4, :], r.t[:, 4, :], -8.0, None, ALU.mult, None, [r.r], [r.r])
        self.ts(r.t[:, 5, :], r.t[:, 4, :], 2.0, None, ALU.mult, None, [r.r], [r.r])
        self.memset(self.vcrow.t[:], 1.0, [self.vcrow.r])
        for g in range(2):
            self.memset(self.qp[g].t[:], 0.0, [self.qp[g].r])

    def headnorm(self, src_ap, np_, nh, gain_ap, out_ap, R, W, shape4=None):
        sq, st = self.sq, self.den
        sv = lambda ap: ap.rearrange("p (h d) -> p h d", d=64)
        self.tt(sq.t[0:np_, 0:nh * 64], src_ap, src_ap, ALU.mult, R, [sq.r])
        self.S.op("dve", lambda: self.nc.vector.tensor_reduce(out=st.t[0:np_, 0:nh], in_=sv(sq.t[0:np_, 0:nh * 64]), axis=AX.X,
                                                              op=ALU.add), [sq.r], [st.r])
        self.ts(st.t[0:np_, 0:nh], st.t[0:np_, 0:nh], 1.0 / 64, EPS, ALU.mult, ALU.add, [st.r], [st.r])
        self.powp(st.t[0:np_, 0:nh], st.t[0:np_, 0:nh], self.expn.t[0:np_, 0:nh], [st.r, self.expn.r], [st.r])
        self.tt(sv(sq.t[0:np_, 0:nh * 64]), sv(src_ap), st.t[0:np_, 0:nh, None].broadcast_to([np_, nh, 64]), ALU.mult,
                R + [st.r], [sq.r])
        if shape4 is None:
            self.tt(out_ap, sv(sq.t[0:np_, 0:nh * 64]), gain_ap[0:np_, None, :].broadcast_to([np_, nh, 64]), ALU.mult,
                    [sq.r] + R, W)
        else:
            a, b = shape4
            in0 = sq.t[0:np_, 0:nh * 64].rearrange("p (a b d) -> p a b d", a=a, b=b, d=64)
            in1 = gain_ap[0:np_, None, None, :].broadcast_to([np_, a, b, 64])
            self.tt(out_ap, in0, in1, ALU.mult, [sq.r] + R, W)

    def l0_post(self, tile, t):
        kind, i, np_, h0 = tile
        z = self.zb[i]
        O = self.outs
        sv = lambda ap: ap.rearrange("p (h d) -> p h d", d=64)
        self.headnorm(z.t[0:np_, 768:896], np_, 2, self.kg.t[:, 1, :], sv(z.t[0:np_, 768:896]), [z.r, self.kg.r], [z.r])
        self.headnorm(z.t[0:np_, 1024:1152], np_, 2, self.kg.t[:, 2, :], sv(z.t[0:np_, 1024:1152]), [z.r, self.kg.r], [z.r])
        if kind == "p":
            self.dma("pool", O["kv_p"][t * 128:(t + 1) * 128, :], z.t[0:128, 512:1024], [z.r], [self.ores], own=z.r)
            if t >= self.NT - min(4, self.NT):
                w0 = (t - (self.NT - min(4, self.NT))) * 128
                self.dma("pool", O["win_p"][w0:w0 + 128, :], z.t[0:128, 1024:1280], [z.r], [self.ores], own=z.r)
            pt = self.next_ptr()
            self.tr(pt.t[:, 0:128], z.t[0:128, 768:896], self.ident.t[:, :], [z.r, self.ident.r], [pt.r])
            self.tr(pt.t[:, 128:256], z.t[0:128, 1024:1152], self.ident.t[:, :], [z.r, self.ident.r], [pt.r])
            self.tr(pt.t[:, 256:384], z.t[0:128, 512:640], self.ident.t[:, :], [z.r, self.ident.r], [pt.r])
            self.tr(pt.t[:, 384:512], z.t[0:128, 640:768], self.ident.t[:, :], [z.r, self.ident.r], [pt.r])
            slot = t % 6
            self.cp(self.ksT.t[:, t * 128:(t + 1) * 128], pt.t[:, 0:128], [pt.r], [self.ksT.r])
            self.cp(self.kwT.t[:, slot * 128:(slot + 1) * 128], pt.t[:, 128:256], [pt.r], [self.kwT.r])
            for g in range(2):
                gs_ = slice(g * 64, (g + 1) * 64)
                self.cp(self.rawTz[g].t[gs_, :, i * 128:(i + 1) * 128], pt.t[gs_, 256:512].rearrange("p (c q) -> p c q", q=128), [pt.r],
                        [self.rawTz[g].r])
            self.cp(self.vs_aug.t[:, t, :, 0:64], sv(z.t[0:128, 896:1024]), [z.r], [self.vs_aug.r], eng="pool")
            self.cp(self.vw_aug.t[:, slot, :, 0:64], sv(z.t[0:128, 1152:1280]), [z.r], [self.vw_aug.r], eng="pool")
        else:
            self.dma("pool", O["kv_s"][:, :], z.t[0:np_, 512:1024], [z.r], [self.ores], own=z.r)

    def compress_macro(self, m):
        nb0 = 8 * m
        for c in range(2):
            pm = self.next_pmm()
            for g in range(2):
                for l in range(32):
                    self.mm(pm.t[0:64, g * 8:(g + 1) * 8], self.w1sb.t[:, c, l, :],
                            self.rawTz[g].t[:, c, l:256:32], l == 0, l == 31, [self.w1sb.r, self.rawTz[g].r], [pm.r])
            if self.dbg and c == 0 and m == self.NM - 1:
                self.cp(self.kcrow.t[0:64, 0:16], pm.t[0:64, 0:16], [pm.r], [self.kcrow.r])
                self.dma("pool", self.outs["dbg_hid"][:, :], self.kcrow.t[0:64, 0:16], [self.kcrow.r], [self.ores], own=self.kcrow.r)
                self.dma("pool", self.outs["dbg_w1"][:, :], self.w1sb.t[:, :, :, :].rearrange("p c l e -> p (c l e)"), [self.w1sb.r], [self.ores], own=self.w1sb.r)
            self.act(self.hid.t[0:64, c, :, :], pm.t[0:64, 0:16].rearrange("p (g n) -> p g n", n=8), AF.Gelu_apprx_tanh,
                     [pm.r, self.cmpb.r], [self.hid.r], bias=self.cmpb.t[0:64, c:c + 1])
            pm2 = self.next_pmm()
            for g in range(2):
                self.mm(pm2.t[0:8, g * 64:(g + 1) * 64], self.hid.t[0:64, c, g, :], self.w2sb.t[0:64, c, :], True, True,
                        [self.hid.r, self.w2sb.r], [pm2.r])
            if c == 0:
                self.cp(self.kcrow.t[0:8, :], pm2.t[0:8, 0:128], [pm2.r], [self.kcrow.r])
                sv = lambda ap: ap.rearrange("p (h d) -> p h d", d=64)
                self.headnorm(self.kcrow.t[0:8, :], 8, 2, self.kg.t[:, 0, :], sv(self.kcrow.t[0:8, :]), [self.kcrow.r, self.kg.r],
                              [self.kcrow.r])
                pt = self.next_ptr()
                self.tr(pt.t[:, 0:8], self.kcrow.t[0:8, :], self.ident.t[0:8, 0:8], [self.kcrow.r, self.ident.r], [pt.r])
                self.cp(self.kcT.t[:, nb0:nb0 + 8], pt.t[:, 0:8], [pt.r], [self.kcT.r])
            else:
                self.cp(self.vcrow.t[0:8, :, 0:64], pm2.t[0:8, 0:128].rearrange("p (g d) -> p g d", d=64), [pm2.r], [self.vcrow.r])
                p0, bt = nb0 % 128, nb0 // 128
                self.dma("pool", self.vc_aug.t[p0:p0 + 8, bt, :, :], self.vcrow.t[0:8, :, :], [self.vcrow.r], [self.vc_aug.r])

    def combine(self, g, br, first):
        o = self.pacc[g].t[:, 0:260].rearrange("p (r e) -> p r e", e=65)
        rd = self.den
        self.ts(rd.t[:, 0:4], o[:, :, 64], 1e-30, None, ALU.max, None, [self.pacc[g].r], [rd.r])
        self.S.op("dve", lambda: self.nc.vector.reciprocal(out=rd.t[:, 0:4], in_=rd.t[:, 0:4]), [rd.r], [rd.r])
        gate = self.gsig.t[:, g * 12:(g + 1) * 12].rearrange("p (r b) -> p r b", b=3)[:, :, br]
        self.tt(rd.t[:, 0:4], rd.t[:, 0:4], gate, ALU.mult, [rd.r, self.gsig.r], [rd.r])
        ov = self.onsa.t[:, g * 256:(g + 1) * 256].rearrange("p (r d) -> p r d", d=64)
        sc = rd.t[:, 0:4, None].broadcast_to([128, 4, 64])
        if first:
            self.tt(ov, o[:, :, 0:64], sc, ALU.mult, [self.pacc[g].r, rd.r], [self.onsa.r])
        else:
            tv = self.otmp.t[:, :].rearrange("p (r d) -> p r d", d=64)
            self.tt(tv, o[:, :, 0:64], sc, ALU.mult, [self.pacc[g].r, rd.r], [self.otmp.r])
            self.tt(ov, ov, tv, ALU.add, [self.onsa.r, self.otmp.r], [self.onsa.r])

    def pipelined(self, items, qk, pv):
        if not items:
            return
        cur = qk(items[0])
        for n, it in enumerate(items):
            nxt = qk(items[n + 1]) if n + 1 < len(items) else None
            pv(it, cur)
            cur = nxt
            if self.side_jobs and n % 2 == 1:
                self.side_jobs.pop(0)()

    def pv4(self, g, pT, nk, v_ap, first, last, vres):
        for r in range(4):
            self.mm(self.pacc[g].t[:, r * 65:(r + 1) * 65], pT.t[0:nk, r * 128:(r + 1) * 128], v_ap, first and r == 0, last,
                    [pT.r, vres], [self.pacc[g].r], skip=True)

    def nsa_prompt_tile(self, t, i):
        nc = self.nc
        nblk, nsb = 4 * t + 4, 2 * t + 2
        topk = nsb > 16
        qf = lambda g: self.qp[g].t[:, :, :].rearrange("p r q -> p (r q)")
        for g in range(2 if topk else 0):
            for r in range(4):
                pm = self.next_pmm()
                self.mm(pm.t[:, 0:nblk], self.qp[g].t[:, r, :], self.kcT.t[:, 0:nblk], True, True,
                        [self.qp[g].r, self.kcT.r], [pm.r])
                self.tt(pm.t[:, nblk - 4:nblk], pm.t[:, nblk - 4:nblk], self.cmpmask.t[:, :], ALU.add, [pm.r, self.cmpmask.r], [pm.r])
                self.act(self.e32.t[:, 0:nblk], pm.t[:, 0:nblk], AF.Exp, [pm.r], [self.e32.r, self.den.r],
                         accum_out=self.den.t[:, 4:5])
                self.ts(self.den.t[:, 4:5], self.den.t[:, 4:5], 1e-30, None, ALU.max, None, [self.den.r], [self.den.r])
                self.S.op("dve", lambda: nc.vector.reciprocal(out=self.den.t[:, 4:5], in_=self.den.t[:, 4:5]), [self.den.r], [self.den.r])
                if r == 0:
                    self.ts(self.impacc.t[:, 0:nblk], self.e32.t[:, 0:nblk], self.den.t[:, 4:5], None, ALU.mult, None,
                            [self.e32.r, self.den.r], [self.impacc.r])
                else:
                    self.stt(self.impacc.t[:, 0:nblk], self.e32.t[:, 0:nblk], self.den.t[:, 4:5], self.impacc.t[:, 0:nblk],
                             ALU.mult, ALU.add, [self.e32.r, self.den.r, self.impacc.r], [self.impacc.r])
            if topk:
                imp = self.imp
                self.S.op("dve", lambda: nc.vector.tensor_reduce(
                    out=imp.t[:, 0:nsb], in_=self.impacc.t[:, 0:nblk].rearrange("p (n two) -> p n two", two=2), axis=AX.X,
                    op=ALU.add), [self.impacc.r], [imp.r])
                self.ts(imp.t[:, 2 * t:2 * t + 1], imp.t[:, 2 * t:2 * t + 1], self.f12.t[:, 0:1], None, ALU.max, None,
                        [imp.r, self.f12.r], [imp.r])
                self.cp(imp.t[:, 2 * t + 1:2 * t + 2], self.f12.t[:, 1:2], [self.f12.r, imp.r], [imp.r])
                self.memset(imp.t[:, 0:1], 1e4, [imp.r], eng="dve")
                self.S.op("dve", lambda: nc.vector.max(out=self.m8.t[:, 0:8], in_=imp.t[:, 0:nsb]), [imp.r], [self.m8.r])
                self.S.op("dve", lambda: nc.vector.match_replace(out=self.imp2.t[:, 0:nsb], in_to_replace=self.m8.t[:, 0:8],
                                                                 in_values=imp.t[:, 0:nsb], imm_value=-2.0),
                          [imp.r, self.m8.r], [self.imp2.r])
                self.S.op("dve", lambda: nc.vector.max(out=self.m8.t[:, 8:16], in_=self.imp2.t[:, 0:nsb]), [self.imp2.r], [self.m8.r])
                self.ts(self.negsel.t[:, 0:nsb], imp.t[:, 0:nsb], self.m8.t[:, 15:16], NEG, ALU.is_lt, ALU.mult,
                        [imp.r, self.m8.r], [self.negsel.r])
                pt = self.next_ptr()
                self.tr(pt.t[0:nsb, 0:128], self.negsel.t[:, 0:nsb], self.ident.t[:, :], [self.negsel.r, self.ident.r], [pt.r])
                self.cp(self.negT4[g].t[0:nsb, :, :], pt.t[0:nsb, None, 0:128].broadcast_to([nsb, 4, 128]), [pt.r], [self.negT4[g].r])
        nbt = (nblk + 127) // 128
        for g in range(2):
            for bt in range(nbt):
                nb_t = min(128, nblk - bt * 128)
                last = bt == nbt - 1
                pa = self.next_patt()
                self.mm(pa.t[0:nb_t, :], self.kcT.t[:, bt * 128:bt * 128 + nb_t], qf(g), True, not last,
                        [self.kcT.r, self.qp[g].r], [pa.r])
                if last:
                    n0 = nb_t - 4
                    self.mm(pa.t[0:nb_t, :], self.cmpZ.t[0:4, 124 - n0:124 - n0 + nb_t], self.cmpR.t[0:4, :], False, True,
                            [self.cmpZ.r, self.cmpR.r], [pa.r])
                pT = self.next_pT()
                self.act(pT.t[0:nb_t, :], pa.t[0:nb_t, :], AF.Exp, [pa.r], [pT.r])
                self.pv4(g, pT, nb_t, self.vc_aug.t[0:nb_t, bt, g, :], bt == 0, last, self.vc_aug.r)
            self.combine(g, 0, True)
        def sel_qk(it):
            g, kt = it
            diag = kt == t
            pa = self.next_patt()
            self.mm(pa.t[:, :], self.ksT.t[:, kt * 128:(kt + 1) * 128], qf(g), True, not (topk or diag),
                    [self.ksT.r, self.qp[g].r], [pa.r])
            if topk:
                base = 64 * ((2 * kt) // 64)
                nrow = min(64, nsb - base)
                pi_ = kt % 32
                self.mm(pa.t[:, :], self.esmall.t[base:base + nrow, pi_ * 128:(pi_ + 1) * 128],
                        self.negT4[g].t[base:base + nrow, :, :].rearrange("p r q -> p (r q)"), False, not diag,
                        [self.esmall.r, self.negT4[g].r], [pa.r])
            if diag:
                self.mm(pa.t[:, :], self.identb.t[:, :], self.causal4.t[:, :], False, True, [self.identb.r, self.causal4.r], [pa.r])
            return pa

        def sel_pv(it, pa):
            g, kt = it
            pT = self.next_pT()
            self.act(pT.t[:, :], pa.t[:, :], AF.Exp, [pa.r], [pT.r])
            self.pv4(g, pT, 128, self.vs_aug.t[:, kt, g, :], kt == 0, kt == t, self.vs_aug.r)
            if kt == t:
                self.combine(g, 1, False)

        self.pipelined([(g, kt) for g in range(2) for kt in range(t + 1)], sel_qk, sel_pv)
        k0 = max(0, t - 4)

        def win_qk(it):
            g, kt = it
            diag, far = kt == t, kt == t - 4
            slot = kt % 6
            pa = self.next_patt()
            self.mm(pa.t[:, :], self.kwT.t[:, slot * 128:(slot + 1) * 128], qf(g), True, not (far or diag),
                    [self.kwT.r, self.qp[g].r], [pa.r])
            if far:
                self.mm(pa.t[:, :], self.identb.t[:, :], self.anti4.t[:, :], False, True, [self.identb.r, self.anti4.r], [pa.r])
            if diag:
                self.mm(pa.t[:, :], self.identb.t[:, :], self.causal4.t[:, :], False, True, [self.identb.r, self.causal4.r], [pa.r])
            return pa

        def win_pv(it, pa):
            g, kt = it
            slot = kt % 6
            pT = self.next_pT()
            self.act(pT.t[:, :], pa.t[:, :], AF.Exp, [pa.r], [pT.r])
            self.pv4(g, pT, 128, self.vw_aug.t[:, slot, g, :], kt == k0, kt == t, self.vw_aug.r)
            if kt == t:
                self.combine(g, 2, False)

        self.pipelined([(g, kt) for g in range(2) for kt in range(k0, t + 1)], win_qk, win_pv)
        if self.dbg:
            self.dma("pool", self.outs["dbg_onsa"][t * 128:(t + 1) * 128, :], self.onsa.t[:, :], [self.onsa.r], [self.ores], own=self.onsa.r)
        pt = self.next_ptr()
        for k in range(4):
            self.tr(pt.t[:, k * 128:(k + 1) * 128], self.onsa.t[:, k * 128:(k + 1) * 128], self.ident.t[:, :],
                    [self.onsa.r, self.ident.r], [pt.r])
        self.cp(self.catT.t[:, 0:4, i * 128:(i + 1) * 128], pt.t[:, :].rearrange("p (k q) -> p k q", q=128), [pt.r], [self.catT.r])

    def rglru_macro(self):
        nc = self.nc
        xc, r_, i_, a_, a2, u_, hs = self.rg
        c = self.rgc
        Wp = 256
        for j in range(4):
            self.side_jobs.append(lambda j=j: self.rglru_chunk(j))

    def rglru_chunk(self, j):
        nc = self.nc
        xc, r_, i_, a_, a2, u_, hs = self.rg
        c = self.rgc
        Wp = 256
        if True:
            xr = self.xrbuf
            self.ts(xc.t[:, :], xr.t[:, j, 0:Wp], self.convw.t[:, j, 0:1], c.t[:, 0, j:j + 1], ALU.mult, ALU.add,
                    [xr.r, self.convw.r, c.r], [xc.r])
            for k in range(1, 4):
                self.stt(xc.t[:, :], xr.t[:, j, k:k + Wp], self.convw.t[:, j, k:k + 1], xc.t[:, :], ALU.mult, ALU.add,
                         [xr.r, self.convw.r, xc.r], [xc.r])
            self.cp(self.xcb.t[:, :], xc.t[:, :], [xc.r], [self.xcb.r])
            pr = self.next_pmm()
            self.mm(pr.t[:, 0:Wp], self.wabd.t[:, j, :], self.xcb.t[:, :], True, True, [self.wabd.r, self.xcb.r], [pr.r])
            pi = self.next_pmm()
            self.mm(pi.t[:, 0:Wp], self.wxbd.t[:, j, :], self.xcb.t[:, :], True, True, [self.wxbd.r, self.xcb.r], [pi.r])
            self.act(r_.t[:, :], pr.t[:, 0:Wp], AF.Sigmoid, [pr.r, c.r], [r_.r], bias=c.t[:, 1, j:j + 1])
            self.act(i_.t[:, :], pi.t[:, 0:Wp], AF.Sigmoid, [pi.r, c.r], [i_.r], bias=c.t[:, 2, j:j + 1])
            self.act(a_.t[:, :], r_.t[:, :], AF.Exp, [r_.r, c.r], [a_.r], scale=c.t[:, 4, j:j + 1])
            self.act(a2.t[:, :], r_.t[:, :], AF.Exp, [r_.r, c.r], [a2.r], scale=c.t[:, 5, j:j + 1])
            self.ts(a2.t[:, :], a2.t[:, :], -1.0, 1.0, ALU.mult, ALU.add, [a2.r], [a2.r])
            self.act(a2.t[:, :], a2.t[:, :], AF.Sqrt, [a2.r], [a2.r])
            self.tt(u_.t[:, :], a2.t[:, :], i_.t[:, :], ALU.mult, [a2.r, i_.r], [u_.r])
            self.tt(u_.t[:, :], u_.t[:, :], xc.t[:, :], ALU.mult, [u_.r, xc.r], [u_.r])
            self.S.op("dve", lambda: nc.vector.tensor_tensor_scan(out=hs.t[:, :], data0=a_.t[:, :], data1=u_.t[:, :],
                                                                  initial=self.hstate.t[:, j:j + 1], op0=ALU.mult, op1=ALU.add),
                      [a_.r, u_.r, self.hstate.r], [hs.r])
            self.cp(self.hstate.t[:, j:j + 1], hs.t[:, Wp - 1:Wp], [hs.r], [self.hstate.r])
            self.tt(self.catT.t[:, 4 + j, 0:Wp], hs.t[:, :], self.ggr.t[:, j, 0:Wp], ALU.mult, [hs.r, self.ggr.r], [self.catT2r])
            self.cp(self.rg[1].t[:, 0:3], xr.t[:, j, Wp:Wp + 3], [xr.r, r_.r], [r_.r])
            self.cp(xr.t[:, j, 0:3], self.rg[1].t[:, 0:3], [r_.r, xr.r], [xr.r])

    def proj_fm(self, src, c0, lo, hi, post):
        slab = self.load_slab(src, 0, 8, c0, 512)
        for j in range(4):
            pm = self.next_pmm()
            for k in range(8):
                self.mm(pm.t[:, 0:hi - lo], slab.t[:, k, j * 128:(j + 1) * 128], self.hT.t[:, k, lo:hi], k == 0, k == 7,
                        [slab.r, self.hT.r], [pm.r])
            post(j, pm)

    def pass1_macro(self, m):
        I, O = self.ins, self.outs
        ws = False
        tiles = self.tiles_of(m, ws)
        Wt = 256
        for (kind, i, np_, h0) in tiles:
            t = m * self.MT + i
            self.dma("sp", self.xb[i].t[:, :], I["xp"][t * 128:(t + 1) * 128, :], [self.wres], [self.xb[i].r])
        for tile in tiles:
            self.norm_tile(tile, 0)
        self.proj_tm(tiles, ("w_in_ab", None), 0, 512, 0)
        self.proj_tm(tiles, ("w_in_ab", None), 512, 512, 512)
        self.proj_tm(tiles, ("w_in_ab", None), 1024, 280, 1024)

        def post_xr(j, pm):
            self.cp(self.xrbuf.t[:, j, 3:3 + 256], pm.t[:, 0:256], [pm.r], [self.xrbuf.r])
            if ws:
                self.cp(self.xrs.t[:, j, :], pm.t[:, 256:272], [pm.r], [self.xrs.r])

        def post_gr(j, pm):
            self.act(self.ggr.t[:, j, 0:Wt], pm.t[:, 0:Wt], AF.Gelu_apprx_tanh, [pm.r], [self.ggr.r])

        self.proj_fm(("w_in_ab", None), 1304, 0, Wt, post_xr)
        self.proj_fm(("w_in_ab", None), 1816, 0, Wt, post_gr)
        for tile in tiles:
            if tile[0] == "p":
                self.l0_post(tile, m * self.MT + tile[1])
                if tile[1] == self.MT - 1:
                    self.compress_macro(m)
        self.rglru_macro()
        for tile in tiles:
            if tile[0] == "p":
                t = m * self.MT + tile[1]
                self.rebuild_qT(tile)
                self.nsa_prompt_tile(t, tile[1])
        while self.side_jobs:
            self.side_jobs.pop(0)()
        for half in range(2):
            slab = self.load_slab(("w_out_ab", None), 0, 8, half * 512, 512)
            for tile in tiles:
                kind, i, np_, h0 = tile
                pm = self.next_pmm()
                for k in range(8):
                    self.mm(pm.t[0:np_, :], self.catT.t[:, k, h0:h0 + np_], slab.t[:, k, :], k == 0, k == 7, [self.catT.r, slab.r], [pm.r])
                self.resid_add(tile, half, pm, 0)
        if self.dbg:
            for (kind, i, np_, h0) in tiles:
                if kind == "p":
                    t = m * self.MT + i
                    self.dma("pool", O["dbg_xmid"][t * 128:(t + 1) * 128, :], self.xb[i].t[:, :], [self.xb[i].r], [self.ores], own=self.xb[i].r)
        for tile in tiles:
            self.norm_tile(tile, 1)
        self.ffn(tiles, 0, lambda tile, half, ps: self.resid_add(tile, half, ps, 1))
        for (kind, i, np_, h0) in tiles:
            if kind == "p":
                t = m * self.MT + i
                self.dma("pool", self.x1d[t * 128:(t + 1) * 128, :], self.xb[i].t[:, :], [self.xb[i].r], [self.x1res[t]], own=self.xb[i].r)
                if self.dbg:
                    self.dma("pool", O["dbg_x1"][t * 128:(t + 1) * 128, :], self.xb[i].t[:, :], [self.xb[i].r], [self.ores], own=self.xb[i].r)
            elif self.dbg:
                self.dma("pool", O["dbg_x1s"][:, :], self.xb[i].t[0:np_, :], [self.xb[i].r], [self.ores], own=self.xb[i].r)

    def rebuild_qT(self, tile):
        kind, i, np_, h0 = tile
        z = self.zb[i]
        qout = self.qb.t[0:np_, :].rearrange("p (r g d) -> p g r d", r=4, g=2, d=64)
        self.headnorm(z.t[0:np_, 0:512], np_, 8, self.qg.t, qout, [z.r, self.qg.r], [self.qb.r], shape4=(2, 4))
        self.act(self.gsig.t[0:np_, :], z.t[0:np_, 1280:1304], AF.Sigmoid, [z.r], [self.gsig.r])
        pt = self.next_ptr()
        ptb = pt.t[:].bitcast(BF16)
        for r in range(4):
            self.tr(ptb[:, r * 128:r * 128 + np_], self.qb.t[0:np_, r * 128:(r + 1) * 128], self.identb.t[0:np_, 0:np_],
                    [self.qb.r, self.identb.r], [pt.r])
        for g in range(2):
            gs_ = slice(g * 64, (g + 1) * 64)
            self.cp(self.qp[g].t[gs_, :, 0:np_], ptb[gs_, 0:512].rearrange("p (r q) -> p r q", q=128)[:, :, 0:np_], [pt.r], [self.qp[g].r])


    def state_copies(self):
        I, O = self.ins, self.outs
        for s_ in range(self.NS):
            self.dma("act", O["win_s"][s_, 0:511, :], I["state_win"][s_, 1:512, :], [self.wres], [self.ores], own=self.ores)
            self.dma("act", O["dil_s"][s_, 0:2047, :], I["state_dil"][s_, 1:2048, :], [self.wres], [self.ores], own=self.ores)
        self.dma("act", O["conv_s"][:, 0:2, :], I["state_conv"][:, 1:3, :], [self.wres], [self.ores], own=self.ores)

    def alloc_l0_samp(self):
        sb = self.sb
        self.idx = sb("idx", [128, 256], I32); self.ptbc = sb("ptbc", [128, 256], I32); self.iop = sb("iop", [128, 2])
        self.pg = [sb(f"pg{i}", [128, 512]) for i in range(3)]
        self.ksTs = sb("ksTs", [128, 16 * 128], BF16)
        self.vsas = sb("vsas", [128, 16, 2, 65], BF16)
        self.rawTzs = [sb(f"rawTzs{g}", [128, 2, 512], BF16) for g in range(2)]
        self.kcTs = sb("kcTs", [128, 64], BF16); self.vcs = sb("vcs", [128, 2, 65], BF16)
        self.hids = sb("hids", [128, 2, 2, 64], BF16)
        self.kwTs = sb("kwTs", [128, 4 * 128], BF16); self.vwas = sb("vwas", [128, 4, 2, 65], BF16)
        self.wtile = [sb(f"wtile{i}", [128, 256]) for i in range(2)]
        self.selfT = sb("selfT", [128, 2, 16], BF16)
        self.vself = sb("vself", [128, 2, 2, 65], BF16)
        self.negT4s = [sb(f"negT4s{g}", [128, 4], BF16) for g in range(2)]
        self.orow = sb("orow", [128, 3, 512]); self.osamp = sb("osamp", [128, 3, 512])
        self.cstm = sb("cstm", [128, 512])
        self.csT = sb("csT", [128, 3, 4, 16]); self.h0T = sb("h0T", [128, 4, 16]); self.hsT = sb("hsT", [128, 4, 16])

    def l0_samples_pass(self):
        I, O, nc = self.ins, self.outs, self.nc
        NS = self.NS
        tile = ("s", self.MT, NS, self.MT * 128)
        lo, hi = tile[3], tile[3] + NS
        z = self.zb[2]
        sv = lambda ap: ap.rearrange("p (h d) -> p h d", d=64)
        for b_ in (self.vsas, self.vcs, self.vwas, self.vself):
            self.memset(b_.t[:], 1.0, [b_.r])
        for g in range(2):
            self.memset(self.rawTzs[g].t[:], 0.0, [self.rawTzs[g].r])
        self.dma("sp", self.ptbc.t[:, :], I["page_table"].rearrange("s j -> (s j)").rearrange("(o n) -> o n", o=1).broadcast_to([128, 256]),
                 [self.wres], [self.ptbc.r])
        self.S.op("pool", lambda: nc.gpsimd.iota(self.iop.t[:, 0:1], pattern=[[0, 1]], base=0, channel_multiplier=1,
                                                 allow_small_or_imprecise_dtypes=True), [], [self.iop.r])
        self.ts(self.idx.t[:, :], self.ptbc.t[:, :], 128.0, self.iop.t[:, 0:1], ALU.mult, ALU.add, [self.ptbc.r, self.iop.r], [self.idx.r])
        self.dma("sp", self.xb[2].t[0:NS, :], I["xs"][:, :], [self.wres], [self.xb[2].r])
        self.norm_tile(tile, 0)
        self.proj_tm([tile], ("w_in_ab", None), 0, 512, 0)
        self.proj_tm([tile], ("w_in_ab", None), 512, 512, 512)
        self.proj_tm([tile], ("w_in_ab", None), 1024, 280, 1024)
        self.proj_fm(("w_in_ab", None), 1304, lo, hi, lambda j, pm: self.cp(self.xrs.t[:, j, :], pm.t[:, 0:NS], [pm.r], [self.xrs.r]))
        self.proj_fm(("w_in_ab", None), 1816, lo, hi,
                     lambda j, pm: self.act(self.ggr.t[:, j, lo:hi], pm.t[:, 0:NS], AF.Gelu_apprx_tanh, [pm.r], [self.ggr.r]))
        self.l0_post(tile, None)
        self.dma("sp", O["win_s"][:, 511, :], z.t[0:NS, 1024:1280], [z.r], [self.ores], own=z.r)
        self.rebuild_qT(tile)
        pt = self.next_ptr()
        self.tr(pt.t[:, 0:NS], z.t[0:NS, 768:896], self.ident.t[0:NS, 0:NS], [z.r, self.ident.r], [pt.r])
        self.tr(pt.t[:, 16:16 + NS], z.t[0:NS, 1024:1152], self.ident.t[0:NS, 0:NS], [z.r, self.ident.r], [pt.r])
        self.cp(self.selfT.t[:, :, 0:NS], pt.t[:, 0:32].rearrange("p (w s) -> p w s", s=16)[:, :, 0:NS], [pt.r], [self.selfT.r])
        self.cp(self.vself.t[0:NS, 0, :, 0:64], sv(z.t[0:NS, 896:1024]), [z.r], [self.vself.r])
        self.cp(self.vself.t[0:NS, 1, :, 0:64], sv(z.t[0:NS, 1152:1280]), [z.r], [self.vself.r])
        self.rglru_samples(tile)
        for s_ in range(NS):
            self.nsa_sample(s_)
        for br in range(3):
            for g in range(2):
                gate = self.gsig.t[0:NS, g * 12:(g + 1) * 12].rearrange("p (r b) -> p r b", b=3)[:, :, br]
                src = self.osamp.t[0:NS, br, g * 256:(g + 1) * 256].rearrange("p (r d) -> p r d", d=64)
                ov = self.onsa.t[0:NS, g * 256:(g + 1) * 256].rearrange("p (r d) -> p r d", d=64)
                gb = gate[:, :, None].broadcast_to([NS, 4, 64])
                if br == 0:
                    self.tt(ov, src, gb, ALU.mult, [self.osamp.r, self.gsig.r], [self.onsa.r])
                else:
                    tv = self.otmp.t[0:NS, :].rearrange("p (r d) -> p r d", d=64)
                    self.tt(tv, src, gb, ALU.mult, [self.osamp.r, self.gsig.r], [self.otmp.r])
                    self.tt(ov, ov, tv, ALU.add, [self.onsa.r, self.otmp.r], [self.onsa.r])
        pt = self.next_ptr()
        for k in range(4):
            self.tr(pt.t[:, k * 16:k * 16 + NS], self.onsa.t[0:NS, k * 128:(k + 1) * 128], self.ident.t[0:NS, 0:NS],
                    [self.onsa.r, self.ident.r], [pt.r])
        self.cp(self.catT.t[:, 0:4, lo:hi], pt.t[:, 0:64].rearrange("p (k q) -> p k q", q=16)[:, :, 0:NS], [pt.r], [self.catT.r])
        for half in range(2):
            slab = self.load_slab(("w_out_ab", None), 0, 8, half * 512, 512)
            pm = self.next_pmm()
            for k in range(8):
                self.mm(pm.t[0:NS, :], self.catT.t[:, k, lo:hi], slab.t[:, k, :], k == 0, k == 7, [self.catT.r, slab.r], [pm.r])
            self.resid_add(tile, half, pm, 0)
        self.norm_tile(tile, 1)
        self.ffn([tile], 0, lambda tl, half, ps: self.resid_add(tl, half, ps, 1))
        if self.dbg:
            self.dma("sp", O["dbg_x1s"][:, :], self.xb[2].t[0:NS, :], [self.xb[2].r], [self.ores], own=self.xb[2].r)

    def rglru_samples(self, tile):
        I, O, nc = self.ins, self.outs, self.nc
        NS = self.NS
        lo, hi = tile[3], tile[3] + NS
        c = self.rgc
        xc, r_, i_, a_, a2, u_, hs = [b for b in self.rg]
        w = lambda b: b.t[:, 0:NS]
        for i3 in range(3):
            self.dma("sp", self.cstm.t[0:NS, :], I["state_conv"][:, i3, :], [self.wres], [self.cstm.r])
            pt = self.next_ptr()
            for j in range(4):
                self.tr(pt.t[:, j * 16:j * 16 + NS], self.cstm.t[0:NS, j * 128:(j + 1) * 128], self.ident.t[0:NS, 0:NS],
                        [self.cstm.r, self.ident.r], [pt.r])
            self.cp(self.csT.t[:, i3, :, 0:NS], pt.t[:, 0:64].rearrange("p (j q) -> p j q", q=16)[:, :, 0:NS], [pt.r], [self.csT.r])
        self.dma("sp", self.cstm.t[0:NS, :], I["state_h"][:, :], [self.wres], [self.cstm.r])
        pt = self.next_ptr()
        for j in range(4):
            self.tr(pt.t[:, j * 16:j * 16 + NS], self.cstm.t[0:NS, j * 128:(j + 1) * 128], self.ident.t[0:NS, 0:NS],
                    [self.cstm.r, self.ident.r], [pt.r])
        self.cp(self.h0T.t[:, :, 0:NS], pt.t[:, 0:64].rearrange("p (j q) -> p j q", q=16)[:, :, 0:NS], [pt.r], [self.h0T.r])
        for j in range(4):
            self.ts(w(xc), self.csT.t[:, 0, j, 0:NS], self.convw.t[:, j, 0:1], c.t[:, 0, j:j + 1], ALU.mult, ALU.add,
                    [self.csT.r, self.convw.r, c.r], [xc.r])
            for k in (1, 2):
                self.stt(w(xc), self.csT.t[:, k, j, 0:NS], self.convw.t[:, j, k:k + 1], w(xc), ALU.mult, ALU.add,
                         [self.csT.r, self.convw.r, xc.r], [xc.r])
            self.stt(w(xc), self.xrs.t[:, j, 0:NS], self.convw.t[:, j, 3:4], w(xc), ALU.mult, ALU.add,
                     [self.xrs.r, self.convw.r, xc.r], [xc.r])
            self.cp(self.xcb.t[:, 0:NS], w(xc), [xc.r], [self.xcb.r])
            pr = self.next_pmm()
            self.mm(pr.t[:, 0:NS], self.wabd.t[:, j, :], self.xcb.t[:, 0:NS], True, True, [self.wabd.r, self.xcb.r], [pr.r])
            pi = self.next_pmm()
            self.mm(pi.t[:, 0:NS], self.wxbd.t[:, j, :], self.xcb.t[:, 0:NS], True, True, [self.wxbd.r, self.xcb.r], [pi.r])
            self.act(w(r_), pr.t[:, 0:NS], AF.Sigmoid, [pr.r, c.r], [r_.r], bias=c.t[:, 1, j:j + 1])
            self.act(w(i_), pi.t[:, 0:NS], AF.Sigmoid, [pi.r, c.r], [i_.r], bias=c.t[:, 2, j:j + 1])
            self.act(w(a_), w(r_), AF.Exp, [r_.r, c.r], [a_.r], scale=c.t[:, 4, j:j + 1])
            self.act(w(a2), w(r_), AF.Exp, [r_.r, c.r], [a2.r], scale=c.t[:, 5, j:j + 1])
            self.ts(w(a2), w(a2), -1.0, 1.0, ALU.mult, ALU.add, [a2.r], [a2.r])
            self.act(w(a2), w(a2), AF.Sqrt, [a2.r], [a2.r])
            self.tt(w(u_), w(a2), w(i_), ALU.mult, [a2.r, i_.r], [u_.r])
            self.tt(w(u_), w(u_), w(xc), ALU.mult, [u_.r, xc.r], [u_.r])
            self.tt(w(hs), w(a_), self.h0T.t[:, j, 0:NS], ALU.mult, [a_.r, self.h0T.r], [hs.r])
            self.tt(self.hsT.t[:, j, 0:NS], w(hs), w(u_), ALU.add, [hs.r, u_.r], [self.hsT.r])
            self.tt(self.catT.t[:, 4 + j, lo:hi], self.hsT.t[:, j, 0:NS], self.ggr.t[:, j, lo:hi], ALU.mult,
                    [self.hsT.r, self.ggr.r], [self.catT.r])
        for src, dst in ((self.hsT, O["h_s"][:, :]), (self.xrs, O["conv_s"][:, 2, :])):
            pt = self.next_ptr()
            for j in range(4):
                self.tr(pt.t[0:NS, j * 128:(j + 1) * 128], src.t[:, j, 0:NS], self.ident.t[:, :], [src.r, self.ident.r], [pt.r])
            self.cp(self.cstm.t[0:NS, :], pt.t[0:NS, :], [pt.r], [self.cstm.r])
            self.dma("sp", dst, self.cstm.t[0:NS, :], [self.cstm.r], [self.ores], own=self.cstm.r)

    def fin_row(self, g, br):
        o = self.pacc[g].t[0:1, 0:260].rearrange("p (r e) -> p r e", e=65)
        rd = self.den
        self.ts(rd.t[0:1, 0:4], o[:, :, 64], 1e-30, None, ALU.max, None, [self.pacc[g].r], [rd.r])
        self.S.op("dve", lambda: self.nc.vector.reciprocal(out=rd.t[0:1, 0:4], in_=rd.t[0:1, 0:4]), [rd.r], [rd.r])
        ov = self.orow.t[0:1, br, g * 256:(g + 1) * 256].rearrange("p (r d) -> p r d", d=64)
        self.tt(ov, o[:, :, 0:64], rd.t[0:1, 0:4, None].broadcast_to([1, 4, 64]), ALU.mult, [self.pacc[g].r, rd.r], [self.orow.r])

    def pv4s(self, g, pT, nk, v_ap, first, last, vres):
        for r in range(4):
            self.mm(self.pacc[g].t[0:1, r * 65:(r + 1) * 65], pT.t[0:nk, r:r + 1], v_ap, first and r == 0, last,
                    [pT.r, vres], [self.pacc[g].r], skip=True)

    def nsa_sample(self, s_):
        I, nc = self.ins, self.nc
        sv = lambda ap: ap.rearrange("p (h d) -> p h d", d=64)
        qc = lambda g: self.qp[g].t[:, :, s_]
        for pgi in range(16):
            pg = self.pg[pgi % 3]
            col = s_ * 16 + pgi
            self.S.dma_fn("pool", lambda: nc.gpsimd.indirect_dma_start(
                out=pg.t[:, :], out_offset=None, in_=I["cache"][:, :],
                in_offset=bass.IndirectOffsetOnAxis(ap=self.idx.t[:, col:col + 1], axis=0)), [self.wres, self.idx.r], [pg.r])
            pt = self.next_ptr()
            self.tr(pt.t[:, 0:128], pg.t[:, 256:384], self.ident.t[:, :], [pg.r, self.ident.r], [pt.r])
            self.tr(pt.t[:, 128:256], pg.t[:, 0:128], self.ident.t[:, :], [pg.r, self.ident.r], [pt.r])
            self.tr(pt.t[:, 256:384], pg.t[:, 128:256], self.ident.t[:, :], [pg.r, self.ident.r], [pt.r])
            self.cp(self.ksTs.t[:, pgi * 128:(pgi + 1) * 128], pt.t[:, 0:128], [pt.r], [self.ksTs.r])
            q4 = pgi % 4
            for g in range(2):
                gs_ = slice(g * 64, (g + 1) * 64)
                self.cp(self.rawTzs[g].t[gs_, :, q4 * 128:(q4 + 1) * 128], pt.t[gs_, 128:384].rearrange("p (c q) -> p c q", q=128),
                        [pt.r], [self.rawTzs[g].r])
            self.act(self.vsas.t[:, pgi, :, 0:64], sv(pg.t[:, 384:512]), AF.Copy, [pg.r], [self.vsas.r])
            if q4 == 3:
                grp = pgi // 4
                for c in range(2):
                    pm = self.next_pmm()
                    for g in range(2):
                        for l in range(32):
                            self.mm(pm.t[0:64, g * 16:(g + 1) * 16], self.w1sb.t[:, c, l, :], self.rawTzs[g].t[:, c, l:512:32],
                                    l == 0, l == 31, [self.w1sb.r, self.rawTzs[g].r], [pm.r])
                    self.act(self.hids.t[0:64, c, :, grp * 16:(grp + 1) * 16], pm.t[0:64, 0:32].rearrange("p (g n) -> p g n", n=16),
                             AF.Gelu_apprx_tanh, [pm.r, self.cmpb.r], [self.hids.r], bias=self.cmpb.t[0:64, c:c + 1])
        for c in range(2):
            pm2 = self.next_pmm()
            for g in range(2):
                self.mm(pm2.t[0:64, g * 64:(g + 1) * 64], self.hids.t[0:64, c, g, :], self.w2sb.t[0:64, c, :], True, True,
                        [self.hids.r, self.w2sb.r], [pm2.r])
            if c == 0:
                self.cp(self.kcrow.t[0:64, :], pm2.t[0:64, 0:128], [pm2.r], [self.kcrow.r])
                self.headnorm(self.kcrow.t[0:64, :], 64, 2, self.kg.t[:, 0, :], sv(self.kcrow.t[0:64, :]), [self.kcrow.r, self.kg.r],
                              [self.kcrow.r])
                pt = self.next_ptr()
                self.tr(pt.t[:, 0:64], self.kcrow.t[0:64, :], self.ident.t[0:64, 0:64], [self.kcrow.r, self.ident.r], [pt.r])
                self.cp(self.kcTs.t[:, :], pt.t[:, 0:64], [pt.r], [self.kcTs.r])
            else:
                self.cp(self.vcs.t[0:64, :, 0:64], sv(pm2.t[0:64, 0:128]), [pm2.r], [self.vcs.r])
        for g in range(2):
            pm = self.next_pmm()
            for r in range(4):
                self.mm(pm.t[0:1, r * 64:(r + 1) * 64], self.qp[g].t[:, r, s_:s_ + 1], self.kcTs.t[:, :], True, True,
                        [self.qp[g].r, self.kcTs.r], [pm.r])
            for r in range(4):
                self.act(self.e32.t[0:1, r * 64:(r + 1) * 64], pm.t[0:1, r * 64:(r + 1) * 64], AF.Exp, [pm.r], [self.e32.r, self.den.r],
                         accum_out=self.den.t[0:1, 4 + r:5 + r])
            self.ts(self.den.t[0:1, 4:8], self.den.t[0:1, 4:8], 1e-30, None, ALU.max, None, [self.den.r], [self.den.r])
            self.S.op("dve", lambda: nc.vector.reciprocal(out=self.den.t[0:1, 4:8], in_=self.den.t[0:1, 4:8]), [self.den.r], [self.den.r])
            for r in range(4):
                if r == 0:
                    self.ts(self.impacc.t[0:1, 0:64], self.e32.t[0:1, 0:64], self.den.t[0:1, 4:5], None, ALU.mult, None,
                            [self.e32.r, self.den.r], [self.impacc.r])
                else:
                    self.stt(self.impacc.t[0:1, 0:64], self.e32.t[0:1, r * 64:(r + 1) * 64], self.den.t[0:1, 4 + r:5 + r],
                             self.impacc.t[0:1, 0:64], ALU.mult, ALU.add, [self.e32.r, self.den.r, self.impacc.r], [self.impacc.r])
            imp = self.imp
            self.S.op("dve", lambda: nc.vector.tensor_reduce(out=imp.t[0:1, 0:32],
                                                             in_=self.impacc.t[0:1, 0:64].rearrange("p (n two) -> p n two", two=2),
                                                             axis=AX.X, op=ALU.add), [self.impacc.r], [imp.r])
            self.memset(imp.t[0:1, 32:33], 1e4, [imp.r], eng="dve")
            self.memset(imp.t[0:1, 0:1], 1e4, [imp.r], eng="dve")
            self.S.op("dve", lambda: nc.vector.max(out=self.m8.t[0:1, 0:8], in_=imp.t[0:1, 0:33]), [imp.r], [self.m8.r])
            self.S.op("dve", lambda: nc.vector.match_replace(out=self.imp2.t[0:1, 0:33], in_to_replace=self.m8.t[0:1, 0:8],
                                                             in_values=imp.t[0:1, 0:33], imm_value=-2.0), [imp.r, self.m8.r], [self.imp2.r])
            self.S.op("dve", lambda: nc.vector.max(out=self.m8.t[0:1, 8:16], in_=self.imp2.t[0:1, 0:33]), [self.imp2.r], [self.m8.r])
            self.ts(self.negsel.t[0:1, 0:33], imp.t[0:1, 0:33], self.m8.t[0:1, 15:16], NEG, ALU.is_lt, ALU.mult,
                    [imp.r, self.m8.r], [self.negsel.r])
            pt = self.next_ptr()
            self.tr(pt.t[0:33, 0:1], self.negsel.t[0:1, 0:33], self.ident.t[0:1, 0:1], [self.negsel.r, self.ident.r], [pt.r])
            self.cp(self.negT4s[g].t[0:33, :], pt.t[0:33, 0:1].broadcast_to([33, 4]), [pt.r], [self.negT4s[g].r])
        for g in range(2):
            pa = self.next_patt()
            self.mm(pa.t[0:64, 0:4], self.kcTs.t[:, :], qc(g), True, True, [self.kcTs.r, self.qp[g].r], [pa.r])
            pT = self.next_pT()
            self.act(pT.t[0:64, 0:4], pa.t[0:64, 0:4], AF.Exp, [pa.r], [pT.r])
            self.pv4s(g, pT, 64, self.vcs.t[0:64, g, :], True, True, self.vcs.r)
            self.fin_row(g, 0)
        def s_qk(it):
            g, kt = it
            pa = self.next_patt()
            if kt < 16:
                self.mm(pa.t[:, 0:4], self.ksTs.t[:, kt * 128:(kt + 1) * 128], qc(g), True, False, [self.ksTs.r, self.qp[g].r], [pa.r])
                self.mm(pa.t[:, 0:4], self.esmall.t[0:33, kt * 128:(kt + 1) * 128], self.negT4s[g].t[0:33, :], False, True,
                        [self.esmall.r, self.negT4s[g].r], [pa.r])
            else:
                self.mm(pa.t[0:16, 0:4], self.selfT.t[:, 0, :], qc(g), True, True, [self.selfT.r, self.qp[g].r], [pa.r])
            return pa

        def s_pv(it, pa):
            g, kt = it
            pT = self.next_pT()
            if kt < 16:
                self.act(pT.t[:, 0:4], pa.t[:, 0:4], AF.Exp, [pa.r], [pT.r])
                self.pv4s(g, pT, 128, self.vsas.t[:, kt, g, :], kt == 0, False, self.vsas.r)
            else:
                self.act(pT.t[0:16, 0:4], pa.t[0:16, 0:4], AF.Exp, [pa.r], [pT.r])
                self.ts(pT.t[0:16, 0:4], pT.t[0:16, 0:4], self.eye16.t[0:16, s_:s_ + 1], None, ALU.mult, None, [pT.r, self.eye16.r], [pT.r])
                self.pv4s(g, pT, 16, self.vself.t[0:16, 0, g, :], False, True, self.vself.r)
                self.fin_row(g, 1)

        for wt in range(4):
            w = self.wtile[wt % 2]
            self.dma("sp", w.t[:, :], I["state_win"][s_, wt * 128:(wt + 1) * 128, :], [self.wres], [w.r])
            pt = self.next_ptr()
            self.tr(pt.t[:, 0:128], w.t[:, 0:128], self.ident.t[:, :], [w.r, self.ident.r], [pt.r])
            self.cp(self.kwTs.t[:, wt * 128:(wt + 1) * 128], pt.t[:, 0:128], [pt.r], [self.kwTs.r])
            self.act(self.vwas.t[:, wt, :, 0:64], sv(w.t[:, 128:256]), AF.Copy, [w.r], [self.vwas.r])

        def w_qk(it):
            g, wt = it
            pa = self.next_patt()
            if wt < 4:
                self.mm(pa.t[:, 0:4], self.kwTs.t[:, wt * 128:(wt + 1) * 128], qc(g), True, True, [self.kwTs.r, self.qp[g].r], [pa.r])
            else:
                self.mm(pa.t[0:16, 0:4], self.selfT.t[:, 1, :], qc(g), True, True, [self.selfT.r, self.qp[g].r], [pa.r])
            return pa

        def w_pv(it, pa):
            g, wt = it
            pT = self.next_pT()
            if wt < 4:
                self.act(pT.t[:, 0:4], pa.t[:, 0:4], AF.Exp, [pa.r], [pT.r])
                self.pv4s(g, pT, 128, self.vwas.t[:, wt, g, :], wt == 0, False, self.vwas.r)
            else:
                self.act(pT.t[0:16, 0:4], pa.t[0:16, 0:4], AF.Exp, [pa.r], [pT.r])
                self.ts(pT.t[0:16, 0:4], pT.t[0:16, 0:4], self.eye16.t[0:16, s_:s_ + 1], None, ALU.mult, None, [pT.r, self.eye16.r], [pT.r])
                self.pv4s(g, pT, 16, self.vself.t[0:16, 1, g, :], False, True, self.vself.r)
                self.fin_row(g, 2)

        self.pipelined([(g, kt) for g in range(2) for kt in range(17)], s_qk, s_pv)
        self.pipelined([(g, wt) for g in range(2) for wt in range(5)], w_qk, w_pv)
        self.dma("sp", self.osamp.t[s_:s_ + 1, :, :], self.orow.t[0:1, :, :], [self.orow.r], [self.osamp.r])

    def finish_l0_outputs(self):
        O = self.outs
        if self.dbg:
            self.dma("pool", O["dbg_kcT"][:, :], self.kcT.t[:, :], [self.kcT.r], [self.ores], own=self.kcT.r)
            self.dma("pool", O["dbg_vc"][:, :], self.vc_aug.t[:, 0, :, :].rearrange("p g e -> p (g e)"), [self.vc_aug.r], [self.ores], own=self.vc_aug.r)
        self.dma("pool", O["h_p"].rearrange("(c p) -> p c", p=128), self.hstate.t[:, :], [self.hstate.r], [self.ores],
                 own=self.hstate.r, allow_slow_non_contiguous=True)
        for i_ in range(3):
            self.dma("pool", O["conv_p"][i_].rearrange("(c p) -> p c", p=128), self.xrbuf.t[:, :, i_], [self.xrbuf.r], [self.ores],
                     own=self.xrbuf.r, allow_slow_non_contiguous=True)


class ProgL1(ProgL0):
    def alloc_l1(self):
        sb = self.sb
        self.NR = min(self.NT, 18)
        self.kT2 = sb("kT2", [128, 4, self.NR * 128], BF16)
        self.v2 = sb("v2", [128, self.NR, 8, 65], BF16)
        self.wsT = sb("wsT", [128, 8, 128], BF16); self.bsT = sb("bsT", [128, 8])
        self.vgain = sb("vgain", [128, 512]); self.dqg = sb("dqg", [128, 64]); self.dkg = sb("dkg", [128, 64])
        self.dilm = sb("dilm", [128, 17 * 128], BF16)
        self.tril = sb("tril", [128, 128])
        self.sq = sb("sq1", [128, 512]); self.den = sb("den1", [128, 8])
        self.qb = sb("qb1", [128, 512], BF16); self.qp2 = [sb(f"qp2_{g}", [128, 4, 128], BF16) for g in range(2)]
        self.vnb = sb("vnb", [128, 512], BF16)
        self.oc = sb("oc", [128, 512]); self.od = sb("od", [128, 512])
        self.pT = [sb(f"pT1_{i}", [128, 512], BF16) for i in range(2)]
        self.zb = [sb(f"zc{i}", [128, 2560]) for i in range(self.MT)] + [sb("zcs", [128, 2560])]
        if self.do_samples:
            self.wsb = sb("wsb", [128, 8]); self.bsb = sb("bsb", [128, 8])
            self.dtile = [sb(f"dtile{i}", [128, 1024]) for i in range(2)]
            self.kTs = sb("kTs", [128, 4, 128], BF16); self.vas = sb("vas", [128, 8, 65], BF16)
            self.kselfT = sb("kselfT", [128, 4, 16], BF16); self.vself2 = sb("vself2", [128, 8, 65], BF16)
            self.odrow = sb("odrow", [128, 512]); self.ods = sb("ods", [128, 512])

    def setup_l1(self):
        I = self.ins
        slow = dict(allow_slow_non_contiguous=True)
        self.ld(self.vgain, I["gmlp_v_gain"][0:1, :].broadcast_to([128, 512]))
        self.ld(self.dqg, I["dil_q_gain"][0:1, :].broadcast_to([128, 64]))
        self.ts(self.dqg.t[:], self.dqg.t[:], 0.125, None, ALU.mult, None, [self.dqg.r], [self.dqg.r])
        self.ld(self.dkg, I["dil_k_gain"][0:1, :].broadcast_to([128, 64]))
        self.ld(self.dilm, I["c_dilmult"][:, :])
        self.ld(self.tril, I["c_tril"][:, :])
        self.dma("pool", self.bsT.t[:, :], I["gmlp_bs"].rearrange("g t -> t g"), [self.wres], [self.bsT.r], **slow)
        for g in range(8):
            w = self.oc
            self.dma("pool", w.t[:, 0:128], I["gmlp_ws"][g], [self.wres], [w.r])
            self.tt(w.t[:, 0:128], w.t[:, 0:128], self.tril.t[:, :], ALU.mult, [w.r, self.tril.r], [w.r])
            pt = self.next_ptr()
            self.tr(pt.t[:, 0:128], w.t[:, 0:128], self.ident.t[:, :], [w.r, self.ident.r], [pt.r])
            self.cp(self.wsT.t[:, g, :], pt.t[:, 0:128], [pt.r], [self.wsT.r])
        self.memset(self.v2.t[:], 1.0, [self.v2.r])
        if self.do_samples:
            self.dma("pool", self.wsb.t[:, :], I["gmlp_ws"][:, 0:1, 0:1].rearrange("g a b -> (a b) g").broadcast_to([128, 8]),
                     [self.wres], [self.wsb.r], **slow)
            self.dma("pool", self.bsb.t[:, :], I["gmlp_bs"][:, 0:1].rearrange("g a -> a g").broadcast_to([128, 8]),
                     [self.wres], [self.bsb.r], **slow)
            self.memset(self.vas.t[:], 1.0, [self.vas.r]); self.memset(self.vself2.t[:], 1.0, [self.vself2.r])
        self.memset(self.kT2.t[:], 0.0, [self.kT2.r])
        for g in range(2):
            self.memset(self.qp2[g].t[:], 0.0, [self.qp2[g].r])

    def l1_post(self, tile, t):
        kind, i, np_, h0 = tile
        z = self.zb[i]
        O = self.outs
        st = self.den
        sv = lambda ap: ap.rearrange("p (h d) -> p h d", d=64)
        self.act(z.t[0:np_, 0:1024], z.t[0:np_, 0:1024], AF.Gelu_apprx_tanh, [z.r], [z.r])
        self.act(self.junk.t[0:np_, 0:512], z.t[0:np_, 512:1024], AF.Square, [z.r], [self.junk.r, st.r], accum_out=st.t[0:np_, 0:1])
        self.ts(st.t[0:np_, 0:1], st.t[0:np_, 0:1], 1.0 / 512, EPS, ALU.mult, ALU.add, [st.r], [st.r])
        self.powp(st.t[0:np_, 0:1], st.t[0:np_, 0:1], self.expn.t[0:np_, 0:1], [st.r, self.expn.r], [st.r])
        self.stt(z.t[0:np_, 512:1024], z.t[0:np_, 512:1024], st.t[0:np_, 0:1], self.vgain.t[0:np_, :], ALU.mult, ALU.mult,
                 [z.r, st.r, self.vgain.r], [z.r])
        self.headnorm(z.t[0:np_, 1024:1536], np_, 8, self.dqg.t, sv(self.qb.t[0:np_, :]), [z.r, self.dqg.r], [self.qb.r])
        self.headnorm(z.t[0:np_, 1536:2048], np_, 8, self.dkg.t, sv(z.t[0:np_, 1536:2048]), [z.r, self.dkg.r], [z.r])
        pt = self.next_ptr()
        ptb = pt.t[:].bitcast(BF16)
        for j in range(4):
            self.tr(ptb[:, j * 128:j * 128 + np_], self.qb.t[0:np_, j * 128:(j + 1) * 128], self.identb.t[0:np_, 0:np_],
                    [self.qb.r, self.identb.r], [pt.r])
        for g in range(2):
            gs_ = slice(g * 64, (g + 1) * 64)
            self.cp(self.qp2[g].t[gs_, :, 0:np_], ptb[gs_, 0:512].rearrange("p (r q) -> p r q", q=128)[:, :, 0:np_], [pt.r], [self.qp2[g].r])
        if kind == "p":
            self.cp(self.vnb.t[:, :], z.t[0:128, 512:1024], [z.r], [self.vnb.r])
            base = self.T - min(2048, self.T)
            if t * 128 >= base:
                self.dma("pool", O["dil_p"][t * 128 - base:(t + 1) * 128 - base, :], z.t[0:128, 1536:2560], [z.r], [self.ores], own=z.r)
            slot = t % self.NR
            pt = self.next_ptr()
            for j in range(4):
                self.tr(pt.t[:, j * 128:(j + 1) * 128], z.t[0:128, 1536 + j * 128:1536 + (j + 1) * 128], self.ident.t[:, :],
                        [z.r, self.ident.r], [pt.r])
            self.cp(self.kT2.t[:, :, slot * 128:(slot + 1) * 128], pt.t[:, :].rearrange("p (j q) -> p j q", q=128), [pt.r], [self.kT2.r])
            self.cp(self.v2.t[:, slot, :, 0:64], sv(z.t[0:128, 2048:2560]), [z.r], [self.v2.r], eng="pool")
        else:
            self.dma("pool", O["gv_s"][:, :], z.t[0:np_, 512:1024], [z.r], [self.ores], own=z.r)

    def gmlp_prompt_tile(self, tile):
        kind, i, np_, h0 = tile
        z = self.zb[i]
        pm = self.next_pmm()
        for g in range(8):
            self.mm(pm.t[:, g * 64:(g + 1) * 64], self.wsT.t[:, g, :], self.vnb.t[:, g * 64:(g + 1) * 64], True, True,
                    [self.wsT.r, self.vnb.r], [pm.r])
        for g in range(8):
            cs = slice(g * 64, (g + 1) * 64)
            self.stt(self.oc.t[:, cs], pm.t[:, cs], self.bsT.t[:, g:g + 1], z.t[0:128, cs], ALU.add, ALU.mult,
                     [pm.r, self.bsT.r, z.r], [self.oc.r])
        pt = self.next_ptr()
        for k in range(4):
            self.tr(pt.t[:, k * 128:(k + 1) * 128], self.oc.t[:, k * 128:(k + 1) * 128], self.ident.t[:, :], [self.oc.r, self.ident.r], [pt.r])
        self.cp(self.catT.t[:, 0:4, h0:h0 + 128], pt.t[:, :].rearrange("p (k q) -> p k q", q=128), [pt.r], [self.catT.r])

    def dil_prompt_tile(self, tile, t):
        kind, i, np_, h0 = tile
        nc = self.nc
        dls = list(range(min(16, t), -1, -1))

        def d_qk(it):
            hg, idx, dl = it
            kt = t - dl
            slot = kt % self.NR
            pa = self.next_patt()
            for jj in range(4):
                h = 4 * hg + jj
                par, j = h % 2, h // 2
                self.mm(pa.t[:, jj * 128:(jj + 1) * 128], self.kT2.t[:, j, slot * 128:(slot + 1) * 128],
                        self.qp2[par].t[:, j, :], True, True, [self.kT2.r, self.qp2[par].r], [pa.r])
            return pa

        def d_pv(it, pa):
            hg, idx, dl = it
            kt = t - dl
            slot = kt % self.NR
            pT = self.next_pT()
            self.act(pT.t[:, :], pa.t[:, :], AF.Exp, [pa.r], [pT.r])
            pv = pT.t[:, :].rearrange("p (r q) -> p r q", q=128)
            self.tt(pv, pv, self.dilm.t[:, None, dl * 128:(dl + 1) * 128].broadcast_to([128, 4, 128]), ALU.mult,
                    [pT.r, self.dilm.r], [pT.r])
            for jj in range(4):
                h = 4 * hg + jj
                self.mm(self.pacc[hg].t[:, jj * 65:(jj + 1) * 65], pT.t[:, jj * 128:(jj + 1) * 128], self.v2.t[:, slot, h, :],
                        idx == 0 and jj == 0, idx == len(dls) - 1, [pT.r, self.v2.r], [self.pacc[hg].r], skip=True)
            if idx == len(dls) - 1:
                o = self.pacc[hg].t[:, 0:260].rearrange("p (r e) -> p r e", e=65)
                rd = self.den
                self.ts(rd.t[:, 0:4], o[:, :, 64], 1e-30, None, ALU.max, None, [self.pacc[hg].r], [rd.r])
                self.S.op("dve", lambda: nc.vector.reciprocal(out=rd.t[:, 0:4], in_=rd.t[:, 0:4]), [rd.r], [rd.r])
                ov = self.od.t[:, hg * 256:(hg + 1) * 256].rearrange("p (r d) -> p r d", d=64)
                self.tt(ov, o[:, :, 0:64], rd.t[:, 0:4, None].broadcast_to([128, 4, 64]), ALU.mult, [self.pacc[hg].r, rd.r], [self.od.r])

        self.pipelined([(hg, idx, dl) for hg in range(2) for idx, dl in enumerate(dls)], d_qk, d_pv)
        pt = self.next_ptr()
        for k in range(4):
            self.tr(pt.t[:, k * 128:(k + 1) * 128], self.od.t[:, k * 128:(k + 1) * 128], self.ident.t[:, :], [self.od.r, self.ident.r], [pt.r])
        self.cp(self.catT.t[:, 4:8, h0:h0 + 128], pt.t[:, :].rearrange("p (k q) -> p k q", q=128), [pt.r], [self.catT.r])

    def l1_samples(self, tile):
        I, O, nc = self.ins, self.outs, self.nc
        kind, i, NS, lo = tile
        hi = lo + NS
        z = self.zb[i]
        sv = lambda ap: ap.rearrange("p (h d) -> p h d", d=64)
        for g in range(8):
            cs = slice(g * 64, (g + 1) * 64)
            self.ts(self.oc.t[0:NS, cs], z.t[0:NS, 512 + g * 64:512 + (g + 1) * 64], self.wsb.t[0:NS, g:g + 1], self.bsb.t[0:NS, g:g + 1],
                    ALU.mult, ALU.add, [z.r, self.wsb.r, self.bsb.r], [self.oc.r])
            self.tt(self.oc.t[0:NS, cs], self.oc.t[0:NS, cs], z.t[0:NS, cs], ALU.mult, [self.oc.r, z.r], [self.oc.r])
        pt = self.next_ptr()
        for k in range(4):
            self.tr(pt.t[:, k * 16:k * 16 + NS], self.oc.t[0:NS, k * 128:(k + 1) * 128], self.ident.t[0:NS, 0:NS], [self.oc.r, self.ident.r], [pt.r])
        self.cp(self.catT.t[:, 0:4, lo:hi], pt.t[:, 0:64].rearrange("p (k q) -> p k q", q=16)[:, :, 0:NS], [pt.r], [self.catT.r])
        self.dma("sp", O["dil_s"][:, 2047, :], z.t[0:NS, 1536:2560], [z.r], [self.ores], own=z.r)
        pt = self.next_ptr()
        for j in range(4):
            self.tr(pt.t[:, j * 16:j * 16 + NS], z.t[0:NS, 1536 + j * 128:1536 + (j + 1) * 128], self.ident.t[0:NS, 0:NS], [z.r, self.ident.r], [pt.r])
        self.cp(self.kselfT.t[:, :, 0:NS], pt.t[:, 0:64].rearrange("p (j q) -> p j q", q=16)[:, :, 0:NS], [pt.r], [self.kselfT.r])
        self.cp(self.vself2.t[0:NS, :, 0:64], sv(z.t[0:NS, 2048:2560]), [z.r], [self.vself2.r])
        for s_ in range(NS):
            for pi_, (d, start) in enumerate(((1, 1920), (4, 1536), (16, 0))):
                dt_ = self.dtile[pi_ % 2]
                self.dma("sp", dt_.t[:, :], I["state_dil"][s_, start:2048:d, :], [self.wres], [dt_.r])
                pt = self.next_ptr()
                for j in range(4):
                    self.tr(pt.t[:, j * 128:(j + 1) * 128], dt_.t[:, j * 128:(j + 1) * 128], self.ident.t[:, :], [dt_.r, self.ident.r], [pt.r])
                self.cp(self.kTs.t[:, :, :], pt.t[:, :].rearrange("p (j q) -> p j q", q=128), [pt.r], [self.kTs.r])
                self.act(self.vas.t[:, :, 0:64], sv(dt_.t[:, 512:1024]), AF.Copy, [dt_.r], [self.vas.r])
                pa = self.next_patt()
                for h in range(8):
                    par, j = h % 2, h // 2
                    self.mm(pa.t[:, h:h + 1], self.kTs.t[:, j, :], self.qp2[par].t[:, j, s_:s_ + 1], True, True,
                            [self.kTs.r, self.qp2[par].r], [pa.r])
                pT = self.next_pT()
                self.act(pT.t[:, 0:8], pa.t[:, 0:8], AF.Exp, [pa.r], [pT.r])
                for h in range(8):
                    self.mm(self.pacc[h // 4].t[0:1, (h % 4) * 65:(h % 4 + 1) * 65], pT.t[:, h:h + 1], self.vas.t[:, h, :],
                            pi_ == 0 and h % 4 == 0, False, [pT.r, self.vas.r], [self.pacc[h // 4].r], skip=True)
            pa = self.next_patt()
            for h in range(8):
                par, j = h % 2, h // 2
                self.mm(pa.t[0:16, h:h + 1], self.kselfT.t[:, j, :], self.qp2[par].t[:, j, s_:s_ + 1], True, True,
                        [self.kselfT.r, self.qp2[par].r], [pa.r])
            pT = self.next_pT()
            self.act(pT.t[0:16, 0:8], pa.t[0:16, 0:8], AF.Exp, [pa.r], [pT.r])
            self.ts(pT.t[0:16, 0:8], pT.t[0:16, 0:8], self.eye16.t[0:16, s_:s_ + 1], None, ALU.mult, None, [pT.r, self.eye16.r], [pT.r])
            self.ts(pT.t[0:16, 0:8], pT.t[0:16, 0:8], 3.0, None, ALU.mult, None, [pT.r], [pT.r])
            for h in range(8):
                self.mm(self.pacc[h // 4].t[0:1, (h % 4) * 65:(h % 4 + 1) * 65], pT.t[0:16, h:h + 1], self.vself2.t[0:16, h, :],
                        False, True, [pT.r, self.vself2.r], [self.pacc[h // 4].r], skip=True)
            for hg in range(2):
                o = self.pacc[hg].t[0:1, 0:260].rearrange("p (r e) -> p r e", e=65)
                rd = self.den
                self.ts(rd.t[0:1, 0:4], o[:, :, 64], 1e-30, None, ALU.max, None, [self.pacc[hg].r], [rd.r])
                self.S.op("dve", lambda: nc.vector.reciprocal(out=rd.t[0:1, 0:4], in_=rd.t[0:1, 0:4]), [rd.r], [rd.r])
                ov = self.odrow.t[0:1, hg * 256:(hg + 1) * 256].rearrange("p (r d) -> p r d", d=64)
                self.tt(ov, o[:, :, 0:64], rd.t[0:1, 0:4, None].broadcast_to([1, 4, 64]), ALU.mult, [self.pacc[hg].r, rd.r], [self.odrow.r])
            self.dma("sp", self.ods.t[s_:s_ + 1, :], self.odrow.t[0:1, :], [self.odrow.r], [self.ods.r])
        pt = self.next_ptr()
        for k in range(4):
            self.tr(pt.t[:, k * 16:k * 16 + NS], self.ods.t[0:NS, k * 128:(k + 1) * 128], self.ident.t[0:NS, 0:NS], [self.ods.r, self.ident.r], [pt.r])
        self.cp(self.catT.t[:, 4:8, lo:hi], pt.t[:, 0:64].rearrange("p (k q) -> p k q", q=16)[:, :, 0:NS], [pt.r], [self.catT.r])

    def pass2_macro(self, m):
        I, O = self.ins, self.outs
        ws = self.do_samples and m == 0
        tiles = self.tiles_of(m, ws)
        for (kind, i, np_, h0) in tiles:
            if kind == "p":
                t = m * self.MT + i
                self.dma("sp", self.xb[i].t[:, :], self.x1d[t * 128:(t + 1) * 128, :], [self.x1res[t]], [self.xb[i].r])
        for tile in tiles:
            self.norm_tile(tile, 0)
        for s in range(5):
            self.proj_tm(tiles, ("w_in_cd", None), s * 512, 512, s * 512)
        for tile in tiles:
            if tile[0] == "p":
                t = m * self.MT + tile[1]
                self.l1_post(tile, t)
                self.gmlp_prompt_tile(tile)
                self.dil_prompt_tile(tile, t)
            else:
                self.l1_post(tile, None)
                self.l1_samples(tile)
        for half in range(2):
            slab = self.load_slab(("w_out_cd", None), 0, 8, half * 512, 512)
            for tile in tiles:
                kind, i, np_, h0 = tile
                pm = self.next_pmm()
                for k in range(8):
                    self.mm(pm.t[0:np_, :], self.catT.t[:, k, h0:h0 + np_], slab.t[:, k, :], k == 0, k == 7, [self.catT.r, slab.r], [pm.r])
                self.resid_add(tile, half, pm, 0)
        for tile in tiles:
            self.norm_tile(tile, 1)
        self.ffn(tiles, 1, lambda tile, half, ps: self.resid_add(tile, half, ps, 1))
        for (kind, i, np_, h0) in tiles:
            if kind == "p":
                t = m * self.MT + i
                self.dma("pool", O["y_p"][t * 128:(t + 1) * 128, :], self.xb[i].t[:, :], [self.xb[i].r], [self.ores], own=self.xb[i].r)
            else:
                self.dma("pool", O["y_s"][:, :], self.xb[i].t[0:np_, :], [self.xb[i].r], [self.ores], own=self.xb[i].r)

    def build(self):
        self.declare()
        self.alloc()
        self.setup_consts()
        self.setup_mod_inputs()
        common = self.st
        with ExitStack() as st1:
            self.st = st1
            self.alloc_l0()
            self.precast(["w_in_ab", "w_out_ab", "w_ffn_gate", "w_ffn_up", "w_ffn_down"])
            self.compute_mod(0)
            if self.do_l1:
                self.precast(["w_in_cd", "w_out_cd"])
            self.setup_l0()
            if self.do_samples:
                self.state_copies()
                with ExitStack() as sts:
                    self.st = sts
                    self.alloc_l0_samp()
                    self.l0_samples_pass()
                    self.S.barrier()
                self.st = st1
            with ExitStack() as stp:
                self.st = stp
                self.alloc_l0_prompt()
                for m in range(self.NM):
                    self.pass1_macro(m)
                self.finish_l0_outputs()
                self.S.barrier()
            self.st = st1
        if self.do_l1:
            with ExitStack() as st2:
                self.st = st2
                self.alloc_l1()
                self.compute_mod(1)
                self.setup_l1()
                for m in range(self.NM):
                    self.pass2_macro(m)
                self.S.barrier()
        self.st = common
        self.S.finish("sp")
        self.st.close()
        return self.nc


def core_inputs(inp, c, T, NS, b, s0):
    f = lambda a: np.ascontiguousarray(a, dtype=np.float32)
    cm = np.zeros((33, D), np.float32)
    cm[0:NS] = inp["c_sample"][s0:s0 + NS]
    cm[32] = inp["c_prompt"][b]
    m = {
        "xp": f(inp["x_prompt"][b]), "xs": f(inp["x_sample"][s0:s0 + NS, 0]), "cmat": cm,
        "norm_mix_g": f(inp["norm_mix_g"]), "norm_ffn_g": f(inp["norm_ffn_g"]), "w_ada": f(inp["w_ada"]), "b_ada": f(inp["b_ada"]),
        "w_ffn_gate": f(inp["w_ffn_gate"]), "w_ffn_up": f(inp["w_ffn_up"]), "w_ffn_down": f(inp["w_ffn_down"]),
        "w_in_ab": f(inp["w_in_ab"][0]), "w_out_ab": f(inp["w_out_ab"][0]),
        "nsa_q_gain": f(inp["nsa_q_gain"]), "nsa_k_gain": f(inp["nsa_k_gain"][0]),
        "nsa_cmp_w1": f(inp["nsa_cmp_w1"][0]), "nsa_cmp_w2": f(inp["nsa_cmp_w2"][0]), "nsa_cmp_pos": f(inp["nsa_cmp_pos"][0]),
        "rg_conv_w": f(inp["rg_conv_w"][0]), "rg_conv_b": f(inp["rg_conv_b"]), "rg_wa": f(inp["rg_wa"][0]), "rg_ba": f(inp["rg_ba"]),
        "rg_wx": f(inp["rg_wx"][0]), "rg_bx": f(inp["rg_bx"]), "rg_lambda": f(inp["rg_lambda"]),
        "w_in_cd": f(inp["w_in_cd"][0]), "w_out_cd": f(inp["w_out_cd"][0]),
        "gmlp_v_gain": f(inp["gmlp_v_gain"]), "gmlp_ws": f(inp["gmlp_ws"][0]), "gmlp_bs": f(inp["gmlp_bs"][0]),
        "dil_q_gain": f(inp["dil_q_gain"]), "dil_k_gain": f(inp["dil_k_gain"]),
        "cache": f(inp["cache_nsa_kv"][0]).reshape(-1, 512),
        "state_win": f(inp["state_nsa_win"][0, s0:s0 + NS]).reshape(NS, 512, 256),
        "state_h": f(inp["state_rglru_h"][0, s0:s0 + NS]), "state_conv": f(inp["state_rglru_conv"][0, s0:s0 + NS]),
        "state_dil": f(inp["state_dil_kv"][0, s0:s0 + NS]).reshape(NS, 2048, 1024),
        "page_table": np.ascontiguousarray(inp["page_table"][s0:s0 + NS], dtype=np.int32),
    }
    m.update(make_consts(T))
    return m


T_FULL = 8192
NS_CORE = 16
N_CORES = 8


def kernel(**inputs):
    inp = {k: np.asarray(v) for k, v in inputs.items()}
    T, NS = T_FULL, NS_CORE
    prog = ProgL1(T, NS, npool_rows=inp["cache_nsa_kv"].shape[1] * 128, dbg=False)
    nc = prog.build()
    in_maps = [core_inputs(inp, c, T, NS, b=c % 2, s0=NS * c) for c in range(N_CORES)]
    res = run_bass_kernel_spmd(nc, in_maps, core_ids=list(range(N_CORES))).results
    cat = lambda name: np.concatenate([np.asarray(res[c][name]) for c in range(N_CORES)], axis=0)
    two = lambda name: np.stack([np.asarray(res[0][name]), np.asarray(res[1][name])], axis=0)
    f32 = lambda a: np.ascontiguousarray(a, dtype=np.float32)
    out = (
        f32(two("y_p").reshape(2, T, D)),
        f32(cat("y_s").reshape(128, 1, D)),
        f32(two("kv_p").reshape(1, 2, T, 4, 2, 64)),
        f32(cat("kv_s").reshape(1, 128, 1, 4, 2, 64)),
        f32(two("win_p").reshape(1, 2, 512, 2, 2, 64)),
        f32(cat("win_s").reshape(1, 128, 512, 2, 2, 64)),
        f32(two("h_p").reshape(1, 2, 512)),
        f32(cat("h_s").reshape(1, 128, 512)),
        f32(two("conv_p").reshape(1, 2, 3, 512)),
        f32(cat("conv_s").reshape(1, 128, 3, 512)),
        f32(two("dil_p").reshape(1, 2, 2048, 2, 8, 64)),
        f32(cat("dil_s").reshape(1, 128, 2048, 2, 8, 64)),
        f32(cat("gv_s").reshape(1, 128, 1, 512)),
    )
    return out
```
